# Optimizing a Trainium2 kernel written in Bass

```python
import math
import jax, jax.numpy as jnp
from jax import lax
import numpy as np

D_MODEL = 2048
BATCH = 4
SEQ = 4096
DEPTH = 4

GRID_W = 64
CTX_LEN = 256

f32 = jnp.float32
NORM_EPS = 1e-6
ROPE_BASE = 10000.0
ROT_DIM = 64
Q_BLOCK = 128
GROUP_WIDTH = D_MODEL // 4
MIX_WIDTH = 4 * GROUP_WIDTH

DA_HEAD_DIM = ROT_DIM
DA_V_DIM = 2 * DA_HEAD_DIM
DA_HEADS = GROUP_WIDTH // DA_V_DIM
DA_QK_WIDTH = 2 * DA_HEADS * DA_HEAD_DIM
DA_WIDTH = DA_HEADS * DA_V_DIM
DA_SCALE = DA_HEAD_DIM ** -0.5

S5_WIDTH = GROUP_WIDTH
S5_GROUP = 16
S5_GROUPS = S5_WIDTH // S5_GROUP
S5_STATE = 64
S5_DT_MIN = 0.001
S5_DT_MAX = 0.1

MLA_NOPE_DIM = 128
MLA_ROPE_DIM = ROT_DIM
MLA_V_DIM = 128
MLA_HEADS = GROUP_WIDTH // MLA_V_DIM
MLA_Q_RANK = 512
MLA_KV_RANK = 256
MLA_IN_WIDTH = MLA_Q_RANK + MLA_KV_RANK + MLA_ROPE_DIM
MLA_WIDTH = MLA_HEADS * MLA_V_DIM
MLA_SCALE = (MLA_NOPE_DIM + MLA_ROPE_DIM) ** -0.5

CONV_CH = GROUP_WIDTH
CONV_K = 31

IN_SPLITS = (DA_QK_WIDTH,
             2 * DA_QK_WIDTH,
             2 * DA_QK_WIDTH + DA_WIDTH,
             2 * DA_QK_WIDTH + DA_WIDTH + S5_WIDTH,
             2 * DA_QK_WIDTH + DA_WIDTH + S5_WIDTH + MLA_IN_WIDTH)
IN_WIDTH = IN_SPLITS[-1] + 2 * CONV_CH

D_FF = -(-8 * D_MODEL // (3 * 256)) * 256

kernel_name = 'hybrid_parallel_groups_diffusion_block'


def _rms(x, g):
    xf = x.astype(f32)
    y = xf * lax.rsqrt(jnp.mean(xf * xf, axis=-1, keepdims=True) + NORM_EPS)
    return (y * g.astype(f32)).astype(x.dtype)


def _layernorm(x, g, b):
    xf = x.astype(f32)
    mu = jnp.mean(xf, axis=-1, keepdims=True)
    var = jnp.mean(jnp.square(xf - mu), axis=-1, keepdims=True)
    y = (xf - mu) * lax.rsqrt(var + NORM_EPS) * g.astype(f32) + b.astype(f32)
    return y.astype(x.dtype)


def _modulate(h, shift, scale):
    return h * (1.0 + scale) + shift


def _axial_rope_tables(n_tokens, rot_dim):
    n_rows = n_tokens // GRID_W
    row = jnp.repeat(jnp.arange(n_rows, dtype=f32), GRID_W)
    col = jnp.tile(jnp.arange(GRID_W, dtype=f32), n_rows)
    n_freq = rot_dim // 4
    inv = ROPE_BASE ** (-jnp.arange(n_freq, dtype=f32) / n_freq)
    ang = jnp.concatenate([row[:, None] * inv, col[:, None] * inv], axis=-1)
    return jnp.cos(ang), jnp.sin(ang)


def _rope(x, cos, sin):
    half = x.shape[-1] // 2
    x1, x2 = x[..., :half], x[..., half:]
    c = cos[None, :, None, :].astype(x.dtype)
    s = sin[None, :, None, :].astype(x.dtype)
    return jnp.concatenate([x1 * c - x2 * s, x1 * s + x2 * c], axis=-1)


def _attend(q, k, v, scale):
    bsz, lq, nh, dq = q.shape
    nb = lq // Q_BLOCK
    qb = q.reshape(bsz, nb, Q_BLOCK, nh, dq).transpose(1, 0, 2, 3, 4)

    def block(qblk):
        s = jnp.einsum('bqhd,bkhd->bhqk', qblk, k).astype(f32) * scale
        p = jax.nn.softmax(s, axis=-1).astype(v.dtype)
        return jnp.einsum('bhqk,bkhe->bqhe', p, v)

    o = lax.map(block, qb)
    return o.transpose(1, 0, 2, 3, 4).reshape(bsz, lq, nh, v.shape[-1])


def _da_heads(q, k, v):
    bsz, n, _ = q.shape
    q = q.reshape(bsz, n, DA_HEADS, 2, DA_HEAD_DIM)
    k = k.reshape(bsz, n, DA_HEADS, 2, DA_HEAD_DIM)
    q = jnp.concatenate([q[:, :, :, 0], q[:, :, :, 1]], axis=2)
    k = jnp.concatenate([k[:, :, :, 0], k[:, :, :, 1]], axis=2)
    v = v.reshape(bsz, n, DA_HEADS, DA_V_DIM)
    return q, k, jnp.concatenate([v, v], axis=2)


def _diff_combine(o, lam, subln_g, lambda_init):
    bsz, n = o.shape[0], o.shape[1]
    d = o[:, :, :DA_HEADS] - lam.astype(o.dtype) * o[:, :, DA_HEADS:]
    return (_rms(d, subln_g) * (1.0 - lambda_init)).reshape(bsz, n, DA_WIDTH)


def _mla_qkv(pm, p, rope):
    bsz, n, _ = pm.shape
    cq, ckv, kr = jnp.split(pm, [MLA_Q_RANK, MLA_Q_RANK + MLA_KV_RANK], axis=-1)
    q = (_rms(cq, p['mla_q_norm']) @ p['mla_w_uq']).reshape(bsz, n, MLA_HEADS, MLA_NOPE_DIM + MLA_ROPE_DIM)
    kv = (_rms(ckv, p['mla_kv_norm']) @ p['mla_w_ukv']).reshape(bsz, n, MLA_HEADS, MLA_NOPE_DIM + MLA_V_DIM)
    q_nope, q_rope = q[..., :MLA_NOPE_DIM], q[..., MLA_NOPE_DIM:]
    k_nope, v = kv[..., :MLA_NOPE_DIM], kv[..., MLA_NOPE_DIM:]
    kr = kr[:, :, None, :]
    if rope is not None:
        q_rope = _rope(q_rope, rope[0], rope[1])
        kr = _rope(kr, rope[0], rope[1])
    q = jnp.concatenate([q_nope, q_rope], axis=-1)
    k = jnp.concatenate([k_nope, jnp.broadcast_to(kr, (bsz, n, MLA_HEADS, MLA_ROPE_DIM))], axis=-1)
    return q, k, v


def _cmul(ar, ai, br, bi):
    return ar * br - ai * bi, ar * bi + ai * br


def _scan_combine(e1, e2):
    a1r, a1i, b1r, b1i = e1
    a2r, a2i, b2r, b2i = e2
    ar, ai = _cmul(a2r, a2i, a1r, a1i)
    br, bi = _cmul(a2r, a2i, b1r, b1i)
    return ar, ai, br + b2r, bi + b2i


def _s5_discretise(lam_re, lam_im, log_step, b_re, b_im):
    lam_re = lam_re.astype(f32)
    lam_im = lam_im.astype(f32)
    step = jnp.exp(log_step.astype(f32))[:, None]
    mag = jnp.exp(lam_re * step)
    abar_re = mag * jnp.cos(lam_im * step)
    abar_im = mag * jnp.sin(lam_im * step)
    den = lam_re * lam_re + lam_im * lam_im
    f_re = ((abar_re - 1.0) * lam_re + abar_im * lam_im) / den
    f_im = (abar_im * lam_re - (abar_re - 1.0) * lam_im) / den
    bbar_re, bbar_im = _cmul(f_re[..., None], f_im[..., None], b_re.astype(f32), b_im.astype(f32))
    return abar_re, abar_im, bbar_re, bbar_im


def _s5_scan(u, abar_re, abar_im, bbar_re, bbar_im, h0_re, h0_im, reverse):
    if reverse:
        u = jnp.flip(u, axis=1)
    bu_re = jnp.einsum('blgh,gph->blgp', u, bbar_re)
    bu_im = jnp.einsum('blgh,gph->blgp', u, bbar_im)
    i_re, i_im = _cmul(abar_re, abar_im, h0_re, h0_im)
    bu_re = bu_re.at[:, 0].add(i_re)
    bu_im = bu_im.at[:, 0].add(i_im)
    a_re = jnp.broadcast_to(abar_re, bu_re.shape)
    a_im = jnp.broadcast_to(abar_im, bu_im.shape)
    _, _, s_re, s_im = lax.associative_scan(_scan_combine, (a_re, a_im, bu_re, bu_im), axis=1)
    fin_re, fin_im = s_re[:, -1], s_im[:, -1]
    if reverse:
        s_re = jnp.flip(s_re, axis=1)
        s_im = jnp.flip(s_im, axis=1)
    return s_re, s_im, fin_re, fin_im


def _s5_input(ps):
    bsz, n, _ = ps.shape
    return ps.astype(f32).reshape(bsz, n, S5_GROUPS, S5_GROUP)


def _s5_readout(u, states, p, out_dtype):
    bsz, n = u.shape[0], u.shape[1]
    y = p['s5_d'].astype(f32).reshape(S5_GROUPS, S5_GROUP) * u
    for d in range(2):
        s_re, s_im = states[d][0], states[d][1]
        y = y + jnp.einsum('blgp,ghp->blgh', s_re, p['s5_c_re'][d].astype(f32)) \
              - jnp.einsum('blgp,ghp->blgh', s_im, p['s5_c_im'][d].astype(f32))
    g = jax.nn.gelu(y.reshape(bsz, n, S5_WIDTH))
    return (g * jax.nn.sigmoid(g @ p['s5_w_glu'].astype(f32))).astype(out_dtype)


def _conv_module(pcv, p):
    val, gate = pcv[..., :CONV_CH], pcv[..., CONV_CH:]
    u = val * jax.nn.sigmoid(gate)
    w = p['conv_w'].astype(u.dtype)[:, None, :]
    y = lax.conv_general_dilated(u, w, window_strides=(1,), padding=[(CONV_K // 2, CONV_K // 2)],
                                 dimension_numbers=('NWC', 'WIO', 'NWC'), feature_group_count=CONV_CH)
    y = y + p['conv_b']
    return jax.nn.silu(_layernorm(y, p['conv_ln_g'], p['conv_ln_b']))


def _mixer(hx, hc, p, rope, lambda_init, ctx_out):
    bsz, n, _ = hx.shape
    cos, sin = rope
    px = jnp.split(hx @ p['w_in'], IN_SPLITS, axis=-1)
    pc = jnp.split(hc @ p['w_in'], IN_SPLITS, axis=-1)

    qx, kx, vx = _da_heads(px[0], px[1], px[2])
    qx = _rope(qx, cos, sin)
    kx = _rope(kx, cos, sin)
    qc, kc, vc = _da_heads(pc[0], pc[1], pc[2])
    lam = (jnp.exp(jnp.sum(p['da_lam_q1'].astype(f32) * p['da_lam_k1'].astype(f32)))
           - jnp.exp(jnp.sum(p['da_lam_q2'].astype(f32) * p['da_lam_k2'].astype(f32))) + lambda_init)
    y_da_x = _diff_combine(_attend(qx, jnp.concatenate([kc, kx], 1), jnp.concatenate([vc, vx], 1), DA_SCALE),
                           lam, p['da_subln'], lambda_init)

    disc = [_s5_discretise(p['s5_lam_re'][d], p['s5_lam_im'][d], p['s5_log_step'][d],
                           p['s5_b_re'][d], p['s5_b_im'][d]) for d in range(2)]
    u_x = _s5_input(px[3])
    u_c = _s5_input(pc[3])
    zero = jnp.zeros((bsz, S5_GROUPS, S5_STATE), f32)
    st_c = [_s5_scan(u_c, *disc[d], zero, zero, d == 1) for d in range(2)]
    st_x = [_s5_scan(u_x, *disc[d], st_c[d][2], st_c[d][3], d == 1) for d in range(2)]
    y_s5_x = _s5_readout(u_x, st_x, p, hx.dtype)

    mq_x, mk_x, mv_x = _mla_qkv(px[4], p, rope)
    mq_c, mk_c, mv_c = _mla_qkv(pc[4], p, None)
    y_mla_x = _attend(mq_x, jnp.concatenate([mk_c, mk_x], 1), jnp.concatenate([mv_c, mv_x], 1),
                      MLA_SCALE).reshape(bsz, n, MLA_WIDTH)

    y_cv_x = _conv_module(px[5], p)

    yx = jnp.concatenate([y_da_x, y_s5_x, y_mla_x, y_cv_x], axis=-1) @ p['w_out']
    if not ctx_out:
        return yx, None

    m = hc.shape[1]
    y_da_c = _diff_combine(_attend(qc, kc, vc, DA_SCALE), lam, p['da_subln'], lambda_init)
    y_s5_c = _s5_readout(u_c, st_c, p, hc.dtype)
    y_mla_c = _attend(mq_c, mk_c, mv_c, MLA_SCALE).reshape(bsz, m, MLA_WIDTH)
    y_cv_c = _conv_module(pc[5], p)
    yc = jnp.concatenate([y_da_c, y_s5_c, y_mla_c, y_cv_c], axis=-1) @ p['w_out']
    return yx, yc


def _swiglu(h, w_in, w_out):
    gu = h @ w_in
    return (jax.nn.silu(gu[..., :D_FF]) * gu[..., D_FF:]) @ w_out


def setup_inputs(seed: int = 0) -> dict:
    key = jax.random.key(seed)
    ks = iter(jax.random.split(key, 48))

    def nrm(shape, scale):
        return jax.random.normal(next(ks), shape, f32) * scale

    def gain(shape):
        return 1.0 + nrm(shape, 0.02)

    L, G, P, H = DEPTH, S5_GROUPS, S5_STATE, S5_GROUP
    return {
        'x': nrm((BATCH, SEQ, D_MODEL), 1.0),
        'c': nrm((BATCH, D_MODEL), 1.0),
        'ctx': nrm((BATCH, CTX_LEN, D_MODEL), 1.0),
        'c_ctx': nrm((D_MODEL,), 1.0),
        'w_ada': nrm((L, D_MODEL, 6 * D_MODEL), 0.5 * D_MODEL ** -0.5),
        'b_ada': nrm((L, 6 * D_MODEL), 0.02),
        'norm_pre_mix': gain((L, D_MODEL)),
        'norm_post_mix': gain((L, D_MODEL)),
        'norm_pre_ffn': gain((L, D_MODEL)),
        'norm_post_ffn': gain((L, D_MODEL)),
        'w_in': nrm((L, D_MODEL, IN_WIDTH), D_MODEL ** -0.5),
        'w_out': nrm((L, MIX_WIDTH, D_MODEL), MIX_WIDTH ** -0.5),
        'da_lam_q1': nrm((L, DA_HEAD_DIM), 0.1),
        'da_lam_k1': nrm((L, DA_HEAD_DIM), 0.1),
        'da_lam_q2': nrm((L, DA_HEAD_DIM), 0.1),
        'da_lam_k2': nrm((L, DA_HEAD_DIM), 0.1),
        'da_subln': gain((L, DA_V_DIM)),
        's5_lam_re': -0.5 + nrm((L, 2, G, P), 0.01),
        's5_lam_im': jnp.arange(P, dtype=f32) * math.pi + nrm((L, 2, G, P), 0.01),
        's5_log_step': jax.random.uniform(next(ks), (L, 2, G), f32, math.log(S5_DT_MIN), math.log(S5_DT_MAX)),
        's5_b_re': nrm((L, 2, G, P, H), (2 * H) ** -0.5),
        's5_b_im': nrm((L, 2, G, P, H), (2 * H) ** -0.5),
        's5_c_re': nrm((L, 2, G, H, P), (2 * P) ** -0.5 * 4.0),
        's5_c_im': nrm((L, 2, G, H, P), (2 * P) ** -0.5 * 4.0),
        's5_d': nrm((L, S5_WIDTH), 0.5),
        's5_w_glu': nrm((L, S5_WIDTH, S5_WIDTH), S5_WIDTH ** -0.5),
        'mla_q_norm': gain((L, MLA_Q_RANK)),
        'mla_kv_norm': gain((L, MLA_KV_RANK)),
        'mla_w_uq': nrm((L, MLA_Q_RANK, MLA_HEADS * (MLA_NOPE_DIM + MLA_ROPE_DIM)), MLA_Q_RANK ** -0.5),
        'mla_w_ukv': nrm((L, MLA_KV_RANK, MLA_HEADS * (MLA_NOPE_DIM + MLA_V_DIM)), MLA_KV_RANK ** -0.5),
        'conv_w': nrm((L, CONV_K, CONV_CH), CONV_K ** -0.5),
        'conv_b': nrm((L, CONV_CH), 0.02),
        'conv_ln_g': gain((L, CONV_CH)),
        'conv_ln_b': nrm((L, CONV_CH), 0.02),
        'w_ffn_in': nrm((L, D_MODEL, 2 * D_FF), D_MODEL ** -0.5),
        'w_ffn_out': nrm((L, D_FF, D_MODEL), D_FF ** -0.5),
    }


def reference(x, c, ctx, c_ctx, w_ada, b_ada, norm_pre_mix, norm_post_mix, norm_pre_ffn, norm_post_ffn,
              w_in, w_out, da_lam_q1, da_lam_k1, da_lam_q2, da_lam_k2, da_subln,
              s5_lam_re, s5_lam_im, s5_log_step, s5_b_re, s5_b_im, s5_c_re, s5_c_im, s5_d, s5_w_glu,
              mla_q_norm, mla_kv_norm, mla_w_uq, mla_w_ukv,
              conv_w, conv_b, conv_ln_g, conv_ln_b, w_ffn_in, w_ffn_out):
    rope = _axial_rope_tables(x.shape[1], ROT_DIM)
    silu_c = jax.nn.silu(c)
    silu_cc = jax.nn.silu(c_ctx)
    for l in range(DEPTH):
        last = l == DEPTH - 1
        lambda_init = 0.8 - 0.6 * math.exp(-0.3 * l)
        p = {
            'w_in': w_in[l], 'w_out': w_out[l],
            'da_lam_q1': da_lam_q1[l], 'da_lam_k1': da_lam_k1[l],
            'da_lam_q2': da_lam_q2[l], 'da_lam_k2': da_lam_k2[l], 'da_subln': da_subln[l],
            's5_lam_re': s5_lam_re[l], 's5_lam_im': s5_lam_im[l], 's5_log_step': s5_log_step[l],
            's5_b_re': s5_b_re[l], 's5_b_im': s5_b_im[l], 's5_c_re': s5_c_re[l], 's5_c_im': s5_c_im[l],
            's5_d': s5_d[l], 's5_w_glu': s5_w_glu[l],
            'mla_q_norm': mla_q_norm[l], 'mla_kv_norm': mla_kv_norm[l],
            'mla_w_uq': mla_w_uq[l], 'mla_w_ukv': mla_w_ukv[l],
            'conv_w': conv_w[l], 'conv_b': conv_b[l], 'conv_ln_g': conv_ln_g[l], 'conv_ln_b': conv_ln_b[l],
        }
        mod_x = jnp.split((silu_c @ w_ada[l] + b_ada[l])[:, None, :], 6, axis=-1)
        mod_c = jnp.split(silu_cc @ w_ada[l] + b_ada[l], 6, axis=-1)

        hx = _modulate(_rms(x, norm_pre_mix[l]), mod_x[0], mod_x[1])
        hc = _modulate(_rms(ctx, norm_pre_mix[l]), mod_c[0], mod_c[1])
        yx, yc = _mixer(hx, hc, p, rope, lambda_init, not last)

        x = x + mod_x[2] * _rms(yx, norm_post_mix[l])
        fx = _swiglu(_modulate(_rms(x, norm_pre_ffn[l]), mod_x[3], mod_x[4]), w_ffn_in[l], w_ffn_out[l])
        x = x + mod_x[5] * _rms(fx, norm_post_ffn[l])

        if not last:
            ctx = ctx + mod_c[2] * _rms(yc, norm_post_mix[l])
            fc = _swiglu(_modulate(_rms(ctx, norm_pre_ffn[l]), mod_c[3], mod_c[4]), w_ffn_in[l], w_ffn_out[l])
            ctx = ctx + mod_c[5] * _rms(fc, norm_post_ffn[l])
    return x
```

```python
import os
import contextlib
import math
import numpy as np
import concourse.bass as bass
import concourse.mybir as mybir
from concourse.bass_utils import run_bass_kernel_spmd

F32 = mybir.dt.float32
BF16 = mybir.dt.bfloat16
ALU = mybir.AluOpType
AF = mybir.ActivationFunctionType

D = 2048
NCTX = 256
NLAT = 4096
NT = NCTX + NLAT
DEPTH = 4
DFF = 5632
INW = 3904
CHUNKS = [(0, 256)] + [(256 + 512 * i, 512) for i in range(8)]
EPS = 1e-6


class Res:
    __slots__ = ("lw", "rd")

    def __init__(self):
        self.lw = None
        self.rd = []


class Op:
    __slots__ = ("eng", "fn", "deps", "sig", "key", "inc", "val")


class Prog:
    NQ = 12

    def __init__(self, nc, st):
        self.nc = nc
        self.sem = {}
        for k in ["pe", "act", "dve", "pool"]:
            self.sem[k] = st.enter_context(nc.semaphore("s_" + k))
        for q in ["sp", "pool"]:
            for s in range(self.NQ):
                k = "d_%s_%d" % (q, s)
                self.sem[k] = st.enter_context(nc.semaphore("s_" + k))
        self.semval = {k: 0 for k in self.sem}
        self.dqn = {"sp": 0, "pool": 0}
        self.slot_last = {}
        self.reset_stage()
        self.res = []

    def R(self):
        r = Res()
        self.res.append(r)
        return r

    def reset_stage(self):
        self.ops = {e: [] for e in ("pe", "act", "dve", "pool", "sp")}
        self.order = []

    def _mk(self, eng, fn, reads, writes, key, inc, acc=False):
        op = Op()
        op.eng = eng; op.fn = fn; op.sig = False; op.key = key; op.inc = inc; op.val = None
        deps = []
        for r in reads:
            if r.lw is not None:
                deps.append(r.lw)
        for r in writes:
            if r.lw is not None:
                deps.append(r.lw)
            deps.extend(r.rd)
        if acc:
            deps = [d for d in deps if not (d.eng == eng and d.key == key)]
        op.deps = deps
        for r in reads:
            r.rd.append(op)
        for r in writes:
            r.lw = op
            r.rd = []
        self.ops[eng].append(op)
        self.order.append(op)
        return op

    def pe(self, fn, r=(), w=(), acc=True):
        return self._mk("pe", fn, r, w, "pe", 1, acc)

    def act(self, fn, r=(), w=()):
        return self._mk("act", fn, r, w, "act", 1)

    def dve(self, fn, r=(), w=()):
        return self._mk("dve", fn, r, w, "dve", 1)

    def pool(self, fn, r=(), w=()):
        return self._mk("pool", fn, r, w, "pool", 1)

    def dma(self, q, fn, r=(), w=()):
        i = self.dqn[q]
        self.dqn[q] += 1
        key = "d_%s_%d" % (q, i % self.NQ)
        op = self._mk(q, fn, r, w, key, 16)
        prev = self.slot_last.get(key)
        if prev is not None:
            op.deps.append(prev)
        self.slot_last[key] = op
        op.sig = True
        return op

    def ld(self, fn, r=(), w=()):
        return self.dma("sp", fn, r, w)

    def stq(self, fn, r=(), w=()):
        return self.dma("pool", fn, r, w)

    def end_stage(self):
        lasts = []
        for e in ("pe", "act", "dve", "pool"):
            c = [o for o in self.ops[e] if o.key == e]
            if c:
                lasts.append(c[-1])
        for k, o in self.slot_last.items():
            if o is not None:
                lasts.append(o)
        for e in ("pe", "act", "dve", "pool", "sp"):
            b = Op()
            b.eng = e; b.fn = None; b.deps = list(lasts); b.sig = False; b.key = None; b.inc = 0; b.val = None
            self.ops[e].append(b)
            self.order.append(b)
        staged = set(id(o) for o in self.order)
        for o in self.order:
            o.deps = [d for d in o.deps if id(d) in staged]
            for d in o.deps:
                d.sig = True
        for e in ("pe", "act", "dve", "pool", "sp"):
            for o in self.ops[e]:
                if o.fn is not None and o.sig:
                    self.semval[o.key] += o.inc
                    o.val = self.semval[o.key]
        nc = self.nc
        sem = self.sem

        def run(engine, lst):
            known = {}
            for o in lst:
                need = {}
                for d in o.deps:
                    if need.get(d.key, 0) < d.val:
                        need[d.key] = d.val
                for k, v in need.items():
                    if known.get(k, 0) >= v:
                        continue
                    engine.wait_ge(sem[k], v)
                    known[k] = v
                if o.fn is None:
                    continue
                ins = o.fn(engine)
                if o.sig:
                    ins.then_inc(sem[o.key], o.inc)

        with nc.Block() as block:
            @block.tensor
            def _(e):
                run(e, self.ops["pe"])

            @block.scalar
            def _(e):
                run(e, self.ops["act"])

            @block.vector
            def _(e):
                run(e, self.ops["dve"])

            @block.gpsimd
            def _(e):
                run(e, self.ops["pool"])

            @block.sync
            def _(e):
                run(e, self.ops["sp"])
        for r in self.res:
            r.lw = None
            r.rd = []
        self.res = []
        self.slot_last = {}
        self.reset_stage()


class Stage:
    CNT = 0

    def __init__(self, P):
        self.P = P
        self.nc = P.nc
        self.st = contextlib.ExitStack()
        self.n = 0

    def __enter__(self):
        self.st.__enter__()
        return self

    def __exit__(self, *a):
        self.P.end_stage()
        return self.st.__exit__(*a)

    def sb(self, shape, dt, nbuf=1):
        out = []
        for i in range(nbuf):
            Stage.CNT += 1
            t = self.st.enter_context(self.nc.sbuf_tensor("t%d" % Stage.CNT, list(shape), dt))
            out.append((t, self.P.R()))
        return out if nbuf > 1 else out[0]

    def ps(self, shape, dt, nbuf=1):
        out = []
        for i in range(nbuf):
            Stage.CNT += 1
            t = self.st.enter_context(self.nc.psum_tensor("p%d" % Stage.CNT, list(shape), dt))
            out.append((t, self.P.R()))
        return out if nbuf > 1 else out[0]


class Rot:
    def __init__(self, items):
        self.items = [items] if isinstance(items, tuple) else items
        self.i = 0

    def next(self):
        x = self.items[self.i % len(self.items)]
        self.i += 1
        return x


def build(nlayers=DEPTH, stop_after=None, taps=False):
    nc = bass.Bass("TRN2", target_bir_lowering=False)
    kin = "ExternalInput"
    dbg = "ExternalOutput" if taps else "Internal"

    def din(name, shape, dt=F32):
        return nc.dram_tensor(name, list(shape), dt, kind=kin).ap()

    def dscr(name, shape, dt=F32):
        return nc.dram_tensor(name, list(shape), dt, kind=dbg).ap()

    I = {}
    I["xc"] = din("xc", [NT, D])
    I["cvec"] = din("cvec", [2, D])
    I["ropeC"] = din("ropeC", [128, NT])
    I["ropeS"] = din("ropeS", [128, NT])
    I["perm"] = din("perm", [128, 128])
    I["ident"] = din("ident", [128, 128])
    shapes = dict(
        w_ada=[DEPTH, D, 6 * D], b_ada=[DEPTH, 6 * D], norm_pre_mix=[DEPTH, D], norm_post_mix=[DEPTH, D],
        norm_pre_ffn=[DEPTH, D], norm_post_ffn=[DEPTH, D], w_in=[DEPTH, D, INW], w_out=[DEPTH, D, D],
        da_lam_q1=[DEPTH, 64], da_lam_k1=[DEPTH, 64], da_lam_q2=[DEPTH, 64], da_lam_k2=[DEPTH, 64], da_subln=[DEPTH, 128],
        s5_lam_re=[DEPTH, 2, 32, 64], s5_lam_im=[DEPTH, 2, 32, 64], s5_log_step=[DEPTH, 2, 32],
        s5_b_re=[DEPTH, 2, 32, 64, 16], s5_b_im=[DEPTH, 2, 32, 64, 16], s5_c_re=[DEPTH, 2, 32, 16, 64],
        s5_c_im=[DEPTH, 2, 32, 16, 64], s5_d=[DEPTH, 512], s5_w_glu=[DEPTH, 512, 512], mla_q_norm=[DEPTH, 512],
        mla_kv_norm=[DEPTH, 256], mla_w_uq=[DEPTH, 512, 768], mla_w_ukv=[DEPTH, 256, 1024], conv_w=[DEPTH, 31, 512],
        conv_b=[DEPTH, 512], conv_ln_g=[DEPTH, 512], conv_ln_b=[DEPTH, 512], w_ffn_in=[DEPTH, D, 2 * DFF],
        w_ffn_out=[DEPTH, DFF, D])
    for k, s in shapes.items():
        I[k] = din(k, s)

    xres = nc.dram_tensor("xres", [NT, D], F32, kind="ExternalOutput").ap()
    S = {}
    S["modv"] = dscr("modv", [2, 6, D])
    S["hxT"] = dscr("hxT", [D, NT], BF16)
    S["qT"] = dscr("qT", [512, NT], BF16)
    S["kT"] = dscr("kT", [512, NT], BF16)
    S["vda"] = dscr("vda", [NT, 512], BF16)
    S["uT"] = dscr("uT", [512, NT])
    S["mlaT"] = dscr("mlaT", [832, NT])
    S["cuT"] = dscr("cuT", [512, NT], BF16)
    S["ymixT"] = dscr("ymixT", [D, NT], BF16)
    S["mqT"] = dscr("mqT", [4, 192, NT], BF16)
    S["mkT"] = dscr("mkT", [4, 128, NT], BF16)
    S["mkrT"] = dscr("mkrT", [64, NT], BF16)
    S["mv"] = dscr("mv", [NT, 512], BF16)
    S["h2T"] = dscr("h2T", [D, NT], BF16)
    ys5a = dscr("ys5a", [512, NCTX + NT])
    ys5b = dscr("ys5b", [512, NCTX + NT])
    S["ys5"] = [ys5a, ys5b]
    WW = []
    for i in range(2):
        Wd = {}
        Wd["w_in"] = nc.dram_tensor("wb_in%d" % i, [D, INW], BF16, kind="Internal").ap()
        Wd["w_out"] = nc.dram_tensor("wb_out%d" % i, [D, D], BF16, kind="Internal").ap()
        Wd["w_ffi"] = nc.dram_tensor("wb_ffi%d" % i, [D, 2 * DFF], BF16, kind="Internal").ap()
        Wd["w_ffo"] = nc.dram_tensor("wb_ffo%d" % i, [DFF, D], BF16, kind="Internal").ap()
        Wd["w_uq"] = nc.dram_tensor("wb_uq%d" % i, [512, 768], BF16, kind="Internal").ap()
        Wd["w_ukv"] = nc.dram_tensor("wb_ukv%d" % i, [256, 1024], BF16, kind="Internal").ap()
        Wd["w_glu"] = nc.dram_tensor("wb_glu%d" % i, [512, 512], BF16, kind="Internal").ap()
        WW.append(Wd)

    top = contextlib.ExitStack()
    with top:
        P = Prog(nc, top)
        def gsb(name, shape, dt):
            return top.enter_context(nc.sbuf_tensor(name, list(shape), dt))
        identb = gsb("identb", [128, 128], BF16)
        identf = gsb("identf", [128, 128], F32)
        permf = gsb("permf", [128, 128], F32)
        onesf = gsb("onesf", [128, 128], F32)
        onesb = gsb("onesb", [128, 128], BF16)
        scT = gsb("scT", [128, 2, 16], F32)
        with Stage(P) as sg:
            r = P.R()
            P.stq(lambda e: e.dma_start(out=identb[:], in_=I["ident"]), w=[r])
            P.ld(lambda e: e.dma_start(out=identf[:], in_=I["ident"]), w=[r])
            P.ld(lambda e: e.dma_start(out=permf[:], in_=I["perm"]), w=[r])
            P.dve(lambda e: e.memset(onesf[:], 1.0), w=[r])
            P.dve(lambda e: e.memset(onesb[:], 1.0), w=[r])
            ct, cr = sg.sb([128, 2, 16], F32)
            for rr in range(2):
                P.ld(lambda e, rr=rr: e.dma_start(out=ct[:, rr, :], in_=I["cvec"][rr].rearrange("(k p) -> p k", p=128), allow_slow_non_contiguous=True), w=[cr])
            P.act(lambda e: e.activation(out=scT[:], in_=ct[:], func=AF.Silu), r=[cr], w=[r])

        C = dict(identb=identb, identf=identf, permf=permf, onesf=onesf, onesb=onesb, scT=scT)
        for l in range(nlayers):
            last = (l == DEPTH - 1)
            xsrc = I["xc"] if l == 0 else xres
            W = WW[l % 2]
            if l == 0:
                with Stage(P) as sg0:
                    emit_cast(P, I, W, 0)
            stage_ada(P, I, S, C, l)
            if stop_after == "ada":
                break
            stage_prenorm(P, S, C, xsrc, S["hxT"], 0)
            if stop_after == "prenorm":
                break
            stage_win(P, I, S, W, C)
            if stop_after == "win":
                break
            stage_da(P, I, S, C, l, (lambda l=l: emit_cast(P, I, WW[(l + 1) % 2], l + 1)) if l + 1 < nlayers else None)
            if stop_after == "da":
                break
            stage_mla(P, I, S, W, C, l)
            if stop_after == "mla":
                break
            stage_conv(P, I, S, C, l)
            if stop_after == "conv":
                break
            if os.environ.get("SKIP_S5") != "1":
                stage_s5(P, I, S, W, C, l)
            if stop_after == "s5":
                break
            stage_wout(P, I, S, W, C, l, xsrc, xres)
            if stop_after == "wout":
                break
            stage_ffn(P, I, S, W, C, l, xres)
    return nc


def emit_cast(P, I, W, l):
    r = P.R()
    def cp(dst, src, rows, step):
        for i in range(0, rows, step):
            P.stq(lambda e, i=i: e.dma_start(out=dst[i:i + step, :], in_=src[i:i + step, :]), w=[r])
    cp(W["w_in"], I["w_in"][l], D, D)
    cp(W["w_out"], I["w_out"][l], D, D)
    cp(W["w_ffi"], I["w_ffn_in"][l], D, 1024)
    cp(W["w_ffo"], I["w_ffn_out"][l], DFF, 2816)
    cp(W["w_uq"], I["mla_w_uq"][l], 512, 512)
    cp(W["w_ukv"], I["mla_w_ukv"][l], 256, 256)
    cp(W["w_glu"], I["s5_w_glu"][l], 512, 512)


def stage_ada(P, I, S, C, l):
    nc = P.nc
    with Stage(P) as sg:
        wt = sg.sb([128, 16, 512], F32, 2)
        wrot = Rot(wt)
        bia, biar = sg.sb([2, 6 * D], F32)
        mod, modr = bia, biar
        gv, gvr = sg.sb([2, 4, D], F32)
        outv, outr = sg.sb([2, 6, D], F32)
        pss = Rot(sg.ps([2, 512], F32, 2))
        P.ld(lambda e: e.dma_start(out=bia[:], in_=I["b_ada"][l:l + 1, :].to_broadcast([2, 6 * D])), w=[biar])
        for i, nm in enumerate(["norm_pre_mix", "norm_post_mix", "norm_pre_ffn", "norm_post_ffn"]):
            P.ld(lambda e, i=i, nm=nm: e.dma_start(out=gv[:, i, :], in_=I[nm][l:l + 1, :].to_broadcast([2, D])), w=[gvr])
        wv = I["w_ada"][l].rearrange("(k p) c -> p k c", p=128)
        for j in range(24):
            (w, wr) = wrot.next()
            P.ld(lambda e, w=w, j=j: e.dma_start(out=w[:], in_=wv[:, :, j * 512:(j + 1) * 512]), w=[wr])
            (ps, psr) = pss.next()
            for k in range(16):
                P.pe(lambda e, w=w, k=k, ps=ps: e.matmul(ps[:], lhsT=C["scT"][:, :, k], rhs=w[:, k, :], start=(k == 0), stop=(k == 15)),
                     r=[wr], w=[psr])
            P.dve(lambda e, ps=ps, j=j: e.tensor_tensor(out=mod[:, j * 512:(j + 1) * 512], in0=ps[:], in1=bia[:, j * 512:(j + 1) * 512], op=ALU.add),
                  r=[psr, biar], w=[modr])
        m = lambda i: mod[:, i * D:(i + 1) * D]
        P.dve(lambda e: e.scalar_tensor_tensor(out=outv[:, 0, :], in0=m(1), scalar=1.0, in1=gv[:, 0, :], op0=ALU.add, op1=ALU.mult), r=[modr, gvr], w=[outr])
        P.dve(lambda e: e.tensor_copy(out=outv[:, 1, :], in_=m(0)), r=[modr], w=[outr])
        P.dve(lambda e: e.tensor_tensor(out=outv[:, 2, :], in0=m(2), in1=gv[:, 1, :], op=ALU.mult), r=[modr, gvr], w=[outr])
        P.dve(lambda e: e.scalar_tensor_tensor(out=outv[:, 3, :], in0=m(4), scalar=1.0, in1=gv[:, 2, :], op0=ALU.add, op1=ALU.mult), r=[modr, gvr], w=[outr])
        P.dve(lambda e: e.tensor_copy(out=outv[:, 4, :], in_=m(3)), r=[modr], w=[outr])
        P.dve(lambda e: e.tensor_tensor(out=outv[:, 5, :], in0=m(5), in1=gv[:, 3, :], op=ALU.mult), r=[modr, gvr], w=[outr])
        dr = P.R()
        P.stq(lambda e: e.dma_start(out=S["modv"], in_=outv[:]), r=[outr], w=[dr])


def load_modvec(P, sg, S, idx):
    out = {}
    for kind in range(2):
        t, r = sg.sb([128, D], F32)
        P.ld(lambda e, t=t, kind=kind: e.dma_start(out=t[:], in_=S["modv"][kind:kind + 1, idx, :].to_broadcast([128, D])), w=[r])
        out[kind] = (t, r)
    return out


def rstd_from_ss(P, sg, ss, ssr, n_feat, rot=None):
    t, tr = rot.next() if rot is not None else sg.sb([128, 1], F32)
    P.dve(lambda e: e.tensor_scalar(out=t[:], in0=ss[:], scalar1=1.0 / n_feat, scalar2=EPS, op0=ALU.mult, op1=ALU.add), r=[ssr], w=[tr])
    P.act(lambda e: e.activation(out=t[:], in_=t[:], func=AF.Sqrt), r=[tr], w=[tr])
    P.dve(lambda e: e.reciprocal(out=t[:], in_=t[:]), r=[tr], w=[tr])
    return t, tr


def norm_mod_transpose(P, sg, C, xt, xtr, G, SH, kind, hT, hTr, col0, bufs):
    junk, junkr = bufs["junk"]
    ss, ssr = bufs["ss"].next()
    hb, hbr = bufs["hb"].next()
    P.act(lambda e: e.activation(out=junk[:], in_=xt[:], func=AF.Square, accum_out=ss[:]), r=[xtr], w=[junkr, ssr])
    rs, rsr = rstd_from_ss(P, sg, ss, ssr, D, bufs["rs"])
    tmp, tmpr = bufs["tmp"].next()
    P.dve(lambda e: e.scalar_tensor_tensor(out=tmp[:], in0=xt[:], scalar=rs[:, 0:1], in1=G[kind][0][:], op0=ALU.mult, op1=ALU.mult),
          r=[xtr, rsr, G[kind][1]], w=[tmpr])
    P.pool(lambda e: e.tensor_tensor(out=hb[:], in0=tmp[:], in1=SH[kind][0][:], op=ALU.add), r=[tmpr, SH[kind][1]], w=[hbr])
    for half in range(2):
        pt, ptr = bufs["pt"].next()
        for k in range(8):
            kk = half * 8 + k
            P.pe(lambda e, pt=pt, k=k, kk=kk: e.transpose(pt[:, k * 128:(k + 1) * 128], hb[:, kk * 128:(kk + 1) * 128], C["identb"][:]),
                 r=[hbr], w=[ptr])
        P.act(lambda e, pt=pt, half=half: e.activation(out=hT[:, half * 8:(half + 1) * 8, col0:col0 + 128],
                                                       in_=pt[:].rearrange("p (k t) -> p k t", k=8), func=AF.Copy),
              r=[ptr], w=[hTr])


def norm_bufs(sg):
    return dict(junk=sg.sb([128, D], BF16), ss=Rot(sg.sb([128, 1], F32, 2)), rs=Rot(sg.sb([128, 1], F32, 2)),
                hb=Rot(sg.sb([128, D], BF16, 2)), tmp=Rot(sg.sb([128, D], F32, 2)), pt=Rot(sg.ps([128, 1024], BF16, 2)))


def stage_prenorm(P, S, C, xsrc, dstT, midx):
    nc = P.nc
    with Stage(P) as sg:
        G = load_modvec(P, sg, S, midx)
        SH = load_modvec(P, sg, S, midx + 1)
        xts = Rot(sg.sb([128, D], F32, 2))
        hTs = Rot(sg.sb([128, 16, 512], BF16, 2))
        bufs = norm_bufs(sg)
        dr = P.R()
        dv = dstT.rearrange("(k p) t -> p k t", p=128)
        for (t0, n) in CHUNKS:
            hT, hTr = hTs.next()
            for i in range(n // 128):
                xt, xtr = xts.next()
                P.ld(lambda e, xt=xt, a=t0 + i * 128: e.dma_start(out=xt[:], in_=xsrc[a:a + 128, :]), w=[xtr])
                norm_mod_transpose(P, sg, C, xt, xtr, G, SH, 0 if t0 >= NCTX else 1, hT, hTr, i * 128, bufs)
            P.stq(lambda e, hT=hT, t0=t0, n=n: e.dma_start(out=dv[:, :, t0:t0 + n], in_=hT[:, :, 0:n]), r=[hTr], w=[dr])


def rope_tile(P, sg, C, ps, psr, rows, t0, n, tabs, bufs, dst_dram, dstres, scale_ap=None):
    qf, qfr = bufs["qf"].next()
    if scale_ap is None:
        P.act(lambda e: e.activation(out=qf[0:rows, 0:n], in_=ps[0:rows, 0:n], func=AF.Copy), r=[psr], w=[qfr])
    else:
        P.dve(lambda e: e.tensor_tensor(out=qf[0:rows, 0:n], in0=ps[0:rows, 0:n], in1=scale_ap[0][0:rows, 0:n], op=ALU.mult), r=[psr, scale_ap[1]], w=[qfr])
    p2, p2r = bufs["p2"].next()
    P.pe(lambda e: e.matmul(p2[0:rows, 0:n], lhsT=C["permf"][0:rows, 0:rows], rhs=qf[0:rows, 0:n], start=True, stop=True), r=[qfr], w=[p2r])
    t1, t1r = bufs["t1"].next()
    (tc, tcr), (ts, tsr) = tabs
    P.pool(lambda e: e.tensor_tensor(out=t1[0:rows, 0:n], in0=qf[0:rows, 0:n], in1=tc[0:rows, t0:t0 + n], op=ALU.mult), r=[qfr, tcr], w=[t1r])
    t2, t2r = bufs["t2"].next()
    P.dve(lambda e: e.tensor_tensor(out=t2[0:rows, 0:n], in0=p2[0:rows, 0:n], in1=ts[0:rows, t0:t0 + n], op=ALU.mult), r=[p2r, tsr], w=[t2r])
    ob, obr = bufs["ob"].next()
    P.dve(lambda e: e.tensor_tensor(out=ob[0:rows, 0:n], in0=t1[0:rows, 0:n], in1=t2[0:rows, 0:n], op=ALU.add), r=[t1r, t2r], w=[obr])
    P.stq(lambda e: e.dma_start(out=dst_dram, in_=ob[0:rows, 0:n]), r=[obr], w=[dstres])


def rope_bufs(sg):
    return dict(qf=Rot(sg.sb([128, 512], F32, 2)), p2=Rot(sg.ps([128, 512], F32, 2)), t1=Rot(sg.sb([128, 512], F32, 2)),
                t2=Rot(sg.sb([128, 512], F32, 2)), ob=Rot(sg.sb([128, 512], BF16, 2)))


def load_rope_tabs(P, sg, I):
    tc, tcr = sg.sb([128, NT], F32)
    ts, tsr = sg.sb([128, NT], F32)
    P.ld(lambda e: e.dma_start(out=tc[:], in_=I["ropeC"]), w=[tcr])
    P.ld(lambda e: e.dma_start(out=ts[:], in_=I["ropeS"]), w=[tsr])
    return (tc, tcr), (ts, tsr)


def stage_win(P, I, S, W, C):
    nc = P.nc
    with Stage(P) as sg:
        tabs = load_rope_tabs(P, sg, I)
        rb = rope_bufs(sg)
        hTs = Rot(sg.sb([128, 16, 512], BF16, 2))
        wts = Rot(sg.sb([128, 16, 128], BF16, 3))
        wv, wvr = sg.sb([128, 16, 512], BF16)
        pss = Rot(sg.ps([128, 512], F32, 4))
        ev = Rot(sg.sb([128, 512], F32, 3))
        evb = Rot(sg.sb([128, 512], BF16, 3))
        dr = P.R()
        hv = S["hxT"].rearrange("(k p) t -> p k t", p=128)
        wiv = W["w_in"].rearrange("(k p) f -> p k f", p=128)
        P.ld(lambda e: e.dma_start(out=wv[:], in_=wiv[:, :, 1024:1536]), w=[wvr])
        tiles = []
        for j in range(4):
            tiles.append((j * 128, 128, "rope", S["qT"][j * 128:(j + 1) * 128, :]))
        for j in range(4):
            tiles.append((512 + j * 128, 128, "rope", S["kT"][j * 128:(j + 1) * 128, :]))
        for j in range(4):
            tiles.append((1536 + j * 128, 128, "f32", S["uT"][j * 128:(j + 1) * 128, :]))
        for j in range(6):
            tiles.append((2048 + j * 128, 128, "f32", S["mlaT"][j * 128:(j + 1) * 128, :]))
        tiles.append((2816, 64, "f32", S["mlaT"][768:832, :]))
        for j in range(4):
            tiles.append((2880 + j * 128, 128, "cval", j))
        def do_chunk(t0, n):
            hT, hTr = hTs.next()
            P.ld(lambda e, hT=hT, t0=t0, n=n: e.dma_start(out=hT[:, :, 0:n], in_=hv[:, :, t0:t0 + n]), w=[hTr])

            def proj(c0, rows):
                w, wr = wts.next()
                P.ld(lambda e: e.dma_start(out=w[:, :, 0:rows], in_=wiv[:, :, c0:c0 + rows]), w=[wr])
                ps, psr = pss.next()
                for k in range(16):
                    P.pe(lambda e, k=k: e.matmul(ps[0:rows, 0:n], lhsT=w[:, k, 0:rows], rhs=hT[:, k, 0:n], start=(k == 0), stop=(k == 15)),
                         r=[wr, hTr], w=[psr])
                return ps, psr
            for (c0, rows, kind, dst) in tiles:
                ps, psr = proj(c0, rows)
                if kind == "rope":
                    rope_tile(P, sg, C, ps, psr, rows, t0, n, tabs, rb, dst[:, t0:t0 + n], dr)
                elif kind == "f32":
                    o, orr = ev.next()
                    P.act(lambda e, o=o, ps=ps, rows=rows: e.activation(out=o[0:rows, 0:n], in_=ps[0:rows, 0:n], func=AF.Copy), r=[psr], w=[orr])
                    P.stq(lambda e, o=o, dst=dst, rows=rows: e.dma_start(out=dst[:, t0:t0 + n], in_=o[0:rows, 0:n]), r=[orr], w=[dr])
                else:
                    j = dst
                    psg, psgr = proj(2880 + 512 + j * 128, 128)
                    sgm, sgmr = ev.next()
                    P.act(lambda e, sgm=sgm, psg=psg: e.activation(out=sgm[:, 0:n], in_=psg[:, 0:n], func=AF.Sigmoid), r=[psgr], w=[sgmr])
                    ob, obr = evb.next()
                    P.dve(lambda e, ob=ob, ps=ps, sgm=sgm: e.tensor_tensor(out=ob[:, 0:n], in0=ps[:, 0:n], in1=sgm[:, 0:n], op=ALU.mult), r=[psr, sgmr], w=[obr])
                    P.stq(lambda e, ob=ob, j=j: e.dma_start(out=S["cuT"][j * 128:(j + 1) * 128, t0:t0 + n], in_=ob[:, 0:n]), r=[obr], w=[dr])
            for i in range(n // 128):
                ps, psr = pss.next()
                for k in range(16):
                    P.pe(lambda e, k=k, i=i, ps=ps: e.matmul(ps[:, :], lhsT=hT[:, k, i * 128:(i + 1) * 128], rhs=wv[:, k, :], start=(k == 0), stop=(k == 15)),
                         r=[wvr, hTr], w=[psr])
                ob, obr = evb.next()
                P.act(lambda e, ob=ob, ps=ps: e.activation(out=ob[:], in_=ps[:], func=AF.Copy), r=[psr], w=[obr])
                P.stq(lambda e, ob=ob, a=t0 + i * 128: e.dma_start(out=S["vda"][a:a + 128, :], in_=ob[:]), r=[obr], w=[dr])
        for (t0, n) in CHUNKS:
            do_chunk(t0, n)


def attention(P, sg, C, parts, V, Vr, scale, chunk, pbufs):
    t0, n = chunk
    nkt = 2 if t0 < NCTX else NT // 128
    o, orr = pbufs["o"].next()
    z, zr = pbufs["z"].next()
    for kt in range(nkt):
        s, sr = pbufs["s"].next()
        for pi, (KT, QT, rr) in enumerate(parts):
            P.pe(lambda e, KT=KT, QT=QT, pi=pi, s=s, kt=kt: e.matmul(s[:, 0:n], lhsT=KT[:, kt * 128:(kt + 1) * 128], rhs=QT[:, t0:t0 + n],
                                                                  start=(pi == 0), stop=(pi == len(parts) - 1)), r=[rr], w=[sr])
        p, pr = pbufs["p"].next()
        P.act(lambda e, p=p, s=s: e.activation(out=p[:, 0:n], in_=s[:, 0:n], func=AF.Exp, scale=scale), r=[sr], w=[pr])
        P.pe(lambda e, p=p, kt=kt: e.matmul(o[:, 0:n], lhsT=V[:, kt, :], rhs=p[:, 0:n], start=(kt == 0), stop=(kt == nkt - 1)), r=[pr, Vr], w=[orr])
        P.pe(lambda e, p=p, kt=kt: e.matmul(z[:, 0:n], lhsT=C["onesb"][:], rhs=p[:, 0:n], start=(kt == 0), stop=(kt == nkt - 1)), r=[pr], w=[zr])
    return (o, orr), (z, zr)


def stage_da(P, I, S, C, l, prefetch=None):
    nc = P.nc
    li = 0.8 - 0.6 * math.exp(-0.3 * l)
    with Stage(P) as sg:
        if prefetch is not None:
            prefetch()
        lv, lvr = sg.sb([128, 4, 64], F32)
        for i, nm in enumerate(["da_lam_q1", "da_lam_k1", "da_lam_q2", "da_lam_k2"]):
            P.ld(lambda e, i=i, nm=nm: e.dma_start(out=lv[:, i, :], in_=I[nm][l:l + 1, :].to_broadcast([128, 64])), w=[lvr])
        lp, lpr = sg.sb([128, 2, 64], F32)
        P.dve(lambda e: e.tensor_tensor(out=lp[:, 0, :], in0=lv[:, 0, :], in1=lv[:, 1, :], op=ALU.mult), r=[lvr], w=[lpr])
        P.dve(lambda e: e.tensor_tensor(out=lp[:, 1, :], in0=lv[:, 2, :], in1=lv[:, 3, :], op=ALU.mult), r=[lvr], w=[lpr])
        lsum, lsr = sg.sb([128, 2], F32)
        P.dve(lambda e: e.tensor_reduce(out=lsum[:], in_=lp[:], axis=mybir.AxisListType.X, op=ALU.add), r=[lpr], w=[lsr])
        P.act(lambda e: e.activation(out=lsum[:], in_=lsum[:], func=AF.Exp), r=[lsr], w=[lsr])
        nlam, nlr = sg.sb([128, 1], F32)
        P.dve(lambda e: e.tensor_tensor(out=nlam[:], in0=lsum[:, 1:2], in1=lsum[:, 0:1], op=ALU.subtract), r=[lsr], w=[nlr])
        P.dve(lambda e: e.tensor_scalar(out=nlam[:], in0=nlam[:], scalar1=-li, scalar2=None, op0=ALU.add), r=[nlr], w=[nlr])
        gs, gsr = sg.sb([128, 1], F32)
        P.ld(lambda e: e.dma_start(out=gs[:], in_=I["da_subln"][l].rearrange("(p o) -> p o", o=1), allow_slow_non_contiguous=True), w=[gsr])
        P.dve(lambda e: e.tensor_scalar(out=gs[:], in0=gs[:], scalar1=(1.0 - li), scalar2=None, op0=ALU.mult), r=[gsr], w=[gsr])

        KT, KTr = sg.sb([128, NT], BF16)
        QT, QTr = sg.sb([128, NT], BF16)
        V, Vr = sg.sb([128, 34, 128], BF16)
        pb = dict(o=Rot(sg.ps([128, 512], F32, 2)), z=Rot(sg.ps([128, 512], F32, 2)), s=Rot(sg.ps([128, 512], F32, 3)),
                  p=Rot(sg.sb([128, 512], BF16, 4)), acc=Rot(sg.sb([128, 512], F32, 4)))
        ssp = sg.ps([128, 512], F32)
        wk = Rot(sg.sb([128, 512], F32, 6))
        ob = Rot(sg.sb([128, 512], BF16, 2))
        dr = P.R()
        for h in range(4):
            P.ld(lambda e, h=h: e.dma_start(out=KT[:], in_=S["kT"][h * 128:(h + 1) * 128, :]), w=[KTr])
            P.ld(lambda e, h=h: e.dma_start(out=QT[:], in_=S["qT"][h * 128:(h + 1) * 128, :]), w=[QTr])
            P.ld(lambda e, h=h: e.dma_start(out=V[:], in_=S["vda"][:, h * 128:(h + 1) * 128].rearrange("(k p) d -> p k d", p=128)), w=[Vr])
            def do_chunk(h, chunk):
                t0, n = chunk
                nrm = []
                for m in range(2):
                    parts = [(KT[m * 64:(m + 1) * 64, :], QT[m * 64:(m + 1) * 64, :], KTr if False else QTr)]
                    (o, orr), (z, zr) = attention_rw(P, sg, C, parts, [KTr, QTr], V, Vr, 0.125, chunk, pb)
                    rz, rzr = wk.next()
                    P.dve(lambda e, rz=rz, z=z: e.reciprocal(out=rz[:, 0:n], in_=z[:, 0:n]), r=[zr], w=[rzr])
                    a, ar = wk.next()
                    P.dve(lambda e, a=a, o=o, rz=rz: e.tensor_tensor(out=a[:, 0:n], in0=o[:, 0:n], in1=rz[:, 0:n], op=ALU.mult), r=[orr, rzr], w=[ar])
                    nrm.append((a, ar))
                d, ddr = wk.next()
                (a1, a1r), (a2, a2r) = nrm
                P.dve(lambda e, d=d, a1=a1, a2=a2: e.scalar_tensor_tensor(out=d[:, 0:n], in0=a2[:, 0:n], scalar=nlam[:, 0:1], in1=a1[:, 0:n], op0=ALU.mult, op1=ALU.add),
                      r=[a1r, a2r, nlr], w=[ddr])
                sq, sqr = wk.next()
                P.act(lambda e, sq=sq, d=d: e.activation(out=sq[:, 0:n], in_=d[:, 0:n], func=AF.Square), r=[ddr], w=[sqr])
                (sp_, spr) = ssp
                P.pe(lambda e, sq=sq: e.matmul(sp_[:, 0:n], lhsT=C["onesf"][:], rhs=sq[:, 0:n], start=True, stop=True), r=[sqr], w=[spr])
                rs, rsr = wk.next()
                P.dve(lambda e, rs=rs: e.tensor_scalar(out=rs[:, 0:n], in0=sp_[:, 0:n], scalar1=1.0 / 128, scalar2=EPS, op0=ALU.mult, op1=ALU.add), r=[spr], w=[rsr])
                P.act(lambda e, rs=rs: e.activation(out=rs[:, 0:n], in_=rs[:, 0:n], func=AF.Sqrt), r=[rsr], w=[rsr])
                P.dve(lambda e, rs=rs: e.reciprocal(out=rs[:, 0:n], in_=rs[:, 0:n]), r=[rsr], w=[rsr])
                y, yr = ob.next()
                P.dve(lambda e, y=y, d=d, rs=rs: e.scalar_tensor_tensor(out=y[:, 0:n], in0=d[:, 0:n], scalar=gs[:, 0:1], in1=rs[:, 0:n], op0=ALU.mult, op1=ALU.mult),
                      r=[ddr, rsr, gsr], w=[yr])
                P.stq(lambda e, y=y, h=h, t0=t0, n=n: e.dma_start(out=S["ymixT"][h * 128:(h + 1) * 128, t0:t0 + n], in_=y[:, 0:n]), r=[yr], w=[dr])
            for chunk in CHUNKS:
                do_chunk(h, chunk)


def attention_rw(P, sg, C, parts, rres, V, Vr, scale, chunk, pbufs):
    t0, n = chunk
    nkt = 2 if t0 < NCTX else NT // 128
    o, orr = pbufs["o"].next()
    z, zr = pbufs["z"].next()

    def qk(kt):
        s, sr = pbufs["s"].next()
        for pi, (KT, QT, _) in enumerate(parts):
            P.pe(lambda e, KT=KT, QT=QT, pi=pi: e.matmul(s[:, 0:n], lhsT=KT[:, kt * 128:(kt + 1) * 128], rhs=QT[:, t0:t0 + n],
                                                        start=(pi == 0), stop=(pi == len(parts) - 1)), r=rres, w=[sr])
        p, pr = pbufs["p"].next()
        P.act(lambda e: e.activation(out=p[:, 0:n], in_=s[:, 0:n], func=AF.Exp, scale=scale), r=[sr], w=[pr])
        return p, pr

    accs = [pbufs["acc"].next()]

    def pv(kt, p, pr):
        P.pe(lambda e: e.matmul(o[:, 0:n], lhsT=V[:, kt, :], rhs=p[:, 0:n], start=(kt == 0), stop=(kt == nkt - 1)), r=[pr, Vr], w=[orr])
        a, ar = accs[0]
        eng = P.dve
        if kt < 1:
            eng(lambda e: e.tensor_copy(out=a[:, 0:n], in_=p[:, 0:n]), r=[pr], w=[ar])
        else:
            eng(lambda e: e.tensor_tensor(out=a[:, 0:n], in0=a[:, 0:n], in1=p[:, 0:n], op=ALU.add), r=[pr, ar], w=[ar])
    prev = qk(0)
    for kt in range(1, nkt):
        cur = qk(kt)
        pv(kt - 1, *prev)
        prev = cur
    pv(nkt - 1, *prev)
    a, ar = accs[0]
    P.pe(lambda e: e.matmul(z[:, 0:n], lhsT=C["onesf"][:], rhs=a[:, 0:n], start=True, stop=True), r=[ar], w=[zr])
    return (o, orr), (z, zr)


def stage_mla(P, I, S, W, C, l):
    nc = P.nc
    with Stage(P) as sg:
        tabs = load_rope_tabs(P, sg, I)
        rb = rope_bufs(sg)
        wuq, wuqr = sg.sb([128, 4, 768], BF16)
        wuk, wukr = sg.sb([128, 2, 4, 128], BF16)
        wuv, wuvr = sg.sb([128, 2, 4, 128], BF16)
        gq, gqr = sg.sb([128, 4], F32)
        gkv, gkvr = sg.sb([128, 2], F32)
        P.ld(lambda e: e.dma_start(out=wuq[:], in_=W["w_uq"].rearrange("(k p) f -> p k f", p=128)), w=[wuqr])
        ukv = W["w_ukv"].rearrange("(k p) (h c) -> p k h c", p=128, c=256)
        for k in range(2):
            P.ld(lambda e, k=k: e.dma_start(out=wuk[:, k, :, :], in_=ukv[:, k, :, 0:128]), w=[wukr])
            P.ld(lambda e, k=k: e.dma_start(out=wuv[:, k, :, :], in_=ukv[:, k, :, 128:256]), w=[wuvr])
        P.ld(lambda e: e.dma_start(out=gq[:], in_=I["mla_q_norm"][l].rearrange("(k p) -> p k", p=128), allow_slow_non_contiguous=True), w=[gqr])
        P.ld(lambda e: e.dma_start(out=gkv[:], in_=I["mla_kv_norm"][l].rearrange("(k p) -> p k", p=128), allow_slow_non_contiguous=True), w=[gkvr])
        lat = Rot(sg.sb([128, 6, 512], F32, 2))
        krs = Rot(sg.sb([64, 512], F32, 2))
        sqs = Rot(sg.sb([128, 6, 512], F32, 1))
        lsb = Rot(sg.sb([128, 6, 512], BF16, 2))
        rst = Rot(sg.sb([128, 2, 512], F32, 2))
        pss = Rot(sg.ps([128, 512], F32, 4))
        ob = Rot(sg.sb([128, 512], BF16, 3))
        sm = Rot(sg.sb([128, 1], F32, 4))
        dr = P.R()
        mv3 = S["mlaT"][0:768, :].rearrange("(k p) t -> p k t", p=128)

        def do_chunk(t0, n):
            x, xr = lat.next()
            P.ld(lambda e: e.dma_start(out=x[:, :, 0:n], in_=mv3[:, :, t0:t0 + n]), w=[xr])
            kr, krr = krs.next()
            P.ld(lambda e: e.dma_start(out=kr[:, 0:n], in_=S["mlaT"][768:832, t0:t0 + n]), w=[krr])
            sq, sqr = sqs.next()
            P.act(lambda e: e.activation(out=sq[:, :, 0:n], in_=x[:, :, 0:n], func=AF.Square), r=[xr], w=[sqr])
            rs, rsr = rst.next()
            for (which, k0, nk, nf) in ((0, 0, 4, 512), (1, 4, 2, 256)):
                ps, psr = pss.next()
                for k in range(nk):
                    P.pe(lambda e, k=k, ps=ps, k0=k0, nk=nk: e.matmul(ps[:, 0:n], lhsT=C["onesf"][:], rhs=sq[:, k0 + k, 0:n], start=(k == 0), stop=(k == nk - 1)), r=[sqr], w=[psr])
                P.dve(lambda e, ps=ps, which=which, nf=nf: e.tensor_scalar(out=rs[:, which, 0:n], in0=ps[:, 0:n], scalar1=1.0 / nf, scalar2=EPS, op0=ALU.mult, op1=ALU.add), r=[psr], w=[rsr])
            P.act(lambda e: e.activation(out=rs[:, :, 0:n], in_=rs[:, :, 0:n], func=AF.Sqrt), r=[rsr], w=[rsr])
            P.dve(lambda e: e.reciprocal(out=rs[:, :, 0:n], in_=rs[:, :, 0:n]), r=[rsr], w=[rsr])
            xb, xbr = lsb.next()
            for k in range(6):
                g = gq[:, k:k + 1] if k < 4 else gkv[:, k - 4:k - 3]
                P.act(lambda e, k=k, g=g: e.activation(out=xb[:, k, 0:n], in_=x[:, k, 0:n], func=AF.Copy, scale=g), r=[xr, gqr, gkvr], w=[xbr])
            for h in range(4):
                ps, psr = pss.next()
                for k in range(4):
                    P.pe(lambda e, k=k, ps=ps, h=h: e.matmul(ps[:, 0:n], lhsT=wuq[:, k, h * 192:h * 192 + 128], rhs=xb[:, k, 0:n], start=(k == 0), stop=(k == 3)), r=[wuqr, xbr], w=[psr])
                o, orr = ob.next()
                P.dve(lambda e, o=o, ps=ps: e.tensor_tensor(out=o[:, 0:n], in0=ps[:, 0:n], in1=rs[:, 0, 0:n], op=ALU.mult), r=[psr, rsr], w=[orr])
                P.stq(lambda e, o=o, h=h: e.dma_start(out=S["mqT"][h, 0:128, t0:t0 + n], in_=o[:, 0:n]), r=[orr], w=[dr])
                ps2, ps2r = pss.next()
                for k in range(4):
                    P.pe(lambda e, k=k, ps2=ps2, h=h: e.matmul(ps2[0:64, 0:n], lhsT=wuq[:, k, h * 192 + 128:h * 192 + 192], rhs=xb[:, k, 0:n], start=(k == 0), stop=(k == 3)), r=[wuqr, xbr], w=[ps2r])
                rope_tile(P, sg, C, ps2, ps2r, 64, t0, n, tabs, rb, S["mqT"][h, 128:192, t0:t0 + n], dr, scale_ap=(rs[:, 0, :], rsr))
                ps3, ps3r = pss.next()
                for k in range(2):
                    P.pe(lambda e, k=k, ps3=ps3, h=h: e.matmul(ps3[:, 0:n], lhsT=wuk[:, k, h, :], rhs=xb[:, 4 + k, 0:n], start=(k == 0), stop=(k == 1)), r=[wukr, xbr], w=[ps3r])
                o2, o2r = ob.next()
                P.dve(lambda e, o2=o2, ps3=ps3: e.tensor_tensor(out=o2[:, 0:n], in0=ps3[:, 0:n], in1=rs[:, 1, 0:n], op=ALU.mult), r=[ps3r, rsr], w=[o2r])
                P.stq(lambda e, o2=o2, h=h: e.dma_start(out=S["mkT"][h, :, t0:t0 + n], in_=o2[:, 0:n]), r=[o2r], w=[dr])
            rope_tile(P, sg, C, kr, krr, 64, t0, n, tabs, rb, S["mkrT"][:, t0:t0 + n], dr)
            for i in range(n // 128):
                pv, pvr = pss.next()
                for k in range(2):
                    P.pe(lambda e, k=k, pv=pv, i=i: e.matmul(pv[:, :], lhsT=xb[:, 4 + k, i * 128:(i + 1) * 128], rhs=wuv[:, k, :, :].rearrange("p h c -> p (h c)"), start=(k == 0), stop=(k == 1)), r=[wuvr, xbr], w=[pvr])
                pt, ptr = pss.next()
                for k in range(2):
                    P.pe(lambda e, k=k, pt=pt, i=i: e.matmul(pt[:, 0:1], lhsT=sq[:, 4 + k, i * 128:(i + 1) * 128], rhs=C["onesf"][:, 0:1], start=(k == 0), stop=(k == 1)), r=[sqr], w=[ptr])
                r1, r1r = sm.next()
                P.dve(lambda e, r1=r1, pt=pt: e.tensor_scalar(out=r1[:], in0=pt[:, 0:1], scalar1=1.0 / 256, scalar2=EPS, op0=ALU.mult, op1=ALU.add), r=[ptr], w=[r1r])
                P.act(lambda e, r1=r1: e.activation(out=r1[:], in_=r1[:], func=AF.Sqrt), r=[r1r], w=[r1r])
                P.dve(lambda e, r1=r1: e.reciprocal(out=r1[:], in_=r1[:]), r=[r1r], w=[r1r])
                o3, o3r = ob.next()
                P.act(lambda e, o3=o3, pv=pv, r1=r1: e.activation(out=o3[:], in_=pv[:], func=AF.Copy, scale=r1[:, 0:1]), r=[pvr, r1r], w=[o3r])
                P.stq(lambda e, o3=o3, a=t0 + i * 128: e.dma_start(out=S["mv"][a:a + 128, :], in_=o3[:]), r=[o3r], w=[dr])
        for (t0, n) in CHUNKS:
            do_chunk(t0, n)
    with Stage(P) as sg:
        KN, KNr = sg.sb([128, NT], BF16)
        QN, QNr = sg.sb([128, NT], BF16)
        KR, KRr = sg.sb([64, NT], BF16)
        QR, QRr = sg.sb([64, NT], BF16)
        V, Vr = sg.sb([128, 34, 128], BF16)
        pb = dict(o=Rot(sg.ps([128, 512], F32, 2)), z=Rot(sg.ps([128, 512], F32, 2)), s=Rot(sg.ps([128, 512], F32, 4)),
                  p=Rot(sg.sb([128, 512], BF16, 4)), acc=Rot(sg.sb([128, 512], F32, 4)))
        wk = Rot(sg.sb([128, 512], F32, 3))
        ob = Rot(sg.sb([128, 512], BF16, 2))
        dr = P.R()
        P.ld(lambda e: e.dma_start(out=KR[:], in_=S["mkrT"]), w=[KRr])

        def do_chunk(h, chunk):
            t0, n = chunk
            parts = [(KN[:, :], QN[:, :], None), (KR[:, :], QR[:, :], None)]
            (o, orr), (z, zr) = attention_rw(P, sg, C, parts, [KNr, QNr, KRr, QRr], V, Vr, 192.0 ** -0.5, chunk, pb)
            rz, rzr = wk.next()
            P.dve(lambda e: e.reciprocal(out=rz[:, 0:n], in_=z[:, 0:n]), r=[zr], w=[rzr])
            y, yr = ob.next()
            P.dve(lambda e: e.tensor_tensor(out=y[:, 0:n], in0=o[:, 0:n], in1=rz[:, 0:n], op=ALU.mult), r=[orr, rzr], w=[yr])
            P.stq(lambda e: e.dma_start(out=S["ymixT"][1024 + h * 128:1024 + (h + 1) * 128, t0:t0 + n], in_=y[:, 0:n]), r=[yr], w=[dr])
        for h in range(4):
            P.ld(lambda e, h=h: e.dma_start(out=KN[:], in_=S["mkT"][h]), w=[KNr])
            P.ld(lambda e, h=h: e.dma_start(out=QN[:], in_=S["mqT"][h, 0:128, :]), w=[QNr])
            P.ld(lambda e, h=h: e.dma_start(out=QR[:], in_=S["mqT"][h, 128:192, :]), w=[QRr])
            P.ld(lambda e, h=h: e.dma_start(out=V[:], in_=S["mv"][:, h * 128:(h + 1) * 128].rearrange("(k p) d -> p k d", p=128)), w=[Vr])
            for chunk in CHUNKS:
                do_chunk(h, chunk)


CPAD = NT + 45


def stage_conv(P, I, S, C, l):
    nc = P.nc
    with Stage(P) as sg:
        U, Ur = sg.sb([128, 4, CPAD], BF16)
        Dj, Djr = sg.sb([128, 4, 31, 128], BF16)
        wc, wcr = sg.sb([128, 4, 31], F32)
        pv, pvr = sg.sb([128, 3, 4], F32)
        P.pool(lambda e: e.memset(U[:], 0.0), w=[Ur])
        cu = S["cuT"].rearrange("(f p) t -> p f t", p=128)
        P.ld(lambda e: e.dma_start(out=U[:, :, 15:15 + NCTX], in_=cu[:, :, 0:NCTX]), w=[Ur])
        P.ld(lambda e: e.dma_start(out=U[:, :, 286:286 + NLAT], in_=cu[:, :, NCTX:NT]), w=[Ur])
        for ft in range(4):
            P.ld(lambda e, ft=ft: e.dma_start(out=wc[:, ft, :], in_=I["conv_w"][l][:, ft * 128:(ft + 1) * 128].rearrange("j c -> c j"), allow_slow_non_contiguous=True), w=[wcr])
        for i, nm in enumerate(["conv_b", "conv_ln_g", "conv_ln_b"]):
            P.ld(lambda e, i=i, nm=nm: e.dma_start(out=pv[:, i, :], in_=I[nm][l].rearrange("(f p) -> p f", p=128), allow_slow_non_contiguous=True), w=[pvr])
        for ft in range(4):
            for j in range(31):
                P.dve(lambda e, ft=ft, j=j: e.tensor_scalar(out=Dj[:, ft, j, :], in0=C["identf"][:], scalar1=wc[:, ft, j:j + 1], scalar2=None, op0=ALU.mult), r=[wcr], w=[Djr])
        pss = Rot(sg.ps([128, 512], F32, 4))
        st1 = sg.ps([128, 512], F32)
        st2 = sg.ps([128, 512], F32)
        ys = Rot(sg.sb([128, 4, 512], F32, 2))
        sqs = Rot(sg.sb([128, 4, 512], F32, 1))
        wk = Rot(sg.sb([128, 512], F32, 2))
        stb = Rot(sg.sb([128, 512], F32, 3))
        ob = Rot(sg.sb([128, 512], BF16, 3))
        dr = P.R()

        def do_block(base, t0, n):
            y, yr = ys.next()
            sq, sqr = sqs.next()
            for ft in range(4):
                ps, psr = pss.next()
                for j in range(31):
                    P.pe(lambda e, ft=ft, j=j, ps=ps: e.matmul(ps[:, 0:n], lhsT=Dj[:, ft, j, :], rhs=U[:, ft, base + j - 15:base + j - 15 + n], start=(j == 0), stop=(j == 30)), r=[Djr, Ur], w=[psr])
                P.act(lambda e, ft=ft, ps=ps: e.activation(out=y[:, ft, 0:n], in_=ps[:, 0:n], func=AF.Identity, bias=pv[:, 0, ft:ft + 1]), r=[psr, pvr], w=[yr])
            P.act(lambda e: e.activation(out=sq[:, :, 0:n], in_=y[:, :, 0:n], func=AF.Square), r=[yr], w=[sqr])
            (s1, s1r), (s2, s2r) = st1, st2
            for ft in range(4):
                P.pe(lambda e, ft=ft: e.matmul(s1[:, 0:n], lhsT=C["onesf"][:], rhs=y[:, ft, 0:n], start=(ft == 0), stop=(ft == 3)), r=[yr], w=[s1r])
            for ft in range(4):
                P.pe(lambda e, ft=ft: e.matmul(s2[:, 0:n], lhsT=C["onesf"][:], rhs=sq[:, ft, 0:n], start=(ft == 0), stop=(ft == 3)), r=[sqr], w=[s2r])
            mu, mur = stb.next()
            P.dve(lambda e: e.tensor_scalar(out=mu[:, 0:n], in0=s1[:, 0:n], scalar1=1.0 / 512, scalar2=None, op0=ALU.mult), r=[s1r], w=[mur])
            m2, m2r = stb.next()
            P.dve(lambda e: e.tensor_tensor(out=m2[:, 0:n], in0=mu[:, 0:n], in1=mu[:, 0:n], op=ALU.mult), r=[mur], w=[m2r])
            va, var_ = stb.next()
            P.dve(lambda e: e.scalar_tensor_tensor(out=va[:, 0:n], in0=s2[:, 0:n], scalar=1.0 / 512, in1=m2[:, 0:n], op0=ALU.mult, op1=ALU.subtract), r=[s2r, m2r], w=[var_])
            P.dve(lambda e: e.tensor_scalar(out=va[:, 0:n], in0=va[:, 0:n], scalar1=EPS, scalar2=None, op0=ALU.add), r=[var_], w=[var_])
            P.act(lambda e: e.activation(out=va[:, 0:n], in_=va[:, 0:n], func=AF.Sqrt), r=[var_], w=[var_])
            P.dve(lambda e: e.reciprocal(out=va[:, 0:n], in_=va[:, 0:n]), r=[var_], w=[var_])
            for ft in range(4):
                t, tr = wk.next()
                P.dve(lambda e, ft=ft, t=t: e.tensor_tensor(out=t[:, 0:n], in0=y[:, ft, 0:n], in1=mu[:, 0:n], op=ALU.subtract), r=[yr, mur], w=[tr])
                P.dve(lambda e, ft=ft, t=t: e.scalar_tensor_tensor(out=t[:, 0:n], in0=t[:, 0:n], scalar=pv[:, 1, ft:ft + 1], in1=va[:, 0:n], op0=ALU.mult, op1=ALU.mult), r=[tr, var_, pvr], w=[tr])
                o, orr = ob.next()
                P.act(lambda e, ft=ft, t=t, o=o: e.activation(out=o[:, 0:n], in_=t[:, 0:n], func=AF.Silu, bias=pv[:, 2, ft:ft + 1]), r=[tr, pvr], w=[orr])
                P.stq(lambda e, ft=ft, o=o: e.dma_start(out=S["ymixT"][1536 + ft * 128:1536 + (ft + 1) * 128, t0:t0 + n], in_=o[:, 0:n]), r=[orr], w=[dr])
        do_block(15, 0, 256)
        for b in range(8):
            do_block(286 + 512 * b, 256 + 512 * b, 512)


NB = NCTX + NLAT + NCTX
NCH = NT // 16
TWO_PI = 2.0 * math.pi


def stage_s5(P, I, S, W, C, l):
    nc = P.nc
    I32 = mybir.dt.int32
    ys5 = S["ys5"]
    keep = contextlib.ExitStack()
    with keep:
        def ksb(shape, dt, stack=None):
            Stage.CNT += 1
            return (stack or keep).enter_context(nc.sbuf_tensor("k%d" % Stage.CNT, list(shape), dt)), P.R()
        Kmat, Kmr = ksb([128, 16, 6, 128], BF16)
        CAr_, CArr = ksb([128, 16, 16, 32], BF16)
        CAi_, CAir = ksb([128, 16, 16, 32], BF16)
        Hbr_, Hbrr = ksb([128, 16, NCH], BF16)
        Hbi_, Hbir = ksb([128, 16, NCH], BF16)
        Qre, Qrer = ksb([128, 9, 16], F32)
        Qim, Qimr = ksb([128, 9, 16], F32)

        def load_u(stack):
            U, Ur = ksb([128, 6, NB], BF16, stack)
            with Stage(P) as sg:
                P.pool(lambda e: e.memset(U[:], 0.0), w=[Ur])
                for ft in range(6):
                    nr = 96 if ft < 5 else 32
                    P.stq(lambda e, ft=ft, nr=nr: e.dma_start(out=U[0:nr, ft, 0:NT], in_=S["uT"][96 * ft:96 * ft + nr, 0:NT]), w=[Ur])
                    P.stq(lambda e, ft=ft, nr=nr: e.dma_start(out=U[0:nr, ft, NT:NB], in_=S["uT"][96 * ft:96 * ft + nr, 0:NCTX]), w=[Ur])
            return U, Ur
        for d in range(2):
            off = 0 if d == 0 else NCTX
            with contextlib.ExitStack() as st_abt:
                ABTr_, ABTrr = ksb([128, 16, 6, 128], BF16, st_abt)
                ABTi_, ABTir = ksb([128, 16, 6, 128], BF16, st_abt)
                s5_params(P, I, C, l, d, Kmat, Kmr, ABTr_, ABTrr, ABTi_, ABTir, CAr_, CArr, CAi_, CAir, Qre, Qrer, Qim, Qimr)
                with contextlib.ExitStack() as st_u:
                    U, Ur = load_u(st_u)
                    s5_states(P, C, d, off, U, Ur, ABTr_, ABTrr, ABTi_, ABTir, Qre, Qrer, Qim, Qimr, Hbr_, Hbrr, Hbi_, Hbir)
            with contextlib.ExitStack() as st_u:
                U, Ur = load_u(st_u)
                s5_outputs(P, C, d, off, U, Ur, Kmat, Kmr, CAr_, CArr, CAi_, CAir, Hbr_, Hbrr, Hbi_, Hbir, ys5[d])
    s5_epilogue(P, I, S, W, C, l)


def s5_params(P, I, C, l, d, Kmat, Kmr, ABTr_, ABTrr, ABTi_, ABTir, CAr_, CArr, CAi_, CAir, Qre, Qrer, Qim, Qimr):
    nc = P.nc
    I32 = mybir.dt.int32
    with Stage(P) as sg:
        def t16(n=1):
            return sg.sb([128, 16], F32) if n == 1 else sg.sb([128, n, 16], F32)
        lr, lrr = t16(); li, lir = t16(); st, str_ = t16()
        P.ld(lambda e: e.dma_start(out=lr[:], in_=I["s5_lam_re"][l, d].rearrange("(j g) p -> (g p) j", g=2), allow_slow_non_contiguous=True), w=[lrr])
        P.ld(lambda e: e.dma_start(out=li[:], in_=I["s5_lam_im"][l, d].rearrange("(j g) p -> (g p) j", g=2), allow_slow_non_contiguous=True), w=[lir])
        lsv = I["s5_log_step"][l, d].rearrange("(j g) -> g j", g=2)
        for g2 in range(2):
            P.ld(lambda e, g2=g2: e.dma_start(out=st[g2 * 64:(g2 + 1) * 64, :], in_=lsv[g2:g2 + 1, :].to_broadcast([64, 16]), allow_slow_non_contiguous=True), w=[str_])
        Br, Brr = sg.sb([128, 16, 16], F32); Bi, Bir = sg.sb([128, 16, 16], F32)
        P.ld(lambda e: e.dma_start(out=Br[:], in_=I["s5_b_re"][l, d].rearrange("(j g) p h -> (g p) j h", g=2)), w=[Brr])
        P.ld(lambda e: e.dma_start(out=Bi[:], in_=I["s5_b_im"][l, d].rearrange("(j g) p h -> (g p) j h", g=2)), w=[Bir])
        Cr, Crr = sg.sb([128, 16, 16], F32); Ci, Cir = sg.sb([128, 16, 16], F32)
        cps = sg.ps([128, 32, 16], F32)
        cns = Rot(sg.sb([16, 16, 64], F32, 2))
        for (src, dstt, dstr) in (("s5_c_re", Cr, Crr), ("s5_c_im", Ci, Cir)):
            cv = I[src][l, d].rearrange("(j g) h p -> g h j p", g=2)
            for g2 in range(2):
                cn, cnr = cns.next()
                P.ld(lambda e, cn=cn, cv=cv, g2=g2: e.dma_start(out=cn[:], in_=cv[g2]), w=[cnr])
                for j in range(16):
                    P.pe(lambda e, cn=cn, g2=g2, j=j: e.matmul(cps[0][64 * g2:64 * g2 + 64, j, :], lhsT=cn[:, j, :], rhs=C["identf"][0:16, 0:16], start=True, stop=True), r=[cnr], w=[cps[1]])
            P.act(lambda e, dstt=dstt: e.activation(out=dstt[:], in_=cps[0][:, 0:16, :], func=AF.Copy), r=[cps[1]], w=[dstr])
        V = P.dve
        P.act(lambda e: e.activation(out=st[:], in_=st[:], func=AF.Exp), r=[str_], w=[str_])
        mg, mgr = t16(); th, thr = t16()
        V(lambda e: e.tensor_tensor(out=mg[:], in0=lr[:], in1=st[:], op=ALU.mult), r=[lrr, str_], w=[mgr])
        P.act(lambda e: e.activation(out=mg[:], in_=mg[:], func=AF.Exp), r=[mgr], w=[mgr])
        V(lambda e: e.tensor_tensor(out=th[:], in0=li[:], in1=st[:], op=ALU.mult), r=[lir, str_], w=[thr])
        ki, kir = sg.sb([128, 16], I32); kf, kfr = t16(); msk, mskr = t16()

        def fold(x, xr):
            V(lambda e: e.tensor_scalar(out=msk[:], in0=x[:], scalar1=math.pi, scalar2=-TWO_PI, op0=ALU.is_gt, op1=ALU.mult), r=[xr], w=[mskr])
            V(lambda e: e.tensor_tensor(out=x[:], in0=x[:], in1=msk[:], op=ALU.add), r=[xr, mskr], w=[xr])
            V(lambda e: e.tensor_scalar(out=msk[:], in0=x[:], scalar1=-math.pi, scalar2=TWO_PI, op0=ALU.is_lt, op1=ALU.mult), r=[xr], w=[mskr])
            V(lambda e: e.tensor_tensor(out=x[:], in0=x[:], in1=msk[:], op=ALU.add), r=[xr, mskr], w=[xr])
        V(lambda e: e.tensor_scalar(out=kf[:], in0=th[:], scalar1=1.0 / TWO_PI, scalar2=None, op0=ALU.mult), r=[thr], w=[kfr])
        V(lambda e: e.tensor_copy(out=ki[:], in_=kf[:]), r=[kfr], w=[kir])
        V(lambda e: e.tensor_copy(out=kf[:], in_=ki[:]), r=[kir], w=[kfr])
        V(lambda e: e.scalar_tensor_tensor(out=th[:], in0=kf[:], scalar=-TWO_PI, in1=th[:], op0=ALU.mult, op1=ALU.add), r=[kfr, thr], w=[thr])
        fold(th, thr)
        sn, snr = t16(); cs, csr = t16()
        P.act(lambda e: e.activation(out=sn[:], in_=th[:], func=AF.Sin), r=[thr], w=[snr])
        V(lambda e: e.tensor_scalar(out=th[:], in0=th[:], scalar1=math.pi / 2, scalar2=None, op0=ALU.add), r=[thr, snr], w=[thr])
        fold(th, thr)
        P.act(lambda e: e.activation(out=cs[:], in_=th[:], func=AF.Sin), r=[thr], w=[csr])
        Apr, Aprr = t16(17); Api, Apir = t16(17)
        V(lambda e: e.memset(Apr[:, 0, :], 1.0), w=[Aprr])
        V(lambda e: e.memset(Api[:, 0, :], 0.0), w=[Apir])
        V(lambda e: e.tensor_tensor(out=Apr[:, 1, :], in0=mg[:], in1=cs[:], op=ALU.mult), r=[mgr, csr], w=[Aprr])
        V(lambda e: e.tensor_tensor(out=Api[:, 1, :], in0=mg[:], in1=sn[:], op=ALU.mult), r=[mgr, snr], w=[Apir])
        t1, t1r = t16(8); t2, t2r = t16(8)

        def cmul(outr, outi, ores, ar, ai, ares, br, bi, bres, shape):
            a = lambda t: t
            V(lambda e: e.tensor_tensor(out=shape(t1), in0=ar, in1=br, op=ALU.mult), r=ares + bres, w=[t1r])
            V(lambda e: e.tensor_tensor(out=shape(t2), in0=ai, in1=bi, op=ALU.mult), r=ares + bres, w=[t2r])
            V(lambda e: e.tensor_tensor(out=outr, in0=shape(t1), in1=shape(t2), op=ALU.subtract), r=[t1r, t2r], w=[ores[0]])
            V(lambda e: e.tensor_tensor(out=shape(t1), in0=ar, in1=bi, op=ALU.mult), r=ares + bres + [ores[0]], w=[t1r])
            V(lambda e: e.tensor_tensor(out=shape(t2), in0=ai, in1=br, op=ALU.mult), r=ares + bres, w=[t2r])
            V(lambda e: e.tensor_tensor(out=outi, in0=shape(t1), in1=shape(t2), op=ALU.add), r=[t1r, t2r], w=[ores[1]])
        m = 1
        while m < 16:
            bre = Apr[:, m:m + 1, :].to_broadcast([128, m, 16]); bim = Api[:, m:m + 1, :].to_broadcast([128, m, 16])
            cmul(Apr[:, m + 1:2 * m + 1, :], Api[:, m + 1:2 * m + 1, :], [Aprr, Apir], Apr[:, 1:m + 1, :], Api[:, 1:m + 1, :], [Aprr, Apir],
                 bre, bim, [Aprr, Apir], (lambda t, m=m: t[:, 0:m, :]))
            m *= 2
        V(lambda e: e.tensor_copy(out=Qre[:, 0, :], in_=Apr[:, 16, :]), r=[Aprr], w=[Qrer])
        V(lambda e: e.tensor_copy(out=Qim[:, 0, :], in_=Api[:, 16, :]), r=[Apir], w=[Qimr])
        for j in range(8):
            cmul(Qre[:, j + 1:j + 2, :], Qim[:, j + 1:j + 2, :], [Qrer, Qimr], Qre[:, j:j + 1, :], Qim[:, j:j + 1, :], [Qrer, Qimr],
                 Qre[:, j:j + 1, :], Qim[:, j:j + 1, :], [Qrer, Qimr], (lambda t: t[:, 0:1, :]))
        den, denr = t16(); am1, am1r = t16(); fr, frr = t16(); fi, fir = t16(); w1, w1r = t16(); w2, w2r = t16()
        V(lambda e: e.tensor_tensor(out=den[:], in0=lr[:], in1=lr[:], op=ALU.mult), r=[lrr], w=[denr])
        V(lambda e: e.tensor_tensor(out=w1[:], in0=li[:], in1=li[:], op=ALU.mult), r=[lir], w=[w1r])
        V(lambda e: e.tensor_tensor(out=den[:], in0=den[:], in1=w1[:], op=ALU.add), r=[denr, w1r], w=[denr])
        V(lambda e: e.reciprocal(out=den[:], in_=den[:]), r=[denr], w=[denr])
        V(lambda e: e.tensor_scalar(out=am1[:], in0=Apr[:, 1, :], scalar1=-1.0, scalar2=None, op0=ALU.add), r=[Aprr], w=[am1r])
        V(lambda e: e.tensor_tensor(out=w1[:], in0=am1[:], in1=lr[:], op=ALU.mult), r=[am1r, lrr, denr], w=[w1r])
        V(lambda e: e.tensor_tensor(out=w2[:], in0=Api[:, 1, :], in1=li[:], op=ALU.mult), r=[Apir, lir], w=[w2r])
        V(lambda e: e.tensor_tensor(out=fr[:], in0=w1[:], in1=w2[:], op=ALU.add), r=[w1r, w2r], w=[frr])
        V(lambda e: e.tensor_tensor(out=fr[:], in0=fr[:], in1=den[:], op=ALU.mult), r=[frr, denr], w=[frr])
        V(lambda e: e.tensor_tensor(out=w1[:], in0=Api[:, 1, :], in1=lr[:], op=ALU.mult), r=[Apir, lrr, frr], w=[w1r])
        V(lambda e: e.tensor_tensor(out=w2[:], in0=am1[:], in1=li[:], op=ALU.mult), r=[am1r, lir, frr], w=[w2r])
        V(lambda e: e.tensor_tensor(out=fi[:], in0=w1[:], in1=w2[:], op=ALU.subtract), r=[w1r, w2r], w=[fir])
        V(lambda e: e.tensor_tensor(out=fi[:], in0=fi[:], in1=den[:], op=ALU.mult), r=[fir, denr], w=[fir])
        Bbr, Bbrr = sg.sb([128, 16, 32], F32); Bbi, Bbir = sg.sb([128, 16, 32], F32)
        Cbr, Cbrr = sg.sb([128, 16, 32], F32); Cbi, Cbir = sg.sb([128, 16, 32], F32); Cbn, Cbnr = sg.sb([128, 16, 32], F32)
        x1, x1r = sg.sb([128, 16, 32], F32); x2, x2r = sg.sb([128, 16, 32], F32)
        for (t, tr) in ((Bbr, Bbrr), (Bbi, Bbir), (Cbr, Cbrr), (Cbi, Cbir)):
            P.pool(lambda e, t=t: e.memset(t[:], 0.0), w=[tr])
        for g2 in range(2):
            rows = slice(g2 * 64, (g2 + 1) * 64); cols = slice(g2 * 16, (g2 + 1) * 16)
            fb_r = lambda g2=g2: fr[g2 * 64:(g2 + 1) * 64, :].unsqueeze(2).to_broadcast([64, 16, 16])
            fb_i = lambda g2=g2: fi[g2 * 64:(g2 + 1) * 64, :].unsqueeze(2).to_broadcast([64, 16, 16])
            V(lambda e, rows=rows, cols=cols, fb_r=fb_r: e.tensor_tensor(out=x1[rows, :, 0:16], in0=Br[rows, :, :], in1=fb_r(), op=ALU.mult), r=[Brr, frr], w=[x1r])
            V(lambda e, rows=rows, cols=cols, fb_i=fb_i: e.tensor_tensor(out=x2[rows, :, 0:16], in0=Bi[rows, :, :], in1=fb_i(), op=ALU.mult), r=[Bir, fir], w=[x2r])
            V(lambda e, rows=rows, cols=cols: e.tensor_tensor(out=Bbr[rows, :, cols], in0=x1[rows, :, 0:16], in1=x2[rows, :, 0:16], op=ALU.subtract), r=[x1r, x2r], w=[Bbrr])
            V(lambda e, rows=rows, cols=cols, fb_r=fb_r: e.tensor_tensor(out=x1[rows, :, 0:16], in0=Bi[rows, :, :], in1=fb_r(), op=ALU.mult), r=[Bir, frr, Bbrr], w=[x1r])
            V(lambda e, rows=rows, cols=cols, fb_i=fb_i: e.tensor_tensor(out=x2[rows, :, 0:16], in0=Br[rows, :, :], in1=fb_i(), op=ALU.mult), r=[Brr, fir, Bbrr], w=[x2r])
            V(lambda e, rows=rows, cols=cols: e.tensor_tensor(out=Bbi[rows, :, cols], in0=x1[rows, :, 0:16], in1=x2[rows, :, 0:16], op=ALU.add), r=[x1r, x2r], w=[Bbir])
            P.pool(lambda e, rows=rows, cols=cols: e.tensor_copy(out=Cbr[rows, :, cols], in_=Cr[rows, :, :]), r=[Crr], w=[Cbrr])
            P.pool(lambda e, rows=rows, cols=cols: e.tensor_copy(out=Cbi[rows, :, cols], in_=Ci[rows, :, :]), r=[Cir], w=[Cbir])
        P.pool(lambda e: e.tensor_scalar(out=Cbn[:], in0=Cbi[:], scalar1=-1.0, scalar2=None, op0=ALU.mult), r=[Cbir], w=[Cbnr])
        P.pool(lambda e: e.memset(Kmat[:], 0.0), w=[Kmr])
        ABs = Rot([(sg.sb([128, 16, 32], F32), sg.sb([128, 16, 32], F32)) for _ in range(2)])
        kks = Rot(sg.ps([128, 16, 32], F32, 2))
        ttr = Rot(sg.ps([128, 8, 128], F32, 1))
        tti = Rot(sg.ps([128, 8, 128], F32, 1))

        def bc(t, e_):
            return t[:, e_, :].unsqueeze(2).to_broadcast([128, 16, 32])

        def cmul_bd(outr, outrr, outi, outir, e_, Xr, Xrr, Xi, Xir, sign_im=1.0):
            V(lambda e: e.tensor_tensor(out=x1[:], in0=Xr[:], in1=bc(Apr, e_), op=ALU.mult), r=[Xrr, Aprr], w=[x1r])
            V(lambda e: e.tensor_tensor(out=x2[:], in0=Xi[:], in1=bc(Api, e_), op=ALU.mult), r=[Xir, Apir], w=[x2r])
            V(lambda e: e.tensor_tensor(out=outr, in0=x1[:], in1=x2[:], op=ALU.subtract), r=[x1r, x2r], w=[outrr])
            V(lambda e: e.tensor_tensor(out=x1[:], in0=Xi[:], in1=bc(Apr, e_), op=ALU.mult), r=[Xir, Aprr, outrr], w=[x1r])
            V(lambda e: e.tensor_tensor(out=x2[:], in0=Xr[:], in1=bc(Api, e_), op=ALU.mult), r=[Xrr, Apir, outrr], w=[x2r])
            if sign_im > 0:
                V(lambda e: e.tensor_tensor(out=outi, in0=x1[:], in1=x2[:], op=ALU.add), r=[x1r, x2r], w=[outir])
            else:
                V(lambda e: e.scalar_tensor_tensor(out=outi, in0=x1[:], scalar=-1.0, in1=x2[:], op0=ALU.mult, op1=ALU.subtract), r=[x1r, x2r], w=[outir])

        def do_e(e_):
            (ABr, ABrr), (ABi, ABir) = ABs.next()
            cmul_bd(ABr[:], ABrr, ABi[:], ABir, e_, Bbr, Bbrr, Bbi, Bbir)
            kk, kkr = kks.next(); tr_, trr = ttr.next(); ti_, tir = tti.next()
            for j in range(16):
                ft, q = j // 3, j % 3
                P.pe(lambda e, j=j, ft=ft, q=q: e.matmul(kk[32 * q:32 * q + 32, ft, :], lhsT=ABr[:, j, :], rhs=Cbr[:, j, :], start=True, stop=False), r=[ABrr, Cbrr], w=[kkr])
                P.pe(lambda e, j=j, ft=ft, q=q: e.matmul(kk[32 * q:32 * q + 32, ft, :], lhsT=ABi[:, j, :], rhs=Cbn[:, j, :], start=False, stop=True), r=[ABir, Cbnr], w=[kkr])
                P.pe(lambda e, j=j, ft=ft, q=q: e.matmul(tr_[32 * q:32 * q + 32, ft, :], lhsT=ABr[:, j, :], rhs=C["identf"][:], start=True, stop=True), r=[ABrr], w=[trr])
                P.pe(lambda e, j=j, ft=ft, q=q: e.matmul(ti_[32 * q:32 * q + 32, ft, :], lhsT=ABi[:, j, :], rhs=C["identf"][:], start=True, stop=True), r=[ABir], w=[tir])
            for q in range(3):
                nf = 6 if q == 0 else 5
                P.act(lambda e, q=q, nf=nf: e.activation(out=Kmat[32 * q:32 * q + 32, e_, 0:nf, 32 * q:32 * q + 32], in_=kk[32 * q:32 * q + 32, 0:nf, :], func=AF.Copy), r=[kkr], w=[Kmr])
            P.act(lambda e: e.activation(out=ABTr_[0:96, e_, 0:5, :], in_=tr_[0:96, 0:5, :], func=AF.Copy), r=[trr], w=[ABTrr])
            P.act(lambda e: e.activation(out=ABTi_[0:96, e_, 0:5, :], in_=ti_[0:96, 0:5, :], func=AF.Copy), r=[tir], w=[ABTir])
            P.act(lambda e: e.activation(out=ABTr_[0:32, e_, 5, :], in_=tr_[0:32, 5, :], func=AF.Copy), r=[trr], w=[ABTrr])
            P.act(lambda e: e.activation(out=ABTi_[0:32, e_, 5, :], in_=ti_[0:32, 5, :], func=AF.Copy), r=[tir], w=[ABTir])
            cmul_bd(CAr_[:, e_, :, :], CArr, CAi_[:, e_, :, :], CAir, e_ + 1, Cbr, Cbrr, Cbi, Cbir, sign_im=-1.0)
        for e_ in range(16):
            do_e(e_)


def s5_states(P, C, d, off, U, Ur, ABTr_, ABTrr, ABTi_, ABTir, Qre, Qrer, Qim, Qimr, Hbr_, Hbrr, Hbi_, Hbir):
    nc = P.nc
    n = NCH
    with Stage(P) as sg:
        NP_ = 2
        bufs = [sg.sb([128, NP_, NCH], F32) for _ in range(4)]
        tA = [sg.sb([128, NP_, NCH], F32) for _ in range(2)]
        tB = [sg.sb([128, NP_, NCH], F32) for _ in range(2)]
        pss = Rot(sg.ps([128, 512], F32, 4))

        def do_half(hf):
            (Xr, Xrr), (Xi, Xir), (Yr, Yrr), (Yi, Yir) = bufs
            for jj in range(NP_):
                j = hf * NP_ + jj
                ft, q = j // 3, j % 3
                for (ABT, ABTres, X, Xres) in ((ABTr_, ABTrr, Xr, Xrr), (ABTi_, ABTir, Xi, Xir)):
                    ps, psr = pss.next()
                    for r in range(16):
                        e_ = (15 - r) if d == 0 else r
                        rhs = U[32 * q:32 * q + 32, ft, off:off + NT].rearrange("p (c r) -> p c r", r=16)[:, :, r]
                        P.pe(lambda e, ps=ps, ABT=ABT, e_=e_, rhs=rhs, r=r, q=q, ft=ft: e.matmul(ps[:, 0:NCH], lhsT=ABT[32 * q:32 * q + 32, e_, ft, :], rhs=rhs, start=(r == 0), stop=(r == 15)),
                             r=[ABTres, Ur], w=[psr])
                    P.act(lambda e, ps=ps, X=X, jj=jj: e.activation(out=X[:, jj, :], in_=ps[:, 0:NCH], func=AF.Copy), r=[psr], w=[Xres])
            cur = (bufs[0], bufs[1]); nxt = (bufs[2], bufs[3])
            for step in range(9):
                sh = 1 << step
                (Xr, Xrr), (Xi, Xir) = cur
                (Yr, Yrr), (Yi, Yir) = nxt
                if d == 0:
                    dst = slice(sh, n); src = slice(0, n - sh); keep_ = slice(0, sh)
                else:
                    dst = slice(0, n - sh); src = slice(sh, n); keep_ = slice(n - sh, n)
                w_ = n - sh
                qr = Qre[:, step, hf * NP_:(hf + 1) * NP_].unsqueeze(2).to_broadcast([128, NP_, w_])
                qi = Qim[:, step, hf * NP_:(hf + 1) * NP_].unsqueeze(2).to_broadcast([128, NP_, w_])
                (a1, a1r), (a2, a2r) = tA
                (b1, b1r), (b2, b2r) = tB
                V = P.dve; G = P.pool
                V(lambda e, Xr=Xr, qr=qr, src=src, w_=w_: e.tensor_tensor(out=a1[:, :, 0:w_], in0=Xr[:, :, src], in1=qr, op=ALU.mult), r=[Xrr, Qrer], w=[a1r])
                V(lambda e, Xi=Xi, qi=qi, src=src, w_=w_: e.tensor_tensor(out=a2[:, :, 0:w_], in0=Xi[:, :, src], in1=qi, op=ALU.mult), r=[Xir, Qimr], w=[a2r])
                V(lambda e, w_=w_: e.tensor_tensor(out=a1[:, :, 0:w_], in0=a1[:, :, 0:w_], in1=a2[:, :, 0:w_], op=ALU.subtract), r=[a1r, a2r], w=[a1r])
                V(lambda e, Xr=Xr, Yr=Yr, dst=dst, w_=w_: e.tensor_tensor(out=Yr[:, :, dst], in0=Xr[:, :, dst], in1=a1[:, :, 0:w_], op=ALU.add), r=[Xrr, a1r], w=[Yrr])
                V(lambda e, Xr=Xr, Yr=Yr, keep_=keep_: e.tensor_copy(out=Yr[:, :, keep_], in_=Xr[:, :, keep_]), r=[Xrr], w=[Yrr])
                G(lambda e, Xi=Xi, qr=qr, src=src, w_=w_: e.tensor_tensor(out=b1[:, :, 0:w_], in0=Xi[:, :, src], in1=qr, op=ALU.mult), r=[Xir, Qrer], w=[b1r])
                G(lambda e, Xr=Xr, qi=qi, src=src, w_=w_: e.tensor_tensor(out=b2[:, :, 0:w_], in0=Xr[:, :, src], in1=qi, op=ALU.mult), r=[Xrr, Qimr], w=[b2r])
                G(lambda e, w_=w_: e.tensor_tensor(out=b1[:, :, 0:w_], in0=b1[:, :, 0:w_], in1=b2[:, :, 0:w_], op=ALU.add), r=[b1r, b2r], w=[b1r])
                G(lambda e, Xi=Xi, Yi=Yi, dst=dst, w_=w_: e.tensor_tensor(out=Yi[:, :, dst], in0=Xi[:, :, dst], in1=b1[:, :, 0:w_], op=ALU.add), r=[Xir, b1r], w=[Yir])
                G(lambda e, Xi=Xi, Yi=Yi, keep_=keep_: e.tensor_copy(out=Yi[:, :, keep_], in_=Xi[:, :, keep_]), r=[Xir], w=[Yir])
                cur, nxt = nxt, cur
            (Xr, Xrr), (Xi, Xir) = cur
            P.act(lambda e, Xr=Xr: e.activation(out=Hbr_[:, hf * NP_:(hf + 1) * NP_, :], in_=Xr[:], func=AF.Copy), r=[Xrr], w=[Hbrr])
            P.act(lambda e, Xi=Xi: e.activation(out=Hbi_[:, hf * NP_:(hf + 1) * NP_, :], in_=Xi[:], func=AF.Copy), r=[Xir], w=[Hbir])
        for hf in range(16 // NP_):
            do_half(hf)


def s5_outputs(P, C, d, off, U, Ur, Kmat, Kmr, CAr_, CArr, CAi_, CAir, Hbr_, Hbrr, Hbi_, Hbir, ydst):
    nc = P.nc
    with Stage(P) as sg:
        pss = Rot(sg.ps([128, 512], F32, 4))
        ev = Rot(sg.sb([128, 512], F32, 3))
        dr = P.R()
        if d == 0:
            blocks = [(0, 256)] + [(256 + 512 * i, 512) for i in range(8)]
        else:
            blocks = [(256 + 512 * i, 512) for i in range(8)] + [(NT, 256)]

        def do_block(ft, b0, n):
            nb = n // 16
            c0 = (b0 - off) // 16
            ps, psr = pss.next()
            pv = ps[:, 0:n].rearrange("p (c r) -> p c r", r=16)
            uv = U[:, ft, b0:b0 + n].rearrange("p (c r) -> p c r", r=16)
            for k in range(16):
                if d == 0:
                    o_ap = pv[:, :, k:16]; r_ap = uv[:, :, 0:16 - k]
                else:
                    o_ap = pv[:, :, 0:16 - k]; r_ap = uv[:, :, k:16]
                P.pe(lambda e, k=k, o_ap=o_ap, r_ap=r_ap: e.matmul(o_ap, lhsT=Kmat[:, k, ft, :], rhs=r_ap, start=(k == 0), stop=False, skip_group_check=True),
                     r=[Kmr, Ur], w=[psr])
            npair = 3 if ft < 5 else 1
            for q in range(npair):
                j = ft * 3 + q
                for r in range(16):
                    if d == 0:
                        e_ = r
                        lo = 1 if c0 == 0 else 0
                        oc = slice(lo, nb); hc = slice(c0 + lo - 1, c0 + nb - 1)
                    else:
                        e_ = 15 - r
                        hi = nb - 1 if c0 + nb == NCH else nb
                        oc = slice(0, hi); hc = slice(c0 + 1, c0 + hi + 1)
                    last = (q == npair - 1 and r == 15)
                    P.pe(lambda e, j=j, q=q, r=r, e_=e_, oc=oc, hc=hc: e.matmul(pv[32 * q:32 * q + 32, oc, r], lhsT=CAr_[:, e_, j, :], rhs=Hbr_[:, j, hc], start=False, stop=False, skip_group_check=True),
                         r=[CArr, Hbrr], w=[psr])
                    P.pe(lambda e, j=j, q=q, r=r, e_=e_, oc=oc, hc=hc, last=last: e.matmul(pv[32 * q:32 * q + 32, oc, r], lhsT=CAi_[:, e_, j, :], rhs=Hbi_[:, j, hc], start=False, stop=last, skip_group_check=True),
                         r=[CAir, Hbir], w=[psr])
            o, orr = ev.next()
            P.act(lambda e: e.activation(out=o[:, 0:n], in_=ps[:, 0:n], func=AF.Copy), r=[psr], w=[orr])
            nr = 32 * npair
            P.stq(lambda e: e.dma_start(out=ydst[ft * 96:ft * 96 + nr, b0:b0 + n], in_=o[0:nr, 0:n]), r=[orr], w=[dr])
        for (b0, n) in blocks:
            for ft in range(6):
                do_block(ft, b0, n)


def s5_epilogue(P, I, S, W, C, l):
    nc = P.nc
    ys5 = S["ys5"]
    with Stage(P) as sg:
        wg, wgr = sg.sb([128, 4, 512], BF16)
        P.ld(lambda e: e.dma_start(out=wg[:], in_=W["w_glu"].rearrange("(k p) f -> p k f", p=128)), w=[wgr])
        dv, dvr = sg.sb([128, 4], F32)
        P.ld(lambda e: e.dma_start(out=dv[:], in_=I["s5_d"][l].rearrange("(f p) -> p f", p=128), allow_slow_non_contiguous=True), w=[dvr])
        yfs = Rot(sg.sb([128, 4, 512], F32, 2)); ybs = Rot(sg.sb([128, 4, 512], F32, 2)); us = Rot(sg.sb([128, 4, 512], F32, 2))
        gs = Rot(sg.sb([128, 4, 512], F32, 2)); gbs = Rot(sg.sb([128, 4, 512], BF16, 2))
        wk = Rot(sg.sb([128, 4, 512], F32, 2))
        sgs = Rot(sg.sb([128, 512], F32, 2)); ob = Rot(sg.sb([128, 512], BF16, 3))
        pss = Rot(sg.ps([128, 512], F32, 4))
        dr = P.R()
        yfv = ys5[0].rearrange("(f p) t -> p f t", p=128)
        ybv = ys5[1].rearrange("(f p) t -> p f t", p=128)
        uv = S["uT"].rearrange("(f p) t -> p f t", p=128)

        def do_chunk(t0, n):
            yf, yfr = yfs.next(); yb, ybr = ybs.next(); u, ur = us.next()
            tb = t0 if t0 >= NCTX else NT + t0
            P.ld(lambda e: e.dma_start(out=yf[:, :, 0:n], in_=yfv[:, :, t0:t0 + n]), w=[yfr])
            P.ld(lambda e: e.dma_start(out=yb[:, :, 0:n], in_=ybv[:, :, tb:tb + n]), w=[ybr])
            P.ld(lambda e: e.dma_start(out=u[:, :, 0:n], in_=uv[:, :, t0:t0 + n]), w=[ur])
            P.pool(lambda e: e.tensor_tensor(out=yf[:, :, 0:n], in0=yf[:, :, 0:n], in1=yb[:, :, 0:n], op=ALU.add), r=[yfr, ybr], w=[yfr])
            for ft in range(4):
                P.dve(lambda e, ft=ft: e.scalar_tensor_tensor(out=yf[:, ft, 0:n], in0=u[:, ft, 0:n], scalar=dv[:, ft:ft + 1], in1=yf[:, ft, 0:n], op0=ALU.mult, op1=ALU.add), r=[ur, dvr, yfr], w=[yfr])
            t, tr = wk.next()
            P.act(lambda e: e.activation(out=t[:, :, 0:n], in_=yf[:, :, 0:n], func=AF.Square), r=[yfr], w=[tr])
            P.dve(lambda e: e.tensor_scalar(out=t[:, :, 0:n], in0=t[:, :, 0:n], scalar1=0.044715 * 1.5957691216, scalar2=1.5957691216, op0=ALU.mult, op1=ALU.add), r=[tr], w=[tr])
            P.dve(lambda e: e.tensor_tensor(out=t[:, :, 0:n], in0=t[:, :, 0:n], in1=yf[:, :, 0:n], op=ALU.mult), r=[tr, yfr], w=[tr])
            P.act(lambda e: e.activation(out=t[:, :, 0:n], in_=t[:, :, 0:n], func=AF.Sigmoid), r=[tr], w=[tr])
            g, gr = gs.next(); gb, gbr = gbs.next()
            P.dve(lambda e: e.tensor_tensor(out=g[:, :, 0:n], in0=t[:, :, 0:n], in1=yf[:, :, 0:n], op=ALU.mult), r=[tr, yfr], w=[gr])
            P.pool(lambda e: e.tensor_copy(out=gb[:, :, 0:n], in_=g[:, :, 0:n]), r=[gr], w=[gbr])
            for fo in range(4):
                ps, psr = pss.next()
                for k in range(4):
                    P.pe(lambda e, k=k, fo=fo, ps=ps: e.matmul(ps[:, 0:n], lhsT=wg[:, k, fo * 128:(fo + 1) * 128], rhs=gb[:, k, 0:n], start=(k == 0), stop=(k == 3)), r=[wgr, gbr], w=[psr])
                sgm, sgmr = sgs.next()
                P.act(lambda e, ps=ps, sgm=sgm: e.activation(out=sgm[:, 0:n], in_=ps[:, 0:n], func=AF.Sigmoid), r=[psr], w=[sgmr])
                o, orr = ob.next()
                P.dve(lambda e, fo=fo, sgm=sgm, o=o: e.tensor_tensor(out=o[:, 0:n], in0=g[:, fo, 0:n], in1=sgm[:, 0:n], op=ALU.mult), r=[gr, sgmr], w=[orr])
                P.stq(lambda e, fo=fo, o=o: e.dma_start(out=S["ymixT"][512 + fo * 128:512 + (fo + 1) * 128, t0:t0 + n], in_=o[:, 0:n]), r=[orr], w=[dr])
        for (t0, n) in CHUNKS:
            do_chunk(t0, n)


def load_modvec_kind(P, tiles, S, idxs, kind):
    for (t, r), idx in zip(tiles, idxs):
        P.ld(lambda e, t=t, idx=idx: e.dma_start(out=t[:], in_=S["modv"][kind:kind + 1, idx, :].to_broadcast([128, D])), w=[r])


def stage_wout(P, I, S, W, C, l, xsrc, xres):
    nc = P.nc
    with Stage(P) as sg:
        wo, wor = sg.sb([128, 16, D], BF16)
        P.ld(lambda e: e.dma_start(out=wo[:], in_=W["w_out"].rearrange("(k p) f -> p k f", p=128)), w=[wor])
        mods = [sg.sb([128, D], F32) for _ in range(3)]
        yTs = Rot(sg.sb([128, 16, 512], BF16, 2))
        xts = Rot(sg.sb([128, D], F32, 2))
        hTs = Rot(sg.sb([128, 16, 512], BF16, 2))
        pss = Rot(sg.ps([128, 512], F32, 4))
        junk, junkr = sg.sb([128, 512], BF16)
        ss4 = Rot(sg.sb([128, 4], F32, 2))
        ss1 = Rot(sg.sb([128, 1], F32, 2))
        tmps = Rot(sg.sb([128, 512], F32, 2))
        bufs = dict(junk=sg.sb([128, D], BF16), ss=Rot(sg.sb([128, 1], F32, 2)), rs=Rot(sg.sb([128, 1], F32, 2)),
                    hb=Rot(sg.sb([128, D], BF16, 2)), tmp=Rot(sg.sb([128, D], F32, 1)), pt=Rot(sg.ps([128, 1024], BF16, 2)))
        dr = P.R()
        xr_dram = P.R()
        yv = S["ymixT"].rearrange("(k p) t -> p k t", p=128)
        hv = S["h2T"].rearrange("(k p) t -> p k t", p=128)

        def do_tile(yT, yTr, hT, hTr, i, a, kind):
            G = {kind: mods[1]}
            SH = {kind: mods[2]}
            xt, xtr = xts.next()
            P.ld(lambda e: e.dma_start(out=xt[:], in_=xsrc[a:a + 128, :]), r=[xr_dram], w=[xtr])
            s4, s4r = ss4.next()
            banks = []
            for c in range(4):
                ps, psr = pss.next()
                for k in range(16):
                    P.pe(lambda e, k=k, c=c, ps=ps: e.matmul(ps[:, :], lhsT=yT[:, k, i * 128:(i + 1) * 128], rhs=wo[:, k, c * 512:(c + 1) * 512], start=(k == 0), stop=(k == 15)), r=[yTr, wor], w=[psr])
                P.act(lambda e, c=c, ps=ps: e.activation(out=junk[:], in_=ps[:], func=AF.Square, accum_out=s4[:, c:c + 1]), r=[psr], w=[junkr, s4r])
                banks.append((ps, psr))
            s1, s1r = ss1.next()
            P.dve(lambda e: e.tensor_reduce(out=s1[:], in_=s4[:], axis=mybir.AxisListType.X, op=ALU.add), r=[s4r], w=[s1r])
            rs, rsr = rstd_from_ss(P, sg, s1, s1r, D, bufs["rs"])
            for c, (ps, psr) in enumerate(banks):
                t, tr = tmps.next()
                P.dve(lambda e, c=c, ps=ps, t=t: e.scalar_tensor_tensor(out=t[:], in0=ps[:], scalar=rs[:, 0:1], in1=mods[0][0][:, c * 512:(c + 1) * 512], op0=ALU.mult, op1=ALU.mult), r=[psr, rsr, mods[0][1]], w=[tr])
                P.pool(lambda e, c=c, t=t: e.tensor_tensor(out=xt[:, c * 512:(c + 1) * 512], in0=xt[:, c * 512:(c + 1) * 512], in1=t[:], op=ALU.add), r=[tr, xtr], w=[xtr])
            P.stq(lambda e: e.dma_start(out=xres[a:a + 128, :], in_=xt[:]), r=[xtr], w=[xr_dram])
            norm_mod_transpose(P, sg, C, xt, xtr, G, SH, kind, hT, hTr, i * 128, bufs)

        def do_chunk(t0, n, kind):
            yT, yTr = yTs.next()
            P.ld(lambda e: e.dma_start(out=yT[:, :, 0:n], in_=yv[:, :, t0:t0 + n]), w=[yTr])
            hT, hTr = hTs.next()
            for i in range(n // 128):
                do_tile(yT, yTr, hT, hTr, i, t0 + i * 128, kind)
            P.stq(lambda e: e.dma_start(out=hv[:, :, t0:t0 + n], in_=hT[:, :, 0:n]), r=[hTr], w=[dr])
        for ci, (t0, n) in enumerate(CHUNKS):
            kind = 1 if t0 < NCTX else 0
            if ci < 2:
                load_modvec_kind(P, mods, S, [2, 3, 4], kind)
            do_chunk(t0, n, kind)


def stage_ffn(P, I, S, W, C, l, xres):
    nc = P.nc
    with Stage(P) as sg:
        g4 = sg.sb([128, D], F32)
        h2, h2r = sg.sb([128, 16, 512], BF16)
        aT, aTr = sg.sb([128, 44, 512], BF16)
        wgs = Rot(sg.sb([128, 16, 128], BF16, 3))
        wus = Rot(sg.sb([128, 16, 128], BF16, 3))
        wos = Rot(sg.sb([128, 44, 256], BF16, 2))
        fx, fxr = sg.sb([128, 4, D], F32)
        xts = Rot(sg.sb([128, D], F32, 2))
        sil = Rot(sg.sb([128, 512], F32, 2))
        tmp, tmpr = sg.sb([128, D], F32)
        junk, junkr = sg.sb([128, D], BF16)
        ss1 = Rot(sg.sb([128, 1], F32, 2))
        rsb = Rot(sg.sb([128, 1], F32, 2))
        psg = Rot(sg.ps([128, 512], F32, 2))
        psu = Rot(sg.ps([128, 512], F32, 2))
        pso = Rot(sg.ps([128, 512], F32, 4))
        xr_dram = P.R()
        hv = S["h2T"].rearrange("(k p) t -> p k t", p=128)
        wiv = W["w_ffi"].rearrange("(k p) f -> p k f", p=128)
        wov = W["w_ffo"].rearrange("(k p) f -> p k f", p=128)

        def do_chunk(t0, n, kind):
            P.ld(lambda e: e.dma_start(out=h2[:, :, 0:n], in_=hv[:, :, t0:t0 + n]), w=[h2r])
            for f in range(44):
                wg, wgr = wgs.next()
                wu, wur = wus.next()
                P.ld(lambda e, f=f, wg=wg: e.dma_start(out=wg[:], in_=wiv[:, :, f * 128:(f + 1) * 128]), w=[wgr])
                P.ld(lambda e, f=f, wu=wu: e.dma_start(out=wu[:], in_=wiv[:, :, DFF + f * 128:DFF + (f + 1) * 128]), w=[wur])
                pg, pgr = psg.next()
                pu, pur = psu.next()
                for k in range(16):
                    P.pe(lambda e, k=k, wg=wg, pg=pg: e.matmul(pg[:, 0:n], lhsT=wg[:, k, :], rhs=h2[:, k, 0:n], start=(k == 0), stop=(k == 15)), r=[wgr, h2r], w=[pgr])
                for k in range(16):
                    P.pe(lambda e, k=k, wu=wu, pu=pu: e.matmul(pu[:, 0:n], lhsT=wu[:, k, :], rhs=h2[:, k, 0:n], start=(k == 0), stop=(k == 15)), r=[wur, h2r], w=[pur])
                sl, slr = sil.next()
                P.act(lambda e, sl=sl, pg=pg: e.activation(out=sl[:, 0:n], in_=pg[:, 0:n], func=AF.Silu), r=[pgr], w=[slr])
                P.dve(lambda e, f=f, sl=sl, pu=pu: e.tensor_tensor(out=aT[:, f, 0:n], in0=pu[:, 0:n], in1=sl[:, 0:n], op=ALU.mult), r=[pur, slr], w=[aTr])
            nt = n // 128
            for c in range(8):
                wo, wor = wos.next()
                P.ld(lambda e, c=c, wo=wo: e.dma_start(out=wo[:], in_=wov[:, :, c * 256:(c + 1) * 256]), w=[wor])
                for i in range(nt):
                    po, por = pso.next()
                    for f in range(44):
                        P.pe(lambda e, f=f, i=i, wo=wo, po=po: e.matmul(po[:, 0:256], lhsT=aT[:, f, i * 128:(i + 1) * 128], rhs=wo[:, f, :], start=(f == 0), stop=(f == 43)), r=[aTr, wor], w=[por])
                    P.act(lambda e, c=c, i=i, po=po: e.activation(out=fx[:, i, c * 256:(c + 1) * 256], in_=po[:, 0:256], func=AF.Copy), r=[por], w=[fxr])
            for i in range(nt):
                a = t0 + i * 128
                xt, xtr = xts.next()
                P.ld(lambda e, xt=xt, a=a: e.dma_start(out=xt[:], in_=xres[a:a + 128, :]), r=[xr_dram], w=[xtr])
                s1, s1r = ss1.next()
                P.act(lambda e, i=i, s1=s1: e.activation(out=junk[:], in_=fx[:, i, :], func=AF.Square, accum_out=s1[:]), r=[fxr], w=[junkr, s1r])
                rs, rsr = rstd_from_ss(P, sg, s1, s1r, D, rsb)
                P.dve(lambda e, i=i, rs=rs: e.scalar_tensor_tensor(out=tmp[:], in0=fx[:, i, :], scalar=rs[:, 0:1], in1=g4[0][:], op0=ALU.mult, op1=ALU.mult), r=[fxr, rsr, g4[1]], w=[tmpr])
                P.pool(lambda e, xt=xt: e.tensor_tensor(out=xt[:], in0=xt[:], in1=tmp[:], op=ALU.add), r=[tmpr, xtr], w=[xtr])
                P.stq(lambda e, xt=xt, a=a: e.dma_start(out=xres[a:a + 128, :], in_=xt[:]), r=[xtr], w=[xr_dram])
        for ci, (t0, n) in enumerate(CHUNKS):
            kind = 1 if t0 < NCTX else 0
            if ci < 2:
                load_modvec_kind(P, [g4], S, [5], kind)
            do_chunk(t0, n, kind)


def host_consts():
    n = NLAT
    row = np.repeat(np.arange(n // 64, dtype=np.float32), 64)
    col = np.tile(np.arange(64, dtype=np.float32), n // 64)
    inv = (10000.0 ** (-np.arange(16, dtype=np.float32) / 16)).astype(np.float32)
    ang = np.concatenate([row[:, None] * inv, col[:, None] * inv], -1)
    cos = np.cos(ang).astype(np.float32)
    sin = np.sin(ang).astype(np.float32)
    Ct = np.ones((128, NT), np.float32)
    St = np.zeros((128, NT), np.float32)
    for r in range(128):
        dd = r % 64
        i = dd % 32
        Ct[r, NCTX:] = cos[:, i]
        St[r, NCTX:] = (-sin[:, i]) if dd < 32 else sin[:, i]
    perm = np.zeros((128, 128), np.float32)
    for m in range(128):
        dd = m % 64
        partner = m + 32 if dd < 32 else m - 32
        perm[partner, m] = 1.0
    ident = np.eye(128, dtype=np.float32)
    return dict(ropeC=Ct, ropeS=St, perm=perm, ident=ident)


def make_in_maps(inputs, ncores=8):
    hc = host_consts()
    maps = []
    for c in range(ncores):
        b = c % 4
        m = dict(hc)
        m["xc"] = np.ascontiguousarray(np.concatenate([inputs["ctx"][b], inputs["x"][b]], 0))
        m["cvec"] = np.ascontiguousarray(np.stack([inputs["c"][b], inputs["c_ctx"]], 0))
        for k, v in inputs.items():
            if k in ("x", "c", "ctx", "c_ctx"):
                continue
            m[k] = np.ascontiguousarray(v)
        maps.append(m)
    return maps


def kernel(**inputs):
    inputs = {k: np.asarray(v) for k, v in inputs.items()}
    nc = build()
    maps = make_in_maps(inputs, 8)
    res = run_bass_kernel_spmd(nc, maps, core_ids=list(range(8)))
    out = np.stack([res.results[b]["xres"][NCTX:] for b in range(4)], 0)
    return out.astype(np.float32)
```

```python
import os
import contextlib
import math
import numpy as np
import concourse.bass as bass
import concourse.mybir as mybir
from concourse.bass_utils import run_bass_kernel_spmd

F32 = mybir.dt.float32
BF16 = mybir.dt.bfloat16
ALU = mybir.AluOpType
AF = mybir.ActivationFunctionType

D = 2048
NCTX = 256
NLAT = 4096
NT = NCTX + NLAT
DEPTH = 4
DFF = 5632
INW = 3904
CHUNKS = [(0, 256)] + [(256 + 512 * i, 512) for i in range(8)]
EPS = 1e-6


class Res:
    __slots__ = ("lw", "rd")

    def __init__(self):
        self.lw = None
        self.rd = []


class Op:
    __slots__ = ("eng", "fn", "deps", "sig", "key", "inc", "val")


class Prog:
    NQ = 12
    RELAX = True

    def __init__(self, nc, st):
        self.nc = nc
        self.sem = {}
        for k in ["pe", "act", "dve", "pool"]:
            self.sem[k] = st.enter_context(nc.semaphore("s_" + k))
        for q in ["sp", "pool"]:
            for s in range(self.NQ):
                k = "d_%s_%d" % (q, s)
                self.sem[k] = st.enter_context(nc.semaphore("s_" + k))
        self.semval = {k: 0 for k in self.sem}
        self.dqn = {"sp": 0, "pool": 0}
        self.slot_last = {}
        self.reset_stage()
        self.res = []

    def R(self):
        r = Res()
        self.res.append(r)
        return r

    def reset_stage(self):
        self.ops = {e: [] for e in ("pe", "act", "dve", "pool", "sp")}
        self.order = []

    def _mk(self, eng, fn, reads, writes, key, inc, acc=False):
        op = Op()
        op.eng = eng; op.fn = fn; op.sig = False; op.key = key; op.inc = inc; op.val = None
        deps = []
        for r in reads:
            if r.lw is not None:
                deps.append(r.lw)
        for r in writes:
            if r.lw is not None:
                deps.append(r.lw)
            deps.extend(r.rd)
        if acc:
            deps = [d for d in deps if not (d.eng == eng and d.key == key)]
        if eng in ("dve", "act") and self.RELAX:
            prev = self.ops[eng][-1] if self.ops[eng] else None
            deps = [d for d in deps if not (d.eng == eng and d.key == key and d is not prev)]
        op.deps = deps
        for r in reads:
            r.rd.append(op)
        for r in writes:
            r.lw = op
            r.rd = []
        self.ops[eng].append(op)
        self.order.append(op)
        return op

    def pe(self, fn, r=(), w=(), acc=True):
        return self._mk("pe", fn, r, w, "pe", 1, acc)

    def act(self, fn, r=(), w=()):
        return self._mk("act", fn, r, w, "act", 1)

    def dve(self, fn, r=(), w=()):
        return self._mk("dve", fn, r, w, "dve", 1)

    def pool(self, fn, r=(), w=()):
        return self._mk("pool", fn, r, w, "pool", 1)

    def dma(self, q, fn, r=(), w=()):
        i = self.dqn[q]
        self.dqn[q] += 1
        key = "d_%s_%d" % (q, i % self.NQ)
        op = self._mk(q, fn, r, w, key, 16)
        prev = self.slot_last.get(key)
        if prev is not None:
            op.deps.append(prev)
        self.slot_last[key] = op
        op.sig = True
        return op

    def ld(self, fn, r=(), w=()):
        return self.dma("sp", fn, r, w)

    def stq(self, fn, r=(), w=()):
        return self.dma("pool", fn, r, w)

    def end_stage(self):
        lasts = []
        for e in ("pe", "act", "dve", "pool"):
            c = [o for o in self.ops[e] if o.key == e]
            if c:
                lasts.append(c[-1])
        for k, o in self.slot_last.items():
            if o is not None:
                lasts.append(o)
        for e in ("pe", "act", "dve", "pool", "sp"):
            b = Op()
            b.eng = e; b.fn = None; b.deps = list(lasts); b.sig = False; b.key = None; b.inc = 0; b.val = None
            self.ops[e].append(b)
            self.order.append(b)
        staged = set(id(o) for o in self.order)
        for o in self.order:
            o.deps = [d for d in o.deps if id(d) in staged]
            for d in o.deps:
                d.sig = True
        for e in ("pe", "act", "dve", "pool", "sp"):
            for o in self.ops[e]:
                if o.fn is not None and o.sig:
                    self.semval[o.key] += o.inc
                    o.val = self.semval[o.key]
        nc = self.nc
        sem = self.sem

        def run(engine, lst):
            known = {}
            for o in lst:
                need = {}
                for d in o.deps:
                    if need.get(d.key, 0) < d.val:
                        need[d.key] = d.val
                for k, v in need.items():
                    if known.get(k, 0) >= v:
                        continue
                    engine.wait_ge(sem[k], v)
                    known[k] = v
                if o.fn is None:
                    continue
                ins = o.fn(engine)
                if o.sig:
                    ins.then_inc(sem[o.key], o.inc)

        with nc.Block() as block:
            @block.tensor
            def _(e):
                run(e, self.ops["pe"])

            @block.scalar
            def _(e):
                run(e, self.ops["act"])

            @block.vector
            def _(e):
                run(e, self.ops["dve"])

            @block.gpsimd
            def _(e):
                run(e, self.ops["pool"])

            @block.sync
            def _(e):
                run(e, self.ops["sp"])
        for r in self.res:
            r.lw = None
            r.rd = []
        self.res = []
        self.slot_last = {}
        self.reset_stage()


class Stage:
    CNT = 0

    def __init__(self, P):
        self.P = P
        self.nc = P.nc
        self.st = contextlib.ExitStack()
        self.n = 0

    def __enter__(self):
        self.st.__enter__()
        return self

    def __exit__(self, *a):
        self.P.end_stage()
        return self.st.__exit__(*a)

    def sb(self, shape, dt, nbuf=1):
        out = []
        for i in range(nbuf):
            Stage.CNT += 1
            t = self.st.enter_context(self.nc.sbuf_tensor("t%d" % Stage.CNT, list(shape), dt))
            out.append((t, self.P.R()))
        return out if nbuf > 1 else out[0]

    def ps(self, shape, dt, nbuf=1):
        out = []
        for i in range(nbuf):
            Stage.CNT += 1
            t = self.st.enter_context(self.nc.psum_tensor("p%d" % Stage.CNT, list(shape), dt))
            out.append((t, self.P.R()))
        return out if nbuf > 1 else out[0]


class Rot:
    def __init__(self, items):
        self.items = [items] if isinstance(items, tuple) else items
        self.i = 0

    def next(self):
        x = self.items[self.i % len(self.items)]
        self.i += 1
        return x


def build(nlayers=DEPTH, stop_after=None, taps=False):
    nc = bass.Bass("TRN2", target_bir_lowering=False)
    kin = "ExternalInput"
    dbg = "ExternalOutput" if taps else "Internal"

    def din(name, shape, dt=F32):
        return nc.dram_tensor(name, list(shape), dt, kind=kin).ap()

    def dscr(name, shape, dt=F32):
        return nc.dram_tensor(name, list(shape), dt, kind=dbg).ap()

    I = {}
    I["xc"] = din("xc", [NT, D])
    I["cvec"] = din("cvec", [2, D])
    I["ropeC"] = din("ropeC", [128, NT])
    I["ropeS"] = din("ropeS", [128, NT])
    I["perm"] = din("perm", [128, 128])
    I["ident"] = din("ident", [128, 128])
    shapes = dict(
        w_ada=[DEPTH, D, 6 * D], b_ada=[DEPTH, 6 * D], norm_pre_mix=[DEPTH, D], norm_post_mix=[DEPTH, D],
        norm_pre_ffn=[DEPTH, D], norm_post_ffn=[DEPTH, D], w_in=[DEPTH, D, INW], w_out=[DEPTH, D, D],
        da_lam_q1=[DEPTH, 64], da_lam_k1=[DEPTH, 64], da_lam_q2=[DEPTH, 64], da_lam_k2=[DEPTH, 64], da_subln=[DEPTH, 128],
        s5_lam_re=[DEPTH, 2, 32, 64], s5_lam_im=[DEPTH, 2, 32, 64], s5_log_step=[DEPTH, 2, 32],
        s5_b_re=[DEPTH, 2, 32, 64, 16], s5_b_im=[DEPTH, 2, 32, 64, 16], s5_c_re=[DEPTH, 2, 32, 16, 64],
        s5_c_im=[DEPTH, 2, 32, 16, 64], s5_d=[DEPTH, 512], s5_w_glu=[DEPTH, 512, 512], mla_q_norm=[DEPTH, 512],
        mla_kv_norm=[DEPTH, 256], mla_w_uq=[DEPTH, 512, 768], mla_w_ukv=[DEPTH, 256, 1024], conv_w=[DEPTH, 31, 512],
        conv_b=[DEPTH, 512], conv_ln_g=[DEPTH, 512], conv_ln_b=[DEPTH, 512], w_ffn_in=[DEPTH, D, 2 * DFF],
        w_ffn_out=[DEPTH, DFF, D])
    for k, s in shapes.items():
        I[k] = din(k, s)

    xres = nc.dram_tensor("xres", [NT, D], F32, kind="ExternalOutput").ap()
    S = {}
    S["modv"] = dscr("modv", [2, 6, D])
    S["hxT"] = dscr("hxT", [D, NT], BF16)
    S["qT"] = dscr("qT", [512, NT], BF16)
    S["kT"] = dscr("kT", [512, NT], BF16)
    S["vda"] = dscr("vda", [NT, 512], BF16)
    S["uT"] = dscr("uT", [512, NT])
    S["mlaT"] = dscr("mlaT", [832, NT])
    S["cuT"] = dscr("cuT", [512, NT], BF16)
    S["ymixT"] = dscr("ymixT", [D, NT], BF16)
    S["mqT"] = dscr("mqT", [4, 192, NT], BF16)
    S["mkT"] = dscr("mkT", [4, 128, NT], BF16)
    S["mkrT"] = dscr("mkrT", [64, NT], BF16)
    S["mv"] = dscr("mv", [NT, 512], BF16)
    S["h2T"] = dscr("h2T", [D, NT], BF16)
    ys5a = dscr("ys5a", [512, NCTX + NT])
    ys5b = dscr("ys5b", [512, NCTX + NT])
    S["ys5"] = [ys5a, ys5b]
    WW = []
    for i in range(2):
        Wd = {}
        Wd["w_in"] = nc.dram_tensor("wb_in%d" % i, [D, INW], BF16, kind="Internal").ap()
        Wd["w_out"] = nc.dram_tensor("wb_out%d" % i, [D, D], BF16, kind="Internal").ap()
        Wd["w_ffi"] = nc.dram_tensor("wb_ffi%d" % i, [D, 2 * DFF], BF16, kind="Internal").ap()
        Wd["w_ffo"] = nc.dram_tensor("wb_ffo%d" % i, [DFF, D], BF16, kind="Internal").ap()
        Wd["w_uq"] = nc.dram_tensor("wb_uq%d" % i, [512, 768], BF16, kind="Internal").ap()
        Wd["w_ukv"] = nc.dram_tensor("wb_ukv%d" % i, [256, 1024], BF16, kind="Internal").ap()
        Wd["w_glu"] = nc.dram_tensor("wb_glu%d" % i, [512, 512], BF16, kind="Internal").ap()
        WW.append(Wd)

    top = contextlib.ExitStack()
    with top:
        P = Prog(nc, top)
        def gsb(name, shape, dt):
            return top.enter_context(nc.sbuf_tensor(name, list(shape), dt))
        identb = gsb("identb", [128, 128], BF16)
        identf = gsb("identf", [128, 128], F32)
        permf = gsb("permf", [128, 128], F32)
        onesf = gsb("onesf", [128, 128], F32)
        onesb = gsb("onesb", [128, 128], BF16)
        scT = gsb("scT", [128, 2, 16], F32)
        with Stage(P) as sg:
            r = P.R()
            P.stq(lambda e: e.dma_start(out=identb[:], in_=I["ident"]), w=[r])
            P.ld(lambda e: e.dma_start(out=identf[:], in_=I["ident"]), w=[r])
            P.ld(lambda e: e.dma_start(out=permf[:], in_=I["perm"]), w=[r])
            P.dve(lambda e: e.memset(onesf[:], 1.0), w=[r])
            P.dve(lambda e: e.memset(onesb[:], 1.0), w=[r])
            ct, cr = sg.sb([128, 2, 16], F32)
            for rr in range(2):
                P.ld(lambda e, rr=rr: e.dma_start(out=ct[:, rr, :], in_=I["cvec"][rr].rearrange("(k p) -> p k", p=128), allow_slow_non_contiguous=True), w=[cr])
            P.act(lambda e: e.activation(out=scT[:], in_=ct[:], func=AF.Silu), r=[cr], w=[r])

        C = dict(identb=identb, identf=identf, permf=permf, onesf=onesf, onesb=onesb, scT=scT)
        for l in range(nlayers):
            last = (l == DEPTH - 1)
            xsrc = I["xc"] if l == 0 else xres
            W = WW[l % 2]
            if l == 0:
                with Stage(P) as sg0:
                    emit_cast(P, I, W, 0)
            stage_ada(P, I, S, C, l)
            if stop_after == "ada":
                break
            stage_prenorm(P, S, C, xsrc, S["hxT"], 0)
            if stop_after == "prenorm":
                break
            stage_win(P, I, S, W, C)
            if stop_after == "win":
                break
            stage_da(P, I, S, C, l, (lambda l=l: emit_cast(P, I, WW[(l + 1) % 2], l + 1)) if l + 1 < nlayers else None)
            if stop_after == "da":
                break
            stage_mla(P, I, S, W, C, l)
            if stop_after == "mla":
                break
            stage_conv(P, I, S, C, l)
            if stop_after == "conv":
                break
            if os.environ.get("SKIP_S5") != "1":
                stage_s5(P, I, S, W, C, l)
            if stop_after == "s5":
                break
            stage_wout(P, I, S, W, C, l, xsrc, xres)
            if stop_after == "wout":
                break
            stage_ffn(P, I, S, W, C, l, xres)
    return nc


def emit_cast(P, I, W, l):
    r = P.R()
    def cp(dst, src, rows, step):
        for i in range(0, rows, step):
            P.stq(lambda e, i=i: e.dma_start(out=dst[i:i + step, :], in_=src[i:i + step, :]), w=[r])
    cp(W["w_in"], I["w_in"][l], D, D)
    cp(W["w_out"], I["w_out"][l], D, D)
    cp(W["w_ffi"], I["w_ffn_in"][l], D, 1024)
    cp(W["w_ffo"], I["w_ffn_out"][l], DFF, 2816)
    cp(W["w_uq"], I["mla_w_uq"][l], 512, 512)
    cp(W["w_ukv"], I["mla_w_ukv"][l], 256, 256)
    cp(W["w_glu"], I["s5_w_glu"][l], 512, 512)


def stage_ada(P, I, S, C, l):
    nc = P.nc
    with Stage(P) as sg:
        wt = sg.sb([128, 16, 512], F32, 2)
        wrot = Rot(wt)
        bia, biar = sg.sb([2, 6 * D], F32)
        mod, modr = bia, biar
        gv, gvr = sg.sb([2, 4, D], F32)
        outv, outr = sg.sb([2, 6, D], F32)
        pss = Rot(sg.ps([2, 512], F32, 2))
        P.ld(lambda e: e.dma_start(out=bia[:], in_=I["b_ada"][l:l + 1, :].to_broadcast([2, 6 * D])), w=[biar])
        for i, nm in enumerate(["norm_pre_mix", "norm_post_mix", "norm_pre_ffn", "norm_post_ffn"]):
            P.ld(lambda e, i=i, nm=nm: e.dma_start(out=gv[:, i, :], in_=I[nm][l:l + 1, :].to_broadcast([2, D])), w=[gvr])
        wv = I["w_ada"][l].rearrange("(k p) c -> p k c", p=128)
        for j in range(24):
            (w, wr) = wrot.next()
            P.ld(lambda e, w=w, j=j: e.dma_start(out=w[:], in_=wv[:, :, j * 512:(j + 1) * 512]), w=[wr])
            (ps, psr) = pss.next()
            for k in range(16):
                P.pe(lambda e, w=w, k=k, ps=ps: e.matmul(ps[:], lhsT=C["scT"][:, :, k], rhs=w[:, k, :], start=(k == 0), stop=(k == 15)),
                     r=[wr], w=[psr])
            P.dve(lambda e, ps=ps, j=j: e.tensor_tensor(out=mod[:, j * 512:(j + 1) * 512], in0=ps[:], in1=bia[:, j * 512:(j + 1) * 512], op=ALU.add),
                  r=[psr, biar], w=[modr])
        m = lambda i: mod[:, i * D:(i + 1) * D]
        P.dve(lambda e: e.scalar_tensor_tensor(out=outv[:, 0, :], in0=m(1), scalar=1.0, in1=gv[:, 0, :], op0=ALU.add, op1=ALU.mult), r=[modr, gvr], w=[outr])
        P.dve(lambda e: e.tensor_copy(out=outv[:, 1, :], in_=m(0)), r=[modr], w=[outr])
        P.dve(lambda e: e.tensor_tensor(out=outv[:, 2, :], in0=m(2), in1=gv[:, 1, :], op=ALU.mult), r=[modr, gvr], w=[outr])
        P.dve(lambda e: e.scalar_tensor_tensor(out=outv[:, 3, :], in0=m(4), scalar=1.0, in1=gv[:, 2, :], op0=ALU.add, op1=ALU.mult), r=[modr, gvr], w=[outr])
        P.dve(lambda e: e.tensor_copy(out=outv[:, 4, :], in_=m(3)), r=[modr], w=[outr])
        P.dve(lambda e: e.tensor_tensor(out=outv[:, 5, :], in0=m(5), in1=gv[:, 3, :], op=ALU.mult), r=[modr, gvr], w=[outr])
        dr = P.R()
        P.stq(lambda e: e.dma_start(out=S["modv"], in_=outv[:]), r=[outr], w=[dr])


def load_modvec(P, sg, S, idx):
    out = {}
    for kind in range(2):
        t, r = sg.sb([128, D], F32)
        P.ld(lambda e, t=t, kind=kind: e.dma_start(out=t[:], in_=S["modv"][kind:kind + 1, idx, :].to_broadcast([128, D])), w=[r])
        out[kind] = (t, r)
    return out


def rstd_from_ss(P, sg, ss, ssr, n_feat, rot=None):
    t, tr = rot.next() if rot is not None else sg.sb([128, 1], F32)
    P.dve(lambda e: e.tensor_scalar(out=t[:], in0=ss[:], scalar1=1.0 / n_feat, scalar2=EPS, op0=ALU.mult, op1=ALU.add), r=[ssr], w=[tr])
    P.act(lambda e: e.activation(out=t[:], in_=t[:], func=AF.Sqrt), r=[tr], w=[tr])
    P.dve(lambda e: e.reciprocal(out=t[:], in_=t[:]), r=[tr], w=[tr])
    return t, tr


def norm_mod_transpose(P, sg, C, xt, xtr, G, SH, kind, hT, hTr, col0, bufs):
    junk, junkr = bufs["junk"]
    ss, ssr = bufs["ss"].next()
    hb, hbr = bufs["hb"].next()
    P.act(lambda e: e.activation(out=junk[:], in_=xt[:], func=AF.Square, accum_out=ss[:]), r=[xtr], w=[junkr, ssr])
    rs, rsr = rstd_from_ss(P, sg, ss, ssr, D, bufs["rs"])
    tmp, tmpr = bufs["tmp"].next()
    P.dve(lambda e: e.scalar_tensor_tensor(out=tmp[:], in0=xt[:], scalar=rs[:, 0:1], in1=G[kind][0][:], op0=ALU.mult, op1=ALU.mult),
          r=[xtr, rsr, G[kind][1]], w=[tmpr])
    P.pool(lambda e: e.tensor_tensor(out=hb[:], in0=tmp[:], in1=SH[kind][0][:], op=ALU.add), r=[tmpr, SH[kind][1]], w=[hbr])
    for half in range(2):
        pt, ptr = bufs["pt"].next()
        for k in range(8):
            kk = half * 8 + k
            P.pe(lambda e, pt=pt, k=k, kk=kk: e.transpose(pt[:, k * 128:(k + 1) * 128], hb[:, kk * 128:(kk + 1) * 128], C["identb"][:]),
                 r=[hbr], w=[ptr])
        P.act(lambda e, pt=pt, half=half: e.activation(out=hT[:, half * 8:(half + 1) * 8, col0:col0 + 128],
                                                       in_=pt[:].rearrange("p (k t) -> p k t", k=8), func=AF.Copy),
              r=[ptr], w=[hTr])


def norm_bufs(sg):
    return dict(junk=sg.sb([128, D], BF16), ss=Rot(sg.sb([128, 1], F32, 2)), rs=Rot(sg.sb([128, 1], F32, 2)),
                hb=Rot(sg.sb([128, D], BF16, 2)), tmp=Rot(sg.sb([128, D], F32, 2)), pt=Rot(sg.ps([128, 1024], BF16, 2)))


def stage_prenorm(P, S, C, xsrc, dstT, midx):
    nc = P.nc
    with Stage(P) as sg:
        G = load_modvec(P, sg, S, midx)
        SH = load_modvec(P, sg, S, midx + 1)
        xts = Rot(sg.sb([128, D], F32, 2))
        hTs = Rot(sg.sb([128, 16, 512], BF16, 2))
        bufs = norm_bufs(sg)
        dr = P.R()
        dv = dstT.rearrange("(k p) t -> p k t", p=128)
        for (t0, n) in CHUNKS:
            hT, hTr = hTs.next()
            for i in range(n // 128):
                xt, xtr = xts.next()
                P.ld(lambda e, xt=xt, a=t0 + i * 128: e.dma_start(out=xt[:], in_=xsrc[a:a + 128, :]), w=[xtr])
                norm_mod_transpose(P, sg, C, xt, xtr, G, SH, 0 if t0 >= NCTX else 1, hT, hTr, i * 128, bufs)
            P.stq(lambda e, hT=hT, t0=t0, n=n: e.dma_start(out=dv[:, :, t0:t0 + n], in_=hT[:, :, 0:n]), r=[hTr], w=[dr])


def rope_tile(P, sg, C, ps, psr, rows, t0, n, tabs, bufs, dst_dram, dstres, scale_ap=None):
    qf, qfr = bufs["qf"].next()
    if scale_ap is None:
        P.act(lambda e: e.activation(out=qf[0:rows, 0:n], in_=ps[0:rows, 0:n], func=AF.Copy), r=[psr], w=[qfr])
    else:
        P.dve(lambda e: e.tensor_tensor(out=qf[0:rows, 0:n], in0=ps[0:rows, 0:n], in1=scale_ap[0][0:rows, 0:n], op=ALU.mult), r=[psr, scale_ap[1]], w=[qfr])
    p2, p2r = bufs["p2"].next()
    P.pe(lambda e: e.matmul(p2[0:rows, 0:n], lhsT=C["permf"][0:rows, 0:rows], rhs=qf[0:rows, 0:n], start=True, stop=True), r=[qfr], w=[p2r])
    t1, t1r = bufs["t1"].next()
    (tc, tcr), (ts, tsr) = tabs
    P.pool(lambda e: e.tensor_tensor(out=t1[0:rows, 0:n], in0=qf[0:rows, 0:n], in1=tc[0:rows, t0:t0 + n], op=ALU.mult), r=[qfr, tcr], w=[t1r])
    t2, t2r = bufs["t2"].next()
    P.dve(lambda e: e.tensor_tensor(out=t2[0:rows, 0:n], in0=p2[0:rows, 0:n], in1=ts[0:rows, t0:t0 + n], op=ALU.mult), r=[p2r, tsr], w=[t2r])
    ob, obr = bufs["ob"].next()
    P.dve(lambda e: e.tensor_tensor(out=ob[0:rows, 0:n], in0=t1[0:rows, 0:n], in1=t2[0:rows, 0:n], op=ALU.add), r=[t1r, t2r], w=[obr])
    P.stq(lambda e: e.dma_start(out=dst_dram, in_=ob[0:rows, 0:n]), r=[obr], w=[dstres])


def rope_bufs(sg):
    return dict(qf=Rot(sg.sb([128, 512], F32, 2)), p2=Rot(sg.ps([128, 512], F32, 2)), t1=Rot(sg.sb([128, 512], F32, 2)),
                t2=Rot(sg.sb([128, 512], F32, 2)), ob=Rot(sg.sb([128, 512], BF16, 2)))


def load_rope_tabs(P, sg, I):
    tc, tcr = sg.sb([128, NT], F32)
    ts, tsr = sg.sb([128, NT], F32)
    P.ld(lambda e: e.dma_start(out=tc[:], in_=I["ropeC"]), w=[tcr])
    P.ld(lambda e: e.dma_start(out=ts[:], in_=I["ropeS"]), w=[tsr])
    return (tc, tcr), (ts, tsr)


def stage_win(P, I, S, W, C):
    nc = P.nc
    with Stage(P) as sg:
        tabs = load_rope_tabs(P, sg, I)
        rb = rope_bufs(sg)
        hTs = Rot(sg.sb([128, 16, 512], BF16, 2))
        wts = Rot(sg.sb([128, 16, 128], BF16, 3))
        wv, wvr = sg.sb([128, 16, 512], BF16)
        pss = Rot(sg.ps([128, 512], F32, 4))
        ev = Rot(sg.sb([128, 512], F32, 3))
        evb = Rot(sg.sb([128, 512], BF16, 3))
        dr = P.R()
        hv = S["hxT"].rearrange("(k p) t -> p k t", p=128)
        wiv = W["w_in"].rearrange("(k p) f -> p k f", p=128)
        P.ld(lambda e: e.dma_start(out=wv[:], in_=wiv[:, :, 1024:1536]), w=[wvr])
        tiles = []
        for j in range(4):
            tiles.append((j * 128, 128, "rope", S["qT"][j * 128:(j + 1) * 128, :]))
        for j in range(4):
            tiles.append((512 + j * 128, 128, "rope", S["kT"][j * 128:(j + 1) * 128, :]))
        for j in range(4):
            tiles.append((1536 + j * 128, 128, "f32", S["uT"][j * 128:(j + 1) * 128, :]))
        for j in range(6):
            tiles.append((2048 + j * 128, 128, "f32", S["mlaT"][j * 128:(j + 1) * 128, :]))
        tiles.append((2816, 64, "f32", S["mlaT"][768:832, :]))
        for j in range(4):
            tiles.append((2880 + j * 128, 128, "cval", j))
        def do_chunk(t0, n):
            hT, hTr = hTs.next()
            P.ld(lambda e, hT=hT, t0=t0, n=n: e.dma_start(out=hT[:, :, 0:n], in_=hv[:, :, t0:t0 + n]), w=[hTr])

            def proj(c0, rows):
                w, wr = wts.next()
                P.ld(lambda e: e.dma_start(out=w[:, :, 0:rows], in_=wiv[:, :, c0:c0 + rows]), w=[wr])
                ps, psr = pss.next()
                for k in range(16):
                    P.pe(lambda e, k=k: e.matmul(ps[0:rows, 0:n], lhsT=w[:, k, 0:rows], rhs=hT[:, k, 0:n], start=(k == 0), stop=(k == 15)),
                         r=[wr, hTr], w=[psr])
                return ps, psr
            for (c0, rows, kind, dst) in tiles:
                ps, psr = proj(c0, rows)
                if kind == "rope":
                    rope_tile(P, sg, C, ps, psr, rows, t0, n, tabs, rb, dst[:, t0:t0 + n], dr)
                elif kind == "f32":
                    o, orr = ev.next()
                    P.act(lambda e, o=o, ps=ps, rows=rows: e.activation(out=o[0:rows, 0:n], in_=ps[0:rows, 0:n], func=AF.Copy), r=[psr], w=[orr])
                    P.stq(lambda e, o=o, dst=dst, rows=rows: e.dma_start(out=dst[:, t0:t0 + n], in_=o[0:rows, 0:n]), r=[orr], w=[dr])
                else:
                    j = dst
                    psg, psgr = proj(2880 + 512 + j * 128, 128)
                    sgm, sgmr = ev.next()
                    P.act(lambda e, sgm=sgm, psg=psg: e.activation(out=sgm[:, 0:n], in_=psg[:, 0:n], func=AF.Sigmoid), r=[psgr], w=[sgmr])
                    ob, obr = evb.next()
                    P.dve(lambda e, ob=ob, ps=ps, sgm=sgm: e.tensor_tensor(out=ob[:, 0:n], in0=ps[:, 0:n], in1=sgm[:, 0:n], op=ALU.mult), r=[psr, sgmr], w=[obr])
                    P.stq(lambda e, ob=ob, j=j: e.dma_start(out=S["cuT"][j * 128:(j + 1) * 128, t0:t0 + n], in_=ob[:, 0:n]), r=[obr], w=[dr])
            for i in range(n // 128):
                ps, psr = pss.next()
                for k in range(16):
                    P.pe(lambda e, k=k, i=i, ps=ps: e.matmul(ps[:, :], lhsT=hT[:, k, i * 128:(i + 1) * 128], rhs=wv[:, k, :], start=(k == 0), stop=(k == 15)),
                         r=[wvr, hTr], w=[psr])
                ob, obr = evb.next()
                P.act(lambda e, ob=ob, ps=ps: e.activation(out=ob[:], in_=ps[:], func=AF.Copy), r=[psr], w=[obr])
                P.stq(lambda e, ob=ob, a=t0 + i * 128: e.dma_start(out=S["vda"][a:a + 128, :], in_=ob[:]), r=[obr], w=[dr])
        for (t0, n) in CHUNKS:
            do_chunk(t0, n)


def attention(P, sg, C, parts, V, Vr, scale, chunk, pbufs):
    t0, n = chunk
    nkt = 2 if t0 < NCTX else NT // 128
    o, orr = pbufs["o"].next()
    z, zr = pbufs["z"].next()
    for kt in range(nkt):
        s, sr = pbufs["s"].next()
        for pi, (KT, QT, rr) in enumerate(parts):
            P.pe(lambda e, KT=KT, QT=QT, pi=pi, s=s, kt=kt: e.matmul(s[:, 0:n], lhsT=KT[:, kt * 128:(kt + 1) * 128], rhs=QT[:, t0:t0 + n],
                                                                  start=(pi == 0), stop=(pi == len(parts) - 1)), r=[rr], w=[sr])
        p, pr = pbufs["p"].next()
        P.act(lambda e, p=p, s=s: e.activation(out=p[:, 0:n], in_=s[:, 0:n], func=AF.Exp, scale=scale), r=[sr], w=[pr])
        P.pe(lambda e, p=p, kt=kt: e.matmul(o[:, 0:n], lhsT=V[:, kt, :], rhs=p[:, 0:n], start=(kt == 0), stop=(kt == nkt - 1)), r=[pr, Vr], w=[orr])
        P.pe(lambda e, p=p, kt=kt: e.matmul(z[:, 0:n], lhsT=C["onesb"][:], rhs=p[:, 0:n], start=(kt == 0), stop=(kt == nkt - 1)), r=[pr], w=[zr])
    return (o, orr), (z, zr)


def stage_da(P, I, S, C, l, prefetch=None):
    nc = P.nc
    li = 0.8 - 0.6 * math.exp(-0.3 * l)
    with Stage(P) as sg:
        if prefetch is not None:
            prefetch()
        lv, lvr = sg.sb([128, 4, 64], F32)
        for i, nm in enumerate(["da_lam_q1", "da_lam_k1", "da_lam_q2", "da_lam_k2"]):
            P.ld(lambda e, i=i, nm=nm: e.dma_start(out=lv[:, i, :], in_=I[nm][l:l + 1, :].to_broadcast([128, 64])), w=[lvr])
        lp, lpr = sg.sb([128, 2, 64], F32)
        P.dve(lambda e: e.tensor_tensor(out=lp[:, 0, :], in0=lv[:, 0, :], in1=lv[:, 1, :], op=ALU.mult), r=[lvr], w=[lpr])
        P.dve(lambda e: e.tensor_tensor(out=lp[:, 1, :], in0=lv[:, 2, :], in1=lv[:, 3, :], op=ALU.mult), r=[lvr], w=[lpr])
        lsum, lsr = sg.sb([128, 2], F32)
        P.dve(lambda e: e.tensor_reduce(out=lsum[:], in_=lp[:], axis=mybir.AxisListType.X, op=ALU.add), r=[lpr], w=[lsr])
        P.act(lambda e: e.activation(out=lsum[:], in_=lsum[:], func=AF.Exp), r=[lsr], w=[lsr])
        nlam, nlr = sg.sb([128, 1], F32)
        P.dve(lambda e: e.tensor_tensor(out=nlam[:], in0=lsum[:, 1:2], in1=lsum[:, 0:1], op=ALU.subtract), r=[lsr], w=[nlr])
        P.dve(lambda e: e.tensor_scalar(out=nlam[:], in0=nlam[:], scalar1=-li, scalar2=None, op0=ALU.add), r=[nlr], w=[nlr])
        gs, gsr = sg.sb([128, 1], F32)
        P.ld(lambda e: e.dma_start(out=gs[:], in_=I["da_subln"][l].rearrange("(p o) -> p o", o=1), allow_slow_non_contiguous=True), w=[gsr])
        P.dve(lambda e: e.tensor_scalar(out=gs[:], in0=gs[:], scalar1=(1.0 - li), scalar2=None, op0=ALU.mult), r=[gsr], w=[gsr])

        KT, KTr = sg.sb([128, NT], BF16)
        QT, QTr = sg.sb([128, NT], BF16)
        V, Vr = sg.sb([128, 34, 128], BF16)
        pb = dict(o=Rot(sg.ps([128, 512], F32, 2)), z=Rot(sg.ps([128, 512], F32, 2)), s=Rot(sg.ps([128, 512], F32, 3)),
                  p=Rot(sg.sb([128, 512], BF16, 4)), acc=Rot(sg.sb([128, 512], F32, 4)))
        ssp = sg.ps([128, 512], F32)
        wk = Rot(sg.sb([128, 512], F32, 6))
        ob = Rot(sg.sb([128, 512], BF16, 2))
        dr = P.R()
        for h in range(4):
            P.ld(lambda e, h=h: e.dma_start(out=KT[:], in_=S["kT"][h * 128:(h + 1) * 128, :]), w=[KTr])
            P.ld(lambda e, h=h: e.dma_start(out=QT[:], in_=S["qT"][h * 128:(h + 1) * 128, :]), w=[QTr])
            P.ld(lambda e, h=h: e.dma_start(out=V[:], in_=S["vda"][:, h * 128:(h + 1) * 128].rearrange("(k p) d -> p k d", p=128)), w=[Vr])
            def do_chunk(h, chunk):
                t0, n = chunk
                nrm = []
                for m in range(2):
                    parts = [(KT[m * 64:(m + 1) * 64, :], QT[m * 64:(m + 1) * 64, :], KTr if False else QTr)]
                    (o, orr), (z, zr) = attention_rw(P, sg, C, parts, [KTr, QTr], V, Vr, 0.125, chunk, pb)
                    rz, rzr = wk.next()
                    P.dve(lambda e, rz=rz, z=z: e.reciprocal(out=rz[:, 0:n], in_=z[:, 0:n]), r=[zr], w=[rzr])
                    a, ar = wk.next()
                    P.dve(lambda e, a=a, o=o, rz=rz: e.tensor_tensor(out=a[:, 0:n], in0=o[:, 0:n], in1=rz[:, 0:n], op=ALU.mult), r=[orr, rzr], w=[ar])
                    nrm.append((a, ar))
                d, ddr = wk.next()
                (a1, a1r), (a2, a2r) = nrm
                P.dve(lambda e, d=d, a1=a1, a2=a2: e.scalar_tensor_tensor(out=d[:, 0:n], in0=a2[:, 0:n], scalar=nlam[:, 0:1], in1=a1[:, 0:n], op0=ALU.mult, op1=ALU.add),
                      r=[a1r, a2r, nlr], w=[ddr])
                sq, sqr = wk.next()
                P.act(lambda e, sq=sq, d=d: e.activation(out=sq[:, 0:n], in_=d[:, 0:n], func=AF.Square), r=[ddr], w=[sqr])
                (sp_, spr) = ssp
                P.pe(lambda e, sq=sq: e.matmul(sp_[:, 0:n], lhsT=C["onesf"][:], rhs=sq[:, 0:n], start=True, stop=True), r=[sqr], w=[spr])
                rs, rsr = wk.next()
                P.dve(lambda e, rs=rs: e.tensor_scalar(out=rs[:, 0:n], in0=sp_[:, 0:n], scalar1=1.0 / 128, scalar2=EPS, op0=ALU.mult, op1=ALU.add), r=[spr], w=[rsr])
                P.act(lambda e, rs=rs: e.activation(out=rs[:, 0:n], in_=rs[:, 0:n], func=AF.Sqrt), r=[rsr], w=[rsr])
                P.dve(lambda e, rs=rs: e.reciprocal(out=rs[:, 0:n], in_=rs[:, 0:n]), r=[rsr], w=[rsr])
                y, yr = ob.next()
                P.dve(lambda e, y=y, d=d, rs=rs: e.scalar_tensor_tensor(out=y[:, 0:n], in0=d[:, 0:n], scalar=gs[:, 0:1], in1=rs[:, 0:n], op0=ALU.mult, op1=ALU.mult),
                      r=[ddr, rsr, gsr], w=[yr])
                P.stq(lambda e, y=y, h=h, t0=t0, n=n: e.dma_start(out=S["ymixT"][h * 128:(h + 1) * 128, t0:t0 + n], in_=y[:, 0:n]), r=[yr], w=[dr])
            for chunk in CHUNKS:
                do_chunk(h, chunk)


def attention_rw(P, sg, C, parts, rres, V, Vr, scale, chunk, pbufs):
    t0, n = chunk
    nkt = 2 if t0 < NCTX else NT // 128
    o, orr = pbufs["o"].next()
    z, zr = pbufs["z"].next()

    def qk(kt):
        s, sr = pbufs["s"].next()
        for pi, (KT, QT, _) in enumerate(parts):
            P.pe(lambda e, KT=KT, QT=QT, pi=pi: e.matmul(s[:, 0:n], lhsT=KT[:, kt * 128:(kt + 1) * 128], rhs=QT[:, t0:t0 + n],
                                                        start=(pi == 0), stop=(pi == len(parts) - 1)), r=rres, w=[sr])
        p, pr = pbufs["p"].next()
        P.act(lambda e: e.activation(out=p[:, 0:n], in_=s[:, 0:n], func=AF.Exp, scale=scale), r=[sr], w=[pr])
        return p, pr

    accs = [pbufs["acc"].next()]

    def pv(kt, p, pr):
        P.pe(lambda e: e.matmul(o[:, 0:n], lhsT=V[:, kt, :], rhs=p[:, 0:n], start=(kt == 0), stop=(kt == nkt - 1)), r=[pr, Vr], w=[orr])
        a, ar = accs[0]
        eng = P.dve
        if kt < 1:
            eng(lambda e: e.tensor_copy(out=a[:, 0:n], in_=p[:, 0:n]), r=[pr], w=[ar])
        else:
            eng(lambda e: e.tensor_tensor(out=a[:, 0:n], in0=a[:, 0:n], in1=p[:, 0:n], op=ALU.add), r=[pr, ar], w=[ar])
    prev = qk(0)
    for kt in range(1, nkt):
        cur = qk(kt)
        pv(kt - 1, *prev)
        prev = cur
    pv(nkt - 1, *prev)
    a, ar = accs[0]
    P.pe(lambda e: e.matmul(z[:, 0:n], lhsT=C["onesf"][:], rhs=a[:, 0:n], start=True, stop=True), r=[ar], w=[zr])
    return (o, orr), (z, zr)


def stage_mla(P, I, S, W, C, l):
    nc = P.nc
    with Stage(P) as sg:
        tabs = load_rope_tabs(P, sg, I)
        rb = rope_bufs(sg)
        wuq, wuqr = sg.sb([128, 4, 768], BF16)
        wuk, wukr = sg.sb([128, 2, 4, 128], BF16)
        wuv, wuvr = sg.sb([128, 2, 4, 128], BF16)
        gq, gqr = sg.sb([128, 4], F32)
        gkv, gkvr = sg.sb([128, 2], F32)
        P.ld(lambda e: e.dma_start(out=wuq[:], in_=W["w_uq"].rearrange("(k p) f -> p k f", p=128)), w=[wuqr])
        ukv = W["w_ukv"].rearrange("(k p) (h c) -> p k h c", p=128, c=256)
        for k in range(2):
            P.ld(lambda e, k=k: e.dma_start(out=wuk[:, k, :, :], in_=ukv[:, k, :, 0:128]), w=[wukr])
            P.ld(lambda e, k=k: e.dma_start(out=wuv[:, k, :, :], in_=ukv[:, k, :, 128:256]), w=[wuvr])
        P.ld(lambda e: e.dma_start(out=gq[:], in_=I["mla_q_norm"][l].rearrange("(k p) -> p k", p=128), allow_slow_non_contiguous=True), w=[gqr])
        P.ld(lambda e: e.dma_start(out=gkv[:], in_=I["mla_kv_norm"][l].rearrange("(k p) -> p k", p=128), allow_slow_non_contiguous=True), w=[gkvr])
        lat = Rot(sg.sb([128, 6, 512], F32, 2))
        krs = Rot(sg.sb([64, 512], F32, 2))
        sqs = Rot(sg.sb([128, 6, 512], F32, 1))
        lsb = Rot(sg.sb([128, 6, 512], BF16, 2))
        rst = Rot(sg.sb([128, 2, 512], F32, 2))
        pss = Rot(sg.ps([128, 512], F32, 4))
        ob = Rot(sg.sb([128, 512], BF16, 3))
        sm = Rot(sg.sb([128, 1], F32, 4))
        dr = P.R()
        mv3 = S["mlaT"][0:768, :].rearrange("(k p) t -> p k t", p=128)

        def do_chunk(t0, n):
            x, xr = lat.next()
            P.ld(lambda e: e.dma_start(out=x[:, :, 0:n], in_=mv3[:, :, t0:t0 + n]), w=[xr])
            kr, krr = krs.next()
            P.ld(lambda e: e.dma_start(out=kr[:, 0:n], in_=S["mlaT"][768:832, t0:t0 + n]), w=[krr])
            sq, sqr = sqs.next()
            P.act(lambda e: e.activation(out=sq[:, :, 0:n], in_=x[:, :, 0:n], func=AF.Square), r=[xr], w=[sqr])
            rs, rsr = rst.next()
            for (which, k0, nk, nf) in ((0, 0, 4, 512), (1, 4, 2, 256)):
                ps, psr = pss.next()
                for k in range(nk):
                    P.pe(lambda e, k=k, ps=ps, k0=k0, nk=nk: e.matmul(ps[:, 0:n], lhsT=C["onesf"][:], rhs=sq[:, k0 + k, 0:n], start=(k == 0), stop=(k == nk - 1)), r=[sqr], w=[psr])
                P.dve(lambda e, ps=ps, which=which, nf=nf: e.tensor_scalar(out=rs[:, which, 0:n], in0=ps[:, 0:n], scalar1=1.0 / nf, scalar2=EPS, op0=ALU.mult, op1=ALU.add), r=[psr], w=[rsr])
            P.act(lambda e: e.activation(out=rs[:, :, 0:n], in_=rs[:, :, 0:n], func=AF.Sqrt), r=[rsr], w=[rsr])
            P.dve(lambda e: e.reciprocal(out=rs[:, :, 0:n], in_=rs[:, :, 0:n]), r=[rsr], w=[rsr])
            xb, xbr = lsb.next()
            for k in range(6):
                g = gq[:, k:k + 1] if k < 4 else gkv[:, k - 4:k - 3]
                P.act(lambda e, k=k, g=g: e.activation(out=xb[:, k, 0:n], in_=x[:, k, 0:n], func=AF.Copy, scale=g), r=[xr, gqr, gkvr], w=[xbr])
            for h in range(4):
                ps, psr = pss.next()
                for k in range(4):
                    P.pe(lambda e, k=k, ps=ps, h=h: e.matmul(ps[:, 0:n], lhsT=wuq[:, k, h * 192:h * 192 + 128], rhs=xb[:, k, 0:n], start=(k == 0), stop=(k == 3)), r=[wuqr, xbr], w=[psr])
                o, orr = ob.next()
                P.dve(lambda e, o=o, ps=ps: e.tensor_tensor(out=o[:, 0:n], in0=ps[:, 0:n], in1=rs[:, 0, 0:n], op=ALU.mult), r=[psr, rsr], w=[orr])
                P.stq(lambda e, o=o, h=h: e.dma_start(out=S["mqT"][h, 0:128, t0:t0 + n], in_=o[:, 0:n]), r=[orr], w=[dr])
                ps2, ps2r = pss.next()
                for k in range(4):
                    P.pe(lambda e, k=k, ps2=ps2, h=h: e.matmul(ps2[0:64, 0:n], lhsT=wuq[:, k, h * 192 + 128:h * 192 + 192], rhs=xb[:, k, 0:n], start=(k == 0), stop=(k == 3)), r=[wuqr, xbr], w=[ps2r])
                rope_tile(P, sg, C, ps2, ps2r, 64, t0, n, tabs, rb, S["mqT"][h, 128:192, t0:t0 + n], dr, scale_ap=(rs[:, 0, :], rsr))
                ps3, ps3r = pss.next()
                for k in range(2):
                    P.pe(lambda e, k=k, ps3=ps3, h=h: e.matmul(ps3[:, 0:n], lhsT=wuk[:, k, h, :], rhs=xb[:, 4 + k, 0:n], start=(k == 0), stop=(k == 1)), r=[wukr, xbr], w=[ps3r])
                o2, o2r = ob.next()
                P.dve(lambda e, o2=o2, ps3=ps3: e.tensor_tensor(out=o2[:, 0:n], in0=ps3[:, 0:n], in1=rs[:, 1, 0:n], op=ALU.mult), r=[ps3r, rsr], w=[o2r])
                P.stq(lambda e, o2=o2, h=h: e.dma_start(out=S["mkT"][h, :, t0:t0 + n], in_=o2[:, 0:n]), r=[o2r], w=[dr])
            rope_tile(P, sg, C, kr, krr, 64, t0, n, tabs, rb, S["mkrT"][:, t0:t0 + n], dr)
            for i in range(n // 128):
                pv, pvr = pss.next()
                for k in range(2):
                    P.pe(lambda e, k=k, pv=pv, i=i: e.matmul(pv[:, :], lhsT=xb[:, 4 + k, i * 128:(i + 1) * 128], rhs=wuv[:, k, :, :].rearrange("p h c -> p (h c)"), start=(k == 0), stop=(k == 1)), r=[wuvr, xbr], w=[pvr])
                pt, ptr = pss.next()
                for k in range(2):
                    P.pe(lambda e, k=k, pt=pt, i=i: e.matmul(pt[:, 0:1], lhsT=sq[:, 4 + k, i * 128:(i + 1) * 128], rhs=C["onesf"][:, 0:1], start=(k == 0), stop=(k == 1)), r=[sqr], w=[ptr])
                r1, r1r = sm.next()
                P.dve(lambda e, r1=r1, pt=pt: e.tensor_scalar(out=r1[:], in0=pt[:, 0:1], scalar1=1.0 / 256, scalar2=EPS, op0=ALU.mult, op1=ALU.add), r=[ptr], w=[r1r])
                P.act(lambda e, r1=r1: e.activation(out=r1[:], in_=r1[:], func=AF.Sqrt), r=[r1r], w=[r1r])
                P.dve(lambda e, r1=r1: e.reciprocal(out=r1[:], in_=r1[:]), r=[r1r], w=[r1r])
                o3, o3r = ob.next()
                P.act(lambda e, o3=o3, pv=pv, r1=r1: e.activation(out=o3[:], in_=pv[:], func=AF.Copy, scale=r1[:, 0:1]), r=[pvr, r1r], w=[o3r])
                P.stq(lambda e, o3=o3, a=t0 + i * 128: e.dma_start(out=S["mv"][a:a + 128, :], in_=o3[:]), r=[o3r], w=[dr])
        for (t0, n) in CHUNKS:
            do_chunk(t0, n)
    with Stage(P) as sg:
        KN, KNr = sg.sb([128, NT], BF16)
        QN, QNr = sg.sb([128, NT], BF16)
        KR, KRr = sg.sb([64, NT], BF16)
        QR, QRr = sg.sb([64, NT], BF16)
        V, Vr = sg.sb([128, 34, 128], BF16)
        pb = dict(o=Rot(sg.ps([128, 512], F32, 2)), z=Rot(sg.ps([128, 512], F32, 2)), s=Rot(sg.ps([128, 512], F32, 4)),
                  p=Rot(sg.sb([128, 512], BF16, 4)), acc=Rot(sg.sb([128, 512], F32, 4)))
        wk = Rot(sg.sb([128, 512], F32, 3))
        ob = Rot(sg.sb([128, 512], BF16, 2))
        dr = P.R()
        P.ld(lambda e: e.dma_start(out=KR[:], in_=S["mkrT"]), w=[KRr])

        def do_chunk(h, chunk):
            t0, n = chunk
            parts = [(KN[:, :], QN[:, :], None), (KR[:, :], QR[:, :], None)]
            (o, orr), (z, zr) = attention_rw(P, sg, C, parts, [KNr, QNr, KRr, QRr], V, Vr, 192.0 ** -0.5, chunk, pb)
            rz, rzr = wk.next()
            P.dve(lambda e: e.reciprocal(out=rz[:, 0:n], in_=z[:, 0:n]), r=[zr], w=[rzr])
            y, yr = ob.next()
            P.dve(lambda e: e.tensor_tensor(out=y[:, 0:n], in0=o[:, 0:n], in1=rz[:, 0:n], op=ALU.mult), r=[orr, rzr], w=[yr])
            P.stq(lambda e: e.dma_start(out=S["ymixT"][1024 + h * 128:1024 + (h + 1) * 128, t0:t0 + n], in_=y[:, 0:n]), r=[yr], w=[dr])
        for h in range(4):
            P.ld(lambda e, h=h: e.dma_start(out=KN[:], in_=S["mkT"][h]), w=[KNr])
            P.ld(lambda e, h=h: e.dma_start(out=QN[:], in_=S["mqT"][h, 0:128, :]), w=[QNr])
            P.ld(lambda e, h=h: e.dma_start(out=QR[:], in_=S["mqT"][h, 128:192, :]), w=[QRr])
            P.ld(lambda e, h=h: e.dma_start(out=V[:], in_=S["mv"][:, h * 128:(h + 1) * 128].rearrange("(k p) d -> p k d", p=128)), w=[Vr])
            for chunk in CHUNKS:
                do_chunk(h, chunk)


CPAD = NT + 45


def stage_conv(P, I, S, C, l):
    nc = P.nc
    with Stage(P) as sg:
        U, Ur = sg.sb([128, 4, CPAD], BF16)
        Dj, Djr = sg.sb([128, 4, 31, 128], BF16)
        wc, wcr = sg.sb([128, 4, 31], F32)
        pv, pvr = sg.sb([128, 3, 4], F32)
        P.pool(lambda e: e.memset(U[:], 0.0), w=[Ur])
        cu = S["cuT"].rearrange("(f p) t -> p f t", p=128)
        P.ld(lambda e: e.dma_start(out=U[:, :, 15:15 + NCTX], in_=cu[:, :, 0:NCTX]), w=[Ur])
        P.ld(lambda e: e.dma_start(out=U[:, :, 286:286 + NLAT], in_=cu[:, :, NCTX:NT]), w=[Ur])
        for ft in range(4):
            P.ld(lambda e, ft=ft: e.dma_start(out=wc[:, ft, :], in_=I["conv_w"][l][:, ft * 128:(ft + 1) * 128].rearrange("j c -> c j"), allow_slow_non_contiguous=True), w=[wcr])
        for i, nm in enumerate(["conv_b", "conv_ln_g", "conv_ln_b"]):
            P.ld(lambda e, i=i, nm=nm: e.dma_start(out=pv[:, i, :], in_=I[nm][l].rearrange("(f p) -> p f", p=128), allow_slow_non_contiguous=True), w=[pvr])
        for ft in range(4):
            for j in range(31):
                P.dve(lambda e, ft=ft, j=j: e.tensor_scalar(out=Dj[:, ft, j, :], in0=C["identf"][:], scalar1=wc[:, ft, j:j + 1], scalar2=None, op0=ALU.mult), r=[wcr], w=[Djr])
        pss = Rot(sg.ps([128, 512], F32, 4))
        st1 = sg.ps([128, 512], F32)
        st2 = sg.ps([128, 512], F32)
        ys = Rot(sg.sb([128, 4, 512], F32, 2))
        sqs = Rot(sg.sb([128, 4, 512], F32, 1))
        wk = Rot(sg.sb([128, 512], F32, 2))
        stb = Rot(sg.sb([128, 512], F32, 3))
        ob = Rot(sg.sb([128, 512], BF16, 3))
        dr = P.R()

        def do_block(base, t0, n):
            y, yr = ys.next()
            sq, sqr = sqs.next()
            for ft in range(4):
                ps, psr = pss.next()
                for j in range(31):
                    P.pe(lambda e, ft=ft, j=j, ps=ps: e.matmul(ps[:, 0:n], lhsT=Dj[:, ft, j, :], rhs=U[:, ft, base + j - 15:base + j - 15 + n], start=(j == 0), stop=(j == 30)), r=[Djr, Ur], w=[psr])
                P.act(lambda e, ft=ft, ps=ps: e.activation(out=y[:, ft, 0:n], in_=ps[:, 0:n], func=AF.Identity, bias=pv[:, 0, ft:ft + 1]), r=[psr, pvr], w=[yr])
            P.act(lambda e: e.activation(out=sq[:, :, 0:n], in_=y[:, :, 0:n], func=AF.Square), r=[yr], w=[sqr])
            (s1, s1r), (s2, s2r) = st1, st2
            for ft in range(4):
                P.pe(lambda e, ft=ft: e.matmul(s1[:, 0:n], lhsT=C["onesf"][:], rhs=y[:, ft, 0:n], start=(ft == 0), stop=(ft == 3)), r=[yr], w=[s1r])
            for ft in range(4):
                P.pe(lambda e, ft=ft: e.matmul(s2[:, 0:n], lhsT=C["onesf"][:], rhs=sq[:, ft, 0:n], start=(ft == 0), stop=(ft == 3)), r=[sqr], w=[s2r])
            mu, mur = stb.next()
            P.dve(lambda e: e.tensor_scalar(out=mu[:, 0:n], in0=s1[:, 0:n], scalar1=1.0 / 512, scalar2=None, op0=ALU.mult), r=[s1r], w=[mur])
            m2, m2r = stb.next()
            P.dve(lambda e: e.tensor_tensor(out=m2[:, 0:n], in0=mu[:, 0:n], in1=mu[:, 0:n], op=ALU.mult), r=[mur], w=[m2r])
            va, var_ = stb.next()
            P.dve(lambda e: e.scalar_tensor_tensor(out=va[:, 0:n], in0=s2[:, 0:n], scalar=1.0 / 512, in1=m2[:, 0:n], op0=ALU.mult, op1=ALU.subtract), r=[s2r, m2r], w=[var_])
            P.dve(lambda e: e.tensor_scalar(out=va[:, 0:n], in0=va[:, 0:n], scalar1=EPS, scalar2=None, op0=ALU.add), r=[var_], w=[var_])
            P.act(lambda e: e.activation(out=va[:, 0:n], in_=va[:, 0:n], func=AF.Sqrt), r=[var_], w=[var_])
            P.dve(lambda e: e.reciprocal(out=va[:, 0:n], in_=va[:, 0:n]), r=[var_], w=[var_])
            for ft in range(4):
                t, tr = wk.next()
                P.dve(lambda e, ft=ft, t=t: e.tensor_tensor(out=t[:, 0:n], in0=y[:, ft, 0:n], in1=mu[:, 0:n], op=ALU.subtract), r=[yr, mur], w=[tr])
                P.dve(lambda e, ft=ft, t=t: e.scalar_tensor_tensor(out=t[:, 0:n], in0=t[:, 0:n], scalar=pv[:, 1, ft:ft + 1], in1=va[:, 0:n], op0=ALU.mult, op1=ALU.mult), r=[tr, var_, pvr], w=[tr])
                o, orr = ob.next()
                P.act(lambda e, ft=ft, t=t, o=o: e.activation(out=o[:, 0:n], in_=t[:, 0:n], func=AF.Silu, bias=pv[:, 2, ft:ft + 1]), r=[tr, pvr], w=[orr])
                P.stq(lambda e, ft=ft, o=o: e.dma_start(out=S["ymixT"][1536 + ft * 128:1536 + (ft + 1) * 128, t0:t0 + n], in_=o[:, 0:n]), r=[orr], w=[dr])
        do_block(15, 0, 256)
        for b in range(8):
            do_block(286 + 512 * b, 256 + 512 * b, 512)


NB = NCTX + NLAT + NCTX
NCH = NT // 16
TWO_PI = 2.0 * math.pi


def stage_s5(P, I, S, W, C, l):
    nc = P.nc
    I32 = mybir.dt.int32
    ys5 = S["ys5"]
    keep = contextlib.ExitStack()
    with keep:
        def ksb(shape, dt, stack=None):
            Stage.CNT += 1
            return (stack or keep).enter_context(nc.sbuf_tensor("k%d" % Stage.CNT, list(shape), dt)), P.R()
        Kmat, Kmr = ksb([128, 16, 6, 128], BF16)
        CAr_, CArr = ksb([128, 16, 16, 32], BF16)
        CAi_, CAir = ksb([128, 16, 16, 32], BF16)
        Hbr_, Hbrr = ksb([128, 16, NCH], BF16)
        Hbi_, Hbir = ksb([128, 16, NCH], BF16)
        Qre, Qrer = ksb([128, 9, 16], F32)
        Qim, Qimr = ksb([128, 9, 16], F32)

        def load_u(stack):
            U, Ur = ksb([128, 6, NB], BF16, stack)
            with Stage(P) as sg:
                P.pool(lambda e: e.memset(U[:], 0.0), w=[Ur])
                for ft in range(6):
                    nr = 96 if ft < 5 else 32
                    P.stq(lambda e, ft=ft, nr=nr: e.dma_start(out=U[0:nr, ft, 0:NT], in_=S["uT"][96 * ft:96 * ft + nr, 0:NT]), w=[Ur])
                    P.stq(lambda e, ft=ft, nr=nr: e.dma_start(out=U[0:nr, ft, NT:NB], in_=S["uT"][96 * ft:96 * ft + nr, 0:NCTX]), w=[Ur])
            return U, Ur
        for d in range(2):
            off = 0 if d == 0 else NCTX
            with contextlib.ExitStack() as st_abt:
                ABTr_, ABTrr = ksb([128, 16, 6, 128], BF16, st_abt)
                ABTi_, ABTir = ksb([128, 16, 6, 128], BF16, st_abt)
                s5_params(P, I, C, l, d, Kmat, Kmr, ABTr_, ABTrr, ABTi_, ABTir, CAr_, CArr, CAi_, CAir, Qre, Qrer, Qim, Qimr)
                with contextlib.ExitStack() as st_u:
                    U, Ur = load_u(st_u)
                    s5_states(P, C, d, off, U, Ur, ABTr_, ABTrr, ABTi_, ABTir, Qre, Qrer, Qim, Qimr, Hbr_, Hbrr, Hbi_, Hbir)
            with contextlib.ExitStack() as st_u:
                U, Ur = load_u(st_u)
                s5_outputs(P, C, d, off, U, Ur, Kmat, Kmr, CAr_, CArr, CAi_, CAir, Hbr_, Hbrr, Hbi_, Hbir, ys5[d])
    s5_epilogue(P, I, S, W, C, l)


def s5_params(P, I, C, l, d, Kmat, Kmr, ABTr_, ABTrr, ABTi_, ABTir, CAr_, CArr, CAi_, CAir, Qre, Qrer, Qim, Qimr):
    nc = P.nc
    I32 = mybir.dt.int32
    with Stage(P) as sg:
        def t16(n=1):
            return sg.sb([128, 16], F32) if n == 1 else sg.sb([128, n, 16], F32)
        lr, lrr = t16(); li, lir = t16(); st, str_ = t16()
        P.ld(lambda e: e.dma_start(out=lr[:], in_=I["s5_lam_re"][l, d].rearrange("(j g) p -> (g p) j", g=2), allow_slow_non_contiguous=True), w=[lrr])
        P.ld(lambda e: e.dma_start(out=li[:], in_=I["s5_lam_im"][l, d].rearrange("(j g) p -> (g p) j", g=2), allow_slow_non_contiguous=True), w=[lir])
        lsv = I["s5_log_step"][l, d].rearrange("(j g) -> g j", g=2)
        for g2 in range(2):
            P.ld(lambda e, g2=g2: e.dma_start(out=st[g2 * 64:(g2 + 1) * 64, :], in_=lsv[g2:g2 + 1, :].to_broadcast([64, 16]), allow_slow_non_contiguous=True), w=[str_])
        Br, Brr = sg.sb([128, 16, 16], F32); Bi, Bir = sg.sb([128, 16, 16], F32)
        P.ld(lambda e: e.dma_start(out=Br[:], in_=I["s5_b_re"][l, d].rearrange("(j g) p h -> (g p) j h", g=2)), w=[Brr])
        P.ld(lambda e: e.dma_start(out=Bi[:], in_=I["s5_b_im"][l, d].rearrange("(j g) p h -> (g p) j h", g=2)), w=[Bir])
        Cr, Crr = sg.sb([128, 16, 16], F32); Ci, Cir = sg.sb([128, 16, 16], F32)
        cps = sg.ps([128, 32, 16], F32)
        cns = Rot(sg.sb([16, 16, 64], F32, 2))
        for (src, dstt, dstr) in (("s5_c_re", Cr, Crr), ("s5_c_im", Ci, Cir)):
            cv = I[src][l, d].rearrange("(j g) h p -> g h j p", g=2)
            for g2 in range(2):
                cn, cnr = cns.next()
                P.ld(lambda e, cn=cn, cv=cv, g2=g2: e.dma_start(out=cn[:], in_=cv[g2]), w=[cnr])
                for j in range(16):
                    P.pe(lambda e, cn=cn, g2=g2, j=j: e.matmul(cps[0][64 * g2:64 * g2 + 64, j, :], lhsT=cn[:, j, :], rhs=C["identf"][0:16, 0:16], start=True, stop=True), r=[cnr], w=[cps[1]])
            P.act(lambda e, dstt=dstt: e.activation(out=dstt[:], in_=cps[0][:, 0:16, :], func=AF.Copy), r=[cps[1]], w=[dstr])
        V = P.dve
        P.act(lambda e: e.activation(out=st[:], in_=st[:], func=AF.Exp), r=[str_], w=[str_])
        mg, mgr = t16(); th, thr = t16()
        V(lambda e: e.tensor_tensor(out=mg[:], in0=lr[:], in1=st[:], op=ALU.mult), r=[lrr, str_], w=[mgr])
        P.act(lambda e: e.activation(out=mg[:], in_=mg[:], func=AF.Exp), r=[mgr], w=[mgr])
        V(lambda e: e.tensor_tensor(out=th[:], in0=li[:], in1=st[:], op=ALU.mult), r=[lir, str_], w=[thr])
        ki, kir = sg.sb([128, 16], I32); kf, kfr = t16(); msk, mskr = t16()

        def fold(x, xr):
            V(lambda e: e.tensor_scalar(out=msk[:], in0=x[:], scalar1=math.pi, scalar2=-TWO_PI, op0=ALU.is_gt, op1=ALU.mult), r=[xr], w=[mskr])
            V(lambda e: e.tensor_tensor(out=x[:], in0=x[:], in1=msk[:], op=ALU.add), r=[xr, mskr], w=[xr])
            V(lambda e: e.tensor_scalar(out=msk[:], in0=x[:], scalar1=-math.pi, scalar2=TWO_PI, op0=ALU.is_lt, op1=ALU.mult), r=[xr], w=[mskr])
            V(lambda e: e.tensor_tensor(out=x[:], in0=x[:], in1=msk[:], op=ALU.add), r=[xr, mskr], w=[xr])
        V(lambda e: e.tensor_scalar(out=kf[:], in0=th[:], scalar1=1.0 / TWO_PI, scalar2=None, op0=ALU.mult), r=[thr], w=[kfr])
        V(lambda e: e.tensor_copy(out=ki[:], in_=kf[:]), r=[kfr], w=[kir])
        V(lambda e: e.tensor_copy(out=kf[:], in_=ki[:]), r=[kir], w=[kfr])
        V(lambda e: e.scalar_tensor_tensor(out=th[:], in0=kf[:], scalar=-TWO_PI, in1=th[:], op0=ALU.mult, op1=ALU.add), r=[kfr, thr], w=[thr])
        fold(th, thr)
        sn, snr = t16(); cs, csr = t16()
        P.act(lambda e: e.activation(out=sn[:], in_=th[:], func=AF.Sin), r=[thr], w=[snr])
        V(lambda e: e.tensor_scalar(out=th[:], in0=th[:], scalar1=math.pi / 2, scalar2=None, op0=ALU.add), r=[thr, snr], w=[thr])
        fold(th, thr)
        P.act(lambda e: e.activation(out=cs[:], in_=th[:], func=AF.Sin), r=[thr], w=[csr])
        Apr, Aprr = t16(17); Api, Apir = t16(17)
        V(lambda e: e.memset(Apr[:, 0, :], 1.0), w=[Aprr])
        V(lambda e: e.memset(Api[:, 0, :], 0.0), w=[Apir])
        V(lambda e: e.tensor_tensor(out=Apr[:, 1, :], in0=mg[:], in1=cs[:], op=ALU.mult), r=[mgr, csr], w=[Aprr])
        V(lambda e: e.tensor_tensor(out=Api[:, 1, :], in0=mg[:], in1=sn[:], op=ALU.mult), r=[mgr, snr], w=[Apir])
        t1, t1r = t16(8); t2, t2r = t16(8)

        def cmul(outr, outi, ores, ar, ai, ares, br, bi, bres, shape):
            a = lambda t: t
            V(lambda e: e.tensor_tensor(out=shape(t1), in0=ar, in1=br, op=ALU.mult), r=ares + bres, w=[t1r])
            V(lambda e: e.tensor_tensor(out=shape(t2), in0=ai, in1=bi, op=ALU.mult), r=ares + bres, w=[t2r])
            V(lambda e: e.tensor_tensor(out=outr, in0=shape(t1), in1=shape(t2), op=ALU.subtract), r=[t1r, t2r], w=[ores[0]])
            V(lambda e: e.tensor_tensor(out=shape(t1), in0=ar, in1=bi, op=ALU.mult), r=ares + bres + [ores[0]], w=[t1r])
            V(lambda e: e.tensor_tensor(out=shape(t2), in0=ai, in1=br, op=ALU.mult), r=ares + bres, w=[t2r])
            V(lambda e: e.tensor_tensor(out=outi, in0=shape(t1), in1=shape(t2), op=ALU.add), r=[t1r, t2r], w=[ores[1]])
        m = 1
        while m < 16:
            bre = Apr[:, m:m + 1, :].to_broadcast([128, m, 16]); bim = Api[:, m:m + 1, :].to_broadcast([128, m, 16])
            cmul(Apr[:, m + 1:2 * m + 1, :], Api[:, m + 1:2 * m + 1, :], [Aprr, Apir], Apr[:, 1:m + 1, :], Api[:, 1:m + 1, :], [Aprr, Apir],
                 bre, bim, [Aprr, Apir], (lambda t, m=m: t[:, 0:m, :]))
            m *= 2
        V(lambda e: e.tensor_copy(out=Qre[:, 0, :], in_=Apr[:, 16, :]), r=[Aprr], w=[Qrer])
        V(lambda e: e.tensor_copy(out=Qim[:, 0, :], in_=Api[:, 16, :]), r=[Apir], w=[Qimr])
        for j in range(8):
            cmul(Qre[:, j + 1:j + 2, :], Qim[:, j + 1:j + 2, :], [Qrer, Qimr], Qre[:, j:j + 1, :], Qim[:, j:j + 1, :], [Qrer, Qimr],
                 Qre[:, j:j + 1, :], Qim[:, j:j + 1, :], [Qrer, Qimr], (lambda t: t[:, 0:1, :]))
        den, denr = t16(); am1, am1r = t16(); fr, frr = t16(); fi, fir = t16(); w1, w1r = t16(); w2, w2r = t16()
        V(lambda e: e.tensor_tensor(out=den[:], in0=lr[:], in1=lr[:], op=ALU.mult), r=[lrr], w=[denr])
        V(lambda e: e.tensor_tensor(out=w1[:], in0=li[:], in1=li[:], op=ALU.mult), r=[lir], w=[w1r])
        V(lambda e: e.tensor_tensor(out=den[:], in0=den[:], in1=w1[:], op=ALU.add), r=[denr, w1r], w=[denr])
        V(lambda e: e.reciprocal(out=den[:], in_=den[:]), r=[denr], w=[denr])
        V(lambda e: e.tensor_scalar(out=am1[:], in0=Apr[:, 1, :], scalar1=-1.0, scalar2=None, op0=ALU.add), r=[Aprr], w=[am1r])
        V(lambda e: e.tensor_tensor(out=w1[:], in0=am1[:], in1=lr[:], op=ALU.mult), r=[am1r, lrr, denr], w=[w1r])
        V(lambda e: e.tensor_tensor(out=w2[:], in0=Api[:, 1, :], in1=li[:], op=ALU.mult), r=[Apir, lir], w=[w2r])
        V(lambda e: e.tensor_tensor(out=fr[:], in0=w1[:], in1=w2[:], op=ALU.add), r=[w1r, w2r], w=[frr])
        V(lambda e: e.tensor_tensor(out=fr[:], in0=fr[:], in1=den[:], op=ALU.mult), r=[frr, denr], w=[frr])
        V(lambda e: e.tensor_tensor(out=w1[:], in0=Api[:, 1, :], in1=lr[:], op=ALU.mult), r=[Apir, lrr, frr], w=[w1r])
        V(lambda e: e.tensor_tensor(out=w2[:], in0=am1[:], in1=li[:], op=ALU.mult), r=[am1r, lir, frr], w=[w2r])
        V(lambda e: e.tensor_tensor(out=fi[:], in0=w1[:], in1=w2[:], op=ALU.subtract), r=[w1r, w2r], w=[fir])
        V(lambda e: e.tensor_tensor(out=fi[:], in0=fi[:], in1=den[:], op=ALU.mult), r=[fir, denr], w=[fir])
        Bbr, Bbrr = sg.sb([128, 16, 32], F32); Bbi, Bbir = sg.sb([128, 16, 32], F32)
        Cbr, Cbrr = sg.sb([128, 16, 32], F32); Cbi, Cbir = sg.sb([128, 16, 32], F32); Cbn, Cbnr = sg.sb([128, 16, 32], F32)
        x1, x1r = sg.sb([128, 16, 32], F32); x2, x2r = sg.sb([128, 16, 32], F32)
        for (t, tr) in ((Bbr, Bbrr), (Bbi, Bbir), (Cbr, Cbrr), (Cbi, Cbir)):
            P.pool(lambda e, t=t: e.memset(t[:], 0.0), w=[tr])
        for g2 in range(2):
            rows = slice(g2 * 64, (g2 + 1) * 64); cols = slice(g2 * 16, (g2 + 1) * 16)
            fb_r = lambda g2=g2: fr[g2 * 64:(g2 + 1) * 64, :].unsqueeze(2).to_broadcast([64, 16, 16])
            fb_i = lambda g2=g2: fi[g2 * 64:(g2 + 1) * 64, :].unsqueeze(2).to_broadcast([64, 16, 16])
            V(lambda e, rows=rows, cols=cols, fb_r=fb_r: e.tensor_tensor(out=x1[rows, :, 0:16], in0=Br[rows, :, :], in1=fb_r(), op=ALU.mult), r=[Brr, frr], w=[x1r])
            V(lambda e, rows=rows, cols=cols, fb_i=fb_i: e.tensor_tensor(out=x2[rows, :, 0:16], in0=Bi[rows, :, :], in1=fb_i(), op=ALU.mult), r=[Bir, fir], w=[x2r])
            V(lambda e, rows=rows, cols=cols: e.tensor_tensor(out=Bbr[rows, :, cols], in0=x1[rows, :, 0:16], in1=x2[rows, :, 0:16], op=ALU.subtract), r=[x1r, x2r], w=[Bbrr])
            V(lambda e, rows=rows, cols=cols, fb_r=fb_r: e.tensor_tensor(out=x1[rows, :, 0:16], in0=Bi[rows, :, :], in1=fb_r(), op=ALU.mult), r=[Bir, frr, Bbrr], w=[x1r])
            V(lambda e, rows=rows, cols=cols, fb_i=fb_i: e.tensor_tensor(out=x2[rows, :, 0:16], in0=Br[rows, :, :], in1=fb_i(), op=ALU.mult), r=[Brr, fir, Bbrr], w=[x2r])
            V(lambda e, rows=rows, cols=cols: e.tensor_tensor(out=Bbi[rows, :, cols], in0=x1[rows, :, 0:16], in1=x2[rows, :, 0:16], op=ALU.add), r=[x1r, x2r], w=[Bbir])
            P.pool(lambda e, rows=rows, cols=cols: e.tensor_copy(out=Cbr[rows, :, cols], in_=Cr[rows, :, :]), r=[Crr], w=[Cbrr])
            P.pool(lambda e, rows=rows, cols=cols: e.tensor_copy(out=Cbi[rows, :, cols], in_=Ci[rows, :, :]), r=[Cir], w=[Cbir])
        P.pool(lambda e: e.tensor_scalar(out=Cbn[:], in0=Cbi[:], scalar1=-1.0, scalar2=None, op0=ALU.mult), r=[Cbir], w=[Cbnr])
        P.pool(lambda e: e.memset(Kmat[:], 0.0), w=[Kmr])
        ABs = Rot([(sg.sb([128, 16, 32], F32), sg.sb([128, 16, 32], F32)) for _ in range(2)])
        kks = Rot(sg.ps([128, 16, 32], F32, 2))
        ttr = Rot(sg.ps([128, 8, 128], F32, 1))
        tti = Rot(sg.ps([128, 8, 128], F32, 1))

        def bc(t, e_):
            return t[:, e_, :].unsqueeze(2).to_broadcast([128, 16, 32])

        def cmul_bd(outr, outrr, outi, outir, e_, Xr, Xrr, Xi, Xir, sign_im=1.0):
            V(lambda e: e.tensor_tensor(out=x1[:], in0=Xr[:], in1=bc(Apr, e_), op=ALU.mult), r=[Xrr, Aprr], w=[x1r])
            V(lambda e: e.tensor_tensor(out=x2[:], in0=Xi[:], in1=bc(Api, e_), op=ALU.mult), r=[Xir, Apir], w=[x2r])
            V(lambda e: e.tensor_tensor(out=outr, in0=x1[:], in1=x2[:], op=ALU.subtract), r=[x1r, x2r], w=[outrr])
            V(lambda e: e.tensor_tensor(out=x1[:], in0=Xi[:], in1=bc(Apr, e_), op=ALU.mult), r=[Xir, Aprr, outrr], w=[x1r])
            V(lambda e: e.tensor_tensor(out=x2[:], in0=Xr[:], in1=bc(Api, e_), op=ALU.mult), r=[Xrr, Apir, outrr], w=[x2r])
            if sign_im > 0:
                V(lambda e: e.tensor_tensor(out=outi, in0=x1[:], in1=x2[:], op=ALU.add), r=[x1r, x2r], w=[outir])
            else:
                V(lambda e: e.scalar_tensor_tensor(out=outi, in0=x1[:], scalar=-1.0, in1=x2[:], op0=ALU.mult, op1=ALU.subtract), r=[x1r, x2r], w=[outir])

        def do_e(e_):
            (ABr, ABrr), (ABi, ABir) = ABs.next()
            cmul_bd(ABr[:], ABrr, ABi[:], ABir, e_, Bbr, Bbrr, Bbi, Bbir)
            kk, kkr = kks.next(); tr_, trr = ttr.next(); ti_, tir = tti.next()
            for j in range(16):
                ft, q = j // 3, j % 3
                P.pe(lambda e, j=j, ft=ft, q=q: e.matmul(kk[32 * q:32 * q + 32, ft, :], lhsT=ABr[:, j, :], rhs=Cbr[:, j, :], start=True, stop=False), r=[ABrr, Cbrr], w=[kkr])
                P.pe(lambda e, j=j, ft=ft, q=q: e.matmul(kk[32 * q:32 * q + 32, ft, :], lhsT=ABi[:, j, :], rhs=Cbn[:, j, :], start=False, stop=True), r=[ABir, Cbnr], w=[kkr])
                P.pe(lambda e, j=j, ft=ft, q=q: e.matmul(tr_[32 * q:32 * q + 32, ft, :], lhsT=ABr[:, j, :], rhs=C["identf"][:], start=True, stop=True), r=[ABrr], w=[trr])
                P.pe(lambda e, j=j, ft=ft, q=q: e.matmul(ti_[32 * q:32 * q + 32, ft, :], lhsT=ABi[:, j, :], rhs=C["identf"][:], start=True, stop=True), r=[ABir], w=[tir])
            for q in range(3):
                nf = 6 if q == 0 else 5
                P.act(lambda e, q=q, nf=nf: e.activation(out=Kmat[32 * q:32 * q + 32, e_, 0:nf, 32 * q:32 * q + 32], in_=kk[32 * q:32 * q + 32, 0:nf, :], func=AF.Copy), r=[kkr], w=[Kmr])
            P.act(lambda e: e.activation(out=ABTr_[0:96, e_, 0:5, :], in_=tr_[0:96, 0:5, :], func=AF.Copy), r=[trr], w=[ABTrr])
            P.act(lambda e: e.activation(out=ABTi_[0:96, e_, 0:5, :], in_=ti_[0:96, 0:5, :], func=AF.Copy), r=[tir], w=[ABTir])
            P.act(lambda e: e.activation(out=ABTr_[0:32, e_, 5, :], in_=tr_[0:32, 5, :], func=AF.Copy), r=[trr], w=[ABTrr])
            P.act(lambda e: e.activation(out=ABTi_[0:32, e_, 5, :], in_=ti_[0:32, 5, :], func=AF.Copy), r=[tir], w=[ABTir])
            cmul_bd(CAr_[:, e_, :, :], CArr, CAi_[:, e_, :, :], CAir, e_ + 1, Cbr, Cbrr, Cbi, Cbir, sign_im=-1.0)
        for e_ in range(16):
            do_e(e_)


def s5_states(P, C, d, off, U, Ur, ABTr_, ABTrr, ABTi_, ABTir, Qre, Qrer, Qim, Qimr, Hbr_, Hbrr, Hbi_, Hbir):
    nc = P.nc
    n = NCH
    with Stage(P) as sg:
        NP_ = 2
        bufs = [sg.sb([128, NP_, NCH], F32) for _ in range(4)]
        tA = [sg.sb([128, NP_, NCH], F32) for _ in range(2)]
        tB = [sg.sb([128, NP_, NCH], F32) for _ in range(2)]
        pss = Rot(sg.ps([128, 512], F32, 4))

        def do_half(hf):
            (Xr, Xrr), (Xi, Xir), (Yr, Yrr), (Yi, Yir) = bufs
            for jj in range(NP_):
                j = hf * NP_ + jj
                ft, q = j // 3, j % 3
                for (ABT, ABTres, X, Xres) in ((ABTr_, ABTrr, Xr, Xrr), (ABTi_, ABTir, Xi, Xir)):
                    ps, psr = pss.next()
                    for r in range(16):
                        e_ = (15 - r) if d == 0 else r
                        rhs = U[32 * q:32 * q + 32, ft, off:off + NT].rearrange("p (c r) -> p c r", r=16)[:, :, r]
                        P.pe(lambda e, ps=ps, ABT=ABT, e_=e_, rhs=rhs, r=r, q=q, ft=ft: e.matmul(ps[:, 0:NCH], lhsT=ABT[32 * q:32 * q + 32, e_, ft, :], rhs=rhs, start=(r == 0), stop=(r == 15)),
                             r=[ABTres, Ur], w=[psr])
                    P.act(lambda e, ps=ps, X=X, jj=jj: e.activation(out=X[:, jj, :], in_=ps[:, 0:NCH], func=AF.Copy), r=[psr], w=[Xres])
            cur = (bufs[0], bufs[1]); nxt = (bufs[2], bufs[3])
            for step in range(9):
                sh = 1 << step
                (Xr, Xrr), (Xi, Xir) = cur
                (Yr, Yrr), (Yi, Yir) = nxt
                if d == 0:
                    dst = slice(sh, n); src = slice(0, n - sh); keep_ = slice(0, sh)
                else:
                    dst = slice(0, n - sh); src = slice(sh, n); keep_ = slice(n - sh, n)
                w_ = n - sh
                qr = Qre[:, step, hf * NP_:(hf + 1) * NP_].unsqueeze(2).to_broadcast([128, NP_, w_])
                qi = Qim[:, step, hf * NP_:(hf + 1) * NP_].unsqueeze(2).to_broadcast([128, NP_, w_])
                (a1, a1r), (a2, a2r) = tA
                (b1, b1r), (b2, b2r) = tB
                V = P.dve; G = P.pool
                V(lambda e, Xr=Xr, qr=qr, src=src, w_=w_: e.tensor_tensor(out=a1[:, :, 0:w_], in0=Xr[:, :, src], in1=qr, op=ALU.mult), r=[Xrr, Qrer], w=[a1r])
                V(lambda e, Xi=Xi, qi=qi, src=src, w_=w_: e.tensor_tensor(out=a2[:, :, 0:w_], in0=Xi[:, :, src], in1=qi, op=ALU.mult), r=[Xir, Qimr], w=[a2r])
                V(lambda e, w_=w_: e.tensor_tensor(out=a1[:, :, 0:w_], in0=a1[:, :, 0:w_], in1=a2[:, :, 0:w_], op=ALU.subtract), r=[a1r, a2r], w=[a1r])
                V(lambda e, Xr=Xr, Yr=Yr, dst=dst, w_=w_: e.tensor_tensor(out=Yr[:, :, dst], in0=Xr[:, :, dst], in1=a1[:, :, 0:w_], op=ALU.add), r=[Xrr, a1r], w=[Yrr])
                V(lambda e, Xr=Xr, Yr=Yr, keep_=keep_: e.tensor_copy(out=Yr[:, :, keep_], in_=Xr[:, :, keep_]), r=[Xrr], w=[Yrr])
                G(lambda e, Xi=Xi, qr=qr, src=src, w_=w_: e.tensor_tensor(out=b1[:, :, 0:w_], in0=Xi[:, :, src], in1=qr, op=ALU.mult), r=[Xir, Qrer], w=[b1r])
                G(lambda e, Xr=Xr, qi=qi, src=src, w_=w_: e.tensor_tensor(out=b2[:, :, 0:w_], in0=Xr[:, :, src], in1=qi, op=ALU.mult), r=[Xrr, Qimr], w=[b2r])
                G(lambda e, w_=w_: e.tensor_tensor(out=b1[:, :, 0:w_], in0=b1[:, :, 0:w_], in1=b2[:, :, 0:w_], op=ALU.add), r=[b1r, b2r], w=[b1r])
                G(lambda e, Xi=Xi, Yi=Yi, dst=dst, w_=w_: e.tensor_tensor(out=Yi[:, :, dst], in0=Xi[:, :, dst], in1=b1[:, :, 0:w_], op=ALU.add), r=[Xir, b1r], w=[Yir])
                G(lambda e, Xi=Xi, Yi=Yi, keep_=keep_: e.tensor_copy(out=Yi[:, :, keep_], in_=Xi[:, :, keep_]), r=[Xir], w=[Yir])
                cur, nxt = nxt, cur
            (Xr, Xrr), (Xi, Xir) = cur
            P.act(lambda e, Xr=Xr: e.activation(out=Hbr_[:, hf * NP_:(hf + 1) * NP_, :], in_=Xr[:], func=AF.Copy), r=[Xrr], w=[Hbrr])
            P.act(lambda e, Xi=Xi: e.activation(out=Hbi_[:, hf * NP_:(hf + 1) * NP_, :], in_=Xi[:], func=AF.Copy), r=[Xir], w=[Hbir])
        for hf in range(16 // NP_):
            do_half(hf)


def s5_outputs(P, C, d, off, U, Ur, Kmat, Kmr, CAr_, CArr, CAi_, CAir, Hbr_, Hbrr, Hbi_, Hbir, ydst):
    nc = P.nc
    with Stage(P) as sg:
        pss = Rot(sg.ps([128, 512], F32, 4))
        ev = Rot(sg.sb([128, 512], F32, 3))
        dr = P.R()
        if d == 0:
            blocks = [(0, 256)] + [(256 + 512 * i, 512) for i in range(8)]
        else:
            blocks = [(256 + 512 * i, 512) for i in range(8)] + [(NT, 256)]

        def do_block(ft, b0, n):
            nb = n // 16
            c0 = (b0 - off) // 16
            ps, psr = pss.next()
            pv = ps[:, 0:n].rearrange("p (c r) -> p c r", r=16)
            uv = U[:, ft, b0:b0 + n].rearrange("p (c r) -> p c r", r=16)
            for k in range(16):
                if d == 0:
                    o_ap = pv[:, :, k:16]; r_ap = uv[:, :, 0:16 - k]
                else:
                    o_ap = pv[:, :, 0:16 - k]; r_ap = uv[:, :, k:16]
                P.pe(lambda e, k=k, o_ap=o_ap, r_ap=r_ap: e.matmul(o_ap, lhsT=Kmat[:, k, ft, :], rhs=r_ap, start=(k == 0), stop=False, skip_group_check=True),
                     r=[Kmr, Ur], w=[psr])
            npair = 3 if ft < 5 else 1
            for q in range(npair):
                j = ft * 3 + q
                for r in range(16):
                    if d == 0:
                        e_ = r
                        lo = 1 if c0 == 0 else 0
                        oc = slice(lo, nb); hc = slice(c0 + lo - 1, c0 + nb - 1)
                    else:
                        e_ = 15 - r
                        hi = nb - 1 if c0 + nb == NCH else nb
                        oc = slice(0, hi); hc = slice(c0 + 1, c0 + hi + 1)
                    last = (q == npair - 1 and r == 15)
                    P.pe(lambda e, j=j, q=q, r=r, e_=e_, oc=oc, hc=hc: e.matmul(pv[32 * q:32 * q + 32, oc, r], lhsT=CAr_[:, e_, j, :], rhs=Hbr_[:, j, hc], start=False, stop=False, skip_group_check=True),
                         r=[CArr, Hbrr], w=[psr])
                    P.pe(lambda e, j=j, q=q, r=r, e_=e_, oc=oc, hc=hc, last=last: e.matmul(pv[32 * q:32 * q + 32, oc, r], lhsT=CAi_[:, e_, j, :], rhs=Hbi_[:, j, hc], start=False, stop=last, skip_group_check=True),
                         r=[CAir, Hbir], w=[psr])
            o, orr = ev.next()
            P.act(lambda e: e.activation(out=o[:, 0:n], in_=ps[:, 0:n], func=AF.Copy), r=[psr], w=[orr])
            nr = 32 * npair
            P.stq(lambda e: e.dma_start(out=ydst[ft * 96:ft * 96 + nr, b0:b0 + n], in_=o[0:nr, 0:n]), r=[orr], w=[dr])
        for (b0, n) in blocks:
            for ft in range(6):
                do_block(ft, b0, n)


def s5_epilogue(P, I, S, W, C, l):
    nc = P.nc
    ys5 = S["ys5"]
    with Stage(P) as sg:
        wg, wgr = sg.sb([128, 4, 512], BF16)
        P.ld(lambda e: e.dma_start(out=wg[:], in_=W["w_glu"].rearrange("(k p) f -> p k f", p=128)), w=[wgr])
        dv, dvr = sg.sb([128, 4], F32)
        P.ld(lambda e: e.dma_start(out=dv[:], in_=I["s5_d"][l].rearrange("(f p) -> p f", p=128), allow_slow_non_contiguous=True), w=[dvr])
        yfs = Rot(sg.sb([128, 4, 512], F32, 2)); ybs = Rot(sg.sb([128, 4, 512], F32, 2)); us = Rot(sg.sb([128, 4, 512], F32, 2))
        gs = Rot(sg.sb([128, 4, 512], F32, 2)); gbs = Rot(sg.sb([128, 4, 512], BF16, 2))
        wk = Rot(sg.sb([128, 4, 512], F32, 2))
        sgs = Rot(sg.sb([128, 512], F32, 2)); ob = Rot(sg.sb([128, 512], BF16, 3))
        pss = Rot(sg.ps([128, 512], F32, 4))
        dr = P.R()
        yfv = ys5[0].rearrange("(f p) t -> p f t", p=128)
        ybv = ys5[1].rearrange("(f p) t -> p f t", p=128)
        uv = S["uT"].rearrange("(f p) t -> p f t", p=128)

        def do_chunk(t0, n):
            yf, yfr = yfs.next(); yb, ybr = ybs.next(); u, ur = us.next()
            tb = t0 if t0 >= NCTX else NT + t0
            P.ld(lambda e: e.dma_start(out=yf[:, :, 0:n], in_=yfv[:, :, t0:t0 + n]), w=[yfr])
            P.ld(lambda e: e.dma_start(out=yb[:, :, 0:n], in_=ybv[:, :, tb:tb + n]), w=[ybr])
            P.ld(lambda e: e.dma_start(out=u[:, :, 0:n], in_=uv[:, :, t0:t0 + n]), w=[ur])
            P.pool(lambda e: e.tensor_tensor(out=yf[:, :, 0:n], in0=yf[:, :, 0:n], in1=yb[:, :, 0:n], op=ALU.add), r=[yfr, ybr], w=[yfr])
            for ft in range(4):
                P.dve(lambda e, ft=ft: e.scalar_tensor_tensor(out=yf[:, ft, 0:n], in0=u[:, ft, 0:n], scalar=dv[:, ft:ft + 1], in1=yf[:, ft, 0:n], op0=ALU.mult, op1=ALU.add), r=[ur, dvr, yfr], w=[yfr])
            t, tr = wk.next()
            P.act(lambda e: e.activation(out=t[:, :, 0:n], in_=yf[:, :, 0:n], func=AF.Square), r=[yfr], w=[tr])
            P.dve(lambda e: e.tensor_scalar(out=t[:, :, 0:n], in0=t[:, :, 0:n], scalar1=0.044715 * 1.5957691216, scalar2=1.5957691216, op0=ALU.mult, op1=ALU.add), r=[tr], w=[tr])
            P.dve(lambda e: e.tensor_tensor(out=t[:, :, 0:n], in0=t[:, :, 0:n], in1=yf[:, :, 0:n], op=ALU.mult), r=[tr, yfr], w=[tr])
            P.act(lambda e: e.activation(out=t[:, :, 0:n], in_=t[:, :, 0:n], func=AF.Sigmoid), r=[tr], w=[tr])
            g, gr = gs.next(); gb, gbr = gbs.next()
            P.dve(lambda e: e.tensor_tensor(out=g[:, :, 0:n], in0=t[:, :, 0:n], in1=yf[:, :, 0:n], op=ALU.mult), r=[tr, yfr], w=[gr])
            P.pool(lambda e: e.tensor_copy(out=gb[:, :, 0:n], in_=g[:, :, 0:n]), r=[gr], w=[gbr])
            for fo in range(4):
                ps, psr = pss.next()
                for k in range(4):
                    P.pe(lambda e, k=k, fo=fo, ps=ps: e.matmul(ps[:, 0:n], lhsT=wg[:, k, fo * 128:(fo + 1) * 128], rhs=gb[:, k, 0:n], start=(k == 0), stop=(k == 3)), r=[wgr, gbr], w=[psr])
                sgm, sgmr = sgs.next()
                P.act(lambda e, ps=ps, sgm=sgm: e.activation(out=sgm[:, 0:n], in_=ps[:, 0:n], func=AF.Sigmoid), r=[psr], w=[sgmr])
                o, orr = ob.next()
                P.dve(lambda e, fo=fo, sgm=sgm, o=o: e.tensor_tensor(out=o[:, 0:n], in0=g[:, fo, 0:n], in1=sgm[:, 0:n], op=ALU.mult), r=[gr, sgmr], w=[orr])
                P.stq(lambda e, fo=fo, o=o: e.dma_start(out=S["ymixT"][512 + fo * 128:512 + (fo + 1) * 128, t0:t0 + n], in_=o[:, 0:n]), r=[orr], w=[dr])
        for (t0, n) in CHUNKS:
            do_chunk(t0, n)


def load_modvec_kind(P, tiles, S, idxs, kind):
    for (t, r), idx in zip(tiles, idxs):
        P.ld(lambda e, t=t, idx=idx: e.dma_start(out=t[:], in_=S["modv"][kind:kind + 1, idx, :].to_broadcast([128, D])), w=[r])


def stage_wout(P, I, S, W, C, l, xsrc, xres):
    nc = P.nc
    with Stage(P) as sg:
        wo, wor = sg.sb([128, 16, D], BF16)
        P.ld(lambda e: e.dma_start(out=wo[:], in_=W["w_out"].rearrange("(k p) f -> p k f", p=128)), w=[wor])
        mods = [sg.sb([128, D], F32) for _ in range(3)]
        yTs = Rot(sg.sb([128, 16, 512], BF16, 2))
        xts = Rot(sg.sb([128, D], F32, 2))
        hTs = Rot(sg.sb([128, 16, 512], BF16, 2))
        pss = Rot(sg.ps([128, 512], F32, 4))
        junk, junkr = sg.sb([128, 512], BF16)
        ss4 = Rot(sg.sb([128, 4], F32, 2))
        ss1 = Rot(sg.sb([128, 1], F32, 2))
        tmps = Rot(sg.sb([128, 512], F32, 2))
        bufs = dict(junk=sg.sb([128, D], BF16), ss=Rot(sg.sb([128, 1], F32, 2)), rs=Rot(sg.sb([128, 1], F32, 2)),
                    hb=Rot(sg.sb([128, D], BF16, 2)), tmp=Rot(sg.sb([128, D], F32, 1)), pt=Rot(sg.ps([128, 1024], BF16, 2)))
        dr = P.R()
        xr_dram = P.R()
        yv = S["ymixT"].rearrange("(k p) t -> p k t", p=128)
        hv = S["h2T"].rearrange("(k p) t -> p k t", p=128)

        def do_tile(yT, yTr, hT, hTr, i, a, kind):
            G = {kind: mods[1]}
            SH = {kind: mods[2]}
            xt, xtr = xts.next()
            P.ld(lambda e: e.dma_start(out=xt[:], in_=xsrc[a:a + 128, :]), r=[xr_dram], w=[xtr])
            s4, s4r = ss4.next()
            banks = []
            for c in range(4):
                ps, psr = pss.next()
                for k in range(16):
                    P.pe(lambda e, k=k, c=c, ps=ps: e.matmul(ps[:, :], lhsT=yT[:, k, i * 128:(i + 1) * 128], rhs=wo[:, k, c * 512:(c + 1) * 512], start=(k == 0), stop=(k == 15)), r=[yTr, wor], w=[psr])
                P.act(lambda e, c=c, ps=ps: e.activation(out=junk[:], in_=ps[:], func=AF.Square, accum_out=s4[:, c:c + 1]), r=[psr], w=[junkr, s4r])
                banks.append((ps, psr))
            s1, s1r = ss1.next()
            P.dve(lambda e: e.tensor_reduce(out=s1[:], in_=s4[:], axis=mybir.AxisListType.X, op=ALU.add), r=[s4r], w=[s1r])
            rs, rsr = rstd_from_ss(P, sg, s1, s1r, D, bufs["rs"])
            for c, (ps, psr) in enumerate(banks):
                t, tr = tmps.next()
                P.dve(lambda e, c=c, ps=ps, t=t: e.scalar_tensor_tensor(out=t[:], in0=ps[:], scalar=rs[:, 0:1], in1=mods[0][0][:, c * 512:(c + 1) * 512], op0=ALU.mult, op1=ALU.mult), r=[psr, rsr, mods[0][1]], w=[tr])
                P.pool(lambda e, c=c, t=t: e.tensor_tensor(out=xt[:, c * 512:(c + 1) * 512], in0=xt[:, c * 512:(c + 1) * 512], in1=t[:], op=ALU.add), r=[tr, xtr], w=[xtr])
            P.stq(lambda e: e.dma_start(out=xres[a:a + 128, :], in_=xt[:]), r=[xtr], w=[xr_dram])
            norm_mod_transpose(P, sg, C, xt, xtr, G, SH, kind, hT, hTr, i * 128, bufs)

        def do_chunk(t0, n, kind):
            yT, yTr = yTs.next()
            P.ld(lambda e: e.dma_start(out=yT[:, :, 0:n], in_=yv[:, :, t0:t0 + n]), w=[yTr])
            hT, hTr = hTs.next()
            for i in range(n // 128):
                do_tile(yT, yTr, hT, hTr, i, t0 + i * 128, kind)
            P.stq(lambda e: e.dma_start(out=hv[:, :, t0:t0 + n], in_=hT[:, :, 0:n]), r=[hTr], w=[dr])
        for ci, (t0, n) in enumerate(CHUNKS):
            kind = 1 if t0 < NCTX else 0
            if ci < 2:
                load_modvec_kind(P, mods, S, [2, 3, 4], kind)
            do_chunk(t0, n, kind)


def stage_ffn(P, I, S, W, C, l, xres):
    nc = P.nc
    with Stage(P) as sg:
        g4 = sg.sb([128, D], F32)
        h2, h2r = sg.sb([128, 16, 512], BF16)
        aT, aTr = sg.sb([128, 44, 512], BF16)
        wgs = Rot(sg.sb([128, 16, 128], BF16, 3))
        wus = Rot(sg.sb([128, 16, 128], BF16, 3))
        wos = Rot(sg.sb([128, 44, 256], BF16, 2))
        fx, fxr = sg.sb([128, 4, D], F32)
        xts = Rot(sg.sb([128, D], F32, 2))
        sil = Rot(sg.sb([128, 512], F32, 2))
        tmp, tmpr = sg.sb([128, D], F32)
        junk, junkr = sg.sb([128, D], BF16)
        ss1 = Rot(sg.sb([128, 1], F32, 2))
        rsb = Rot(sg.sb([128, 1], F32, 2))
        psg = Rot(sg.ps([128, 512], F32, 2))
        psu = Rot(sg.ps([128, 512], F32, 2))
        pso = Rot(sg.ps([128, 512], F32, 4))
        xr_dram = P.R()
        hv = S["h2T"].rearrange("(k p) t -> p k t", p=128)
        wiv = W["w_ffi"].rearrange("(k p) f -> p k f", p=128)
        wov = W["w_ffo"].rearrange("(k p) f -> p k f", p=128)

        def do_chunk(t0, n, kind):
            P.ld(lambda e: e.dma_start(out=h2[:, :, 0:n], in_=hv[:, :, t0:t0 + n]), w=[h2r])
            for f in range(44):
                wg, wgr = wgs.next()
                wu, wur = wus.next()
                P.ld(lambda e, f=f, wg=wg: e.dma_start(out=wg[:], in_=wiv[:, :, f * 128:(f + 1) * 128]), w=[wgr])
                P.ld(lambda e, f=f, wu=wu: e.dma_start(out=wu[:], in_=wiv[:, :, DFF + f * 128:DFF + (f + 1) * 128]), w=[wur])
                pg, pgr = psg.next()
                pu, pur = psu.next()
                for k in range(16):
                    P.pe(lambda e, k=k, wg=wg, pg=pg: e.matmul(pg[:, 0:n], lhsT=wg[:, k, :], rhs=h2[:, k, 0:n], start=(k == 0), stop=(k == 15)), r=[wgr, h2r], w=[pgr])
                for k in range(16):
                    P.pe(lambda e, k=k, wu=wu, pu=pu: e.matmul(pu[:, 0:n], lhsT=wu[:, k, :], rhs=h2[:, k, 0:n], start=(k == 0), stop=(k == 15)), r=[wur, h2r], w=[pur])
                sl, slr = sil.next()
                P.act(lambda e, sl=sl, pg=pg: e.activation(out=sl[:, 0:n], in_=pg[:, 0:n], func=AF.Silu), r=[pgr], w=[slr])
                P.dve(lambda e, f=f, sl=sl, pu=pu: e.tensor_tensor(out=aT[:, f, 0:n], in0=pu[:, 0:n], in1=sl[:, 0:n], op=ALU.mult), r=[pur, slr], w=[aTr])
            nt = n // 128
            for c in range(8):
                wo, wor = wos.next()
                P.ld(lambda e, c=c, wo=wo: e.dma_start(out=wo[:], in_=wov[:, :, c * 256:(c + 1) * 256]), w=[wor])
                for i in range(nt):
                    po, por = pso.next()
                    for f in range(44):
                        P.pe(lambda e, f=f, i=i, wo=wo, po=po: e.matmul(po[:, 0:256], lhsT=aT[:, f, i * 128:(i + 1) * 128], rhs=wo[:, f, :], start=(f == 0), stop=(f == 43)), r=[aTr, wor], w=[por])
                    P.act(lambda e, c=c, i=i, po=po: e.activation(out=fx[:, i, c * 256:(c + 1) * 256], in_=po[:, 0:256], func=AF.Copy), r=[por], w=[fxr])
            for i in range(nt):
                a = t0 + i * 128
                xt, xtr = xts.next()
                P.ld(lambda e, xt=xt, a=a: e.dma_start(out=xt[:], in_=xres[a:a + 128, :]), r=[xr_dram], w=[xtr])
                s1, s1r = ss1.next()
                P.act(lambda e, i=i, s1=s1: e.activation(out=junk[:], in_=fx[:, i, :], func=AF.Square, accum_out=s1[:]), r=[fxr], w=[junkr, s1r])
                rs, rsr = rstd_from_ss(P, sg, s1, s1r, D, rsb)
                P.dve(lambda e, i=i, rs=rs: e.scalar_tensor_tensor(out=tmp[:], in0=fx[:, i, :], scalar=rs[:, 0:1], in1=g4[0][:], op0=ALU.mult, op1=ALU.mult), r=[fxr, rsr, g4[1]], w=[tmpr])
                P.pool(lambda e, xt=xt: e.tensor_tensor(out=xt[:], in0=xt[:], in1=tmp[:], op=ALU.add), r=[tmpr, xtr], w=[xtr])
                P.stq(lambda e, xt=xt, a=a: e.dma_start(out=xres[a:a + 128, :], in_=xt[:]), r=[xtr], w=[xr_dram])
        for ci, (t0, n) in enumerate(CHUNKS):
            kind = 1 if t0 < NCTX else 0
            if ci < 2:
                load_modvec_kind(P, [g4], S, [5], kind)
            do_chunk(t0, n, kind)


def host_consts():
    n = NLAT
    row = np.repeat(np.arange(n // 64, dtype=np.float32), 64)
    col = np.tile(np.arange(64, dtype=np.float32), n // 64)
    inv = (10000.0 ** (-np.arange(16, dtype=np.float32) / 16)).astype(np.float32)
    ang = np.concatenate([row[:, None] * inv, col[:, None] * inv], -1)
    cos = np.cos(ang).astype(np.float32)
    sin = np.sin(ang).astype(np.float32)
    Ct = np.ones((128, NT), np.float32)
    St = np.zeros((128, NT), np.float32)
    for r in range(128):
        dd = r % 64
        i = dd % 32
        Ct[r, NCTX:] = cos[:, i]
        St[r, NCTX:] = (-sin[:, i]) if dd < 32 else sin[:, i]
    perm = np.zeros((128, 128), np.float32)
    for m in range(128):
        dd = m % 64
        partner = m + 32 if dd < 32 else m - 32
        perm[partner, m] = 1.0
    ident = np.eye(128, dtype=np.float32)
    return dict(ropeC=Ct, ropeS=St, perm=perm, ident=ident)


def make_in_maps(inputs, ncores=8):
    hc = host_consts()
    maps = []
    for c in range(ncores):
        b = c % 4
        m = dict(hc)
        m["xc"] = np.ascontiguousarray(np.concatenate([inputs["ctx"][b], inputs["x"][b]], 0))
        m["cvec"] = np.ascontiguousarray(np.stack([inputs["c"][b], inputs["c_ctx"]], 0))
        for k, v in inputs.items():
            if k in ("x", "c", "ctx", "c_ctx"):
                continue
            m[k] = np.ascontiguousarray(v)
        maps.append(m)
    return maps


def kernel(**inputs):
    inputs = {k: np.asarray(v) for k, v in inputs.items()}
    nc = build()
    maps = make_in_maps(inputs, 8)
    res = run_bass_kernel_spmd(nc, maps, core_ids=list(range(8)))
    out = np.stack([res.results[b]["xres"][NCTX:] for b in range(4)], 0)
    return out.astype(np.float32)
```

```python
import os
import contextlib
import math
import numpy as np
import concourse.bass as bass
import concourse.mybir as mybir
from concourse.bass_utils import run_bass_kernel_spmd

F32 = mybir.dt.float32
BF16 = mybir.dt.bfloat16
ALU = mybir.AluOpType
AF = mybir.ActivationFunctionType

D = 2048
NCTX = 256
NLAT = 4096
NT = NCTX + NLAT
DEPTH = 4
DFF = 5632
INW = 3904
CHUNKS = [(0, 256)] + [(256 + 512 * i, 512) for i in range(8)]
EPS = 1e-6


class Res:
    __slots__ = ("lw", "rd")

    def __init__(self):
        self.lw = None
        self.rd = []


class Op:
    __slots__ = ("eng", "fn", "deps", "sig", "key", "inc", "val")


class Prog:
    NQ = 12
    RELAX = True

    def __init__(self, nc, st):
        self.nc = nc
        self.sem = {}
        for k in ["pe", "act", "dve", "pool"]:
            self.sem[k] = st.enter_context(nc.semaphore("s_" + k))
        for q in ["sp", "pool"]:
            for s in range(self.NQ):
                k = "d_%s_%d" % (q, s)
                self.sem[k] = st.enter_context(nc.semaphore("s_" + k))
        self.semval = {k: 0 for k in self.sem}
        self.dqn = {"sp": 0, "pool": 0}
        self.slot_last = {}
        self.reset_stage()
        self.res = []

    def R(self):
        r = Res()
        self.res.append(r)
        return r

    def reset_stage(self):
        self.ops = {e: [] for e in ("pe", "act", "dve", "pool", "sp")}
        self.order = []

    def _mk(self, eng, fn, reads, writes, key, inc, acc=False):
        op = Op()
        op.eng = eng; op.fn = fn; op.sig = False; op.key = key; op.inc = inc; op.val = None
        deps = []
        for r in reads:
            if r.lw is not None:
                deps.append(r.lw)
        for r in writes:
            if r.lw is not None:
                deps.append(r.lw)
            deps.extend(r.rd)
        if acc:
            deps = [d for d in deps if not (d.eng == eng and d.key == key)]
        if eng in ("dve", "act") and self.RELAX:
            prev = self.ops[eng][-1] if self.ops[eng] else None
            deps = [d for d in deps if not (d.eng == eng and d.key == key and d is not prev)]
        op.deps = deps
        for r in reads:
            r.rd.append(op)
        for r in writes:
            r.lw = op
            r.rd = []
        self.ops[eng].append(op)
        self.order.append(op)
        return op

    def pe(self, fn, r=(), w=(), acc=True):
        return self._mk("pe", fn, r, w, "pe", 1, acc)

    def act(self, fn, r=(), w=()):
        return self._mk("act", fn, r, w, "act", 1)

    def dve(self, fn, r=(), w=()):
        return self._mk("dve", fn, r, w, "dve", 1)

    def pool(self, fn, r=(), w=()):
        return self._mk("pool", fn, r, w, "pool", 1)

    def dma(self, q, fn, r=(), w=()):
        i = self.dqn[q]
        self.dqn[q] += 1
        key = "d_%s_%d" % (q, i % self.NQ)
        op = self._mk(q, fn, r, w, key, 16)
        prev = self.slot_last.get(key)
        if prev is not None:
            op.deps.append(prev)
        self.slot_last[key] = op
        op.sig = True
        return op

    def ld(self, fn, r=(), w=()):
        return self.dma("sp", fn, r, w)

    def stq(self, fn, r=(), w=()):
        return self.dma("pool", fn, r, w)

    def end_stage(self):
        lasts = []
        for e in ("pe", "act", "dve", "pool"):
            c = [o for o in self.ops[e] if o.key == e]
            if c:
                lasts.append(c[-1])
        for k, o in self.slot_last.items():
            if o is not None:
                lasts.append(o)
        for e in ("pe", "act", "dve", "pool", "sp"):
            b = Op()
            b.eng = e; b.fn = None; b.deps = list(lasts); b.sig = False; b.key = None; b.inc = 0; b.val = None
            self.ops[e].append(b)
            self.order.append(b)
        staged = set(id(o) for o in self.order)
        for o in self.order:
            o.deps = [d for d in o.deps if id(d) in staged]
            for d in o.deps:
                d.sig = True
        for e in ("pe", "act", "dve", "pool", "sp"):
            for o in self.ops[e]:
                if o.fn is not None and o.sig:
                    self.semval[o.key] += o.inc
                    o.val = self.semval[o.key]
        nc = self.nc
        sem = self.sem

        def run(engine, lst):
            known = {}
            for o in lst:
                need = {}
                for d in o.deps:
                    if need.get(d.key, 0) < d.val:
                        need[d.key] = d.val
                for k, v in need.items():
                    if known.get(k, 0) >= v:
                        continue
                    engine.wait_ge(sem[k], v)
                    known[k] = v
                if o.fn is None:
                    continue
                ins = o.fn(engine)
                if o.sig:
                    ins.then_inc(sem[o.key], o.inc)

        with nc.Block() as block:
            @block.tensor
            def _(e):
                run(e, self.ops["pe"])

            @block.scalar
            def _(e):
                run(e, self.ops["act"])

            @block.vector
            def _(e):
                run(e, self.ops["dve"])

            @block.gpsimd
            def _(e):
                run(e, self.ops["pool"])

            @block.sync
            def _(e):
                run(e, self.ops["sp"])
        for r in self.res:
            r.lw = None
            r.rd = []
        self.res = []
        self.slot_last = {}
        self.reset_stage()


class Stage:
    CNT = 0

    def __init__(self, P):
        self.P = P
        self.nc = P.nc
        self.st = contextlib.ExitStack()
        self.n = 0

    def __enter__(self):
        self.st.__enter__()
        return self

    def __exit__(self, *a):
        self.P.end_stage()
        return self.st.__exit__(*a)

    def sb(self, shape, dt, nbuf=1):
        out = []
        for i in range(nbuf):
            Stage.CNT += 1
            t = self.st.enter_context(self.nc.sbuf_tensor("t%d" % Stage.CNT, list(shape), dt))
            out.append((t, self.P.R()))
        return out if nbuf > 1 else out[0]

    def ps(self, shape, dt, nbuf=1):
        out = []
        for i in range(nbuf):
            Stage.CNT += 1
            t = self.st.enter_context(self.nc.psum_tensor("p%d" % Stage.CNT, list(shape), dt))
            out.append((t, self.P.R()))
        return out if nbuf > 1 else out[0]


class Rot:
    def __init__(self, items):
        self.items = [items] if isinstance(items, tuple) else items
        self.i = 0

    def next(self):
        x = self.items[self.i % len(self.items)]
        self.i += 1
        return x


def build(nlayers=DEPTH, stop_after=None, taps=False):
    nc = bass.Bass("TRN2", target_bir_lowering=False)
    kin = "ExternalInput"
    dbg = "ExternalOutput" if taps else "Internal"

    def din(name, shape, dt=F32):
        return nc.dram_tensor(name, list(shape), dt, kind=kin).ap()

    def dscr(name, shape, dt=F32):
        return nc.dram_tensor(name, list(shape), dt, kind=dbg).ap()

    I = {}
    I["xc"] = din("xc", [NT, D])
    I["cvec"] = din("cvec", [2, D])
    I["ropeC"] = din("ropeC", [128, NT])
    I["ropeS"] = din("ropeS", [128, NT])
    I["perm"] = din("perm", [128, 128])
    I["ident"] = din("ident", [128, 128])
    shapes = dict(
        w_ada=[DEPTH, D, 6 * D], b_ada=[DEPTH, 6 * D], norm_pre_mix=[DEPTH, D], norm_post_mix=[DEPTH, D],
        norm_pre_ffn=[DEPTH, D], norm_post_ffn=[DEPTH, D], w_in=[DEPTH, D, INW], w_out=[DEPTH, D, D],
        da_lam_q1=[DEPTH, 64], da_lam_k1=[DEPTH, 64], da_lam_q2=[DEPTH, 64], da_lam_k2=[DEPTH, 64], da_subln=[DEPTH, 128],
        s5_lam_re=[DEPTH, 2, 32, 64], s5_lam_im=[DEPTH, 2, 32, 64], s5_log_step=[DEPTH, 2, 32],
        s5_b_re=[DEPTH, 2, 32, 64, 16], s5_b_im=[DEPTH, 2, 32, 64, 16], s5_c_re=[DEPTH, 2, 32, 16, 64],
        s5_c_im=[DEPTH, 2, 32, 16, 64], s5_d=[DEPTH, 512], s5_w_glu=[DEPTH, 512, 512], mla_q_norm=[DEPTH, 512],
        mla_kv_norm=[DEPTH, 256], mla_w_uq=[DEPTH, 512, 768], mla_w_ukv=[DEPTH, 256, 1024], conv_w=[DEPTH, 31, 512],
        conv_b=[DEPTH, 512], conv_ln_g=[DEPTH, 512], conv_ln_b=[DEPTH, 512], w_ffn_in=[DEPTH, D, 2 * DFF],
        w_ffn_out=[DEPTH, DFF, D])
    for k, s in shapes.items():
        I[k] = din(k, s)

    xres = nc.dram_tensor("xres", [NT, D], F32, kind="ExternalOutput").ap()
    S = {}
    S["modv"] = dscr("modv", [2, 6, D])
    S["hxT"] = dscr("hxT", [D, NT], BF16)
    S["qT"] = dscr("qT", [512, NT], BF16)
    S["kT"] = dscr("kT", [512, NT], BF16)
    S["vda"] = dscr("vda", [NT, 512], BF16)
    S["uT"] = dscr("uT", [512, NT])
    S["mlaT"] = dscr("mlaT", [832, NT])
    S["cuT"] = dscr("cuT", [512, NT], BF16)
    S["ymixT"] = dscr("ymixT", [D, NT], BF16)
    S["mqT"] = dscr("mqT", [4, 192, NT], BF16)
    S["mkT"] = dscr("mkT", [4, 128, NT], BF16)
    S["mkrT"] = dscr("mkrT", [64, NT], BF16)
    S["mv"] = dscr("mv", [NT, 512], BF16)
    S["h2T"] = dscr("h2T", [D, NT], BF16)
    ys5a = dscr("ys5a", [512, NCTX + NT])
    ys5b = dscr("ys5b", [512, NCTX + NT])
    S["ys5"] = [ys5a, ys5b]
    WW = []
    for i in range(2):
        Wd = {}
        Wd["w_in"] = nc.dram_tensor("wb_in%d" % i, [D, INW], BF16, kind="Internal").ap()
        Wd["w_out"] = nc.dram_tensor("wb_out%d" % i, [D, D], BF16, kind="Internal").ap()
        Wd["w_ffi"] = nc.dram_tensor("wb_ffi%d" % i, [D, 2 * DFF], BF16, kind="Internal").ap()
        Wd["w_ffo"] = nc.dram_tensor("wb_ffo%d" % i, [DFF, D], BF16, kind="Internal").ap()
        Wd["w_uq"] = nc.dram_tensor("wb_uq%d" % i, [512, 768], BF16, kind="Internal").ap()
        Wd["w_ukv"] = nc.dram_tensor("wb_ukv%d" % i, [256, 1024], BF16, kind="Internal").ap()
        Wd["w_glu"] = nc.dram_tensor("wb_glu%d" % i, [512, 512], BF16, kind="Internal").ap()
        WW.append(Wd)

    top = contextlib.ExitStack()
    with top:
        P = Prog(nc, top)
        def gsb(name, shape, dt):
            return top.enter_context(nc.sbuf_tensor(name, list(shape), dt))
        identb = gsb("identb", [128, 128], BF16)
        identf = gsb("identf", [128, 128], F32)
        permf = gsb("permf", [128, 128], F32)
        onesf = gsb("onesf", [128, 128], F32)
        onesb = gsb("onesb", [128, 128], BF16)
        scT = gsb("scT", [128, 2, 16], F32)
        with Stage(P) as sg:
            r = P.R()
            P.stq(lambda e: e.dma_start(out=identb[:], in_=I["ident"]), w=[r])
            P.ld(lambda e: e.dma_start(out=identf[:], in_=I["ident"]), w=[r])
            P.ld(lambda e: e.dma_start(out=permf[:], in_=I["perm"]), w=[r])
            P.dve(lambda e: e.memset(onesf[:], 1.0), w=[r])
            P.dve(lambda e: e.memset(onesb[:], 1.0), w=[r])
            ct, cr = sg.sb([128, 2, 16], F32)
            for rr in range(2):
                P.ld(lambda e, rr=rr: e.dma_start(out=ct[:, rr, :], in_=I["cvec"][rr].rearrange("(k p) -> p k", p=128), allow_slow_non_contiguous=True), w=[cr])
            P.act(lambda e: e.activation(out=scT[:], in_=ct[:], func=AF.Silu), r=[cr], w=[r])

        C = dict(identb=identb, identf=identf, permf=permf, onesf=onesf, onesb=onesb, scT=scT)
        for l in range(nlayers):
            last = (l == DEPTH - 1)
            xsrc = I["xc"] if l == 0 else xres
            W = WW[l % 2]
            if l == 0:
                with Stage(P) as sg0:
                    emit_cast(P, I, W, 0)
            stage_ada(P, I, S, C, l)
            if stop_after == "ada":
                break
            stage_prenorm(P, S, C, xsrc, S["hxT"], 0)
            if stop_after == "prenorm":
                break
            stage_win(P, I, S, W, C)
            if stop_after == "win":
                break
            stage_da(P, I, S, C, l, (lambda l=l: emit_cast(P, I, WW[(l + 1) % 2], l + 1)) if l + 1 < nlayers else None)
            if stop_after == "da":
                break
            stage_mla(P, I, S, W, C, l)
            if stop_after == "mla":
                break
            stage_conv(P, I, S, C, l)
            if stop_after == "conv":
                break
            if os.environ.get("SKIP_S5") != "1":
                stage_s5(P, I, S, W, C, l)
            if stop_after == "s5":
                break
            stage_wout(P, I, S, W, C, l, xsrc, xres)
            if stop_after == "wout":
                break
            stage_ffn(P, I, S, W, C, l, xres)
    return nc


def emit_cast(P, I, W, l):
    r = P.R()
    def cp(dst, src, rows, step):
        for i in range(0, rows, step):
            P.stq(lambda e, i=i: e.dma_start(out=dst[i:i + step, :], in_=src[i:i + step, :]), w=[r])
    cp(W["w_in"], I["w_in"][l], D, D)
    cp(W["w_out"], I["w_out"][l], D, D)
    cp(W["w_ffi"], I["w_ffn_in"][l], D, 1024)
    cp(W["w_ffo"], I["w_ffn_out"][l], DFF, 2816)
    cp(W["w_uq"], I["mla_w_uq"][l], 512, 512)
    cp(W["w_ukv"], I["mla_w_ukv"][l], 256, 256)
    cp(W["w_glu"], I["s5_w_glu"][l], 512, 512)


def stage_ada(P, I, S, C, l):
    nc = P.nc
    with Stage(P) as sg:
        wt = sg.sb([128, 16, 512], F32, 2)
        wrot = Rot(wt)
        bia, biar = sg.sb([2, 6 * D], F32)
        mod, modr = bia, biar
        gv, gvr = sg.sb([2, 4, D], F32)
        outv, outr = sg.sb([2, 6, D], F32)
        pss = Rot(sg.ps([2, 512], F32, 2))
        P.ld(lambda e: e.dma_start(out=bia[:], in_=I["b_ada"][l:l + 1, :].to_broadcast([2, 6 * D])), w=[biar])
        for i, nm in enumerate(["norm_pre_mix", "norm_post_mix", "norm_pre_ffn", "norm_post_ffn"]):
            P.ld(lambda e, i=i, nm=nm: e.dma_start(out=gv[:, i, :], in_=I[nm][l:l + 1, :].to_broadcast([2, D])), w=[gvr])
        wv = I["w_ada"][l].rearrange("(k p) c -> p k c", p=128)
        for j in range(24):
            (w, wr) = wrot.next()
            P.ld(lambda e, w=w, j=j: e.dma_start(out=w[:], in_=wv[:, :, j * 512:(j + 1) * 512]), w=[wr])
            (ps, psr) = pss.next()
            for k in range(16):
                P.pe(lambda e, w=w, k=k, ps=ps: e.matmul(ps[:], lhsT=C["scT"][:, :, k], rhs=w[:, k, :], start=(k == 0), stop=(k == 15)),
                     r=[wr], w=[psr])
            P.dve(lambda e, ps=ps, j=j: e.tensor_tensor(out=mod[:, j * 512:(j + 1) * 512], in0=ps[:], in1=bia[:, j * 512:(j + 1) * 512], op=ALU.add),
                  r=[psr, biar], w=[modr])
        m = lambda i: mod[:, i * D:(i + 1) * D]
        P.dve(lambda e: e.scalar_tensor_tensor(out=outv[:, 0, :], in0=m(1), scalar=1.0, in1=gv[:, 0, :], op0=ALU.add, op1=ALU.mult), r=[modr, gvr], w=[outr])
        P.dve(lambda e: e.tensor_copy(out=outv[:, 1, :], in_=m(0)), r=[modr], w=[outr])
        P.dve(lambda e: e.tensor_tensor(out=outv[:, 2, :], in0=m(2), in1=gv[:, 1, :], op=ALU.mult), r=[modr, gvr], w=[outr])
        P.dve(lambda e: e.scalar_tensor_tensor(out=outv[:, 3, :], in0=m(4), scalar=1.0, in1=gv[:, 2, :], op0=ALU.add, op1=ALU.mult), r=[modr, gvr], w=[outr])
        P.dve(lambda e: e.tensor_copy(out=outv[:, 4, :], in_=m(3)), r=[modr], w=[outr])
        P.dve(lambda e: e.tensor_tensor(out=outv[:, 5, :], in0=m(5), in1=gv[:, 3, :], op=ALU.mult), r=[modr, gvr], w=[outr])
        dr = P.R()
        P.stq(lambda e: e.dma_start(out=S["modv"], in_=outv[:]), r=[outr], w=[dr])


def load_modvec(P, sg, S, idx):
    out = {}
    for kind in range(2):
        t, r = sg.sb([128, D], F32)
        P.ld(lambda e, t=t, kind=kind: e.dma_start(out=t[:], in_=S["modv"][kind:kind + 1, idx, :].to_broadcast([128, D])), w=[r])
        out[kind] = (t, r)
    return out


def rstd_from_ss(P, sg, ss, ssr, n_feat, rot=None):
    t, tr = rot.next() if rot is not None else sg.sb([128, 1], F32)
    P.dve(lambda e: e.tensor_scalar(out=t[:], in0=ss[:], scalar1=1.0 / n_feat, scalar2=EPS, op0=ALU.mult, op1=ALU.add), r=[ssr], w=[tr])
    P.act(lambda e: e.activation(out=t[:], in_=t[:], func=AF.Sqrt), r=[tr], w=[tr])
    P.dve(lambda e: e.reciprocal(out=t[:], in_=t[:]), r=[tr], w=[tr])
    return t, tr


def norm_mod_transpose(P, sg, C, xt, xtr, G, SH, kind, hT, hTr, col0, bufs):
    junk, junkr = bufs["junk"]
    ss, ssr = bufs["ss"].next()
    hb, hbr = bufs["hb"].next()
    P.act(lambda e: e.activation(out=junk[:], in_=xt[:], func=AF.Square, accum_out=ss[:]), r=[xtr], w=[junkr, ssr])
    rs, rsr = rstd_from_ss(P, sg, ss, ssr, D, bufs["rs"])
    tmp, tmpr = bufs["tmp"].next()
    P.dve(lambda e: e.scalar_tensor_tensor(out=tmp[:], in0=xt[:], scalar=rs[:, 0:1], in1=G[kind][0][:], op0=ALU.mult, op1=ALU.mult),
          r=[xtr, rsr, G[kind][1]], w=[tmpr])
    P.pool(lambda e: e.tensor_tensor(out=hb[:], in0=tmp[:], in1=SH[kind][0][:], op=ALU.add), r=[tmpr, SH[kind][1]], w=[hbr])
    for half in range(2):
        pt, ptr = bufs["pt"].next()
        for k in range(8):
            kk = half * 8 + k
            P.pe(lambda e, pt=pt, k=k, kk=kk: e.transpose(pt[:, k * 128:(k + 1) * 128], hb[:, kk * 128:(kk + 1) * 128], C["identb"][:]),
                 r=[hbr], w=[ptr])
        P.act(lambda e, pt=pt, half=half: e.activation(out=hT[:, half * 8:(half + 1) * 8, col0:col0 + 128],
                                                       in_=pt[:].rearrange("p (k t) -> p k t", k=8), func=AF.Copy),
              r=[ptr], w=[hTr])


def norm_bufs(sg):
    return dict(junk=sg.sb([128, D], BF16), ss=Rot(sg.sb([128, 1], F32, 2)), rs=Rot(sg.sb([128, 1], F32, 2)),
                hb=Rot(sg.sb([128, D], BF16, 2)), tmp=Rot(sg.sb([128, D], F32, 2)), pt=Rot(sg.ps([128, 1024], BF16, 2)))


def stage_prenorm(P, S, C, xsrc, dstT, midx):
    nc = P.nc
    with Stage(P) as sg:
        G = load_modvec(P, sg, S, midx)
        SH = load_modvec(P, sg, S, midx + 1)
        xts = Rot(sg.sb([128, D], F32, 2))
        hTs = Rot(sg.sb([128, 16, 512], BF16, 2))
        bufs = norm_bufs(sg)
        dr = P.R()
        dv = dstT.rearrange("(k p) t -> p k t", p=128)
        for (t0, n) in CHUNKS:
            hT, hTr = hTs.next()
            for i in range(n // 128):
                xt, xtr = xts.next()
                P.ld(lambda e, xt=xt, a=t0 + i * 128: e.dma_start(out=xt[:], in_=xsrc[a:a + 128, :]), w=[xtr])
                norm_mod_transpose(P, sg, C, xt, xtr, G, SH, 0 if t0 >= NCTX else 1, hT, hTr, i * 128, bufs)
            P.stq(lambda e, hT=hT, t0=t0, n=n: e.dma_start(out=dv[:, :, t0:t0 + n], in_=hT[:, :, 0:n]), r=[hTr], w=[dr])


def rope_tile(P, sg, C, ps, psr, rows, t0, n, tabs, bufs, dst_dram, dstres, scale_ap=None):
    qf, qfr = bufs["qf"].next()
    if scale_ap is None:
        P.act(lambda e: e.activation(out=qf[0:rows, 0:n], in_=ps[0:rows, 0:n], func=AF.Copy), r=[psr], w=[qfr])
    else:
        P.dve(lambda e: e.tensor_tensor(out=qf[0:rows, 0:n], in0=ps[0:rows, 0:n], in1=scale_ap[0][0:rows, 0:n], op=ALU.mult), r=[psr, scale_ap[1]], w=[qfr])
    p2, p2r = bufs["p2"].next()
    P.pe(lambda e: e.matmul(p2[0:rows, 0:n], lhsT=C["permf"][0:rows, 0:rows], rhs=qf[0:rows, 0:n], start=True, stop=True), r=[qfr], w=[p2r])
    t1, t1r = bufs["t1"].next()
    (tc, tcr), (ts, tsr) = tabs
    P.pool(lambda e: e.tensor_tensor(out=t1[0:rows, 0:n], in0=qf[0:rows, 0:n], in1=tc[0:rows, t0:t0 + n], op=ALU.mult), r=[qfr, tcr], w=[t1r])
    t2, t2r = bufs["t2"].next()
    P.dve(lambda e: e.tensor_tensor(out=t2[0:rows, 0:n], in0=p2[0:rows, 0:n], in1=ts[0:rows, t0:t0 + n], op=ALU.mult), r=[p2r, tsr], w=[t2r])
    ob, obr = bufs["ob"].next()
    P.dve(lambda e: e.tensor_tensor(out=ob[0:rows, 0:n], in0=t1[0:rows, 0:n], in1=t2[0:rows, 0:n], op=ALU.add), r=[t1r, t2r], w=[obr])
    P.stq(lambda e: e.dma_start(out=dst_dram, in_=ob[0:rows, 0:n]), r=[obr], w=[dstres])


def rope_bufs(sg):
    return dict(qf=Rot(sg.sb([128, 512], F32, 2)), p2=Rot(sg.ps([128, 512], F32, 2)), t1=Rot(sg.sb([128, 512], F32, 2)),
                t2=Rot(sg.sb([128, 512], F32, 2)), ob=Rot(sg.sb([128, 512], BF16, 2)))


def load_rope_tabs(P, sg, I):
    tc, tcr = sg.sb([128, NT], F32)
    ts, tsr = sg.sb([128, NT], F32)
    P.ld(lambda e: e.dma_start(out=tc[:], in_=I["ropeC"]), w=[tcr])
    P.ld(lambda e: e.dma_start(out=ts[:], in_=I["ropeS"]), w=[tsr])
    return (tc, tcr), (ts, tsr)


def stage_win(P, I, S, W, C):
    nc = P.nc
    with Stage(P) as sg:
        tabs = load_rope_tabs(P, sg, I)
        rb = rope_bufs(sg)
        hTs = Rot(sg.sb([128, 16, 512], BF16, 2))
        wts = Rot(sg.sb([128, 16, 128], BF16, 3))
        wv, wvr = sg.sb([128, 16, 512], BF16)
        pss = Rot(sg.ps([128, 512], F32, 4))
        ev = Rot(sg.sb([128, 512], F32, 3))
        evb = Rot(sg.sb([128, 512], BF16, 3))
        dr = P.R()
        hv = S["hxT"].rearrange("(k p) t -> p k t", p=128)
        wiv = W["w_in"].rearrange("(k p) f -> p k f", p=128)
        P.ld(lambda e: e.dma_start(out=wv[:], in_=wiv[:, :, 1024:1536]), w=[wvr])
        tiles = []
        for j in range(4):
            tiles.append((j * 128, 128, "rope", S["qT"][j * 128:(j + 1) * 128, :]))
        for j in range(4):
            tiles.append((512 + j * 128, 128, "rope", S["kT"][j * 128:(j + 1) * 128, :]))
        for j in range(4):
            tiles.append((1536 + j * 128, 128, "f32", S["uT"][j * 128:(j + 1) * 128, :]))
        for j in range(6):
            tiles.append((2048 + j * 128, 128, "f32", S["mlaT"][j * 128:(j + 1) * 128, :]))
        tiles.append((2816, 64, "f32", S["mlaT"][768:832, :]))
        for j in range(4):
            tiles.append((2880 + j * 128, 128, "cval", j))
        def do_chunk(t0, n):
            hT, hTr = hTs.next()
            P.ld(lambda e, hT=hT, t0=t0, n=n: e.dma_start(out=hT[:, :, 0:n], in_=hv[:, :, t0:t0 + n]), w=[hTr])

            def proj(c0, rows):
                w, wr = wts.next()
                P.ld(lambda e: e.dma_start(out=w[:, :, 0:rows], in_=wiv[:, :, c0:c0 + rows]), w=[wr])
                ps, psr = pss.next()
                for k in range(16):
                    P.pe(lambda e, k=k: e.matmul(ps[0:rows, 0:n], lhsT=w[:, k, 0:rows], rhs=hT[:, k, 0:n], start=(k == 0), stop=(k == 15)),
                         r=[wr, hTr], w=[psr])
                return ps, psr
            for (c0, rows, kind, dst) in tiles:
                ps, psr = proj(c0, rows)
                if kind == "rope":
                    rope_tile(P, sg, C, ps, psr, rows, t0, n, tabs, rb, dst[:, t0:t0 + n], dr)
                elif kind == "f32":
                    o, orr = ev.next()
                    P.act(lambda e, o=o, ps=ps, rows=rows: e.activation(out=o[0:rows, 0:n], in_=ps[0:rows, 0:n], func=AF.Copy), r=[psr], w=[orr])
                    P.stq(lambda e, o=o, dst=dst, rows=rows: e.dma_start(out=dst[:, t0:t0 + n], in_=o[0:rows, 0:n]), r=[orr], w=[dr])
                else:
                    j = dst
                    psg, psgr = proj(2880 + 512 + j * 128, 128)
                    sgm, sgmr = ev.next()
                    P.act(lambda e, sgm=sgm, psg=psg: e.activation(out=sgm[:, 0:n], in_=psg[:, 0:n], func=AF.Sigmoid), r=[psgr], w=[sgmr])
                    ob, obr = evb.next()
                    P.dve(lambda e, ob=ob, ps=ps, sgm=sgm: e.tensor_tensor(out=ob[:, 0:n], in0=ps[:, 0:n], in1=sgm[:, 0:n], op=ALU.mult), r=[psr, sgmr], w=[obr])
                    P.stq(lambda e, ob=ob, j=j: e.dma_start(out=S["cuT"][j * 128:(j + 1) * 128, t0:t0 + n], in_=ob[:, 0:n]), r=[obr], w=[dr])
            for i in range(n // 128):
                ps, psr = pss.next()
                for k in range(16):
                    P.pe(lambda e, k=k, i=i, ps=ps: e.matmul(ps[:, :], lhsT=hT[:, k, i * 128:(i + 1) * 128], rhs=wv[:, k, :], start=(k == 0), stop=(k == 15)),
                         r=[wvr, hTr], w=[psr])
                ob, obr = evb.next()
                P.act(lambda e, ob=ob, ps=ps: e.activation(out=ob[:], in_=ps[:], func=AF.Copy), r=[psr], w=[obr])
                P.stq(lambda e, ob=ob, a=t0 + i * 128: e.dma_start(out=S["vda"][a:a + 128, :], in_=ob[:]), r=[obr], w=[dr])
        for (t0, n) in CHUNKS:
            do_chunk(t0, n)


def attention(P, sg, C, parts, V, Vr, scale, chunk, pbufs):
    t0, n = chunk
    nkt = 2 if t0 < NCTX else NT // 128
    o, orr = pbufs["o"].next()
    z, zr = pbufs["z"].next()
    for kt in range(nkt):
        s, sr = pbufs["s"].next()
        for pi, (KT, QT, rr) in enumerate(parts):
            P.pe(lambda e, KT=KT, QT=QT, pi=pi, s=s, kt=kt: e.matmul(s[:, 0:n], lhsT=KT[:, kt * 128:(kt + 1) * 128], rhs=QT[:, t0:t0 + n],
                                                                  start=(pi == 0), stop=(pi == len(parts) - 1)), r=[rr], w=[sr])
        p, pr = pbufs["p"].next()
        P.act(lambda e, p=p, s=s: e.activation(out=p[:, 0:n], in_=s[:, 0:n], func=AF.Exp, scale=scale), r=[sr], w=[pr])
        P.pe(lambda e, p=p, kt=kt: e.matmul(o[:, 0:n], lhsT=V[:, kt, :], rhs=p[:, 0:n], start=(kt == 0), stop=(kt == nkt - 1)), r=[pr, Vr], w=[orr])
        P.pe(lambda e, p=p, kt=kt: e.matmul(z[:, 0:n], lhsT=C["onesb"][:], rhs=p[:, 0:n], start=(kt == 0), stop=(kt == nkt - 1)), r=[pr], w=[zr])
    return (o, orr), (z, zr)


def stage_da(P, I, S, C, l, prefetch=None):
    nc = P.nc
    li = 0.8 - 0.6 * math.exp(-0.3 * l)
    with Stage(P) as sg:
        if prefetch is not None:
            prefetch()
        lv, lvr = sg.sb([128, 4, 64], F32)
        for i, nm in enumerate(["da_lam_q1", "da_lam_k1", "da_lam_q2", "da_lam_k2"]):
            P.ld(lambda e, i=i, nm=nm: e.dma_start(out=lv[:, i, :], in_=I[nm][l:l + 1, :].to_broadcast([128, 64])), w=[lvr])
        lp, lpr = sg.sb([128, 2, 64], F32)
        P.dve(lambda e: e.tensor_tensor(out=lp[:, 0, :], in0=lv[:, 0, :], in1=lv[:, 1, :], op=ALU.mult), r=[lvr], w=[lpr])
        P.dve(lambda e: e.tensor_tensor(out=lp[:, 1, :], in0=lv[:, 2, :], in1=lv[:, 3, :], op=ALU.mult), r=[lvr], w=[lpr])
        lsum, lsr = sg.sb([128, 2], F32)
        P.dve(lambda e: e.tensor_reduce(out=lsum[:], in_=lp[:], axis=mybir.AxisListType.X, op=ALU.add), r=[lpr], w=[lsr])
        P.act(lambda e: e.activation(out=lsum[:], in_=lsum[:], func=AF.Exp), r=[lsr], w=[lsr])
        nlam, nlr = sg.sb([128, 1], F32)
        P.dve(lambda e: e.tensor_tensor(out=nlam[:], in0=lsum[:, 1:2], in1=lsum[:, 0:1], op=ALU.subtract), r=[lsr], w=[nlr])
        P.dve(lambda e: e.tensor_scalar(out=nlam[:], in0=nlam[:], scalar1=-li, scalar2=None, op0=ALU.add), r=[nlr], w=[nlr])
        gs, gsr = sg.sb([128, 1], F32)
        P.ld(lambda e: e.dma_start(out=gs[:], in_=I["da_subln"][l].rearrange("(p o) -> p o", o=1), allow_slow_non_contiguous=True), w=[gsr])
        P.dve(lambda e: e.tensor_scalar(out=gs[:], in0=gs[:], scalar1=(1.0 - li), scalar2=None, op0=ALU.mult), r=[gsr], w=[gsr])

        KT, KTr = sg.sb([128, NT], BF16)
        QT, QTr = sg.sb([128, NT], BF16)
        V, Vr = sg.sb([128, 34, 128], BF16)
        pb = dict(o=Rot(sg.ps([128, 512], F32, 2)), z=Rot(sg.ps([128, 512], F32, 2)), s=Rot(sg.ps([128, 512], F32, 3)),
                  p=Rot(sg.sb([128, 512], BF16, 4)), acc=Rot(sg.sb([128, 512], F32, 4)))
        ssp = sg.ps([128, 512], F32)
        wk = Rot(sg.sb([128, 512], F32, 6))
        ob = Rot(sg.sb([128, 512], BF16, 2))
        dr = P.R()
        for h in range(4):
            P.ld(lambda e, h=h: e.dma_start(out=KT[:], in_=S["kT"][h * 128:(h + 1) * 128, :]), w=[KTr])
            P.ld(lambda e, h=h: e.dma_start(out=QT[:], in_=S["qT"][h * 128:(h + 1) * 128, :]), w=[QTr])
            P.ld(lambda e, h=h: e.dma_start(out=V[:], in_=S["vda"][:, h * 128:(h + 1) * 128].rearrange("(k p) d -> p k d", p=128)), w=[Vr])
            def do_chunk(h, chunk):
                t0, n = chunk
                nrm = []
                for m in range(2):
                    parts = [(KT[m * 64:(m + 1) * 64, :], QT[m * 64:(m + 1) * 64, :], KTr if False else QTr)]
                    (o, orr), (z, zr) = attention_rw(P, sg, C, parts, [KTr, QTr], V, Vr, 0.125, chunk, pb)
                    rz, rzr = wk.next()
                    P.dve(lambda e, rz=rz, z=z: e.reciprocal(out=rz[:, 0:n], in_=z[:, 0:n]), r=[zr], w=[rzr])
                    a, ar = wk.next()
                    P.dve(lambda e, a=a, o=o, rz=rz: e.tensor_tensor(out=a[:, 0:n], in0=o[:, 0:n], in1=rz[:, 0:n], op=ALU.mult), r=[orr, rzr], w=[ar])
                    nrm.append((a, ar))
                d, ddr = wk.next()
                (a1, a1r), (a2, a2r) = nrm
                P.dve(lambda e, d=d, a1=a1, a2=a2: e.scalar_tensor_tensor(out=d[:, 0:n], in0=a2[:, 0:n], scalar=nlam[:, 0:1], in1=a1[:, 0:n], op0=ALU.mult, op1=ALU.add),
                      r=[a1r, a2r, nlr], w=[ddr])
                sq, sqr = wk.next()
                P.act(lambda e, sq=sq, d=d: e.activation(out=sq[:, 0:n], in_=d[:, 0:n], func=AF.Square), r=[ddr], w=[sqr])
                (sp_, spr) = ssp
                P.pe(lambda e, sq=sq: e.matmul(sp_[:, 0:n], lhsT=C["onesf"][:], rhs=sq[:, 0:n], start=True, stop=True), r=[sqr], w=[spr])
                rs, rsr = wk.next()
                P.dve(lambda e, rs=rs: e.tensor_scalar(out=rs[:, 0:n], in0=sp_[:, 0:n], scalar1=1.0 / 128, scalar2=EPS, op0=ALU.mult, op1=ALU.add), r=[spr], w=[rsr])
                P.act(lambda e, rs=rs: e.activation(out=rs[:, 0:n], in_=rs[:, 0:n], func=AF.Sqrt), r=[rsr], w=[rsr])
                P.dve(lambda e, rs=rs: e.reciprocal(out=rs[:, 0:n], in_=rs[:, 0:n]), r=[rsr], w=[rsr])
                y, yr = ob.next()
                P.dve(lambda e, y=y, d=d, rs=rs: e.scalar_tensor_tensor(out=y[:, 0:n], in0=d[:, 0:n], scalar=gs[:, 0:1], in1=rs[:, 0:n], op0=ALU.mult, op1=ALU.mult),
                      r=[ddr, rsr, gsr], w=[yr])
                P.stq(lambda e, y=y, h=h, t0=t0, n=n: e.dma_start(out=S["ymixT"][h * 128:(h + 1) * 128, t0:t0 + n], in_=y[:, 0:n]), r=[yr], w=[dr])
            for chunk in CHUNKS:
                do_chunk(h, chunk)


def attention_rw(P, sg, C, parts, rres, V, Vr, scale, chunk, pbufs):
    t0, n = chunk
    nkt = 2 if t0 < NCTX else NT // 128
    o, orr = pbufs["o"].next()
    z, zr = pbufs["z"].next()

    def qk(kt):
        s, sr = pbufs["s"].next()
        for pi, (KT, QT, _) in enumerate(parts):
            P.pe(lambda e, KT=KT, QT=QT, pi=pi: e.matmul(s[:, 0:n], lhsT=KT[:, kt * 128:(kt + 1) * 128], rhs=QT[:, t0:t0 + n],
                                                        start=(pi == 0), stop=(pi == len(parts) - 1)), r=rres, w=[sr])
        p, pr = pbufs["p"].next()
        P.act(lambda e: e.activation(out=p[:, 0:n], in_=s[:, 0:n], func=AF.Exp, scale=scale), r=[sr], w=[pr])
        return p, pr

    accs = [pbufs["acc"].next()]

    def pv(kt, p, pr):
        P.pe(lambda e: e.matmul(o[:, 0:n], lhsT=V[:, kt, :], rhs=p[:, 0:n], start=(kt == 0), stop=(kt == nkt - 1)), r=[pr, Vr], w=[orr])
        a, ar = accs[0]
        eng = P.dve
        if kt < 1:
            eng(lambda e: e.tensor_copy(out=a[:, 0:n], in_=p[:, 0:n]), r=[pr], w=[ar])
        else:
            eng(lambda e: e.tensor_tensor(out=a[:, 0:n], in0=a[:, 0:n], in1=p[:, 0:n], op=ALU.add), r=[pr, ar], w=[ar])
    pend = []
    for kt in range(nkt):
        pend.append((kt, qk(kt)))
        if len(pend) > 2:
            k0, (p0, p0r) = pend.pop(0)
            pv(k0, p0, p0r)
    for k0, (p0, p0r) in pend:
        pv(k0, p0, p0r)
    a, ar = accs[0]
    P.pe(lambda e: e.matmul(z[:, 0:n], lhsT=C["onesf"][:], rhs=a[:, 0:n], start=True, stop=True), r=[ar], w=[zr])
    return (o, orr), (z, zr)


def stage_mla(P, I, S, W, C, l):
    nc = P.nc
    with Stage(P) as sg:
        tabs = load_rope_tabs(P, sg, I)
        rb = rope_bufs(sg)
        wuq, wuqr = sg.sb([128, 4, 768], BF16)
        wuk, wukr = sg.sb([128, 2, 4, 128], BF16)
        wuv, wuvr = sg.sb([128, 2, 4, 128], BF16)
        gq, gqr = sg.sb([128, 4], F32)
        gkv, gkvr = sg.sb([128, 2], F32)
        P.ld(lambda e: e.dma_start(out=wuq[:], in_=W["w_uq"].rearrange("(k p) f -> p k f", p=128)), w=[wuqr])
        ukv = W["w_ukv"].rearrange("(k p) (h c) -> p k h c", p=128, c=256)
        for k in range(2):
            P.ld(lambda e, k=k: e.dma_start(out=wuk[:, k, :, :], in_=ukv[:, k, :, 0:128]), w=[wukr])
            P.ld(lambda e, k=k: e.dma_start(out=wuv[:, k, :, :], in_=ukv[:, k, :, 128:256]), w=[wuvr])
        P.ld(lambda e: e.dma_start(out=gq[:], in_=I["mla_q_norm"][l].rearrange("(k p) -> p k", p=128), allow_slow_non_contiguous=True), w=[gqr])
        P.ld(lambda e: e.dma_start(out=gkv[:], in_=I["mla_kv_norm"][l].rearrange("(k p) -> p k", p=128), allow_slow_non_contiguous=True), w=[gkvr])
        lat = Rot(sg.sb([128, 6, 512], F32, 2))
        krs = Rot(sg.sb([64, 512], F32, 2))
        sqs = Rot(sg.sb([128, 6, 512], F32, 1))
        lsb = Rot(sg.sb([128, 6, 512], BF16, 2))
        rst = Rot(sg.sb([128, 2, 512], F32, 2))
        pss = Rot(sg.ps([128, 512], F32, 4))
        ob = Rot(sg.sb([128, 512], BF16, 3))
        sm = Rot(sg.sb([128, 1], F32, 4))
        dr = P.R()
        mv3 = S["mlaT"][0:768, :].rearrange("(k p) t -> p k t", p=128)

        def do_chunk(t0, n):
            x, xr = lat.next()
            P.ld(lambda e: e.dma_start(out=x[:, :, 0:n], in_=mv3[:, :, t0:t0 + n]), w=[xr])
            kr, krr = krs.next()
            P.ld(lambda e: e.dma_start(out=kr[:, 0:n], in_=S["mlaT"][768:832, t0:t0 + n]), w=[krr])
            sq, sqr = sqs.next()
            P.act(lambda e: e.activation(out=sq[:, :, 0:n], in_=x[:, :, 0:n], func=AF.Square), r=[xr], w=[sqr])
            rs, rsr = rst.next()
            for (which, k0, nk, nf) in ((0, 0, 4, 512), (1, 4, 2, 256)):
                ps, psr = pss.next()
                for k in range(nk):
                    P.pe(lambda e, k=k, ps=ps, k0=k0, nk=nk: e.matmul(ps[:, 0:n], lhsT=C["onesf"][:], rhs=sq[:, k0 + k, 0:n], start=(k == 0), stop=(k == nk - 1)), r=[sqr], w=[psr])
                P.dve(lambda e, ps=ps, which=which, nf=nf: e.tensor_scalar(out=rs[:, which, 0:n], in0=ps[:, 0:n], scalar1=1.0 / nf, scalar2=EPS, op0=ALU.mult, op1=ALU.add), r=[psr], w=[rsr])
            P.act(lambda e: e.activation(out=rs[:, :, 0:n], in_=rs[:, :, 0:n], func=AF.Sqrt), r=[rsr], w=[rsr])
            P.dve(lambda e: e.reciprocal(out=rs[:, :, 0:n], in_=rs[:, :, 0:n]), r=[rsr], w=[rsr])
            xb, xbr = lsb.next()
            for k in range(6):
                g = gq[:, k:k + 1] if k < 4 else gkv[:, k - 4:k - 3]
                P.act(lambda e, k=k, g=g: e.activation(out=xb[:, k, 0:n], in_=x[:, k, 0:n], func=AF.Copy, scale=g), r=[xr, gqr, gkvr], w=[xbr])
            for h in range(4):
                ps, psr = pss.next()
                for k in range(4):
                    P.pe(lambda e, k=k, ps=ps, h=h: e.matmul(ps[:, 0:n], lhsT=wuq[:, k, h * 192:h * 192 + 128], rhs=xb[:, k, 0:n], start=(k == 0), stop=(k == 3)), r=[wuqr, xbr], w=[psr])
                o, orr = ob.next()
                P.dve(lambda e, o=o, ps=ps: e.tensor_tensor(out=o[:, 0:n], in0=ps[:, 0:n], in1=rs[:, 0, 0:n], op=ALU.mult), r=[psr, rsr], w=[orr])
                P.stq(lambda e, o=o, h=h: e.dma_start(out=S["mqT"][h, 0:128, t0:t0 + n], in_=o[:, 0:n]), r=[orr], w=[dr])
                ps2, ps2r = pss.next()
                for k in range(4):
                    P.pe(lambda e, k=k, ps2=ps2, h=h: e.matmul(ps2[0:64, 0:n], lhsT=wuq[:, k, h * 192 + 128:h * 192 + 192], rhs=xb[:, k, 0:n], start=(k == 0), stop=(k == 3)), r=[wuqr, xbr], w=[ps2r])
                rope_tile(P, sg, C, ps2, ps2r, 64, t0, n, tabs, rb, S["mqT"][h, 128:192, t0:t0 + n], dr, scale_ap=(rs[:, 0, :], rsr))
                ps3, ps3r = pss.next()
                for k in range(2):
                    P.pe(lambda e, k=k, ps3=ps3, h=h: e.matmul(ps3[:, 0:n], lhsT=wuk[:, k, h, :], rhs=xb[:, 4 + k, 0:n], start=(k == 0), stop=(k == 1)), r=[wukr, xbr], w=[ps3r])
                o2, o2r = ob.next()
                P.dve(lambda e, o2=o2, ps3=ps3: e.tensor_tensor(out=o2[:, 0:n], in0=ps3[:, 0:n], in1=rs[:, 1, 0:n], op=ALU.mult), r=[ps3r, rsr], w=[o2r])
                P.stq(lambda e, o2=o2, h=h: e.dma_start(out=S["mkT"][h, :, t0:t0 + n], in_=o2[:, 0:n]), r=[o2r], w=[dr])
            rope_tile(P, sg, C, kr, krr, 64, t0, n, tabs, rb, S["mkrT"][:, t0:t0 + n], dr)
            for i in range(n // 128):
                pv, pvr = pss.next()
                for k in range(2):
                    P.pe(lambda e, k=k, pv=pv, i=i: e.matmul(pv[:, :], lhsT=xb[:, 4 + k, i * 128:(i + 1) * 128], rhs=wuv[:, k, :, :].rearrange("p h c -> p (h c)"), start=(k == 0), stop=(k == 1)), r=[wuvr, xbr], w=[pvr])
                pt, ptr = pss.next()
                for k in range(2):
                    P.pe(lambda e, k=k, pt=pt, i=i: e.matmul(pt[:, 0:1], lhsT=sq[:, 4 + k, i * 128:(i + 1) * 128], rhs=C["onesf"][:, 0:1], start=(k == 0), stop=(k == 1)), r=[sqr], w=[ptr])
                r1, r1r = sm.next()
                P.dve(lambda e, r1=r1, pt=pt: e.tensor_scalar(out=r1[:], in0=pt[:, 0:1], scalar1=1.0 / 256, scalar2=EPS, op0=ALU.mult, op1=ALU.add), r=[ptr], w=[r1r])
                P.act(lambda e, r1=r1: e.activation(out=r1[:], in_=r1[:], func=AF.Sqrt), r=[r1r], w=[r1r])
                P.dve(lambda e, r1=r1: e.reciprocal(out=r1[:], in_=r1[:]), r=[r1r], w=[r1r])
                o3, o3r = ob.next()
                P.act(lambda e, o3=o3, pv=pv, r1=r1: e.activation(out=o3[:], in_=pv[:], func=AF.Copy, scale=r1[:, 0:1]), r=[pvr, r1r], w=[o3r])
                P.stq(lambda e, o3=o3, a=t0 + i * 128: e.dma_start(out=S["mv"][a:a + 128, :], in_=o3[:]), r=[o3r], w=[dr])
        for (t0, n) in CHUNKS:
            do_chunk(t0, n)
    with Stage(P) as sg:
        KN, KNr = sg.sb([128, NT], BF16)
        QN, QNr = sg.sb([128, NT], BF16)
        KR, KRr = sg.sb([64, NT], BF16)
        QR, QRr = sg.sb([64, NT], BF16)
        V, Vr = sg.sb([128, 34, 128], BF16)
        pb = dict(o=Rot(sg.ps([128, 512], F32, 2)), z=Rot(sg.ps([128, 512], F32, 2)), s=Rot(sg.ps([128, 512], F32, 4)),
                  p=Rot(sg.sb([128, 512], BF16, 4)), acc=Rot(sg.sb([128, 512], F32, 4)))
        wk = Rot(sg.sb([128, 512], F32, 3))
        ob = Rot(sg.sb([128, 512], BF16, 2))
        dr = P.R()
        P.ld(lambda e: e.dma_start(out=KR[:], in_=S["mkrT"]), w=[KRr])

        def do_chunk(h, chunk):
            t0, n = chunk
            parts = [(KN[:, :], QN[:, :], None), (KR[:, :], QR[:, :], None)]
            (o, orr), (z, zr) = attention_rw(P, sg, C, parts, [KNr, QNr, KRr, QRr], V, Vr, 192.0 ** -0.5, chunk, pb)
            rz, rzr = wk.next()
            P.dve(lambda e: e.reciprocal(out=rz[:, 0:n], in_=z[:, 0:n]), r=[zr], w=[rzr])
            y, yr = ob.next()
            P.dve(lambda e: e.tensor_tensor(out=y[:, 0:n], in0=o[:, 0:n], in1=rz[:, 0:n], op=ALU.mult), r=[orr, rzr], w=[yr])
            P.stq(lambda e: e.dma_start(out=S["ymixT"][1024 + h * 128:1024 + (h + 1) * 128, t0:t0 + n], in_=y[:, 0:n]), r=[yr], w=[dr])
        for h in range(4):
            P.ld(lambda e, h=h: e.dma_start(out=KN[:], in_=S["mkT"][h]), w=[KNr])
            P.ld(lambda e, h=h: e.dma_start(out=QN[:], in_=S["mqT"][h, 0:128, :]), w=[QNr])
            P.ld(lambda e, h=h: e.dma_start(out=QR[:], in_=S["mqT"][h, 128:192, :]), w=[QRr])
            P.ld(lambda e, h=h: e.dma_start(out=V[:], in_=S["mv"][:, h * 128:(h + 1) * 128].rearrange("(k p) d -> p k d", p=128)), w=[Vr])
            for chunk in CHUNKS:
                do_chunk(h, chunk)


CPAD = NT + 45


def stage_conv(P, I, S, C, l):
    nc = P.nc
    with Stage(P) as sg:
        U, Ur = sg.sb([128, 4, CPAD], BF16)
        Dj, Djr = sg.sb([128, 4, 31, 128], BF16)
        wc, wcr = sg.sb([128, 4, 31], F32)
        pv, pvr = sg.sb([128, 3, 4], F32)
        P.pool(lambda e: e.memset(U[:], 0.0), w=[Ur])
        cu = S["cuT"].rearrange("(f p) t -> p f t", p=128)
        P.ld(lambda e: e.dma_start(out=U[:, :, 15:15 + NCTX], in_=cu[:, :, 0:NCTX]), w=[Ur])
        P.ld(lambda e: e.dma_start(out=U[:, :, 286:286 + NLAT], in_=cu[:, :, NCTX:NT]), w=[Ur])
        for ft in range(4):
            P.ld(lambda e, ft=ft: e.dma_start(out=wc[:, ft, :], in_=I["conv_w"][l][:, ft * 128:(ft + 1) * 128].rearrange("j c -> c j"), allow_slow_non_contiguous=True), w=[wcr])
        for i, nm in enumerate(["conv_b", "conv_ln_g", "conv_ln_b"]):
            P.ld(lambda e, i=i, nm=nm: e.dma_start(out=pv[:, i, :], in_=I[nm][l].rearrange("(f p) -> p f", p=128), allow_slow_non_contiguous=True), w=[pvr])
        for ft in range(4):
            for j in range(31):
                P.dve(lambda e, ft=ft, j=j: e.tensor_scalar(out=Dj[:, ft, j, :], in0=C["identf"][:], scalar1=wc[:, ft, j:j + 1], scalar2=None, op0=ALU.mult), r=[wcr], w=[Djr])
        pss = Rot(sg.ps([128, 512], F32, 4))
        st1 = sg.ps([128, 512], F32)
        st2 = sg.ps([128, 512], F32)
        ys = Rot(sg.sb([128, 4, 512], F32, 2))
        sqs = Rot(sg.sb([128, 4, 512], F32, 1))
        wk = Rot(sg.sb([128, 512], F32, 2))
        stb = Rot(sg.sb([128, 512], F32, 3))
        ob = Rot(sg.sb([128, 512], BF16, 3))
        dr = P.R()

        def do_block(base, t0, n):
            y, yr = ys.next()
            sq, sqr = sqs.next()
            for ft in range(4):
                ps, psr = pss.next()
                for j in range(31):
                    P.pe(lambda e, ft=ft, j=j, ps=ps: e.matmul(ps[:, 0:n], lhsT=Dj[:, ft, j, :], rhs=U[:, ft, base + j - 15:base + j - 15 + n], start=(j == 0), stop=(j == 30)), r=[Djr, Ur], w=[psr])
                P.act(lambda e, ft=ft, ps=ps: e.activation(out=y[:, ft, 0:n], in_=ps[:, 0:n], func=AF.Identity, bias=pv[:, 0, ft:ft + 1]), r=[psr, pvr], w=[yr])
            P.act(lambda e: e.activation(out=sq[:, :, 0:n], in_=y[:, :, 0:n], func=AF.Square), r=[yr], w=[sqr])
            (s1, s1r), (s2, s2r) = st1, st2
            for ft in range(4):
                P.pe(lambda e, ft=ft: e.matmul(s1[:, 0:n], lhsT=C["onesf"][:], rhs=y[:, ft, 0:n], start=(ft == 0), stop=(ft == 3)), r=[yr], w=[s1r])
            for ft in range(4):
                P.pe(lambda e, ft=ft: e.matmul(s2[:, 0:n], lhsT=C["onesf"][:], rhs=sq[:, ft, 0:n], start=(ft == 0), stop=(ft == 3)), r=[sqr], w=[s2r])
            mu, mur = stb.next()
            P.dve(lambda e: e.tensor_scalar(out=mu[:, 0:n], in0=s1[:, 0:n], scalar1=1.0 / 512, scalar2=None, op0=ALU.mult), r=[s1r], w=[mur])
            m2, m2r = stb.next()
            P.dve(lambda e: e.tensor_tensor(out=m2[:, 0:n], in0=mu[:, 0:n], in1=mu[:, 0:n], op=ALU.mult), r=[mur], w=[m2r])
            va, var_ = stb.next()
            P.dve(lambda e: e.scalar_tensor_tensor(out=va[:, 0:n], in0=s2[:, 0:n], scalar=1.0 / 512, in1=m2[:, 0:n], op0=ALU.mult, op1=ALU.subtract), r=[s2r, m2r], w=[var_])
            P.dve(lambda e: e.tensor_scalar(out=va[:, 0:n], in0=va[:, 0:n], scalar1=EPS, scalar2=None, op0=ALU.add), r=[var_], w=[var_])
            P.act(lambda e: e.activation(out=va[:, 0:n], in_=va[:, 0:n], func=AF.Sqrt), r=[var_], w=[var_])
            P.dve(lambda e: e.reciprocal(out=va[:, 0:n], in_=va[:, 0:n]), r=[var_], w=[var_])
            for ft in range(4):
                t, tr = wk.next()
                P.dve(lambda e, ft=ft, t=t: e.tensor_tensor(out=t[:, 0:n], in0=y[:, ft, 0:n], in1=mu[:, 0:n], op=ALU.subtract), r=[yr, mur], w=[tr])
                P.dve(lambda e, ft=ft, t=t: e.scalar_tensor_tensor(out=t[:, 0:n], in0=t[:, 0:n], scalar=pv[:, 1, ft:ft + 1], in1=va[:, 0:n], op0=ALU.mult, op1=ALU.mult), r=[tr, var_, pvr], w=[tr])
                o, orr = ob.next()
                P.act(lambda e, ft=ft, t=t, o=o: e.activation(out=o[:, 0:n], in_=t[:, 0:n], func=AF.Silu, bias=pv[:, 2, ft:ft + 1]), r=[tr, pvr], w=[orr])
                P.stq(lambda e, ft=ft, o=o: e.dma_start(out=S["ymixT"][1536 + ft * 128:1536 + (ft + 1) * 128, t0:t0 + n], in_=o[:, 0:n]), r=[orr], w=[dr])
        do_block(15, 0, 256)
        for b in range(8):
            do_block(286 + 512 * b, 256 + 512 * b, 512)


NB = NCTX + NLAT + NCTX
NCH = NT // 16
TWO_PI = 2.0 * math.pi


def stage_s5(P, I, S, W, C, l):
    nc = P.nc
    I32 = mybir.dt.int32
    ys5 = S["ys5"]
    keep = contextlib.ExitStack()
    with keep:
        def ksb(shape, dt, stack=None):
            Stage.CNT += 1
            return (stack or keep).enter_context(nc.sbuf_tensor("k%d" % Stage.CNT, list(shape), dt)), P.R()
        Kmat, Kmr = ksb([128, 16, 6, 128], BF16)
        CAr_, CArr = ksb([128, 16, 16, 32], BF16)
        CAi_, CAir = ksb([128, 16, 16, 32], BF16)
        Hbr_, Hbrr = ksb([128, 16, NCH], BF16)
        Hbi_, Hbir = ksb([128, 16, NCH], BF16)
        Qre, Qrer = ksb([128, 9, 16], F32)
        Qim, Qimr = ksb([128, 9, 16], F32)

        def load_u(stack):
            U, Ur = ksb([128, 6, NB], BF16, stack)
            P.pool(lambda e: e.memset(U[:], 0.0), w=[Ur])
            for ft in range(6):
                nr = 96 if ft < 5 else 32
                P.stq(lambda e, ft=ft, nr=nr: e.dma_start(out=U[0:nr, ft, 0:NT], in_=S["uT"][96 * ft:96 * ft + nr, 0:NT]), w=[Ur])
                P.stq(lambda e, ft=ft, nr=nr: e.dma_start(out=U[0:nr, ft, NT:NB], in_=S["uT"][96 * ft:96 * ft + nr, 0:NCTX]), w=[Ur])
            return U, Ur
        for d in range(2):
            off = 0 if d == 0 else NCTX
            with contextlib.ExitStack() as st_abt:
                ABTr_, ABTrr = ksb([128, 16, 6, 128], BF16, st_abt)
                ABTi_, ABTir = ksb([128, 16, 6, 128], BF16, st_abt)
                s5_params(P, I, C, l, d, Kmat, Kmr, ABTr_, ABTrr, ABTi_, ABTir, CAr_, CArr, CAi_, CAir, Qre, Qrer, Qim, Qimr)
                with contextlib.ExitStack() as st_u:
                    U, Ur = load_u(st_u)
                    s5_states(P, C, d, off, U, Ur, ABTr_, ABTrr, ABTi_, ABTir, Qre, Qrer, Qim, Qimr, Hbr_, Hbrr, Hbi_, Hbir)
            with contextlib.ExitStack() as st_u:
                U, Ur = load_u(st_u)
                s5_outputs(P, C, d, off, U, Ur, Kmat, Kmr, CAr_, CArr, CAi_, CAir, Hbr_, Hbrr, Hbi_, Hbir, ys5[d])
    s5_epilogue(P, I, S, W, C, l)


def s5_params(P, I, C, l, d, Kmat, Kmr, ABTr_, ABTrr, ABTi_, ABTir, CAr_, CArr, CAi_, CAir, Qre, Qrer, Qim, Qimr):
    nc = P.nc
    I32 = mybir.dt.int32
    with Stage(P) as sg:
        def t16(n=1):
            return sg.sb([128, 16], F32) if n == 1 else sg.sb([128, n, 16], F32)
        lr, lrr = t16(); li, lir = t16(); st, str_ = t16()
        P.ld(lambda e: e.dma_start(out=lr[:], in_=I["s5_lam_re"][l, d].rearrange("(j g) p -> (g p) j", g=2), allow_slow_non_contiguous=True), w=[lrr])
        P.ld(lambda e: e.dma_start(out=li[:], in_=I["s5_lam_im"][l, d].rearrange("(j g) p -> (g p) j", g=2), allow_slow_non_contiguous=True), w=[lir])
        lsv = I["s5_log_step"][l, d].rearrange("(j g) -> g j", g=2)
        for g2 in range(2):
            P.ld(lambda e, g2=g2: e.dma_start(out=st[g2 * 64:(g2 + 1) * 64, :], in_=lsv[g2:g2 + 1, :].to_broadcast([64, 16]), allow_slow_non_contiguous=True), w=[str_])
        Br, Brr = sg.sb([128, 16, 16], F32); Bi, Bir = sg.sb([128, 16, 16], F32)
        P.ld(lambda e: e.dma_start(out=Br[:], in_=I["s5_b_re"][l, d].rearrange("(j g) p h -> (g p) j h", g=2)), w=[Brr])
        P.ld(lambda e: e.dma_start(out=Bi[:], in_=I["s5_b_im"][l, d].rearrange("(j g) p h -> (g p) j h", g=2)), w=[Bir])
        Cr, Crr = sg.sb([128, 16, 16], F32); Ci, Cir = sg.sb([128, 16, 16], F32)
        cps = sg.ps([128, 32, 16], F32)
        cns = Rot(sg.sb([16, 16, 64], F32, 2))
        for (src, dstt, dstr) in (("s5_c_re", Cr, Crr), ("s5_c_im", Ci, Cir)):
            cv = I[src][l, d].rearrange("(j g) h p -> g h j p", g=2)
            for g2 in range(2):
                cn, cnr = cns.next()
                P.ld(lambda e, cn=cn, cv=cv, g2=g2: e.dma_start(out=cn[:], in_=cv[g2]), w=[cnr])
                for j in range(16):
                    P.pe(lambda e, cn=cn, g2=g2, j=j: e.matmul(cps[0][64 * g2:64 * g2 + 64, j, :], lhsT=cn[:, j, :], rhs=C["identf"][0:16, 0:16], start=True, stop=True), r=[cnr], w=[cps[1]])
            P.act(lambda e, dstt=dstt: e.activation(out=dstt[:], in_=cps[0][:, 0:16, :], func=AF.Copy), r=[cps[1]], w=[dstr])
        V = P.dve
        P.act(lambda e: e.activation(out=st[:], in_=st[:], func=AF.Exp), r=[str_], w=[str_])
        mg, mgr = t16(); th, thr = t16()
        V(lambda e: e.tensor_tensor(out=mg[:], in0=lr[:], in1=st[:], op=ALU.mult), r=[lrr, str_], w=[mgr])
        P.act(lambda e: e.activation(out=mg[:], in_=mg[:], func=AF.Exp), r=[mgr], w=[mgr])
        V(lambda e: e.tensor_tensor(out=th[:], in0=li[:], in1=st[:], op=ALU.mult), r=[lir, str_], w=[thr])
        ki, kir = sg.sb([128, 16], I32); kf, kfr = t16(); msk, mskr = t16()

        def fold(x, xr):
            V(lambda e: e.tensor_scalar(out=msk[:], in0=x[:], scalar1=math.pi, scalar2=-TWO_PI, op0=ALU.is_gt, op1=ALU.mult), r=[xr], w=[mskr])
            V(lambda e: e.tensor_tensor(out=x[:], in0=x[:], in1=msk[:], op=ALU.add), r=[xr, mskr], w=[xr])
            V(lambda e: e.tensor_scalar(out=msk[:], in0=x[:], scalar1=-math.pi, scalar2=TWO_PI, op0=ALU.is_lt, op1=ALU.mult), r=[xr], w=[mskr])
            V(lambda e: e.tensor_tensor(out=x[:], in0=x[:], in1=msk[:], op=ALU.add), r=[xr, mskr], w=[xr])
        V(lambda e: e.tensor_scalar(out=kf[:], in0=th[:], scalar1=1.0 / TWO_PI, scalar2=None, op0=ALU.mult), r=[thr], w=[kfr])
        V(lambda e: e.tensor_copy(out=ki[:], in_=kf[:]), r=[kfr], w=[kir])
        V(lambda e: e.tensor_copy(out=kf[:], in_=ki[:]), r=[kir], w=[kfr])
        V(lambda e: e.scalar_tensor_tensor(out=th[:], in0=kf[:], scalar=-TWO_PI, in1=th[:], op0=ALU.mult, op1=ALU.add), r=[kfr, thr], w=[thr])
        fold(th, thr)
        sn, snr = t16(); cs, csr = t16()
        P.act(lambda e: e.activation(out=sn[:], in_=th[:], func=AF.Sin), r=[thr], w=[snr])
        V(lambda e: e.tensor_scalar(out=th[:], in0=th[:], scalar1=math.pi / 2, scalar2=None, op0=ALU.add), r=[thr, snr], w=[thr])
        fold(th, thr)
        P.act(lambda e: e.activation(out=cs[:], in_=th[:], func=AF.Sin), r=[thr], w=[csr])
        Apr, Aprr = t16(17); Api, Apir = t16(17)
        V(lambda e: e.memset(Apr[:, 0, :], 1.0), w=[Aprr])
        V(lambda e: e.memset(Api[:, 0, :], 0.0), w=[Apir])
        V(lambda e: e.tensor_tensor(out=Apr[:, 1, :], in0=mg[:], in1=cs[:], op=ALU.mult), r=[mgr, csr], w=[Aprr])
        V(lambda e: e.tensor_tensor(out=Api[:, 1, :], in0=mg[:], in1=sn[:], op=ALU.mult), r=[mgr, snr], w=[Apir])
        t1, t1r = t16(8); t2, t2r = t16(8)

        def cmul(outr, outi, ores, ar, ai, ares, br, bi, bres, shape):
            a = lambda t: t
            V(lambda e: e.tensor_tensor(out=shape(t1), in0=ar, in1=br, op=ALU.mult), r=ares + bres, w=[t1r])
            V(lambda e: e.tensor_tensor(out=shape(t2), in0=ai, in1=bi, op=ALU.mult), r=ares + bres, w=[t2r])
            V(lambda e: e.tensor_tensor(out=outr, in0=shape(t1), in1=shape(t2), op=ALU.subtract), r=[t1r, t2r], w=[ores[0]])
            V(lambda e: e.tensor_tensor(out=shape(t1), in0=ar, in1=bi, op=ALU.mult), r=ares + bres + [ores[0]], w=[t1r])
            V(lambda e: e.tensor_tensor(out=shape(t2), in0=ai, in1=br, op=ALU.mult), r=ares + bres, w=[t2r])
            V(lambda e: e.tensor_tensor(out=outi, in0=shape(t1), in1=shape(t2), op=ALU.add), r=[t1r, t2r], w=[ores[1]])
        m = 1
        while m < 16:
            bre = Apr[:, m:m + 1, :].to_broadcast([128, m, 16]); bim = Api[:, m:m + 1, :].to_broadcast([128, m, 16])
            cmul(Apr[:, m + 1:2 * m + 1, :], Api[:, m + 1:2 * m + 1, :], [Aprr, Apir], Apr[:, 1:m + 1, :], Api[:, 1:m + 1, :], [Aprr, Apir],
                 bre, bim, [Aprr, Apir], (lambda t, m=m: t[:, 0:m, :]))
            m *= 2
        V(lambda e: e.tensor_copy(out=Qre[:, 0, :], in_=Apr[:, 16, :]), r=[Aprr], w=[Qrer])
        V(lambda e: e.tensor_copy(out=Qim[:, 0, :], in_=Api[:, 16, :]), r=[Apir], w=[Qimr])
        for j in range(8):
            cmul(Qre[:, j + 1:j + 2, :], Qim[:, j + 1:j + 2, :], [Qrer, Qimr], Qre[:, j:j + 1, :], Qim[:, j:j + 1, :], [Qrer, Qimr],
                 Qre[:, j:j + 1, :], Qim[:, j:j + 1, :], [Qrer, Qimr], (lambda t: t[:, 0:1, :]))
        den, denr = t16(); am1, am1r = t16(); fr, frr = t16(); fi, fir = t16(); w1, w1r = t16(); w2, w2r = t16()
        V(lambda e: e.tensor_tensor(out=den[:], in0=lr[:], in1=lr[:], op=ALU.mult), r=[lrr], w=[denr])
        V(lambda e: e.tensor_tensor(out=w1[:], in0=li[:], in1=li[:], op=ALU.mult), r=[lir], w=[w1r])
        V(lambda e: e.tensor_tensor(out=den[:], in0=den[:], in1=w1[:], op=ALU.add), r=[denr, w1r], w=[denr])
        V(lambda e: e.reciprocal(out=den[:], in_=den[:]), r=[denr], w=[denr])
        V(lambda e: e.tensor_scalar(out=am1[:], in0=Apr[:, 1, :], scalar1=-1.0, scalar2=None, op0=ALU.add), r=[Aprr], w=[am1r])
        V(lambda e: e.tensor_tensor(out=w1[:], in0=am1[:], in1=lr[:], op=ALU.mult), r=[am1r, lrr, denr], w=[w1r])
        V(lambda e: e.tensor_tensor(out=w2[:], in0=Api[:, 1, :], in1=li[:], op=ALU.mult), r=[Apir, lir], w=[w2r])
        V(lambda e: e.tensor_tensor(out=fr[:], in0=w1[:], in1=w2[:], op=ALU.add), r=[w1r, w2r], w=[frr])
        V(lambda e: e.tensor_tensor(out=fr[:], in0=fr[:], in1=den[:], op=ALU.mult), r=[frr, denr], w=[frr])
        V(lambda e: e.tensor_tensor(out=w1[:], in0=Api[:, 1, :], in1=lr[:], op=ALU.mult), r=[Apir, lrr, frr], w=[w1r])
        V(lambda e: e.tensor_tensor(out=w2[:], in0=am1[:], in1=li[:], op=ALU.mult), r=[am1r, lir, frr], w=[w2r])
        V(lambda e: e.tensor_tensor(out=fi[:], in0=w1[:], in1=w2[:], op=ALU.subtract), r=[w1r, w2r], w=[fir])
        V(lambda e: e.tensor_tensor(out=fi[:], in0=fi[:], in1=den[:], op=ALU.mult), r=[fir, denr], w=[fir])
        Bbr, Bbrr = sg.sb([128, 16, 32], F32); Bbi, Bbir = sg.sb([128, 16, 32], F32)
        Cbr, Cbrr = sg.sb([128, 16, 32], F32); Cbi, Cbir = sg.sb([128, 16, 32], F32); Cbn, Cbnr = sg.sb([128, 16, 32], F32)
        x1, x1r = sg.sb([128, 16, 32], F32); x2, x2r = sg.sb([128, 16, 32], F32)
        for (t, tr) in ((Bbr, Bbrr), (Bbi, Bbir), (Cbr, Cbrr), (Cbi, Cbir)):
            P.pool(lambda e, t=t: e.memset(t[:], 0.0), w=[tr])
        for g2 in range(2):
            rows = slice(g2 * 64, (g2 + 1) * 64); cols = slice(g2 * 16, (g2 + 1) * 16)
            fb_r = lambda g2=g2: fr[g2 * 64:(g2 + 1) * 64, :].unsqueeze(2).to_broadcast([64, 16, 16])
            fb_i = lambda g2=g2: fi[g2 * 64:(g2 + 1) * 64, :].unsqueeze(2).to_broadcast([64, 16, 16])
            V(lambda e, rows=rows, cols=cols, fb_r=fb_r: e.tensor_tensor(out=x1[rows, :, 0:16], in0=Br[rows, :, :], in1=fb_r(), op=ALU.mult), r=[Brr, frr], w=[x1r])
            V(lambda e, rows=rows, cols=cols, fb_i=fb_i: e.tensor_tensor(out=x2[rows, :, 0:16], in0=Bi[rows, :, :], in1=fb_i(), op=ALU.mult), r=[Bir, fir], w=[x2r])
            V(lambda e, rows=rows, cols=cols: e.tensor_tensor(out=Bbr[rows, :, cols], in0=x1[rows, :, 0:16], in1=x2[rows, :, 0:16], op=ALU.subtract), r=[x1r, x2r], w=[Bbrr])
            V(lambda e, rows=rows, cols=cols, fb_r=fb_r: e.tensor_tensor(out=x1[rows, :, 0:16], in0=Bi[rows, :, :], in1=fb_r(), op=ALU.mult), r=[Bir, frr, Bbrr], w=[x1r])
            V(lambda e, rows=rows, cols=cols, fb_i=fb_i: e.tensor_tensor(out=x2[rows, :, 0:16], in0=Br[rows, :, :], in1=fb_i(), op=ALU.mult), r=[Brr, fir, Bbrr], w=[x2r])
            V(lambda e, rows=rows, cols=cols: e.tensor_tensor(out=Bbi[rows, :, cols], in0=x1[rows, :, 0:16], in1=x2[rows, :, 0:16], op=ALU.add), r=[x1r, x2r], w=[Bbir])
            P.pool(lambda e, rows=rows, cols=cols: e.tensor_copy(out=Cbr[rows, :, cols], in_=Cr[rows, :, :]), r=[Crr], w=[Cbrr])
            P.pool(lambda e, rows=rows, cols=cols: e.tensor_copy(out=Cbi[rows, :, cols], in_=Ci[rows, :, :]), r=[Cir], w=[Cbir])
        P.pool(lambda e: e.tensor_scalar(out=Cbn[:], in0=Cbi[:], scalar1=-1.0, scalar2=None, op0=ALU.mult), r=[Cbir], w=[Cbnr])
        P.pool(lambda e: e.memset(Kmat[:], 0.0), w=[Kmr])
        ABs = Rot([(sg.sb([128, 16, 32], F32), sg.sb([128, 16, 32], F32)) for _ in range(2)])
        kks = Rot(sg.ps([128, 16, 32], F32, 2))
        ttr = Rot(sg.ps([128, 8, 128], F32, 1))
        tti = Rot(sg.ps([128, 8, 128], F32, 1))

        def bc(t, e_):
            return t[:, e_, :].unsqueeze(2).to_broadcast([128, 16, 32])

        def cmul_bd(outr, outrr, outi, outir, e_, Xr, Xrr, Xi, Xir, sign_im=1.0):
            V(lambda e: e.tensor_tensor(out=x1[:], in0=Xr[:], in1=bc(Apr, e_), op=ALU.mult), r=[Xrr, Aprr], w=[x1r])
            V(lambda e: e.tensor_tensor(out=x2[:], in0=Xi[:], in1=bc(Api, e_), op=ALU.mult), r=[Xir, Apir], w=[x2r])
            V(lambda e: e.tensor_tensor(out=outr, in0=x1[:], in1=x2[:], op=ALU.subtract), r=[x1r, x2r], w=[outrr])
            V(lambda e: e.tensor_tensor(out=x1[:], in0=Xi[:], in1=bc(Apr, e_), op=ALU.mult), r=[Xir, Aprr, outrr], w=[x1r])
            V(lambda e: e.tensor_tensor(out=x2[:], in0=Xr[:], in1=bc(Api, e_), op=ALU.mult), r=[Xrr, Apir, outrr], w=[x2r])
            if sign_im > 0:
                V(lambda e: e.tensor_tensor(out=outi, in0=x1[:], in1=x2[:], op=ALU.add), r=[x1r, x2r], w=[outir])
            else:
                V(lambda e: e.scalar_tensor_tensor(out=outi, in0=x1[:], scalar=-1.0, in1=x2[:], op0=ALU.mult, op1=ALU.subtract), r=[x1r, x2r], w=[outir])

        def do_e(e_):
            (ABr, ABrr), (ABi, ABir) = ABs.next()
            cmul_bd(ABr[:], ABrr, ABi[:], ABir, e_, Bbr, Bbrr, Bbi, Bbir)
            kk, kkr = kks.next(); tr_, trr = ttr.next(); ti_, tir = tti.next()
            for j in range(16):
                ft, q = j // 3, j % 3
                P.pe(lambda e, j=j, ft=ft, q=q: e.matmul(kk[32 * q:32 * q + 32, ft, :], lhsT=ABr[:, j, :], rhs=Cbr[:, j, :], start=True, stop=False), r=[ABrr, Cbrr], w=[kkr])
                P.pe(lambda e, j=j, ft=ft, q=q: e.matmul(kk[32 * q:32 * q + 32, ft, :], lhsT=ABi[:, j, :], rhs=Cbn[:, j, :], start=False, stop=True), r=[ABir, Cbnr], w=[kkr])
                P.pe(lambda e, j=j, ft=ft, q=q: e.matmul(tr_[32 * q:32 * q + 32, ft, :], lhsT=ABr[:, j, :], rhs=C["identf"][:], start=True, stop=True), r=[ABrr], w=[trr])
                P.pe(lambda e, j=j, ft=ft, q=q: e.matmul(ti_[32 * q:32 * q + 32, ft, :], lhsT=ABi[:, j, :], rhs=C["identf"][:], start=True, stop=True), r=[ABir], w=[tir])
            for q in range(3):
                nf = 6 if q == 0 else 5
                P.act(lambda e, q=q, nf=nf: e.activation(out=Kmat[32 * q:32 * q + 32, e_, 0:nf, 32 * q:32 * q + 32], in_=kk[32 * q:32 * q + 32, 0:nf, :], func=AF.Copy), r=[kkr], w=[Kmr])
            P.act(lambda e: e.activation(out=ABTr_[0:96, e_, 0:5, :], in_=tr_[0:96, 0:5, :], func=AF.Copy), r=[trr], w=[ABTrr])
            P.act(lambda e: e.activation(out=ABTi_[0:96, e_, 0:5, :], in_=ti_[0:96, 0:5, :], func=AF.Copy), r=[tir], w=[ABTir])
            P.act(lambda e: e.activation(out=ABTr_[0:32, e_, 5, :], in_=tr_[0:32, 5, :], func=AF.Copy), r=[trr], w=[ABTrr])
            P.act(lambda e: e.activation(out=ABTi_[0:32, e_, 5, :], in_=ti_[0:32, 5, :], func=AF.Copy), r=[tir], w=[ABTir])
            cmul_bd(CAr_[:, e_, :, :], CArr, CAi_[:, e_, :, :], CAir, e_ + 1, Cbr, Cbrr, Cbi, Cbir, sign_im=-1.0)
        for e_ in range(16):
            do_e(e_)


def s5_states(P, C, d, off, U, Ur, ABTr_, ABTrr, ABTi_, ABTir, Qre, Qrer, Qim, Qimr, Hbr_, Hbrr, Hbi_, Hbir):
    nc = P.nc
    n = NCH
    with Stage(P) as sg:
        NP_ = 2
        bufs = [sg.sb([128, NP_, NCH], F32) for _ in range(4)]
        tA = [sg.sb([128, NP_, NCH], F32) for _ in range(2)]
        tB = [sg.sb([128, NP_, NCH], F32) for _ in range(2)]
        pss = Rot(sg.ps([128, 512], F32, 4))

        def do_half(hf):
            (Xr, Xrr), (Xi, Xir), (Yr, Yrr), (Yi, Yir) = bufs
            for jj in range(NP_):
                j = hf * NP_ + jj
                ft, q = j // 3, j % 3
                for (ABT, ABTres, X, Xres) in ((ABTr_, ABTrr, Xr, Xrr), (ABTi_, ABTir, Xi, Xir)):
                    ps, psr = pss.next()
                    for r in range(16):
                        e_ = (15 - r) if d == 0 else r
                        rhs = U[32 * q:32 * q + 32, ft, off:off + NT].rearrange("p (c r) -> p c r", r=16)[:, :, r]
                        P.pe(lambda e, ps=ps, ABT=ABT, e_=e_, rhs=rhs, r=r, q=q, ft=ft: e.matmul(ps[:, 0:NCH], lhsT=ABT[32 * q:32 * q + 32, e_, ft, :], rhs=rhs, start=(r == 0), stop=(r == 15)),
                             r=[ABTres, Ur], w=[psr])
                    P.act(lambda e, ps=ps, X=X, jj=jj: e.activation(out=X[:, jj, :], in_=ps[:, 0:NCH], func=AF.Copy), r=[psr], w=[Xres])
            cur = (bufs[0], bufs[1]); nxt = (bufs[2], bufs[3])
            for step in range(9):
                sh = 1 << step
                (Xr, Xrr), (Xi, Xir) = cur
                (Yr, Yrr), (Yi, Yir) = nxt
                if d == 0:
                    dst = slice(sh, n); src = slice(0, n - sh); keep_ = slice(0, sh)
                else:
                    dst = slice(0, n - sh); src = slice(sh, n); keep_ = slice(n - sh, n)
                w_ = n - sh
                qr = Qre[:, step, hf * NP_:(hf + 1) * NP_].unsqueeze(2).to_broadcast([128, NP_, w_])
                qi = Qim[:, step, hf * NP_:(hf + 1) * NP_].unsqueeze(2).to_broadcast([128, NP_, w_])
                (a1, a1r), (a2, a2r) = tA
                (b1, b1r), (b2, b2r) = tB
                V = P.dve; G = P.pool
                V(lambda e, Xr=Xr, qr=qr, src=src, w_=w_: e.tensor_tensor(out=a1[:, :, 0:w_], in0=Xr[:, :, src], in1=qr, op=ALU.mult), r=[Xrr, Qrer], w=[a1r])
                V(lambda e, Xi=Xi, qi=qi, src=src, w_=w_: e.tensor_tensor(out=a2[:, :, 0:w_], in0=Xi[:, :, src], in1=qi, op=ALU.mult), r=[Xir, Qimr], w=[a2r])
                V(lambda e, w_=w_: e.tensor_tensor(out=a1[:, :, 0:w_], in0=a1[:, :, 0:w_], in1=a2[:, :, 0:w_], op=ALU.subtract), r=[a1r, a2r], w=[a1r])
                V(lambda e, Xr=Xr, Yr=Yr, dst=dst, w_=w_: e.tensor_tensor(out=Yr[:, :, dst], in0=Xr[:, :, dst], in1=a1[:, :, 0:w_], op=ALU.add), r=[Xrr, a1r], w=[Yrr])
                V(lambda e, Xr=Xr, Yr=Yr, keep_=keep_: e.tensor_copy(out=Yr[:, :, keep_], in_=Xr[:, :, keep_]), r=[Xrr], w=[Yrr])
                G(lambda e, Xi=Xi, qr=qr, src=src, w_=w_: e.tensor_tensor(out=b1[:, :, 0:w_], in0=Xi[:, :, src], in1=qr, op=ALU.mult), r=[Xir, Qrer], w=[b1r])
                G(lambda e, Xr=Xr, qi=qi, src=src, w_=w_: e.tensor_tensor(out=b2[:, :, 0:w_], in0=Xr[:, :, src], in1=qi, op=ALU.mult), r=[Xrr, Qimr], w=[b2r])
                G(lambda e, w_=w_: e.tensor_tensor(out=b1[:, :, 0:w_], in0=b1[:, :, 0:w_], in1=b2[:, :, 0:w_], op=ALU.add), r=[b1r, b2r], w=[b1r])
                G(lambda e, Xi=Xi, Yi=Yi, dst=dst, w_=w_: e.tensor_tensor(out=Yi[:, :, dst], in0=Xi[:, :, dst], in1=b1[:, :, 0:w_], op=ALU.add), r=[Xir, b1r], w=[Yir])
                G(lambda e, Xi=Xi, Yi=Yi, keep_=keep_: e.tensor_copy(out=Yi[:, :, keep_], in_=Xi[:, :, keep_]), r=[Xir], w=[Yir])
                cur, nxt = nxt, cur
            (Xr, Xrr), (Xi, Xir) = cur
            P.act(lambda e, Xr=Xr: e.activation(out=Hbr_[:, hf * NP_:(hf + 1) * NP_, :], in_=Xr[:], func=AF.Copy), r=[Xrr], w=[Hbrr])
            P.act(lambda e, Xi=Xi: e.activation(out=Hbi_[:, hf * NP_:(hf + 1) * NP_, :], in_=Xi[:], func=AF.Copy), r=[Xir], w=[Hbir])
        for hf in range(16 // NP_):
            do_half(hf)


def s5_outputs(P, C, d, off, U, Ur, Kmat, Kmr, CAr_, CArr, CAi_, CAir, Hbr_, Hbrr, Hbi_, Hbir, ydst):
    nc = P.nc
    with Stage(P) as sg:
        pss = Rot(sg.ps([128, 512], F32, 4))
        ev = Rot(sg.sb([128, 512], F32, 3))
        dr = P.R()
        if d == 0:
            blocks = [(0, 256)] + [(256 + 512 * i, 512) for i in range(8)]
        else:
            blocks = [(256 + 512 * i, 512) for i in range(8)] + [(NT, 256)]

        def do_block(ft, b0, n):
            nb = n // 16
            c0 = (b0 - off) // 16
            ps, psr = pss.next()
            pv = ps[:, 0:n].rearrange("p (c r) -> p c r", r=16)
            uv = U[:, ft, b0:b0 + n].rearrange("p (c r) -> p c r", r=16)
            for k in range(16):
                if d == 0:
                    o_ap = pv[:, :, k:16]; r_ap = uv[:, :, 0:16 - k]
                else:
                    o_ap = pv[:, :, 0:16 - k]; r_ap = uv[:, :, k:16]
                P.pe(lambda e, k=k, o_ap=o_ap, r_ap=r_ap: e.matmul(o_ap, lhsT=Kmat[:, k, ft, :], rhs=r_ap, start=(k == 0), stop=False, skip_group_check=True),
                     r=[Kmr, Ur], w=[psr])
            npair = 3 if ft < 5 else 1
            for q in range(npair):
                j = ft * 3 + q
                for r in range(16):
                    if d == 0:
                        e_ = r
                        lo = 1 if c0 == 0 else 0
                        oc = slice(lo, nb); hc = slice(c0 + lo - 1, c0 + nb - 1)
                    else:
                        e_ = 15 - r
                        hi = nb - 1 if c0 + nb == NCH else nb
                        oc = slice(0, hi); hc = slice(c0 + 1, c0 + hi + 1)
                    last = (q == npair - 1 and r == 15)
                    P.pe(lambda e, j=j, q=q, r=r, e_=e_, oc=oc, hc=hc: e.matmul(pv[32 * q:32 * q + 32, oc, r], lhsT=CAr_[:, e_, j, :], rhs=Hbr_[:, j, hc], start=False, stop=False, skip_group_check=True),
                         r=[CArr, Hbrr], w=[psr])
                    P.pe(lambda e, j=j, q=q, r=r, e_=e_, oc=oc, hc=hc, last=last: e.matmul(pv[32 * q:32 * q + 32, oc, r], lhsT=CAi_[:, e_, j, :], rhs=Hbi_[:, j, hc], start=False, stop=last, skip_group_check=True),
                         r=[CAir, Hbir], w=[psr])
            o, orr = ev.next()
            P.act(lambda e: e.activation(out=o[:, 0:n], in_=ps[:, 0:n], func=AF.Copy), r=[psr], w=[orr])
            nr = 32 * npair
            P.stq(lambda e: e.dma_start(out=ydst[ft * 96:ft * 96 + nr, b0:b0 + n], in_=o[0:nr, 0:n]), r=[orr], w=[dr])
        for (b0, n) in blocks:
            for ft in range(6):
                do_block(ft, b0, n)


def s5_epilogue(P, I, S, W, C, l):
    nc = P.nc
    ys5 = S["ys5"]
    with Stage(P) as sg:
        wg, wgr = sg.sb([128, 4, 512], BF16)
        P.ld(lambda e: e.dma_start(out=wg[:], in_=W["w_glu"].rearrange("(k p) f -> p k f", p=128)), w=[wgr])
        dv, dvr = sg.sb([128, 4], F32)
        P.ld(lambda e: e.dma_start(out=dv[:], in_=I["s5_d"][l].rearrange("(f p) -> p f", p=128), allow_slow_non_contiguous=True), w=[dvr])
        yfs = Rot(sg.sb([128, 4, 512], F32, 2)); ybs = Rot(sg.sb([128, 4, 512], F32, 2)); us = Rot(sg.sb([128, 4, 512], F32, 2))
        gs = Rot(sg.sb([128, 4, 512], F32, 2)); gbs = Rot(sg.sb([128, 4, 512], BF16, 2))
        wk = Rot(sg.sb([128, 4, 512], F32, 2))
        sgs = Rot(sg.sb([128, 512], F32, 2)); ob = Rot(sg.sb([128, 512], BF16, 3))
        pss = Rot(sg.ps([128, 512], F32, 4))
        dr = P.R()
        yfv = ys5[0].rearrange("(f p) t -> p f t", p=128)
        ybv = ys5[1].rearrange("(f p) t -> p f t", p=128)
        uv = S["uT"].rearrange("(f p) t -> p f t", p=128)

        def do_chunk(t0, n):
            yf, yfr = yfs.next(); yb, ybr = ybs.next(); u, ur = us.next()
            tb = t0 if t0 >= NCTX else NT + t0
            P.ld(lambda e: e.dma_start(out=yf[:, :, 0:n], in_=yfv[:, :, t0:t0 + n]), w=[yfr])
            P.ld(lambda e: e.dma_start(out=yb[:, :, 0:n], in_=ybv[:, :, tb:tb + n]), w=[ybr])
            P.ld(lambda e: e.dma_start(out=u[:, :, 0:n], in_=uv[:, :, t0:t0 + n]), w=[ur])
            P.pool(lambda e: e.tensor_tensor(out=yf[:, :, 0:n], in0=yf[:, :, 0:n], in1=yb[:, :, 0:n], op=ALU.add), r=[yfr, ybr], w=[yfr])
            for ft in range(4):
                P.dve(lambda e, ft=ft: e.scalar_tensor_tensor(out=yf[:, ft, 0:n], in0=u[:, ft, 0:n], scalar=dv[:, ft:ft + 1], in1=yf[:, ft, 0:n], op0=ALU.mult, op1=ALU.add), r=[ur, dvr, yfr], w=[yfr])
            t, tr = wk.next()
            P.act(lambda e: e.activation(out=t[:, :, 0:n], in_=yf[:, :, 0:n], func=AF.Square), r=[yfr], w=[tr])
            P.dve(lambda e: e.tensor_scalar(out=t[:, :, 0:n], in0=t[:, :, 0:n], scalar1=0.044715 * 1.5957691216, scalar2=1.5957691216, op0=ALU.mult, op1=ALU.add), r=[tr], w=[tr])
            P.dve(lambda e: e.tensor_tensor(out=t[:, :, 0:n], in0=t[:, :, 0:n], in1=yf[:, :, 0:n], op=ALU.mult), r=[tr, yfr], w=[tr])
            P.act(lambda e: e.activation(out=t[:, :, 0:n], in_=t[:, :, 0:n], func=AF.Sigmoid), r=[tr], w=[tr])
            g, gr = gs.next(); gb, gbr = gbs.next()
            P.dve(lambda e: e.tensor_tensor(out=g[:, :, 0:n], in0=t[:, :, 0:n], in1=yf[:, :, 0:n], op=ALU.mult), r=[tr, yfr], w=[gr])
            P.pool(lambda e: e.tensor_copy(out=gb[:, :, 0:n], in_=g[:, :, 0:n]), r=[gr], w=[gbr])
            for fo in range(4):
                ps, psr = pss.next()
                for k in range(4):
                    P.pe(lambda e, k=k, fo=fo, ps=ps: e.matmul(ps[:, 0:n], lhsT=wg[:, k, fo * 128:(fo + 1) * 128], rhs=gb[:, k, 0:n], start=(k == 0), stop=(k == 3)), r=[wgr, gbr], w=[psr])
                sgm, sgmr = sgs.next()
                P.act(lambda e, ps=ps, sgm=sgm: e.activation(out=sgm[:, 0:n], in_=ps[:, 0:n], func=AF.Sigmoid), r=[psr], w=[sgmr])
                o, orr = ob.next()
                P.dve(lambda e, fo=fo, sgm=sgm, o=o: e.tensor_tensor(out=o[:, 0:n], in0=g[:, fo, 0:n], in1=sgm[:, 0:n], op=ALU.mult), r=[gr, sgmr], w=[orr])
                P.stq(lambda e, fo=fo, o=o: e.dma_start(out=S["ymixT"][512 + fo * 128:512 + (fo + 1) * 128, t0:t0 + n], in_=o[:, 0:n]), r=[orr], w=[dr])
        for (t0, n) in CHUNKS:
            do_chunk(t0, n)


def load_modvec_kind(P, tiles, S, idxs, kind):
    for (t, r), idx in zip(tiles, idxs):
        P.ld(lambda e, t=t, idx=idx: e.dma_start(out=t[:], in_=S["modv"][kind:kind + 1, idx, :].to_broadcast([128, D])), w=[r])


def stage_wout(P, I, S, W, C, l, xsrc, xres):
    nc = P.nc
    with Stage(P) as sg:
        wo, wor = sg.sb([128, 16, D], BF16)
        P.ld(lambda e: e.dma_start(out=wo[:], in_=W["w_out"].rearrange("(k p) f -> p k f", p=128)), w=[wor])
        mods = [sg.sb([128, D], F32) for _ in range(3)]
        yTs = Rot(sg.sb([128, 16, 512], BF16, 2))
        xts = Rot(sg.sb([128, D], F32, 2))
        hTs = Rot(sg.sb([128, 16, 512], BF16, 2))
        pss = Rot(sg.ps([128, 512], F32, 4))
        junk, junkr = sg.sb([128, 512], BF16)
        ss4 = Rot(sg.sb([128, 4], F32, 2))
        ss1 = Rot(sg.sb([128, 1], F32, 2))
        tmps = Rot(sg.sb([128, 512], F32, 2))
        bufs = dict(junk=sg.sb([128, D], BF16), ss=Rot(sg.sb([128, 1], F32, 2)), rs=Rot(sg.sb([128, 1], F32, 2)),
                    hb=Rot(sg.sb([128, D], BF16, 2)), tmp=Rot(sg.sb([128, D], F32, 1)), pt=Rot(sg.ps([128, 1024], BF16, 2)))
        dr = P.R()
        xr_dram = P.R()
        yv = S["ymixT"].rearrange("(k p) t -> p k t", p=128)
        hv = S["h2T"].rearrange("(k p) t -> p k t", p=128)

        def do_tile(yT, yTr, hT, hTr, i, a, kind):
            G = {kind: mods[1]}
            SH = {kind: mods[2]}
            xt, xtr = xts.next()
            P.ld(lambda e: e.dma_start(out=xt[:], in_=xsrc[a:a + 128, :]), r=[xr_dram], w=[xtr])
            s4, s4r = ss4.next()
            banks = []
            for c in range(4):
                ps, psr = pss.next()
                for k in range(16):
                    P.pe(lambda e, k=k, c=c, ps=ps: e.matmul(ps[:, :], lhsT=yT[:, k, i * 128:(i + 1) * 128], rhs=wo[:, k, c * 512:(c + 1) * 512], start=(k == 0), stop=(k == 15)), r=[yTr, wor], w=[psr])
                P.act(lambda e, c=c, ps=ps: e.activation(out=junk[:], in_=ps[:], func=AF.Square, accum_out=s4[:, c:c + 1]), r=[psr], w=[junkr, s4r])
                banks.append((ps, psr))
            s1, s1r = ss1.next()
            P.dve(lambda e: e.tensor_reduce(out=s1[:], in_=s4[:], axis=mybir.AxisListType.X, op=ALU.add), r=[s4r], w=[s1r])
            rs, rsr = rstd_from_ss(P, sg, s1, s1r, D, bufs["rs"])
            for c, (ps, psr) in enumerate(banks):
                t, tr = tmps.next()
                P.dve(lambda e, c=c, ps=ps, t=t: e.scalar_tensor_tensor(out=t[:], in0=ps[:], scalar=rs[:, 0:1], in1=mods[0][0][:, c * 512:(c + 1) * 512], op0=ALU.mult, op1=ALU.mult), r=[psr, rsr, mods[0][1]], w=[tr])
                P.pool(lambda e, c=c, t=t: e.tensor_tensor(out=xt[:, c * 512:(c + 1) * 512], in0=xt[:, c * 512:(c + 1) * 512], in1=t[:], op=ALU.add), r=[tr, xtr], w=[xtr])
            P.stq(lambda e: e.dma_start(out=xres[a:a + 128, :], in_=xt[:]), r=[xtr], w=[xr_dram])
            norm_mod_transpose(P, sg, C, xt, xtr, G, SH, kind, hT, hTr, i * 128, bufs)

        def do_chunk(t0, n, kind):
            yT, yTr = yTs.next()
            P.ld(lambda e: e.dma_start(out=yT[:, :, 0:n], in_=yv[:, :, t0:t0 + n]), w=[yTr])
            hT, hTr = hTs.next()
            for i in range(n // 128):
                do_tile(yT, yTr, hT, hTr, i, t0 + i * 128, kind)
            P.stq(lambda e: e.dma_start(out=hv[:, :, t0:t0 + n], in_=hT[:, :, 0:n]), r=[hTr], w=[dr])
        for ci, (t0, n) in enumerate(CHUNKS):
            kind = 1 if t0 < NCTX else 0
            if ci < 2:
                load_modvec_kind(P, mods, S, [2, 3, 4], kind)
            do_chunk(t0, n, kind)


def stage_ffn(P, I, S, W, C, l, xres):
    nc = P.nc
    with Stage(P) as sg:
        g4 = sg.sb([128, D], F32)
        h2, h2r = sg.sb([128, 16, 512], BF16)
        aT, aTr = sg.sb([128, 44, 512], BF16)
        wgs = Rot(sg.sb([128, 16, 128], BF16, 3))
        wus = Rot(sg.sb([128, 16, 128], BF16, 3))
        wos = Rot(sg.sb([128, 44, 256], BF16, 2))
        fx, fxr = sg.sb([128, 4, D], F32)
        xts = Rot(sg.sb([128, D], F32, 2))
        sil = Rot(sg.sb([128, 512], F32, 2))
        tmp, tmpr = sg.sb([128, D], F32)
        junk, junkr = sg.sb([128, D], BF16)
        ss1 = Rot(sg.sb([128, 1], F32, 2))
        rsb = Rot(sg.sb([128, 1], F32, 2))
        psg = Rot(sg.ps([128, 512], F32, 2))
        psu = Rot(sg.ps([128, 512], F32, 2))
        pso = Rot(sg.ps([128, 512], F32, 4))
        xr_dram = P.R()
        hv = S["h2T"].rearrange("(k p) t -> p k t", p=128)
        wiv = W["w_ffi"].rearrange("(k p) f -> p k f", p=128)
        wov = W["w_ffo"].rearrange("(k p) f -> p k f", p=128)

        def do_chunk(t0, n, kind):
            P.ld(lambda e: e.dma_start(out=h2[:, :, 0:n], in_=hv[:, :, t0:t0 + n]), w=[h2r])
            for f in range(44):
                wg, wgr = wgs.next()
                wu, wur = wus.next()
                P.ld(lambda e, f=f, wg=wg: e.dma_start(out=wg[:], in_=wiv[:, :, f * 128:(f + 1) * 128]), w=[wgr])
                P.ld(lambda e, f=f, wu=wu: e.dma_start(out=wu[:], in_=wiv[:, :, DFF + f * 128:DFF + (f + 1) * 128]), w=[wur])
                pg, pgr = psg.next()
                pu, pur = psu.next()
                for k in range(16):
                    P.pe(lambda e, k=k, wg=wg, pg=pg: e.matmul(pg[:, 0:n], lhsT=wg[:, k, :], rhs=h2[:, k, 0:n], start=(k == 0), stop=(k == 15)), r=[wgr, h2r], w=[pgr])
                for k in range(16):
                    P.pe(lambda e, k=k, wu=wu, pu=pu: e.matmul(pu[:, 0:n], lhsT=wu[:, k, :], rhs=h2[:, k, 0:n], start=(k == 0), stop=(k == 15)), r=[wur, h2r], w=[pur])
                sl, slr = sil.next()
                P.act(lambda e, sl=sl, pg=pg: e.activation(out=sl[:, 0:n], in_=pg[:, 0:n], func=AF.Silu), r=[pgr], w=[slr])
                P.dve(lambda e, f=f, sl=sl, pu=pu: e.tensor_tensor(out=aT[:, f, 0:n], in0=pu[:, 0:n], in1=sl[:, 0:n], op=ALU.mult), r=[pur, slr], w=[aTr])
            nt = n // 128
            for c in range(8):
                wo, wor = wos.next()
                P.ld(lambda e, c=c, wo=wo: e.dma_start(out=wo[:], in_=wov[:, :, c * 256:(c + 1) * 256]), w=[wor])
                for i in range(nt):
                    po, por = pso.next()
                    for f in range(44):
                        P.pe(lambda e, f=f, i=i, wo=wo, po=po: e.matmul(po[:, 0:256], lhsT=aT[:, f, i * 128:(i + 1) * 128], rhs=wo[:, f, :], start=(f == 0), stop=(f == 43)), r=[aTr, wor], w=[por])
                    P.act(lambda e, c=c, i=i, po=po: e.activation(out=fx[:, i, c * 256:(c + 1) * 256], in_=po[:, 0:256], func=AF.Copy), r=[por], w=[fxr])
            for i in range(nt):
                a = t0 + i * 128
                xt, xtr = xts.next()
                P.ld(lambda e, xt=xt, a=a: e.dma_start(out=xt[:], in_=xres[a:a + 128, :]), r=[xr_dram], w=[xtr])
                s1, s1r = ss1.next()
                P.act(lambda e, i=i, s1=s1: e.activation(out=junk[:], in_=fx[:, i, :], func=AF.Square, accum_out=s1[:]), r=[fxr], w=[junkr, s1r])
                rs, rsr = rstd_from_ss(P, sg, s1, s1r, D, rsb)
                P.dve(lambda e, i=i, rs=rs: e.scalar_tensor_tensor(out=tmp[:], in0=fx[:, i, :], scalar=rs[:, 0:1], in1=g4[0][:], op0=ALU.mult, op1=ALU.mult), r=[fxr, rsr, g4[1]], w=[tmpr])
                P.pool(lambda e, xt=xt: e.tensor_tensor(out=xt[:], in0=xt[:], in1=tmp[:], op=ALU.add), r=[tmpr, xtr], w=[xtr])
                P.stq(lambda e, xt=xt, a=a: e.dma_start(out=xres[a:a + 128, :], in_=xt[:]), r=[xtr], w=[xr_dram])
        for ci, (t0, n) in enumerate(CHUNKS):
            kind = 1 if t0 < NCTX else 0
            if ci < 2:
                load_modvec_kind(P, [g4], S, [5], kind)
            do_chunk(t0, n, kind)


def host_consts():
    n = NLAT
    row = np.repeat(np.arange(n // 64, dtype=np.float32), 64)
    col = np.tile(np.arange(64, dtype=np.float32), n // 64)
    inv = (10000.0 ** (-np.arange(16, dtype=np.float32) / 16)).astype(np.float32)
    ang = np.concatenate([row[:, None] * inv, col[:, None] * inv], -1)
    cos = np.cos(ang).astype(np.float32)
    sin = np.sin(ang).astype(np.float32)
    Ct = np.ones((128, NT), np.float32)
    St = np.zeros((128, NT), np.float32)
    for r in range(128):
        dd = r % 64
        i = dd % 32
        Ct[r, NCTX:] = cos[:, i]
        St[r, NCTX:] = (-sin[:, i]) if dd < 32 else sin[:, i]
    perm = np.zeros((128, 128), np.float32)
    for m in range(128):
        dd = m % 64
        partner = m + 32 if dd < 32 else m - 32
        perm[partner, m] = 1.0
    ident = np.eye(128, dtype=np.float32)
    return dict(ropeC=Ct, ropeS=St, perm=perm, ident=ident)


def make_in_maps(inputs, ncores=8):
    hc = host_consts()
    maps = []
    for c in range(ncores):
        b = c % 4
        m = dict(hc)
        m["xc"] = np.ascontiguousarray(np.concatenate([inputs["ctx"][b], inputs["x"][b]], 0))
        m["cvec"] = np.ascontiguousarray(np.stack([inputs["c"][b], inputs["c_ctx"]], 0))
        for k, v in inputs.items():
            if k in ("x", "c", "ctx", "c_ctx"):
                continue
            m[k] = np.ascontiguousarray(v)
        maps.append(m)
    return maps


def kernel(**inputs):
    inputs = {k: np.asarray(v) for k, v in inputs.items()}
    nc = build()
    maps = make_in_maps(inputs, 8)
    res = run_bass_kernel_spmd(nc, maps, core_ids=list(range(8)))
    out = np.stack([res.results[b]["xres"][NCTX:] for b in range(4)], 0)
    return out.astype(np.float32)
```

```python
import os
import contextlib
import math
import numpy as np
import concourse.bass as bass
import concourse.mybir as mybir
from concourse.bass_utils import run_bass_kernel_spmd

F32 = mybir.dt.float32
BF16 = mybir.dt.bfloat16
ALU = mybir.AluOpType
AF = mybir.ActivationFunctionType

D = 2048
NCTX = 256
NLAT = 4096
NT = NCTX + NLAT
DEPTH = 4
DFF = 5632
INW = 3904
CHUNKS = [(0, 256)] + [(256 + 512 * i, 512) for i in range(8)]
EPS = 1e-6


class Res:
    __slots__ = ("lw", "rd")

    def __init__(self):
        self.lw = None
        self.rd = []


class Op:
    __slots__ = ("eng", "fn", "deps", "sig", "key", "inc", "val")


class Prog:
    NQ = 12
    RELAX = True

    def __init__(self, nc, st):
        self.nc = nc
        self.sem = {}
        for k in ["pe", "act", "dve", "pool"]:
            self.sem[k] = st.enter_context(nc.semaphore("s_" + k))
        for q in ["sp", "pool"]:
            for s in range(self.NQ):
                k = "d_%s_%d" % (q, s)
                self.sem[k] = st.enter_context(nc.semaphore("s_" + k))
        self.semval = {k: 0 for k in self.sem}
        self.dqn = {"sp": 0, "pool": 0}
        self.slot_last = {}
        self.reset_stage()
        self.res = []

    def R(self):
        r = Res()
        self.res.append(r)
        return r

    def reset_stage(self):
        self.ops = {e: [] for e in ("pe", "act", "dve", "pool", "sp")}
        self.order = []

    def _mk(self, eng, fn, reads, writes, key, inc, acc=False):
        op = Op()
        op.eng = eng; op.fn = fn; op.sig = False; op.key = key; op.inc = inc; op.val = None
        deps = []
        for r in reads:
            if r.lw is not None:
                deps.append(r.lw)
        for r in writes:
            if r.lw is not None:
                deps.append(r.lw)
            deps.extend(r.rd)
        if acc:
            deps = [d for d in deps if not (d.eng == eng and d.key == key)]
        if eng in ("dve", "act") and self.RELAX:
            prev = self.ops[eng][-1] if self.ops[eng] else None
            deps = [d for d in deps if not (d.eng == eng and d.key == key and d is not prev)]
        op.deps = deps
        for r in reads:
            r.rd.append(op)
        for r in writes:
            r.lw = op
            r.rd = []
        self.ops[eng].append(op)
        self.order.append(op)
        return op

    def pe(self, fn, r=(), w=(), acc=True):
        return self._mk("pe", fn, r, w, "pe", 1, acc)

    def act(self, fn, r=(), w=()):
        return self._mk("act", fn, r, w, "act", 1)

    def dve(self, fn, r=(), w=()):
        return self._mk("dve", fn, r, w, "dve", 1)

    def pool(self, fn, r=(), w=()):
        return self._mk("pool", fn, r, w, "pool", 1)

    def dma(self, q, fn, r=(), w=()):
        i = self.dqn[q]
        self.dqn[q] += 1
        key = "d_%s_%d" % (q, i % self.NQ)
        op = self._mk(q, fn, r, w, key, 16)
        prev = self.slot_last.get(key)
        if prev is not None:
            op.deps.append(prev)
        self.slot_last[key] = op
        op.sig = True
        return op

    def ld(self, fn, r=(), w=()):
        return self.dma("sp", fn, r, w)

    def stq(self, fn, r=(), w=()):
        return self.dma("pool", fn, r, w)

    def end_stage(self):
        lasts = []
        for e in ("pe", "act", "dve", "pool"):
            c = [o for o in self.ops[e] if o.key == e]
            if c:
                lasts.append(c[-1])
        for k, o in self.slot_last.items():
            if o is not None:
                lasts.append(o)
        for e in ("pe", "act", "dve", "pool", "sp"):
            b = Op()
            b.eng = e; b.fn = None; b.deps = list(lasts); b.sig = False; b.key = None; b.inc = 0; b.val = None
            self.ops[e].append(b)
            self.order.append(b)
        staged = set(id(o) for o in self.order)
        for o in self.order:
            o.deps = [d for d in o.deps if id(d) in staged]
            for d in o.deps:
                d.sig = True
        for e in ("pe", "act", "dve", "pool", "sp"):
            for o in self.ops[e]:
                if o.fn is not None and o.sig:
                    self.semval[o.key] += o.inc
                    o.val = self.semval[o.key]
        nc = self.nc
        sem = self.sem

        def run(engine, lst):
            known = {}
            for o in lst:
                need = {}
                for d in o.deps:
                    if need.get(d.key, 0) < d.val:
                        need[d.key] = d.val
                for k, v in need.items():
                    if known.get(k, 0) >= v:
                        continue
                    engine.wait_ge(sem[k], v)
                    known[k] = v
                if o.fn is None:
                    continue
                ins = o.fn(engine)
                if o.sig:
                    ins.then_inc(sem[o.key], o.inc)

        with nc.Block() as block:
            @block.tensor
            def _(e):
                run(e, self.ops["pe"])

            @block.scalar
            def _(e):
                run(e, self.ops["act"])

            @block.vector
            def _(e):
                run(e, self.ops["dve"])

            @block.gpsimd
            def _(e):
                run(e, self.ops["pool"])

            @block.sync
            def _(e):
                run(e, self.ops["sp"])
        for r in self.res:
            r.lw = None
            r.rd = []
        self.res = []
        self.slot_last = {}
        self.reset_stage()


class Stage:
    CNT = 0

    def __init__(self, P):
        self.P = P
        self.nc = P.nc
        self.st = contextlib.ExitStack()
        self.n = 0

    def __enter__(self):
        self.st.__enter__()
        return self

    def __exit__(self, *a):
        self.P.end_stage()
        return self.st.__exit__(*a)

    def sb(self, shape, dt, nbuf=1):
        out = []
        for i in range(nbuf):
            Stage.CNT += 1
            t = self.st.enter_context(self.nc.sbuf_tensor("t%d" % Stage.CNT, list(shape), dt))
            out.append((t, self.P.R()))
        return out if nbuf > 1 else out[0]

    def ps(self, shape, dt, nbuf=1):
        out = []
        for i in range(nbuf):
            Stage.CNT += 1
            t = self.st.enter_context(self.nc.psum_tensor("p%d" % Stage.CNT, list(shape), dt))
            out.append((t, self.P.R()))
        return out if nbuf > 1 else out[0]


class Rot:
    def __init__(self, items):
        self.items = [items] if isinstance(items, tuple) else items
        self.i = 0

    def next(self):
        x = self.items[self.i % len(self.items)]
        self.i += 1
        return x


def build(nlayers=DEPTH, stop_after=None, taps=False):
    nc = bass.Bass("TRN2", target_bir_lowering=False)
    kin = "ExternalInput"
    dbg = "ExternalOutput" if taps else "Internal"

    def din(name, shape, dt=F32):
        return nc.dram_tensor(name, list(shape), dt, kind=kin).ap()

    def dscr(name, shape, dt=F32):
        return nc.dram_tensor(name, list(shape), dt, kind=dbg).ap()

    I = {}
    I["xc"] = din("xc", [NT, D])
    I["cvec"] = din("cvec", [2, D])
    I["ropeC"] = din("ropeC", [128, NT])
    I["ropeS"] = din("ropeS", [128, NT])
    I["perm"] = din("perm", [128, 128])
    I["ident"] = din("ident", [128, 128])
    shapes = dict(
        w_ada=[DEPTH, D, 6 * D], b_ada=[DEPTH, 6 * D], norm_pre_mix=[DEPTH, D], norm_post_mix=[DEPTH, D],
        norm_pre_ffn=[DEPTH, D], norm_post_ffn=[DEPTH, D], w_in=[DEPTH, D, INW], w_out=[DEPTH, D, D],
        da_lam_q1=[DEPTH, 64], da_lam_k1=[DEPTH, 64], da_lam_q2=[DEPTH, 64], da_lam_k2=[DEPTH, 64], da_subln=[DEPTH, 128],
        s5_lam_re=[DEPTH, 2, 32, 64], s5_lam_im=[DEPTH, 2, 32, 64], s5_log_step=[DEPTH, 2, 32],
        s5_b_re=[DEPTH, 2, 32, 64, 16], s5_b_im=[DEPTH, 2, 32, 64, 16], s5_c_re=[DEPTH, 2, 32, 16, 64],
        s5_c_im=[DEPTH, 2, 32, 16, 64], s5_d=[DEPTH, 512], s5_w_glu=[DEPTH, 512, 512], mla_q_norm=[DEPTH, 512],
        mla_kv_norm=[DEPTH, 256], mla_w_uq=[DEPTH, 512, 768], mla_w_ukv=[DEPTH, 256, 1024], conv_w=[DEPTH, 31, 512],
        conv_b=[DEPTH, 512], conv_ln_g=[DEPTH, 512], conv_ln_b=[DEPTH, 512], w_ffn_in=[DEPTH, D, 2 * DFF],
        w_ffn_out=[DEPTH, DFF, D])
    for k, s in shapes.items():
        I[k] = din(k, s)

    xres = nc.dram_tensor("xres", [NT, D], F32, kind="ExternalOutput").ap()
    S = {}
    S["modv"] = dscr("modv", [2, 6, D])
    S["hxT"] = dscr("hxT", [D, NT], BF16)
    S["qT"] = dscr("qT", [512, NT], BF16)
    S["kT"] = dscr("kT", [512, NT], BF16)
    S["vda"] = dscr("vda", [NT, 512], BF16)
    S["uT"] = dscr("uT", [512, NT])
    S["mlaT"] = dscr("mlaT", [832, NT])
    S["cuT"] = dscr("cuT", [512, NT], BF16)
    S["ymixT"] = dscr("ymixT", [D, NT], BF16)
    S["mqT"] = dscr("mqT", [4, 192, NT], BF16)
    S["mkT"] = dscr("mkT", [4, 128, NT], BF16)
    S["mkrT"] = dscr("mkrT", [64, NT], BF16)
    S["mv"] = dscr("mv", [NT, 512], BF16)
    S["h2T"] = dscr("h2T", [D, NT], BF16)
    ys5a = dscr("ys5a", [512, NCTX + NT])
    ys5b = dscr("ys5b", [512, NCTX + NT])
    S["ys5"] = [ys5a, ys5b]
    WW = []
    for i in range(2):
        Wd = {}
        Wd["w_in"] = nc.dram_tensor("wb_in%d" % i, [D, INW], BF16, kind="Internal").ap()
        Wd["w_out"] = nc.dram_tensor("wb_out%d" % i, [D, D], BF16, kind="Internal").ap()
        Wd["w_ffi"] = nc.dram_tensor("wb_ffi%d" % i, [D, 2 * DFF], BF16, kind="Internal").ap()
        Wd["w_ffo"] = nc.dram_tensor("wb_ffo%d" % i, [DFF, D], BF16, kind="Internal").ap()
        Wd["w_uq"] = nc.dram_tensor("wb_uq%d" % i, [512, 768], BF16, kind="Internal").ap()
        Wd["w_ukv"] = nc.dram_tensor("wb_ukv%d" % i, [256, 1024], BF16, kind="Internal").ap()
        Wd["w_glu"] = nc.dram_tensor("wb_glu%d" % i, [512, 512], BF16, kind="Internal").ap()
        WW.append(Wd)

    top = contextlib.ExitStack()
    with top:
        P = Prog(nc, top)
        def gsb(name, shape, dt):
            return top.enter_context(nc.sbuf_tensor(name, list(shape), dt))
        identb = gsb("identb", [128, 128], BF16)
        identf = gsb("identf", [128, 128], F32)
        permf = gsb("permf", [128, 128], F32)
        onesf = gsb("onesf", [128, 128], F32)
        onesb = gsb("onesb", [128, 128], BF16)
        scT = gsb("scT", [128, 2, 16], F32)
        with Stage(P) as sg:
            r = P.R()
            P.stq(lambda e: e.dma_start(out=identb[:], in_=I["ident"]), w=[r])
            P.ld(lambda e: e.dma_start(out=identf[:], in_=I["ident"]), w=[r])
            P.ld(lambda e: e.dma_start(out=permf[:], in_=I["perm"]), w=[r])
            P.dve(lambda e: e.memset(onesf[:], 1.0), w=[r])
            P.dve(lambda e: e.memset(onesb[:], 1.0), w=[r])
            ct, cr = sg.sb([128, 2, 16], F32)
            for rr in range(2):
                P.ld(lambda e, rr=rr: e.dma_start(out=ct[:, rr, :], in_=I["cvec"][rr].rearrange("(k p) -> p k", p=128), allow_slow_non_contiguous=True), w=[cr])
            P.act(lambda e: e.activation(out=scT[:], in_=ct[:], func=AF.Silu), r=[cr], w=[r])

        C = dict(identb=identb, identf=identf, permf=permf, onesf=onesf, onesb=onesb, scT=scT)
        for l in range(nlayers):
            last = (l == DEPTH - 1)
            xsrc = I["xc"] if l == 0 else xres
            W = WW[l % 2]
            if l == 0:
                with Stage(P) as sg0:
                    emit_cast(P, I, W, 0)
            stage_ada(P, I, S, C, l)
            if stop_after == "ada":
                break
            stage_prenorm(P, S, C, xsrc, S["hxT"], 0)
            if stop_after == "prenorm":
                break
            stage_win(P, I, S, W, C)
            if stop_after == "win":
                break
            stage_da(P, I, S, C, l, (lambda l=l: emit_cast(P, I, WW[(l + 1) % 2], l + 1)) if l + 1 < nlayers else None)
            if stop_after == "da":
                break
            stage_mla(P, I, S, W, C, l)
            if stop_after == "mla":
                break
            stage_conv(P, I, S, C, l)
            if stop_after == "conv":
                break
            if os.environ.get("SKIP_S5") != "1":
                stage_s5(P, I, S, W, C, l)
            if stop_after == "s5":
                break
            stage_wout(P, I, S, W, C, l, xsrc, xres)
            if stop_after == "wout":
                break
            stage_ffn(P, I, S, W, C, l, xres)
    return nc


def emit_cast(P, I, W, l):
    r = P.R()
    def cp(dst, src, rows, step):
        for i in range(0, rows, step):
            P.stq(lambda e, i=i: e.dma_start(out=dst[i:i + step, :], in_=src[i:i + step, :]), w=[r])
    cp(W["w_in"], I["w_in"][l], D, D)
    cp(W["w_out"], I["w_out"][l], D, D)
    cp(W["w_ffi"], I["w_ffn_in"][l], D, 1024)
    cp(W["w_ffo"], I["w_ffn_out"][l], DFF, 2816)
    cp(W["w_uq"], I["mla_w_uq"][l], 512, 512)
    cp(W["w_ukv"], I["mla_w_ukv"][l], 256, 256)
    cp(W["w_glu"], I["s5_w_glu"][l], 512, 512)


def stage_ada(P, I, S, C, l):
    nc = P.nc
    with Stage(P) as sg:
        wt = sg.sb([128, 16, 512], F32, 2)
        wrot = Rot(wt)
        bia, biar = sg.sb([2, 6 * D], F32)
        mod, modr = bia, biar
        gv, gvr = sg.sb([2, 4, D], F32)
        outv, outr = sg.sb([2, 6, D], F32)
        pss = Rot(sg.ps([2, 512], F32, 2))
        P.ld(lambda e: e.dma_start(out=bia[:], in_=I["b_ada"][l:l + 1, :].to_broadcast([2, 6 * D])), w=[biar])
        for i, nm in enumerate(["norm_pre_mix", "norm_post_mix", "norm_pre_ffn", "norm_post_ffn"]):
            P.ld(lambda e, i=i, nm=nm: e.dma_start(out=gv[:, i, :], in_=I[nm][l:l + 1, :].to_broadcast([2, D])), w=[gvr])
        wv = I["w_ada"][l].rearrange("(k p) c -> p k c", p=128)
        for j in range(24):
            (w, wr) = wrot.next()
            P.ld(lambda e, w=w, j=j: e.dma_start(out=w[:], in_=wv[:, :, j * 512:(j + 1) * 512]), w=[wr])
            (ps, psr) = pss.next()
            for k in range(16):
                P.pe(lambda e, w=w, k=k, ps=ps: e.matmul(ps[:], lhsT=C["scT"][:, :, k], rhs=w[:, k, :], start=(k == 0), stop=(k == 15)),
                     r=[wr], w=[psr])
            P.dve(lambda e, ps=ps, j=j: e.tensor_tensor(out=mod[:, j * 512:(j + 1) * 512], in0=ps[:], in1=bia[:, j * 512:(j + 1) * 512], op=ALU.add),
                  r=[psr, biar], w=[modr])
        m = lambda i: mod[:, i * D:(i + 1) * D]
        P.dve(lambda e: e.scalar_tensor_tensor(out=outv[:, 0, :], in0=m(1), scalar=1.0, in1=gv[:, 0, :], op0=ALU.add, op1=ALU.mult), r=[modr, gvr], w=[outr])
        P.dve(lambda e: e.tensor_copy(out=outv[:, 1, :], in_=m(0)), r=[modr], w=[outr])
        P.dve(lambda e: e.tensor_tensor(out=outv[:, 2, :], in0=m(2), in1=gv[:, 1, :], op=ALU.mult), r=[modr, gvr], w=[outr])
        P.dve(lambda e: e.scalar_tensor_tensor(out=outv[:, 3, :], in0=m(4), scalar=1.0, in1=gv[:, 2, :], op0=ALU.add, op1=ALU.mult), r=[modr, gvr], w=[outr])
        P.dve(lambda e: e.tensor_copy(out=outv[:, 4, :], in_=m(3)), r=[modr], w=[outr])
        P.dve(lambda e: e.tensor_tensor(out=outv[:, 5, :], in0=m(5), in1=gv[:, 3, :], op=ALU.mult), r=[modr, gvr], w=[outr])
        dr = P.R()
        P.stq(lambda e: e.dma_start(out=S["modv"], in_=outv[:]), r=[outr], w=[dr])


def load_modvec(P, sg, S, idx):
    out = {}
    for kind in range(2):
        t, r = sg.sb([128, D], F32)
        P.ld(lambda e, t=t, kind=kind: e.dma_start(out=t[:], in_=S["modv"][kind:kind + 1, idx, :].to_broadcast([128, D])), w=[r])
        out[kind] = (t, r)
    return out


def rstd_from_ss(P, sg, ss, ssr, n_feat, rot=None):
    t, tr = rot.next() if rot is not None else sg.sb([128, 1], F32)
    P.dve(lambda e: e.tensor_scalar(out=t[:], in0=ss[:], scalar1=1.0 / n_feat, scalar2=EPS, op0=ALU.mult, op1=ALU.add), r=[ssr], w=[tr])
    P.act(lambda e: e.activation(out=t[:], in_=t[:], func=AF.Sqrt), r=[tr], w=[tr])
    P.dve(lambda e: e.reciprocal(out=t[:], in_=t[:]), r=[tr], w=[tr])
    return t, tr


def norm_mod_transpose(P, sg, C, xt, xtr, G, SH, kind, hT, hTr, col0, bufs):
    junk, junkr = bufs["junk"]
    ss, ssr = bufs["ss"].next()
    hb, hbr = bufs["hb"].next()
    P.act(lambda e: e.activation(out=junk[:], in_=xt[:], func=AF.Square, accum_out=ss[:]), r=[xtr], w=[junkr, ssr])
    rs, rsr = rstd_from_ss(P, sg, ss, ssr, D, bufs["rs"])
    tmp, tmpr = bufs["tmp"].next()
    P.dve(lambda e: e.scalar_tensor_tensor(out=tmp[:], in0=xt[:], scalar=rs[:, 0:1], in1=G[kind][0][:], op0=ALU.mult, op1=ALU.mult),
          r=[xtr, rsr, G[kind][1]], w=[tmpr])
    P.pool(lambda e: e.tensor_tensor(out=hb[:], in0=tmp[:], in1=SH[kind][0][:], op=ALU.add), r=[tmpr, SH[kind][1]], w=[hbr])
    for half in range(2):
        pt, ptr = bufs["pt"].next()
        for k in range(8):
            kk = half * 8 + k
            P.pe(lambda e, pt=pt, k=k, kk=kk: e.transpose(pt[:, k * 128:(k + 1) * 128], hb[:, kk * 128:(kk + 1) * 128], C["identb"][:]),
                 r=[hbr], w=[ptr])
        P.act(lambda e, pt=pt, half=half: e.activation(out=hT[:, half * 8:(half + 1) * 8, col0:col0 + 128],
                                                       in_=pt[:].rearrange("p (k t) -> p k t", k=8), func=AF.Copy),
              r=[ptr], w=[hTr])


def norm_bufs(sg):
    return dict(junk=sg.sb([128, D], BF16), ss=Rot(sg.sb([128, 1], F32, 2)), rs=Rot(sg.sb([128, 1], F32, 2)),
                hb=Rot(sg.sb([128, D], BF16, 2)), tmp=Rot(sg.sb([128, D], F32, 2)), pt=Rot(sg.ps([128, 1024], BF16, 2)))


def stage_prenorm(P, S, C, xsrc, dstT, midx):
    nc = P.nc
    with Stage(P) as sg:
        G = load_modvec(P, sg, S, midx)
        SH = load_modvec(P, sg, S, midx + 1)
        xts = Rot(sg.sb([128, D], F32, 2))
        hTs = Rot(sg.sb([128, 16, 512], BF16, 2))
        bufs = norm_bufs(sg)
        dr = P.R()
        dv = dstT.rearrange("(k p) t -> p k t", p=128)
        for (t0, n) in CHUNKS:
            hT, hTr = hTs.next()
            for i in range(n // 128):
                xt, xtr = xts.next()
                P.ld(lambda e, xt=xt, a=t0 + i * 128: e.dma_start(out=xt[:], in_=xsrc[a:a + 128, :]), w=[xtr])
                norm_mod_transpose(P, sg, C, xt, xtr, G, SH, 0 if t0 >= NCTX else 1, hT, hTr, i * 128, bufs)
            P.stq(lambda e, hT=hT, t0=t0, n=n: e.dma_start(out=dv[:, :, t0:t0 + n], in_=hT[:, :, 0:n]), r=[hTr], w=[dr])


def rope_tile(P, sg, C, ps, psr, rows, t0, n, tabs, bufs, dst_dram, dstres, scale_ap=None):
    qf, qfr = bufs["qf"].next()
    if scale_ap is None:
        P.act(lambda e: e.activation(out=qf[0:rows, 0:n], in_=ps[0:rows, 0:n], func=AF.Copy), r=[psr], w=[qfr])
    else:
        P.dve(lambda e: e.tensor_tensor(out=qf[0:rows, 0:n], in0=ps[0:rows, 0:n], in1=scale_ap[0][0:rows, 0:n], op=ALU.mult), r=[psr, scale_ap[1]], w=[qfr])
    p2, p2r = bufs["p2"].next()
    P.pe(lambda e: e.matmul(p2[0:rows, 0:n], lhsT=C["permf"][0:rows, 0:rows], rhs=qf[0:rows, 0:n], start=True, stop=True), r=[qfr], w=[p2r])
    t1, t1r = bufs["t1"].next()
    (tc, tcr), (ts, tsr) = tabs
    P.pool(lambda e: e.tensor_tensor(out=t1[0:rows, 0:n], in0=qf[0:rows, 0:n], in1=tc[0:rows, t0:t0 + n], op=ALU.mult), r=[qfr, tcr], w=[t1r])
    t2, t2r = bufs["t2"].next()
    P.dve(lambda e: e.tensor_tensor(out=t2[0:rows, 0:n], in0=p2[0:rows, 0:n], in1=ts[0:rows, t0:t0 + n], op=ALU.mult), r=[p2r, tsr], w=[t2r])
    ob, obr = bufs["ob"].next()
    P.dve(lambda e: e.tensor_tensor(out=ob[0:rows, 0:n], in0=t1[0:rows, 0:n], in1=t2[0:rows, 0:n], op=ALU.add), r=[t1r, t2r], w=[obr])
    P.stq(lambda e: e.dma_start(out=dst_dram, in_=ob[0:rows, 0:n]), r=[obr], w=[dstres])


def rope_bufs(sg):
    return dict(qf=Rot(sg.sb([128, 512], F32, 2)), p2=Rot(sg.ps([128, 512], F32, 2)), t1=Rot(sg.sb([128, 512], F32, 2)),
                t2=Rot(sg.sb([128, 512], F32, 2)), ob=Rot(sg.sb([128, 512], BF16, 2)))


def load_rope_tabs(P, sg, I):
    tc, tcr = sg.sb([128, NT], F32)
    ts, tsr = sg.sb([128, NT], F32)
    P.ld(lambda e: e.dma_start(out=tc[:], in_=I["ropeC"]), w=[tcr])
    P.ld(lambda e: e.dma_start(out=ts[:], in_=I["ropeS"]), w=[tsr])
    return (tc, tcr), (ts, tsr)


def stage_win(P, I, S, W, C):
    nc = P.nc
    with Stage(P) as sg:
        tabs = load_rope_tabs(P, sg, I)
        rb = rope_bufs(sg)
        hTs = Rot(sg.sb([128, 16, 512], BF16, 2))
        wts = Rot(sg.sb([128, 16, 128], BF16, 3))
        wv, wvr = sg.sb([128, 16, 512], BF16)
        pss = Rot(sg.ps([128, 512], F32, 4))
        ev = Rot(sg.sb([128, 512], F32, 3))
        evb = Rot(sg.sb([128, 512], BF16, 3))
        dr = P.R()
        hv = S["hxT"].rearrange("(k p) t -> p k t", p=128)
        wiv = W["w_in"].rearrange("(k p) f -> p k f", p=128)
        P.ld(lambda e: e.dma_start(out=wv[:], in_=wiv[:, :, 1024:1536]), w=[wvr])
        tiles = []
        for j in range(4):
            tiles.append((j * 128, 128, "rope", S["qT"][j * 128:(j + 1) * 128, :]))
        for j in range(4):
            tiles.append((512 + j * 128, 128, "rope", S["kT"][j * 128:(j + 1) * 128, :]))
        for j in range(4):
            tiles.append((1536 + j * 128, 128, "f32", S["uT"][j * 128:(j + 1) * 128, :]))
        for j in range(6):
            tiles.append((2048 + j * 128, 128, "f32", S["mlaT"][j * 128:(j + 1) * 128, :]))
        tiles.append((2816, 64, "f32", S["mlaT"][768:832, :]))
        for j in range(4):
            tiles.append((2880 + j * 128, 128, "cval", j))
        def do_chunk(t0, n):
            hT, hTr = hTs.next()
            P.ld(lambda e, hT=hT, t0=t0, n=n: e.dma_start(out=hT[:, :, 0:n], in_=hv[:, :, t0:t0 + n]), w=[hTr])

            def proj(c0, rows):
                w, wr = wts.next()
                P.ld(lambda e: e.dma_start(out=w[:, :, 0:rows], in_=wiv[:, :, c0:c0 + rows]), w=[wr])
                ps, psr = pss.next()
                for k in range(16):
                    P.pe(lambda e, k=k: e.matmul(ps[0:rows, 0:n], lhsT=w[:, k, 0:rows], rhs=hT[:, k, 0:n], start=(k == 0), stop=(k == 15)),
                         r=[wr, hTr], w=[psr])
                return ps, psr
            for (c0, rows, kind, dst) in tiles:
                ps, psr = proj(c0, rows)
                if kind == "rope":
                    rope_tile(P, sg, C, ps, psr, rows, t0, n, tabs, rb, dst[:, t0:t0 + n], dr)
                elif kind == "f32":
                    o, orr = ev.next()
                    P.act(lambda e, o=o, ps=ps, rows=rows: e.activation(out=o[0:rows, 0:n], in_=ps[0:rows, 0:n], func=AF.Copy), r=[psr], w=[orr])
                    P.stq(lambda e, o=o, dst=dst, rows=rows: e.dma_start(out=dst[:, t0:t0 + n], in_=o[0:rows, 0:n]), r=[orr], w=[dr])
                else:
                    j = dst
                    psg, psgr = proj(2880 + 512 + j * 128, 128)
                    sgm, sgmr = ev.next()
                    P.act(lambda e, sgm=sgm, psg=psg: e.activation(out=sgm[:, 0:n], in_=psg[:, 0:n], func=AF.Sigmoid), r=[psgr], w=[sgmr])
                    ob, obr = evb.next()
                    P.dve(lambda e, ob=ob, ps=ps, sgm=sgm: e.tensor_tensor(out=ob[:, 0:n], in0=ps[:, 0:n], in1=sgm[:, 0:n], op=ALU.mult), r=[psr, sgmr], w=[obr])
                    P.stq(lambda e, ob=ob, j=j: e.dma_start(out=S["cuT"][j * 128:(j + 1) * 128, t0:t0 + n], in_=ob[:, 0:n]), r=[obr], w=[dr])
            for i in range(n // 128):
                ps, psr = pss.next()
                for k in range(16):
                    P.pe(lambda e, k=k, i=i, ps=ps: e.matmul(ps[:, :], lhsT=hT[:, k, i * 128:(i + 1) * 128], rhs=wv[:, k, :], start=(k == 0), stop=(k == 15)),
                         r=[wvr, hTr], w=[psr])
                ob, obr = evb.next()
                P.act(lambda e, ob=ob, ps=ps: e.activation(out=ob[:], in_=ps[:], func=AF.Copy), r=[psr], w=[obr])
                P.stq(lambda e, ob=ob, a=t0 + i * 128: e.dma_start(out=S["vda"][a:a + 128, :], in_=ob[:]), r=[obr], w=[dr])
        for (t0, n) in CHUNKS:
            do_chunk(t0, n)


def attention(P, sg, C, parts, V, Vr, scale, chunk, pbufs):
    t0, n = chunk
    nkt = 2 if t0 < NCTX else NT // 128
    o, orr = pbufs["o"].next()
    z, zr = pbufs["z"].next()
    for kt in range(nkt):
        s, sr = pbufs["s"].next()
        for pi, (KT, QT, rr) in enumerate(parts):
            P.pe(lambda e, KT=KT, QT=QT, pi=pi, s=s, kt=kt: e.matmul(s[:, 0:n], lhsT=KT[:, kt * 128:(kt + 1) * 128], rhs=QT[:, t0:t0 + n],
                                                                  start=(pi == 0), stop=(pi == len(parts) - 1)), r=[rr], w=[sr])
        p, pr = pbufs["p"].next()
        P.act(lambda e, p=p, s=s: e.activation(out=p[:, 0:n], in_=s[:, 0:n], func=AF.Exp, scale=scale), r=[sr], w=[pr])
        P.pe(lambda e, p=p, kt=kt: e.matmul(o[:, 0:n], lhsT=V[:, kt, :], rhs=p[:, 0:n], start=(kt == 0), stop=(kt == nkt - 1)), r=[pr, Vr], w=[orr])
        P.pe(lambda e, p=p, kt=kt: e.matmul(z[:, 0:n], lhsT=C["onesb"][:], rhs=p[:, 0:n], start=(kt == 0), stop=(kt == nkt - 1)), r=[pr], w=[zr])
    return (o, orr), (z, zr)


def stage_da(P, I, S, C, l, prefetch=None):
    nc = P.nc
    li = 0.8 - 0.6 * math.exp(-0.3 * l)
    with Stage(P) as sg:
        if prefetch is not None:
            prefetch()
        lv, lvr = sg.sb([128, 4, 64], F32)
        for i, nm in enumerate(["da_lam_q1", "da_lam_k1", "da_lam_q2", "da_lam_k2"]):
            P.ld(lambda e, i=i, nm=nm: e.dma_start(out=lv[:, i, :], in_=I[nm][l:l + 1, :].to_broadcast([128, 64])), w=[lvr])
        lp, lpr = sg.sb([128, 2, 64], F32)
        P.dve(lambda e: e.tensor_tensor(out=lp[:, 0, :], in0=lv[:, 0, :], in1=lv[:, 1, :], op=ALU.mult), r=[lvr], w=[lpr])
        P.dve(lambda e: e.tensor_tensor(out=lp[:, 1, :], in0=lv[:, 2, :], in1=lv[:, 3, :], op=ALU.mult), r=[lvr], w=[lpr])
        lsum, lsr = sg.sb([128, 2], F32)
        P.dve(lambda e: e.tensor_reduce(out=lsum[:], in_=lp[:], axis=mybir.AxisListType.X, op=ALU.add), r=[lpr], w=[lsr])
        P.act(lambda e: e.activation(out=lsum[:], in_=lsum[:], func=AF.Exp), r=[lsr], w=[lsr])
        nlam, nlr = sg.sb([128, 1], F32)
        P.dve(lambda e: e.tensor_tensor(out=nlam[:], in0=lsum[:, 1:2], in1=lsum[:, 0:1], op=ALU.subtract), r=[lsr], w=[nlr])
        P.dve(lambda e: e.tensor_scalar(out=nlam[:], in0=nlam[:], scalar1=-li, scalar2=None, op0=ALU.add), r=[nlr], w=[nlr])
        gs, gsr = sg.sb([128, 1], F32)
        P.ld(lambda e: e.dma_start(out=gs[:], in_=I["da_subln"][l].rearrange("(p o) -> p o", o=1), allow_slow_non_contiguous=True), w=[gsr])
        P.dve(lambda e: e.tensor_scalar(out=gs[:], in0=gs[:], scalar1=(1.0 - li), scalar2=None, op0=ALU.mult), r=[gsr], w=[gsr])

        KT, KTr = sg.sb([128, NT], BF16)
        QT, QTr = sg.sb([128, NT], BF16)
        V, Vr = sg.sb([128, 34, 128], BF16)
        pb = dict(o=Rot(sg.ps([128, 512], F32, 2)), z=Rot(sg.ps([128, 512], F32, 2)), s=Rot(sg.ps([128, 512], F32, 3)),
                  p=Rot(sg.sb([128, 512], BF16, 4)), acc=Rot(sg.sb([128, 512], F32, 4)))
        ssp = sg.ps([128, 512], F32)
        wk = Rot(sg.sb([128, 512], F32, 6))
        ob = Rot(sg.sb([128, 512], BF16, 2))
        dr = P.R()
        for h in range(4):
            P.ld(lambda e, h=h: e.dma_start(out=KT[:], in_=S["kT"][h * 128:(h + 1) * 128, :]), w=[KTr])
            P.ld(lambda e, h=h: e.dma_start(out=QT[:], in_=S["qT"][h * 128:(h + 1) * 128, :]), w=[QTr])
            P.ld(lambda e, h=h: e.dma_start(out=V[:], in_=S["vda"][:, h * 128:(h + 1) * 128].rearrange("(k p) d -> p k d", p=128)), w=[Vr])
            def do_chunk(h, chunk):
                t0, n = chunk
                nrm = []
                for m in range(2):
                    parts = [(KT[m * 64:(m + 1) * 64, :], QT[m * 64:(m + 1) * 64, :], KTr if False else QTr)]
                    (o, orr), (z, zr) = attention_rw(P, sg, C, parts, [KTr, QTr], V, Vr, 0.125, chunk, pb)
                    rz, rzr = wk.next()
                    P.dve(lambda e, rz=rz, z=z: e.reciprocal(out=rz[:, 0:n], in_=z[:, 0:n]), r=[zr], w=[rzr])
                    a, ar = wk.next()
                    P.dve(lambda e, a=a, o=o, rz=rz: e.tensor_tensor(out=a[:, 0:n], in0=o[:, 0:n], in1=rz[:, 0:n], op=ALU.mult), r=[orr, rzr], w=[ar])
                    nrm.append((a, ar))
                d, ddr = wk.next()
                (a1, a1r), (a2, a2r) = nrm
                P.dve(lambda e, d=d, a1=a1, a2=a2: e.scalar_tensor_tensor(out=d[:, 0:n], in0=a2[:, 0:n], scalar=nlam[:, 0:1], in1=a1[:, 0:n], op0=ALU.mult, op1=ALU.add),
                      r=[a1r, a2r, nlr], w=[ddr])
                sq, sqr = wk.next()
                P.act(lambda e, sq=sq, d=d: e.activation(out=sq[:, 0:n], in_=d[:, 0:n], func=AF.Square), r=[ddr], w=[sqr])
                (sp_, spr) = ssp
                P.pe(lambda e, sq=sq: e.matmul(sp_[:, 0:n], lhsT=C["onesf"][:], rhs=sq[:, 0:n], start=True, stop=True), r=[sqr], w=[spr])
                rs, rsr = wk.next()
                P.dve(lambda e, rs=rs: e.tensor_scalar(out=rs[:, 0:n], in0=sp_[:, 0:n], scalar1=1.0 / 128, scalar2=EPS, op0=ALU.mult, op1=ALU.add), r=[spr], w=[rsr])
                P.act(lambda e, rs=rs: e.activation(out=rs[:, 0:n], in_=rs[:, 0:n], func=AF.Sqrt), r=[rsr], w=[rsr])
                P.dve(lambda e, rs=rs: e.reciprocal(out=rs[:, 0:n], in_=rs[:, 0:n]), r=[rsr], w=[rsr])
                y, yr = ob.next()
                P.dve(lambda e, y=y, d=d, rs=rs: e.scalar_tensor_tensor(out=y[:, 0:n], in0=d[:, 0:n], scalar=gs[:, 0:1], in1=rs[:, 0:n], op0=ALU.mult, op1=ALU.mult),
                      r=[ddr, rsr, gsr], w=[yr])
                P.stq(lambda e, y=y, h=h, t0=t0, n=n: e.dma_start(out=S["ymixT"][h * 128:(h + 1) * 128, t0:t0 + n], in_=y[:, 0:n]), r=[yr], w=[dr])
            for chunk in CHUNKS:
                do_chunk(h, chunk)


def attention_rw(P, sg, C, parts, rres, V, Vr, scale, chunk, pbufs):
    t0, n = chunk
    nkt = 2 if t0 < NCTX else NT // 128
    o, orr = pbufs["o"].next()
    z, zr = pbufs["z"].next()

    def qk(kt):
        s, sr = pbufs["s"].next()
        for pi, (KT, QT, _) in enumerate(parts):
            P.pe(lambda e, KT=KT, QT=QT, pi=pi: e.matmul(s[:, 0:n], lhsT=KT[:, kt * 128:(kt + 1) * 128], rhs=QT[:, t0:t0 + n],
                                                        start=(pi == 0), stop=(pi == len(parts) - 1)), r=rres, w=[sr])
        p, pr = pbufs["p"].next()
        P.act(lambda e: e.activation(out=p[:, 0:n], in_=s[:, 0:n], func=AF.Exp, scale=scale), r=[sr], w=[pr])
        return p, pr

    accs = [pbufs["acc"].next()]

    def pv(kt, p, pr):
        P.pe(lambda e: e.matmul(o[:, 0:n], lhsT=V[:, kt, :], rhs=p[:, 0:n], start=(kt == 0), stop=(kt == nkt - 1)), r=[pr, Vr], w=[orr])
        a, ar = accs[0]
        eng = P.dve
        if kt < 1:
            eng(lambda e: e.tensor_copy(out=a[:, 0:n], in_=p[:, 0:n]), r=[pr], w=[ar])
        else:
            eng(lambda e: e.tensor_tensor(out=a[:, 0:n], in0=a[:, 0:n], in1=p[:, 0:n], op=ALU.add), r=[pr, ar], w=[ar])
    pend = []
    for kt in range(nkt):
        pend.append((kt, qk(kt)))
        if len(pend) > 2:
            k0, (p0, p0r) = pend.pop(0)
            pv(k0, p0, p0r)
    for k0, (p0, p0r) in pend:
        pv(k0, p0, p0r)
    a, ar = accs[0]
    P.pe(lambda e: e.matmul(z[:, 0:n], lhsT=C["onesf"][:], rhs=a[:, 0:n], start=True, stop=True), r=[ar], w=[zr])
    return (o, orr), (z, zr)


def stage_mla(P, I, S, W, C, l):
    nc = P.nc
    with Stage(P) as sg:
        tabs = load_rope_tabs(P, sg, I)
        rb = rope_bufs(sg)
        wuq, wuqr = sg.sb([128, 4, 768], BF16)
        wuk, wukr = sg.sb([128, 2, 4, 128], BF16)
        wuv, wuvr = sg.sb([128, 2, 4, 128], BF16)
        gq, gqr = sg.sb([128, 4], F32)
        gkv, gkvr = sg.sb([128, 2], F32)
        P.ld(lambda e: e.dma_start(out=wuq[:], in_=W["w_uq"].rearrange("(k p) f -> p k f", p=128)), w=[wuqr])
        ukv = W["w_ukv"].rearrange("(k p) (h c) -> p k h c", p=128, c=256)
        for k in range(2):
            P.ld(lambda e, k=k: e.dma_start(out=wuk[:, k, :, :], in_=ukv[:, k, :, 0:128]), w=[wukr])
            P.ld(lambda e, k=k: e.dma_start(out=wuv[:, k, :, :], in_=ukv[:, k, :, 128:256]), w=[wuvr])
        P.ld(lambda e: e.dma_start(out=gq[:], in_=I["mla_q_norm"][l].rearrange("(k p) -> p k", p=128), allow_slow_non_contiguous=True), w=[gqr])
        P.ld(lambda e: e.dma_start(out=gkv[:], in_=I["mla_kv_norm"][l].rearrange("(k p) -> p k", p=128), allow_slow_non_contiguous=True), w=[gkvr])
        lat = Rot(sg.sb([128, 6, 512], F32, 2))
        krs = Rot(sg.sb([64, 512], F32, 2))
        sqs = Rot(sg.sb([128, 6, 512], F32, 1))
        lsb = Rot(sg.sb([128, 6, 512], BF16, 2))
        rst = Rot(sg.sb([128, 2, 512], F32, 2))
        pss = Rot(sg.ps([128, 512], F32, 4))
        ob = Rot(sg.sb([128, 512], BF16, 3))
        sm = Rot(sg.sb([128, 1], F32, 4))
        dr = P.R()
        mv3 = S["mlaT"][0:768, :].rearrange("(k p) t -> p k t", p=128)

        def do_chunk(t0, n):
            x, xr = lat.next()
            P.ld(lambda e: e.dma_start(out=x[:, :, 0:n], in_=mv3[:, :, t0:t0 + n]), w=[xr])
            kr, krr = krs.next()
            P.ld(lambda e: e.dma_start(out=kr[:, 0:n], in_=S["mlaT"][768:832, t0:t0 + n]), w=[krr])
            sq, sqr = sqs.next()
            P.act(lambda e: e.activation(out=sq[:, :, 0:n], in_=x[:, :, 0:n], func=AF.Square), r=[xr], w=[sqr])
            rs, rsr = rst.next()
            for (which, k0, nk, nf) in ((0, 0, 4, 512), (1, 4, 2, 256)):
                ps, psr = pss.next()
                for k in range(nk):
                    P.pe(lambda e, k=k, ps=ps, k0=k0, nk=nk: e.matmul(ps[:, 0:n], lhsT=C["onesf"][:], rhs=sq[:, k0 + k, 0:n], start=(k == 0), stop=(k == nk - 1)), r=[sqr], w=[psr])
                P.dve(lambda e, ps=ps, which=which, nf=nf: e.tensor_scalar(out=rs[:, which, 0:n], in0=ps[:, 0:n], scalar1=1.0 / nf, scalar2=EPS, op0=ALU.mult, op1=ALU.add), r=[psr], w=[rsr])
            P.act(lambda e: e.activation(out=rs[:, :, 0:n], in_=rs[:, :, 0:n], func=AF.Sqrt), r=[rsr], w=[rsr])
            P.dve(lambda e: e.reciprocal(out=rs[:, :, 0:n], in_=rs[:, :, 0:n]), r=[rsr], w=[rsr])
            xb, xbr = lsb.next()
            for k in range(6):
                g = gq[:, k:k + 1] if k < 4 else gkv[:, k - 4:k - 3]
                P.act(lambda e, k=k, g=g: e.activation(out=xb[:, k, 0:n], in_=x[:, k, 0:n], func=AF.Copy, scale=g), r=[xr, gqr, gkvr], w=[xbr])
            for h in range(4):
                ps, psr = pss.next()
                for k in range(4):
                    P.pe(lambda e, k=k, ps=ps, h=h: e.matmul(ps[:, 0:n], lhsT=wuq[:, k, h * 192:h * 192 + 128], rhs=xb[:, k, 0:n], start=(k == 0), stop=(k == 3)), r=[wuqr, xbr], w=[psr])
                o, orr = ob.next()
                P.dve(lambda e, o=o, ps=ps: e.tensor_tensor(out=o[:, 0:n], in0=ps[:, 0:n], in1=rs[:, 0, 0:n], op=ALU.mult), r=[psr, rsr], w=[orr])
                P.stq(lambda e, o=o, h=h: e.dma_start(out=S["mqT"][h, 0:128, t0:t0 + n], in_=o[:, 0:n]), r=[orr], w=[dr])
                ps2, ps2r = pss.next()
                for k in range(4):
                    P.pe(lambda e, k=k, ps2=ps2, h=h: e.matmul(ps2[0:64, 0:n], lhsT=wuq[:, k, h * 192 + 128:h * 192 + 192], rhs=xb[:, k, 0:n], start=(k == 0), stop=(k == 3)), r=[wuqr, xbr], w=[ps2r])
                rope_tile(P, sg, C, ps2, ps2r, 64, t0, n, tabs, rb, S["mqT"][h, 128:192, t0:t0 + n], dr, scale_ap=(rs[:, 0, :], rsr))
                ps3, ps3r = pss.next()
                for k in range(2):
                    P.pe(lambda e, k=k, ps3=ps3, h=h: e.matmul(ps3[:, 0:n], lhsT=wuk[:, k, h, :], rhs=xb[:, 4 + k, 0:n], start=(k == 0), stop=(k == 1)), r=[wukr, xbr], w=[ps3r])
                o2, o2r = ob.next()
                P.dve(lambda e, o2=o2, ps3=ps3: e.tensor_tensor(out=o2[:, 0:n], in0=ps3[:, 0:n], in1=rs[:, 1, 0:n], op=ALU.mult), r=[ps3r, rsr], w=[o2r])
                P.stq(lambda e, o2=o2, h=h: e.dma_start(out=S["mkT"][h, :, t0:t0 + n], in_=o2[:, 0:n]), r=[o2r], w=[dr])
            rope_tile(P, sg, C, kr, krr, 64, t0, n, tabs, rb, S["mkrT"][:, t0:t0 + n], dr)
            for i in range(n // 128):
                pv, pvr = pss.next()
                for k in range(2):
                    P.pe(lambda e, k=k, pv=pv, i=i: e.matmul(pv[:, :], lhsT=xb[:, 4 + k, i * 128:(i + 1) * 128], rhs=wuv[:, k, :, :].rearrange("p h c -> p (h c)"), start=(k == 0), stop=(k == 1)), r=[wuvr, xbr], w=[pvr])
                pt, ptr = pss.next()
                for k in range(2):
                    P.pe(lambda e, k=k, pt=pt, i=i: e.matmul(pt[:, 0:1], lhsT=sq[:, 4 + k, i * 128:(i + 1) * 128], rhs=C["onesf"][:, 0:1], start=(k == 0), stop=(k == 1)), r=[sqr], w=[ptr])
                r1, r1r = sm.next()
                P.dve(lambda e, r1=r1, pt=pt: e.tensor_scalar(out=r1[:], in0=pt[:, 0:1], scalar1=1.0 / 256, scalar2=EPS, op0=ALU.mult, op1=ALU.add), r=[ptr], w=[r1r])
                P.act(lambda e, r1=r1: e.activation(out=r1[:], in_=r1[:], func=AF.Sqrt), r=[r1r], w=[r1r])
                P.dve(lambda e, r1=r1: e.reciprocal(out=r1[:], in_=r1[:]), r=[r1r], w=[r1r])
                o3, o3r = ob.next()
                P.act(lambda e, o3=o3, pv=pv, r1=r1: e.activation(out=o3[:], in_=pv[:], func=AF.Copy, scale=r1[:, 0:1]), r=[pvr, r1r], w=[o3r])
                P.stq(lambda e, o3=o3, a=t0 + i * 128: e.dma_start(out=S["mv"][a:a + 128, :], in_=o3[:]), r=[o3r], w=[dr])
        for (t0, n) in CHUNKS:
            do_chunk(t0, n)
    with Stage(P) as sg:
        KN, KNr = sg.sb([128, NT], BF16)
        QN, QNr = sg.sb([128, NT], BF16)
        KR, KRr = sg.sb([64, NT], BF16)
        QR, QRr = sg.sb([64, NT], BF16)
        V, Vr = sg.sb([128, 34, 128], BF16)
        pb = dict(o=Rot(sg.ps([128, 512], F32, 2)), z=Rot(sg.ps([128, 512], F32, 2)), s=Rot(sg.ps([128, 512], F32, 4)),
                  p=Rot(sg.sb([128, 512], BF16, 4)), acc=Rot(sg.sb([128, 512], F32, 4)))
        wk = Rot(sg.sb([128, 512], F32, 3))
        ob = Rot(sg.sb([128, 512], BF16, 2))
        dr = P.R()
        P.ld(lambda e: e.dma_start(out=KR[:], in_=S["mkrT"]), w=[KRr])

        def do_chunk(h, chunk):
            t0, n = chunk
            parts = [(KN[:, :], QN[:, :], None), (KR[:, :], QR[:, :], None)]
            (o, orr), (z, zr) = attention_rw(P, sg, C, parts, [KNr, QNr, KRr, QRr], V, Vr, 192.0 ** -0.5, chunk, pb)
            rz, rzr = wk.next()
            P.dve(lambda e: e.reciprocal(out=rz[:, 0:n], in_=z[:, 0:n]), r=[zr], w=[rzr])
            y, yr = ob.next()
            P.dve(lambda e: e.tensor_tensor(out=y[:, 0:n], in0=o[:, 0:n], in1=rz[:, 0:n], op=ALU.mult), r=[orr, rzr], w=[yr])
            P.stq(lambda e: e.dma_start(out=S["ymixT"][1024 + h * 128:1024 + (h + 1) * 128, t0:t0 + n], in_=y[:, 0:n]), r=[yr], w=[dr])
        for h in range(4):
            P.ld(lambda e, h=h: e.dma_start(out=KN[:], in_=S["mkT"][h]), w=[KNr])
            P.ld(lambda e, h=h: e.dma_start(out=QN[:], in_=S["mqT"][h, 0:128, :]), w=[QNr])
            P.ld(lambda e, h=h: e.dma_start(out=QR[:], in_=S["mqT"][h, 128:192, :]), w=[QRr])
            P.ld(lambda e, h=h: e.dma_start(out=V[:], in_=S["mv"][:, h * 128:(h + 1) * 128].rearrange("(k p) d -> p k d", p=128)), w=[Vr])
            for chunk in CHUNKS:
                do_chunk(h, chunk)


CPAD = NT + 45


def stage_conv(P, I, S, C, l):
    nc = P.nc
    with Stage(P) as sg:
        U, Ur = sg.sb([128, 4, CPAD], BF16)
        Dj, Djr = sg.sb([128, 4, 31, 128], BF16)
        wc, wcr = sg.sb([128, 4, 31], F32)
        pv, pvr = sg.sb([128, 3, 4], F32)
        P.pool(lambda e: e.memset(U[:], 0.0), w=[Ur])
        cu = S["cuT"].rearrange("(f p) t -> p f t", p=128)
        P.ld(lambda e: e.dma_start(out=U[:, :, 15:15 + NCTX], in_=cu[:, :, 0:NCTX]), w=[Ur])
        P.ld(lambda e: e.dma_start(out=U[:, :, 286:286 + NLAT], in_=cu[:, :, NCTX:NT]), w=[Ur])
        for ft in range(4):
            P.ld(lambda e, ft=ft: e.dma_start(out=wc[:, ft, :], in_=I["conv_w"][l][:, ft * 128:(ft + 1) * 128].rearrange("j c -> c j"), allow_slow_non_contiguous=True), w=[wcr])
        for i, nm in enumerate(["conv_b", "conv_ln_g", "conv_ln_b"]):
            P.ld(lambda e, i=i, nm=nm: e.dma_start(out=pv[:, i, :], in_=I[nm][l].rearrange("(f p) -> p f", p=128), allow_slow_non_contiguous=True), w=[pvr])
        for ft in range(4):
            for j in range(31):
                P.dve(lambda e, ft=ft, j=j: e.tensor_scalar(out=Dj[:, ft, j, :], in0=C["identf"][:], scalar1=wc[:, ft, j:j + 1], scalar2=None, op0=ALU.mult), r=[wcr], w=[Djr])
        pss = Rot(sg.ps([128, 512], F32, 4))
        st1 = sg.ps([128, 512], F32)
        st2 = sg.ps([128, 512], F32)
        ys = Rot(sg.sb([128, 4, 512], F32, 2))
        sqs = Rot(sg.sb([128, 4, 512], F32, 1))
        wk = Rot(sg.sb([128, 512], F32, 2))
        stb = Rot(sg.sb([128, 512], F32, 3))
        ob = Rot(sg.sb([128, 512], BF16, 3))
        dr = P.R()

        def do_block(base, t0, n):
            y, yr = ys.next()
            sq, sqr = sqs.next()
            for ft in range(4):
                ps, psr = pss.next()
                for j in range(31):
                    P.pe(lambda e, ft=ft, j=j, ps=ps: e.matmul(ps[:, 0:n], lhsT=Dj[:, ft, j, :], rhs=U[:, ft, base + j - 15:base + j - 15 + n], start=(j == 0), stop=(j == 30)), r=[Djr, Ur], w=[psr])
                P.act(lambda e, ft=ft, ps=ps: e.activation(out=y[:, ft, 0:n], in_=ps[:, 0:n], func=AF.Identity, bias=pv[:, 0, ft:ft + 1]), r=[psr, pvr], w=[yr])
            P.act(lambda e: e.activation(out=sq[:, :, 0:n], in_=y[:, :, 0:n], func=AF.Square), r=[yr], w=[sqr])
            (s1, s1r), (s2, s2r) = st1, st2
            for ft in range(4):
                P.pe(lambda e, ft=ft: e.matmul(s1[:, 0:n], lhsT=C["onesf"][:], rhs=y[:, ft, 0:n], start=(ft == 0), stop=(ft == 3)), r=[yr], w=[s1r])
            for ft in range(4):
                P.pe(lambda e, ft=ft: e.matmul(s2[:, 0:n], lhsT=C["onesf"][:], rhs=sq[:, ft, 0:n], start=(ft == 0), stop=(ft == 3)), r=[sqr], w=[s2r])
            mu, mur = stb.next()
            P.dve(lambda e: e.tensor_scalar(out=mu[:, 0:n], in0=s1[:, 0:n], scalar1=1.0 / 512, scalar2=None, op0=ALU.mult), r=[s1r], w=[mur])
            m2, m2r = stb.next()
            P.dve(lambda e: e.tensor_tensor(out=m2[:, 0:n], in0=mu[:, 0:n], in1=mu[:, 0:n], op=ALU.mult), r=[mur], w=[m2r])
            va, var_ = stb.next()
            P.dve(lambda e: e.scalar_tensor_tensor(out=va[:, 0:n], in0=s2[:, 0:n], scalar=1.0 / 512, in1=m2[:, 0:n], op0=ALU.mult, op1=ALU.subtract), r=[s2r, m2r], w=[var_])
            P.dve(lambda e: e.tensor_scalar(out=va[:, 0:n], in0=va[:, 0:n], scalar1=EPS, scalar2=None, op0=ALU.add), r=[var_], w=[var_])
            P.act(lambda e: e.activation(out=va[:, 0:n], in_=va[:, 0:n], func=AF.Sqrt), r=[var_], w=[var_])
            P.dve(lambda e: e.reciprocal(out=va[:, 0:n], in_=va[:, 0:n]), r=[var_], w=[var_])
            for ft in range(4):
                t, tr = wk.next()
                P.dve(lambda e, ft=ft, t=t: e.tensor_tensor(out=t[:, 0:n], in0=y[:, ft, 0:n], in1=mu[:, 0:n], op=ALU.subtract), r=[yr, mur], w=[tr])
                P.dve(lambda e, ft=ft, t=t: e.scalar_tensor_tensor(out=t[:, 0:n], in0=t[:, 0:n], scalar=pv[:, 1, ft:ft + 1], in1=va[:, 0:n], op0=ALU.mult, op1=ALU.mult), r=[tr, var_, pvr], w=[tr])
                o, orr = ob.next()
                P.act(lambda e, ft=ft, t=t, o=o: e.activation(out=o[:, 0:n], in_=t[:, 0:n], func=AF.Silu, bias=pv[:, 2, ft:ft + 1]), r=[tr, pvr], w=[orr])
                P.stq(lambda e, ft=ft, o=o: e.dma_start(out=S["ymixT"][1536 + ft * 128:1536 + (ft + 1) * 128, t0:t0 + n], in_=o[:, 0:n]), r=[orr], w=[dr])
        do_block(15, 0, 256)
        for b in range(8):
            do_block(286 + 512 * b, 256 + 512 * b, 512)


NB = NCTX + NLAT + NCTX
NCH = NT // 16
TWO_PI = 2.0 * math.pi


def stage_s5(P, I, S, W, C, l):
    nc = P.nc
    I32 = mybir.dt.int32
    ys5 = S["ys5"]
    keep = contextlib.ExitStack()
    with keep:
        def ksb(shape, dt, stack=None):
            Stage.CNT += 1
            return (stack or keep).enter_context(nc.sbuf_tensor("k%d" % Stage.CNT, list(shape), dt)), P.R()
        Kmat, Kmr = ksb([128, 16, 6, 128], BF16)
        CAr_, CArr = ksb([128, 16, 16, 32], BF16)
        CAi_, CAir = ksb([128, 16, 16, 32], BF16)
        Hbr_, Hbrr = ksb([128, 16, NCH], BF16)
        Hbi_, Hbir = ksb([128, 16, NCH], BF16)
        Qre, Qrer = ksb([128, 9, 16], F32)
        Qim, Qimr = ksb([128, 9, 16], F32)

        def load_u(stack):
            U, Ur = ksb([128, 6, NB], BF16, stack)
            for ft in range(6):
                nr = 96 if ft < 5 else 32
                P.stq(lambda e, ft=ft, nr=nr: e.dma_start(out=U[0:nr, ft, 0:NT], in_=S["uT"][96 * ft:96 * ft + nr, 0:NT]), w=[Ur])
                P.stq(lambda e, ft=ft, nr=nr: e.dma_start(out=U[0:nr, ft, NT:NB], in_=S["uT"][96 * ft:96 * ft + nr, 0:NCTX]), w=[Ur])
            return U, Ur
        for d in range(2):
            off = 0 if d == 0 else NCTX
            with contextlib.ExitStack() as st_abt:
                ABTr_, ABTrr = ksb([128, 16, 6, 128], BF16, st_abt)
                ABTi_, ABTir = ksb([128, 16, 6, 128], BF16, st_abt)
                s5_params(P, I, C, l, d, Kmat, Kmr, ABTr_, ABTrr, ABTi_, ABTir, CAr_, CArr, CAi_, CAir, Qre, Qrer, Qim, Qimr)
                with contextlib.ExitStack() as st_u:
                    U, Ur = load_u(st_u)
                    s5_states(P, C, d, off, U, Ur, ABTr_, ABTrr, ABTi_, ABTir, Qre, Qrer, Qim, Qimr, Hbr_, Hbrr, Hbi_, Hbir)
            with contextlib.ExitStack() as st_u:
                U, Ur = load_u(st_u)
                s5_outputs(P, C, d, off, U, Ur, Kmat, Kmr, CAr_, CArr, CAi_, CAir, Hbr_, Hbrr, Hbi_, Hbir, ys5[d])
    s5_epilogue(P, I, S, W, C, l)


def s5_params(P, I, C, l, d, Kmat, Kmr, ABTr_, ABTrr, ABTi_, ABTir, CAr_, CArr, CAi_, CAir, Qre, Qrer, Qim, Qimr):
    nc = P.nc
    I32 = mybir.dt.int32
    with Stage(P) as sg:
        def t16(n=1):
            return sg.sb([128, 16], F32) if n == 1 else sg.sb([128, n, 16], F32)
        lr, lrr = t16(); li, lir = t16(); st, str_ = t16()
        P.ld(lambda e: e.dma_start(out=lr[:], in_=I["s5_lam_re"][l, d].rearrange("(j g) p -> (g p) j", g=2), allow_slow_non_contiguous=True), w=[lrr])
        P.ld(lambda e: e.dma_start(out=li[:], in_=I["s5_lam_im"][l, d].rearrange("(j g) p -> (g p) j", g=2), allow_slow_non_contiguous=True), w=[lir])
        lsv = I["s5_log_step"][l, d].rearrange("(j g) -> g j", g=2)
        for g2 in range(2):
            P.ld(lambda e, g2=g2: e.dma_start(out=st[g2 * 64:(g2 + 1) * 64, :], in_=lsv[g2:g2 + 1, :].to_broadcast([64, 16]), allow_slow_non_contiguous=True), w=[str_])
        Br, Brr = sg.sb([128, 16, 16], F32); Bi, Bir = sg.sb([128, 16, 16], F32)
        P.ld(lambda e: e.dma_start(out=Br[:], in_=I["s5_b_re"][l, d].rearrange("(j g) p h -> (g p) j h", g=2)), w=[Brr])
        P.ld(lambda e: e.dma_start(out=Bi[:], in_=I["s5_b_im"][l, d].rearrange("(j g) p h -> (g p) j h", g=2)), w=[Bir])
        Cr, Crr = sg.sb([128, 16, 16], F32); Ci, Cir = sg.sb([128, 16, 16], F32)
        cps = sg.ps([128, 32, 16], F32)
        cns = Rot(sg.sb([16, 16, 64], F32, 2))
        for (src, dstt, dstr) in (("s5_c_re", Cr, Crr), ("s5_c_im", Ci, Cir)):
            cv = I[src][l, d].rearrange("(j g) h p -> g h j p", g=2)
            for g2 in range(2):
                cn, cnr = cns.next()
                P.ld(lambda e, cn=cn, cv=cv, g2=g2: e.dma_start(out=cn[:], in_=cv[g2]), w=[cnr])
                for j in range(16):
                    P.pe(lambda e, cn=cn, g2=g2, j=j: e.matmul(cps[0][64 * g2:64 * g2 + 64, j, :], lhsT=cn[:, j, :], rhs=C["identf"][0:16, 0:16], start=True, stop=True), r=[cnr], w=[cps[1]])
            P.act(lambda e, dstt=dstt: e.activation(out=dstt[:], in_=cps[0][:, 0:16, :], func=AF.Copy), r=[cps[1]], w=[dstr])
        V = P.dve
        P.act(lambda e: e.activation(out=st[:], in_=st[:], func=AF.Exp), r=[str_], w=[str_])
        mg, mgr = t16(); th, thr = t16()
        V(lambda e: e.tensor_tensor(out=mg[:], in0=lr[:], in1=st[:], op=ALU.mult), r=[lrr, str_], w=[mgr])
        P.act(lambda e: e.activation(out=mg[:], in_=mg[:], func=AF.Exp), r=[mgr], w=[mgr])
        V(lambda e: e.tensor_tensor(out=th[:], in0=li[:], in1=st[:], op=ALU.mult), r=[lir, str_], w=[thr])
        ki, kir = sg.sb([128, 16], I32); kf, kfr = t16(); msk, mskr = t16()

        def fold(x, xr):
            V(lambda e: e.tensor_scalar(out=msk[:], in0=x[:], scalar1=math.pi, scalar2=-TWO_PI, op0=ALU.is_gt, op1=ALU.mult), r=[xr], w=[mskr])
            V(lambda e: e.tensor_tensor(out=x[:], in0=x[:], in1=msk[:], op=ALU.add), r=[xr, mskr], w=[xr])
            V(lambda e: e.tensor_scalar(out=msk[:], in0=x[:], scalar1=-math.pi, scalar2=TWO_PI, op0=ALU.is_lt, op1=ALU.mult), r=[xr], w=[mskr])
            V(lambda e: e.tensor_tensor(out=x[:], in0=x[:], in1=msk[:], op=ALU.add), r=[xr, mskr], w=[xr])
        V(lambda e: e.tensor_scalar(out=kf[:], in0=th[:], scalar1=1.0 / TWO_PI, scalar2=None, op0=ALU.mult), r=[thr], w=[kfr])
        V(lambda e: e.tensor_copy(out=ki[:], in_=kf[:]), r=[kfr], w=[kir])
        V(lambda e: e.tensor_copy(out=kf[:], in_=ki[:]), r=[kir], w=[kfr])
        V(lambda e: e.scalar_tensor_tensor(out=th[:], in0=kf[:], scalar=-TWO_PI, in1=th[:], op0=ALU.mult, op1=ALU.add), r=[kfr, thr], w=[thr])
        fold(th, thr)
        sn, snr = t16(); cs, csr = t16()
        P.act(lambda e: e.activation(out=sn[:], in_=th[:], func=AF.Sin), r=[thr], w=[snr])
        V(lambda e: e.tensor_scalar(out=th[:], in0=th[:], scalar1=math.pi / 2, scalar2=None, op0=ALU.add), r=[thr, snr], w=[thr])
        fold(th, thr)
        P.act(lambda e: e.activation(out=cs[:], in_=th[:], func=AF.Sin), r=[thr], w=[csr])
        Apr, Aprr = t16(17); Api, Apir = t16(17)
        V(lambda e: e.memset(Apr[:, 0, :], 1.0), w=[Aprr])
        V(lambda e: e.memset(Api[:, 0, :], 0.0), w=[Apir])
        V(lambda e: e.tensor_tensor(out=Apr[:, 1, :], in0=mg[:], in1=cs[:], op=ALU.mult), r=[mgr, csr], w=[Aprr])
        V(lambda e: e.tensor_tensor(out=Api[:, 1, :], in0=mg[:], in1=sn[:], op=ALU.mult), r=[mgr, snr], w=[Apir])
        t1, t1r = t16(8); t2, t2r = t16(8)

        def cmul(outr, outi, ores, ar, ai, ares, br, bi, bres, shape):
            a = lambda t: t
            V(lambda e: e.tensor_tensor(out=shape(t1), in0=ar, in1=br, op=ALU.mult), r=ares + bres, w=[t1r])
            V(lambda e: e.tensor_tensor(out=shape(t2), in0=ai, in1=bi, op=ALU.mult), r=ares + bres, w=[t2r])
            V(lambda e: e.tensor_tensor(out=outr, in0=shape(t1), in1=shape(t2), op=ALU.subtract), r=[t1r, t2r], w=[ores[0]])
            V(lambda e: e.tensor_tensor(out=shape(t1), in0=ar, in1=bi, op=ALU.mult), r=ares + bres + [ores[0]], w=[t1r])
            V(lambda e: e.tensor_tensor(out=shape(t2), in0=ai, in1=br, op=ALU.mult), r=ares + bres, w=[t2r])
            V(lambda e: e.tensor_tensor(out=outi, in0=shape(t1), in1=shape(t2), op=ALU.add), r=[t1r, t2r], w=[ores[1]])
        m = 1
        while m < 16:
            bre = Apr[:, m:m + 1, :].to_broadcast([128, m, 16]); bim = Api[:, m:m + 1, :].to_broadcast([128, m, 16])
            cmul(Apr[:, m + 1:2 * m + 1, :], Api[:, m + 1:2 * m + 1, :], [Aprr, Apir], Apr[:, 1:m + 1, :], Api[:, 1:m + 1, :], [Aprr, Apir],
                 bre, bim, [Aprr, Apir], (lambda t, m=m: t[:, 0:m, :]))
            m *= 2
        V(lambda e: e.tensor_copy(out=Qre[:, 0, :], in_=Apr[:, 16, :]), r=[Aprr], w=[Qrer])
        V(lambda e: e.tensor_copy(out=Qim[:, 0, :], in_=Api[:, 16, :]), r=[Apir], w=[Qimr])
        for j in range(8):
            cmul(Qre[:, j + 1:j + 2, :], Qim[:, j + 1:j + 2, :], [Qrer, Qimr], Qre[:, j:j + 1, :], Qim[:, j:j + 1, :], [Qrer, Qimr],
                 Qre[:, j:j + 1, :], Qim[:, j:j + 1, :], [Qrer, Qimr], (lambda t: t[:, 0:1, :]))
        den, denr = t16(); am1, am1r = t16(); fr, frr = t16(); fi, fir = t16(); w1, w1r = t16(); w2, w2r = t16()
        V(lambda e: e.tensor_tensor(out=den[:], in0=lr[:], in1=lr[:], op=ALU.mult), r=[lrr], w=[denr])
        V(lambda e: e.tensor_tensor(out=w1[:], in0=li[:], in1=li[:], op=ALU.mult), r=[lir], w=[w1r])
        V(lambda e: e.tensor_tensor(out=den[:], in0=den[:], in1=w1[:], op=ALU.add), r=[denr, w1r], w=[denr])
        V(lambda e: e.reciprocal(out=den[:], in_=den[:]), r=[denr], w=[denr])
        V(lambda e: e.tensor_scalar(out=am1[:], in0=Apr[:, 1, :], scalar1=-1.0, scalar2=None, op0=ALU.add), r=[Aprr], w=[am1r])
        V(lambda e: e.tensor_tensor(out=w1[:], in0=am1[:], in1=lr[:], op=ALU.mult), r=[am1r, lrr, denr], w=[w1r])
        V(lambda e: e.tensor_tensor(out=w2[:], in0=Api[:, 1, :], in1=li[:], op=ALU.mult), r=[Apir, lir], w=[w2r])
        V(lambda e: e.tensor_tensor(out=fr[:], in0=w1[:], in1=w2[:], op=ALU.add), r=[w1r, w2r], w=[frr])
        V(lambda e: e.tensor_tensor(out=fr[:], in0=fr[:], in1=den[:], op=ALU.mult), r=[frr, denr], w=[frr])
        V(lambda e: e.tensor_tensor(out=w1[:], in0=Api[:, 1, :], in1=lr[:], op=ALU.mult), r=[Apir, lrr, frr], w=[w1r])
        V(lambda e: e.tensor_tensor(out=w2[:], in0=am1[:], in1=li[:], op=ALU.mult), r=[am1r, lir, frr], w=[w2r])
        V(lambda e: e.tensor_tensor(out=fi[:], in0=w1[:], in1=w2[:], op=ALU.subtract), r=[w1r, w2r], w=[fir])
        V(lambda e: e.tensor_tensor(out=fi[:], in0=fi[:], in1=den[:], op=ALU.mult), r=[fir, denr], w=[fir])
        Bbr, Bbrr = sg.sb([128, 16, 32], F32); Bbi, Bbir = sg.sb([128, 16, 32], F32)
        Cbr, Cbrr = sg.sb([128, 16, 32], F32); Cbi, Cbir = sg.sb([128, 16, 32], F32); Cbn, Cbnr = sg.sb([128, 16, 32], F32)
        x1, x1r = sg.sb([128, 16, 32], F32); x2, x2r = sg.sb([128, 16, 32], F32)
        for (t, tr) in ((Bbr, Bbrr), (Bbi, Bbir), (Cbr, Cbrr), (Cbi, Cbir)):
            P.pool(lambda e, t=t: e.memset(t[:], 0.0), w=[tr])
        for g2 in range(2):
            rows = slice(g2 * 64, (g2 + 1) * 64); cols = slice(g2 * 16, (g2 + 1) * 16)
            fb_r = lambda g2=g2: fr[g2 * 64:(g2 + 1) * 64, :].unsqueeze(2).to_broadcast([64, 16, 16])
            fb_i = lambda g2=g2: fi[g2 * 64:(g2 + 1) * 64, :].unsqueeze(2).to_broadcast([64, 16, 16])
            V(lambda e, rows=rows, cols=cols, fb_r=fb_r: e.tensor_tensor(out=x1[rows, :, 0:16], in0=Br[rows, :, :], in1=fb_r(), op=ALU.mult), r=[Brr, frr], w=[x1r])
            V(lambda e, rows=rows, cols=cols, fb_i=fb_i: e.tensor_tensor(out=x2[rows, :, 0:16], in0=Bi[rows, :, :], in1=fb_i(), op=ALU.mult), r=[Bir, fir], w=[x2r])
            V(lambda e, rows=rows, cols=cols: e.tensor_tensor(out=Bbr[rows, :, cols], in0=x1[rows, :, 0:16], in1=x2[rows, :, 0:16], op=ALU.subtract), r=[x1r, x2r], w=[Bbrr])
            V(lambda e, rows=rows, cols=cols, fb_r=fb_r: e.tensor_tensor(out=x1[rows, :, 0:16], in0=Bi[rows, :, :], in1=fb_r(), op=ALU.mult), r=[Bir, frr, Bbrr], w=[x1r])
            V(lambda e, rows=rows, cols=cols, fb_i=fb_i: e.tensor_tensor(out=x2[rows, :, 0:16], in0=Br[rows, :, :], in1=fb_i(), op=ALU.mult), r=[Brr, fir, Bbrr], w=[x2r])
            V(lambda e, rows=rows, cols=cols: e.tensor_tensor(out=Bbi[rows, :, cols], in0=x1[rows, :, 0:16], in1=x2[rows, :, 0:16], op=ALU.add), r=[x1r, x2r], w=[Bbir])
            P.pool(lambda e, rows=rows, cols=cols: e.tensor_copy(out=Cbr[rows, :, cols], in_=Cr[rows, :, :]), r=[Crr], w=[Cbrr])
            P.pool(lambda e, rows=rows, cols=cols: e.tensor_copy(out=Cbi[rows, :, cols], in_=Ci[rows, :, :]), r=[Cir], w=[Cbir])
        P.pool(lambda e: e.tensor_scalar(out=Cbn[:], in0=Cbi[:], scalar1=-1.0, scalar2=None, op0=ALU.mult), r=[Cbir], w=[Cbnr])
        P.pool(lambda e: e.memset(Kmat[:], 0.0), w=[Kmr])
        ABs = Rot([(sg.sb([128, 16, 32], F32), sg.sb([128, 16, 32], F32)) for _ in range(2)])
        kks = Rot(sg.ps([128, 16, 32], F32, 2))
        ttr = Rot(sg.ps([128, 8, 128], F32, 1))
        tti = Rot(sg.ps([128, 8, 128], F32, 1))

        def bc(t, e_):
            return t[:, e_, :].unsqueeze(2).to_broadcast([128, 16, 32])

        def cmul_bd(outr, outrr, outi, outir, e_, Xr, Xrr, Xi, Xir, sign_im=1.0):
            V(lambda e: e.tensor_tensor(out=x1[:], in0=Xr[:], in1=bc(Apr, e_), op=ALU.mult), r=[Xrr, Aprr], w=[x1r])
            V(lambda e: e.tensor_tensor(out=x2[:], in0=Xi[:], in1=bc(Api, e_), op=ALU.mult), r=[Xir, Apir], w=[x2r])
            V(lambda e: e.tensor_tensor(out=outr, in0=x1[:], in1=x2[:], op=ALU.subtract), r=[x1r, x2r], w=[outrr])
            V(lambda e: e.tensor_tensor(out=x1[:], in0=Xi[:], in1=bc(Apr, e_), op=ALU.mult), r=[Xir, Aprr, outrr], w=[x1r])
            V(lambda e: e.tensor_tensor(out=x2[:], in0=Xr[:], in1=bc(Api, e_), op=ALU.mult), r=[Xrr, Apir, outrr], w=[x2r])
            if sign_im > 0:
                V(lambda e: e.tensor_tensor(out=outi, in0=x1[:], in1=x2[:], op=ALU.add), r=[x1r, x2r], w=[outir])
            else:
                V(lambda e: e.scalar_tensor_tensor(out=outi, in0=x1[:], scalar=-1.0, in1=x2[:], op0=ALU.mult, op1=ALU.subtract), r=[x1r, x2r], w=[outir])

        def do_e(e_):
            (ABr, ABrr), (ABi, ABir) = ABs.next()
            cmul_bd(ABr[:], ABrr, ABi[:], ABir, e_, Bbr, Bbrr, Bbi, Bbir)
            kk, kkr = kks.next(); tr_, trr = ttr.next(); ti_, tir = tti.next()
            for j in range(16):
                ft, q = j // 3, j % 3
                P.pe(lambda e, j=j, ft=ft, q=q: e.matmul(kk[32 * q:32 * q + 32, ft, :], lhsT=ABr[:, j, :], rhs=Cbr[:, j, :], start=True, stop=False), r=[ABrr, Cbrr], w=[kkr])
                P.pe(lambda e, j=j, ft=ft, q=q: e.matmul(kk[32 * q:32 * q + 32, ft, :], lhsT=ABi[:, j, :], rhs=Cbn[:, j, :], start=False, stop=True), r=[ABir, Cbnr], w=[kkr])
                P.pe(lambda e, j=j, ft=ft, q=q: e.matmul(tr_[32 * q:32 * q + 32, ft, :], lhsT=ABr[:, j, :], rhs=C["identf"][:], start=True, stop=True), r=[ABrr], w=[trr])
                P.pe(lambda e, j=j, ft=ft, q=q: e.matmul(ti_[32 * q:32 * q + 32, ft, :], lhsT=ABi[:, j, :], rhs=C["identf"][:], start=True, stop=True), r=[ABir], w=[tir])
            for q in range(3):
                nf = 6 if q == 0 else 5
                P.act(lambda e, q=q, nf=nf: e.activation(out=Kmat[32 * q:32 * q + 32, e_, 0:nf, 32 * q:32 * q + 32], in_=kk[32 * q:32 * q + 32, 0:nf, :], func=AF.Copy), r=[kkr], w=[Kmr])
            P.act(lambda e: e.activation(out=ABTr_[0:96, e_, 0:5, :], in_=tr_[0:96, 0:5, :], func=AF.Copy), r=[trr], w=[ABTrr])
            P.act(lambda e: e.activation(out=ABTi_[0:96, e_, 0:5, :], in_=ti_[0:96, 0:5, :], func=AF.Copy), r=[tir], w=[ABTir])
            P.act(lambda e: e.activation(out=ABTr_[0:32, e_, 5, :], in_=tr_[0:32, 5, :], func=AF.Copy), r=[trr], w=[ABTrr])
            P.act(lambda e: e.activation(out=ABTi_[0:32, e_, 5, :], in_=ti_[0:32, 5, :], func=AF.Copy), r=[tir], w=[ABTir])
            cmul_bd(CAr_[:, e_, :, :], CArr, CAi_[:, e_, :, :], CAir, e_ + 1, Cbr, Cbrr, Cbi, Cbir, sign_im=-1.0)
        for e_ in range(16):
            do_e(e_)


def s5_states(P, C, d, off, U, Ur, ABTr_, ABTrr, ABTi_, ABTir, Qre, Qrer, Qim, Qimr, Hbr_, Hbrr, Hbi_, Hbir):
    nc = P.nc
    n = NCH
    with Stage(P) as sg:
        NP_ = 2
        bufs = [sg.sb([128, NP_, NCH], F32) for _ in range(4)]
        tA = [sg.sb([128, NP_, NCH], F32) for _ in range(2)]
        tB = [sg.sb([128, NP_, NCH], F32) for _ in range(2)]
        pss = Rot(sg.ps([128, 512], F32, 4))

        def do_half(hf):
            (Xr, Xrr), (Xi, Xir), (Yr, Yrr), (Yi, Yir) = bufs
            for jj in range(NP_):
                j = hf * NP_ + jj
                ft, q = j // 3, j % 3
                for (ABT, ABTres, X, Xres) in ((ABTr_, ABTrr, Xr, Xrr), (ABTi_, ABTir, Xi, Xir)):
                    ps, psr = pss.next()
                    for r in range(16):
                        e_ = (15 - r) if d == 0 else r
                        rhs = U[32 * q:32 * q + 32, ft, off:off + NT].rearrange("p (c r) -> p c r", r=16)[:, :, r]
                        P.pe(lambda e, ps=ps, ABT=ABT, e_=e_, rhs=rhs, r=r, q=q, ft=ft: e.matmul(ps[:, 0:NCH], lhsT=ABT[32 * q:32 * q + 32, e_, ft, :], rhs=rhs, start=(r == 0), stop=(r == 15)),
                             r=[ABTres, Ur], w=[psr])
                    P.act(lambda e, ps=ps, X=X, jj=jj: e.activation(out=X[:, jj, :], in_=ps[:, 0:NCH], func=AF.Copy), r=[psr], w=[Xres])
            cur = (bufs[0], bufs[1]); nxt = (bufs[2], bufs[3])
            for step in range(9):
                sh = 1 << step
                (Xr, Xrr), (Xi, Xir) = cur
                (Yr, Yrr), (Yi, Yir) = nxt
                if d == 0:
                    dst = slice(sh, n); src = slice(0, n - sh); keep_ = slice(0, sh)
                else:
                    dst = slice(0, n - sh); src = slice(sh, n); keep_ = slice(n - sh, n)
                w_ = n - sh
                qr = Qre[:, step, hf * NP_:(hf + 1) * NP_].unsqueeze(2).to_broadcast([128, NP_, w_])
                qi = Qim[:, step, hf * NP_:(hf + 1) * NP_].unsqueeze(2).to_broadcast([128, NP_, w_])
                (a1, a1r), (a2, a2r) = tA
                (b1, b1r), (b2, b2r) = tB
                V = P.dve; G = P.pool
                V(lambda e, Xr=Xr, qr=qr, src=src, w_=w_: e.tensor_tensor(out=a1[:, :, 0:w_], in0=Xr[:, :, src], in1=qr, op=ALU.mult), r=[Xrr, Qrer], w=[a1r])
                V(lambda e, Xi=Xi, qi=qi, src=src, w_=w_: e.tensor_tensor(out=a2[:, :, 0:w_], in0=Xi[:, :, src], in1=qi, op=ALU.mult), r=[Xir, Qimr], w=[a2r])
                V(lambda e, w_=w_: e.tensor_tensor(out=a1[:, :, 0:w_], in0=a1[:, :, 0:w_], in1=a2[:, :, 0:w_], op=ALU.subtract), r=[a1r, a2r], w=[a1r])
                V(lambda e, Xr=Xr, Yr=Yr, dst=dst, w_=w_: e.tensor_tensor(out=Yr[:, :, dst], in0=Xr[:, :, dst], in1=a1[:, :, 0:w_], op=ALU.add), r=[Xrr, a1r], w=[Yrr])
                V(lambda e, Xr=Xr, Yr=Yr, keep_=keep_: e.tensor_copy(out=Yr[:, :, keep_], in_=Xr[:, :, keep_]), r=[Xrr], w=[Yrr])
                G(lambda e, Xi=Xi, qr=qr, src=src, w_=w_: e.tensor_tensor(out=b1[:, :, 0:w_], in0=Xi[:, :, src], in1=qr, op=ALU.mult), r=[Xir, Qrer], w=[b1r])
                G(lambda e, Xr=Xr, qi=qi, src=src, w_=w_: e.tensor_tensor(out=b2[:, :, 0:w_], in0=Xr[:, :, src], in1=qi, op=ALU.mult), r=[Xrr, Qimr], w=[b2r])
                G(lambda e, w_=w_: e.tensor_tensor(out=b1[:, :, 0:w_], in0=b1[:, :, 0:w_], in1=b2[:, :, 0:w_], op=ALU.add), r=[b1r, b2r], w=[b1r])
                G(lambda e, Xi=Xi, Yi=Yi, dst=dst, w_=w_: e.tensor_tensor(out=Yi[:, :, dst], in0=Xi[:, :, dst], in1=b1[:, :, 0:w_], op=ALU.add), r=[Xir, b1r], w=[Yir])
                G(lambda e, Xi=Xi, Yi=Yi, keep_=keep_: e.tensor_copy(out=Yi[:, :, keep_], in_=Xi[:, :, keep_]), r=[Xir], w=[Yir])
                cur, nxt = nxt, cur
            (Xr, Xrr), (Xi, Xir) = cur
            P.act(lambda e, Xr=Xr: e.activation(out=Hbr_[:, hf * NP_:(hf + 1) * NP_, :], in_=Xr[:], func=AF.Copy), r=[Xrr], w=[Hbrr])
            P.act(lambda e, Xi=Xi: e.activation(out=Hbi_[:, hf * NP_:(hf + 1) * NP_, :], in_=Xi[:], func=AF.Copy), r=[Xir], w=[Hbir])
        for hf in range(16 // NP_):
            do_half(hf)


def s5_outputs(P, C, d, off, U, Ur, Kmat, Kmr, CAr_, CArr, CAi_, CAir, Hbr_, Hbrr, Hbi_, Hbir, ydst):
    nc = P.nc
    with Stage(P) as sg:
        pss = Rot(sg.ps([128, 512], F32, 4))
        ev = Rot(sg.sb([128, 512], F32, 3))
        dr = P.R()
        if d == 0:
            blocks = [(0, 256)] + [(256 + 512 * i, 512) for i in range(8)]
        else:
            blocks = [(256 + 512 * i, 512) for i in range(8)] + [(NT, 256)]

        def do_block(ft, b0, n):
            nb = n // 16
            c0 = (b0 - off) // 16
            ps, psr = pss.next()
            pv = ps[:, 0:n].rearrange("p (c r) -> p c r", r=16)
            nrk = 96 if ft < 5 else 32
            uv = U[0:nrk, ft, b0:b0 + n].rearrange("p (c r) -> p c r", r=16)
            for k in range(16):
                if d == 0:
                    o_ap = pv[:, :, k:16]; r_ap = uv[:, :, 0:16 - k]
                else:
                    o_ap = pv[:, :, 0:16 - k]; r_ap = uv[:, :, k:16]
                P.pe(lambda e, k=k, o_ap=o_ap, r_ap=r_ap: e.matmul(o_ap, lhsT=Kmat[0:nrk, k, ft, :], rhs=r_ap, start=(k == 0), stop=False, skip_group_check=True),
                     r=[Kmr, Ur], w=[psr])
            npair = 3 if ft < 5 else 1
            for q in range(npair):
                j = ft * 3 + q
                for r in range(16):
                    if d == 0:
                        e_ = r
                        lo = 1 if c0 == 0 else 0
                        oc = slice(lo, nb); hc = slice(c0 + lo - 1, c0 + nb - 1)
                    else:
                        e_ = 15 - r
                        hi = nb - 1 if c0 + nb == NCH else nb
                        oc = slice(0, hi); hc = slice(c0 + 1, c0 + hi + 1)
                    last = (q == npair - 1 and r == 15)
                    P.pe(lambda e, j=j, q=q, r=r, e_=e_, oc=oc, hc=hc: e.matmul(pv[32 * q:32 * q + 32, oc, r], lhsT=CAr_[:, e_, j, :], rhs=Hbr_[:, j, hc], start=False, stop=False, skip_group_check=True),
                         r=[CArr, Hbrr], w=[psr])
                    P.pe(lambda e, j=j, q=q, r=r, e_=e_, oc=oc, hc=hc, last=last: e.matmul(pv[32 * q:32 * q + 32, oc, r], lhsT=CAi_[:, e_, j, :], rhs=Hbi_[:, j, hc], start=False, stop=last, skip_group_check=True),
                         r=[CAir, Hbir], w=[psr])
            o, orr = ev.next()
            P.act(lambda e: e.activation(out=o[:, 0:n], in_=ps[:, 0:n], func=AF.Copy), r=[psr], w=[orr])
            nr = 32 * npair
            P.stq(lambda e: e.dma_start(out=ydst[ft * 96:ft * 96 + nr, b0:b0 + n], in_=o[0:nr, 0:n]), r=[orr], w=[dr])
        for (b0, n) in blocks:
            for ft in range(6):
                do_block(ft, b0, n)


def s5_epilogue(P, I, S, W, C, l):
    nc = P.nc
    ys5 = S["ys5"]
    with Stage(P) as sg:
        wg, wgr = sg.sb([128, 4, 512], BF16)
        P.ld(lambda e: e.dma_start(out=wg[:], in_=W["w_glu"].rearrange("(k p) f -> p k f", p=128)), w=[wgr])
        dv, dvr = sg.sb([128, 4], F32)
        P.ld(lambda e: e.dma_start(out=dv[:], in_=I["s5_d"][l].rearrange("(f p) -> p f", p=128), allow_slow_non_contiguous=True), w=[dvr])
        yfs = Rot(sg.sb([128, 4, 512], F32, 2)); ybs = Rot(sg.sb([128, 4, 512], F32, 2)); us = Rot(sg.sb([128, 4, 512], F32, 2))
        gs = Rot(sg.sb([128, 4, 512], F32, 2)); gbs = Rot(sg.sb([128, 4, 512], BF16, 2))
        wk = Rot(sg.sb([128, 4, 512], F32, 2))
        sgs = Rot(sg.sb([128, 512], F32, 2)); ob = Rot(sg.sb([128, 512], BF16, 3))
        pss = Rot(sg.ps([128, 512], F32, 4))
        dr = P.R()
        yfv = ys5[0].rearrange("(f p) t -> p f t", p=128)
        ybv = ys5[1].rearrange("(f p) t -> p f t", p=128)
        uv = S["uT"].rearrange("(f p) t -> p f t", p=128)

        def do_chunk(t0, n):
            yf, yfr = yfs.next(); yb, ybr = ybs.next(); u, ur = us.next()
            tb = t0 if t0 >= NCTX else NT + t0
            P.ld(lambda e: e.dma_start(out=yf[:, :, 0:n], in_=yfv[:, :, t0:t0 + n]), w=[yfr])
            P.ld(lambda e: e.dma_start(out=yb[:, :, 0:n], in_=ybv[:, :, tb:tb + n]), w=[ybr])
            P.ld(lambda e: e.dma_start(out=u[:, :, 0:n], in_=uv[:, :, t0:t0 + n]), w=[ur])
            P.pool(lambda e: e.tensor_tensor(out=yf[:, :, 0:n], in0=yf[:, :, 0:n], in1=yb[:, :, 0:n], op=ALU.add), r=[yfr, ybr], w=[yfr])
            for ft in range(4):
                P.dve(lambda e, ft=ft: e.scalar_tensor_tensor(out=yf[:, ft, 0:n], in0=u[:, ft, 0:n], scalar=dv[:, ft:ft + 1], in1=yf[:, ft, 0:n], op0=ALU.mult, op1=ALU.add), r=[ur, dvr, yfr], w=[yfr])
            t, tr = wk.next()
            P.act(lambda e: e.activation(out=t[:, :, 0:n], in_=yf[:, :, 0:n], func=AF.Square), r=[yfr], w=[tr])
            P.dve(lambda e: e.tensor_scalar(out=t[:, :, 0:n], in0=t[:, :, 0:n], scalar1=0.044715 * 1.5957691216, scalar2=1.5957691216, op0=ALU.mult, op1=ALU.add), r=[tr], w=[tr])
            P.dve(lambda e: e.tensor_tensor(out=t[:, :, 0:n], in0=t[:, :, 0:n], in1=yf[:, :, 0:n], op=ALU.mult), r=[tr, yfr], w=[tr])
            P.act(lambda e: e.activation(out=t[:, :, 0:n], in_=t[:, :, 0:n], func=AF.Sigmoid), r=[tr], w=[tr])
            g, gr = gs.next(); gb, gbr = gbs.next()
            P.dve(lambda e: e.tensor_tensor(out=g[:, :, 0:n], in0=t[:, :, 0:n], in1=yf[:, :, 0:n], op=ALU.mult), r=[tr, yfr], w=[gr])
            P.pool(lambda e: e.tensor_copy(out=gb[:, :, 0:n], in_=g[:, :, 0:n]), r=[gr], w=[gbr])
            for fo in range(4):
                ps, psr = pss.next()
                for k in range(4):
                    P.pe(lambda e, k=k, fo=fo, ps=ps: e.matmul(ps[:, 0:n], lhsT=wg[:, k, fo * 128:(fo + 1) * 128], rhs=gb[:, k, 0:n], start=(k == 0), stop=(k == 3)), r=[wgr, gbr], w=[psr])
                sgm, sgmr = sgs.next()
                P.act(lambda e, ps=ps, sgm=sgm: e.activation(out=sgm[:, 0:n], in_=ps[:, 0:n], func=AF.Sigmoid), r=[psr], w=[sgmr])
                o, orr = ob.next()
                P.dve(lambda e, fo=fo, sgm=sgm, o=o: e.tensor_tensor(out=o[:, 0:n], in0=g[:, fo, 0:n], in1=sgm[:, 0:n], op=ALU.mult), r=[gr, sgmr], w=[orr])
                P.stq(lambda e, fo=fo, o=o: e.dma_start(out=S["ymixT"][512 + fo * 128:512 + (fo + 1) * 128, t0:t0 + n], in_=o[:, 0:n]), r=[orr], w=[dr])
        for (t0, n) in CHUNKS:
            do_chunk(t0, n)


def load_modvec_kind(P, tiles, S, idxs, kind):
    for (t, r), idx in zip(tiles, idxs):
        P.ld(lambda e, t=t, idx=idx: e.dma_start(out=t[:], in_=S["modv"][kind:kind + 1, idx, :].to_broadcast([128, D])), w=[r])


def stage_wout(P, I, S, W, C, l, xsrc, xres):
    nc = P.nc
    with Stage(P) as sg:
        wo, wor = sg.sb([128, 16, D], BF16)
        P.ld(lambda e: e.dma_start(out=wo[:], in_=W["w_out"].rearrange("(k p) f -> p k f", p=128)), w=[wor])
        mods = [sg.sb([128, D], F32) for _ in range(3)]
        yTs = Rot(sg.sb([128, 16, 512], BF16, 2))
        xts = Rot(sg.sb([128, D], F32, 2))
        hTs = Rot(sg.sb([128, 16, 512], BF16, 2))
        pss = Rot(sg.ps([128, 512], F32, 4))
        junk, junkr = sg.sb([128, 512], BF16)
        ss4 = Rot(sg.sb([128, 4], F32, 2))
        ss1 = Rot(sg.sb([128, 1], F32, 2))
        tmps = Rot(sg.sb([128, 512], F32, 2))
        bufs = dict(junk=sg.sb([128, D], BF16), ss=Rot(sg.sb([128, 1], F32, 2)), rs=Rot(sg.sb([128, 1], F32, 2)),
                    hb=Rot(sg.sb([128, D], BF16, 2)), tmp=Rot(sg.sb([128, D], F32, 1)), pt=Rot(sg.ps([128, 1024], BF16, 2)))
        dr = P.R()
        xr_dram = P.R()
        yv = S["ymixT"].rearrange("(k p) t -> p k t", p=128)
        hv = S["h2T"].rearrange("(k p) t -> p k t", p=128)

        def do_tile(yT, yTr, hT, hTr, i, a, kind):
            G = {kind: mods[1]}
            SH = {kind: mods[2]}
            xt, xtr = xts.next()
            P.ld(lambda e: e.dma_start(out=xt[:], in_=xsrc[a:a + 128, :]), r=[xr_dram], w=[xtr])
            s4, s4r = ss4.next()
            banks = []
            for c in range(4):
                ps, psr = pss.next()
                for k in range(16):
                    P.pe(lambda e, k=k, c=c, ps=ps: e.matmul(ps[:, :], lhsT=yT[:, k, i * 128:(i + 1) * 128], rhs=wo[:, k, c * 512:(c + 1) * 512], start=(k == 0), stop=(k == 15)), r=[yTr, wor], w=[psr])
                P.act(lambda e, c=c, ps=ps: e.activation(out=junk[:], in_=ps[:], func=AF.Square, accum_out=s4[:, c:c + 1]), r=[psr], w=[junkr, s4r])
                banks.append((ps, psr))
            s1, s1r = ss1.next()
            P.dve(lambda e: e.tensor_reduce(out=s1[:], in_=s4[:], axis=mybir.AxisListType.X, op=ALU.add), r=[s4r], w=[s1r])
            rs, rsr = rstd_from_ss(P, sg, s1, s1r, D, bufs["rs"])
            for c, (ps, psr) in enumerate(banks):
                t, tr = tmps.next()
                P.dve(lambda e, c=c, ps=ps, t=t: e.scalar_tensor_tensor(out=t[:], in0=ps[:], scalar=rs[:, 0:1], in1=mods[0][0][:, c * 512:(c + 1) * 512], op0=ALU.mult, op1=ALU.mult), r=[psr, rsr, mods[0][1]], w=[tr])
                P.pool(lambda e, c=c, t=t: e.tensor_tensor(out=xt[:, c * 512:(c + 1) * 512], in0=xt[:, c * 512:(c + 1) * 512], in1=t[:], op=ALU.add), r=[tr, xtr], w=[xtr])
            P.stq(lambda e: e.dma_start(out=xres[a:a + 128, :], in_=xt[:]), r=[xtr], w=[xr_dram])
            norm_mod_transpose(P, sg, C, xt, xtr, G, SH, kind, hT, hTr, i * 128, bufs)

        def do_chunk(t0, n, kind):
            yT, yTr = yTs.next()
            P.ld(lambda e: e.dma_start(out=yT[:, :, 0:n], in_=yv[:, :, t0:t0 + n]), w=[yTr])
            hT, hTr = hTs.next()
            for i in range(n // 128):
                do_tile(yT, yTr, hT, hTr, i, t0 + i * 128, kind)
            P.stq(lambda e: e.dma_start(out=hv[:, :, t0:t0 + n], in_=hT[:, :, 0:n]), r=[hTr], w=[dr])
        for ci, (t0, n) in enumerate(CHUNKS):
            kind = 1 if t0 < NCTX else 0
            if ci < 2:
                load_modvec_kind(P, mods, S, [2, 3, 4], kind)
            do_chunk(t0, n, kind)


def stage_ffn(P, I, S, W, C, l, xres):
    nc = P.nc
    with Stage(P) as sg:
        g4 = sg.sb([128, D], F32)
        h2, h2r = sg.sb([128, 16, 512], BF16)
        aT, aTr = sg.sb([128, 44, 512], BF16)
        wgs = Rot(sg.sb([128, 16, 128], BF16, 3))
        wus = Rot(sg.sb([128, 16, 128], BF16, 3))
        wos = Rot(sg.sb([128, 44, 256], BF16, 2))
        fx, fxr = sg.sb([128, 4, D], F32)
        xts = Rot(sg.sb([128, D], F32, 2))
        sil = Rot(sg.sb([128, 512], F32, 2))
        tmp, tmpr = sg.sb([128, D], F32)
        junk, junkr = sg.sb([128, D], BF16)
        ss1 = Rot(sg.sb([128, 1], F32, 2))
        rsb = Rot(sg.sb([128, 1], F32, 2))
        psg = Rot(sg.ps([128, 512], F32, 2))
        psu = Rot(sg.ps([128, 512], F32, 2))
        pso = Rot(sg.ps([128, 512], F32, 4))
        xr_dram = P.R()
        hv = S["h2T"].rearrange("(k p) t -> p k t", p=128)
        wiv = W["w_ffi"].rearrange("(k p) f -> p k f", p=128)
        wov = W["w_ffo"].rearrange("(k p) f -> p k f", p=128)

        def do_chunk(t0, n, kind):
            P.ld(lambda e: e.dma_start(out=h2[:, :, 0:n], in_=hv[:, :, t0:t0 + n]), w=[h2r])
            for f in range(44):
                wg, wgr = wgs.next()
                wu, wur = wus.next()
                P.ld(lambda e, f=f, wg=wg: e.dma_start(out=wg[:], in_=wiv[:, :, f * 128:(f + 1) * 128]), w=[wgr])
                P.ld(lambda e, f=f, wu=wu: e.dma_start(out=wu[:], in_=wiv[:, :, DFF + f * 128:DFF + (f + 1) * 128]), w=[wur])
                pg, pgr = psg.next()
                pu, pur = psu.next()
                for k in range(16):
                    P.pe(lambda e, k=k, wg=wg, pg=pg: e.matmul(pg[:, 0:n], lhsT=wg[:, k, :], rhs=h2[:, k, 0:n], start=(k == 0), stop=(k == 15)), r=[wgr, h2r], w=[pgr])
                for k in range(16):
                    P.pe(lambda e, k=k, wu=wu, pu=pu: e.matmul(pu[:, 0:n], lhsT=wu[:, k, :], rhs=h2[:, k, 0:n], start=(k == 0), stop=(k == 15)), r=[wur, h2r], w=[pur])
                sl, slr = sil.next()
                P.act(lambda e, sl=sl, pg=pg: e.activation(out=sl[:, 0:n], in_=pg[:, 0:n], func=AF.Silu), r=[pgr], w=[slr])
                P.dve(lambda e, f=f, sl=sl, pu=pu: e.tensor_tensor(out=aT[:, f, 0:n], in0=pu[:, 0:n], in1=sl[:, 0:n], op=ALU.mult), r=[pur, slr], w=[aTr])
            nt = n // 128
            for c in range(8):
                wo, wor = wos.next()
                P.ld(lambda e, c=c, wo=wo: e.dma_start(out=wo[:], in_=wov[:, :, c * 256:(c + 1) * 256]), w=[wor])
                for i in range(nt):
                    po, por = pso.next()
                    for f in range(44):
                        P.pe(lambda e, f=f, i=i, wo=wo, po=po: e.matmul(po[:, 0:256], lhsT=aT[:, f, i * 128:(i + 1) * 128], rhs=wo[:, f, :], start=(f == 0), stop=(f == 43)), r=[aTr, wor], w=[por])
                    P.act(lambda e, c=c, i=i, po=po: e.activation(out=fx[:, i, c * 256:(c + 1) * 256], in_=po[:, 0:256], func=AF.Copy), r=[por], w=[fxr])
            for i in range(nt):
                a = t0 + i * 128
                xt, xtr = xts.next()
                P.ld(lambda e, xt=xt, a=a: e.dma_start(out=xt[:], in_=xres[a:a + 128, :]), r=[xr_dram], w=[xtr])
                s1, s1r = ss1.next()
                P.act(lambda e, i=i, s1=s1: e.activation(out=junk[:], in_=fx[:, i, :], func=AF.Square, accum_out=s1[:]), r=[fxr], w=[junkr, s1r])
                rs, rsr = rstd_from_ss(P, sg, s1, s1r, D, rsb)
                P.dve(lambda e, i=i, rs=rs: e.scalar_tensor_tensor(out=tmp[:], in0=fx[:, i, :], scalar=rs[:, 0:1], in1=g4[0][:], op0=ALU.mult, op1=ALU.mult), r=[fxr, rsr, g4[1]], w=[tmpr])
                P.pool(lambda e, xt=xt: e.tensor_tensor(out=xt[:], in0=xt[:], in1=tmp[:], op=ALU.add), r=[tmpr, xtr], w=[xtr])
                P.stq(lambda e, xt=xt, a=a: e.dma_start(out=xres[a:a + 128, :], in_=xt[:]), r=[xtr], w=[xr_dram])
        for ci, (t0, n) in enumerate(CHUNKS):
            kind = 1 if t0 < NCTX else 0
            if ci < 2:
                load_modvec_kind(P, [g4], S, [5], kind)
            do_chunk(t0, n, kind)


def host_consts():
    n = NLAT
    row = np.repeat(np.arange(n // 64, dtype=np.float32), 64)
    col = np.tile(np.arange(64, dtype=np.float32), n // 64)
    inv = (10000.0 ** (-np.arange(16, dtype=np.float32) / 16)).astype(np.float32)
    ang = np.concatenate([row[:, None] * inv, col[:, None] * inv], -1)
    cos = np.cos(ang).astype(np.float32)
    sin = np.sin(ang).astype(np.float32)
    Ct = np.ones((128, NT), np.float32)
    St = np.zeros((128, NT), np.float32)
    for r in range(128):
        dd = r % 64
        i = dd % 32
        Ct[r, NCTX:] = cos[:, i]
        St[r, NCTX:] = (-sin[:, i]) if dd < 32 else sin[:, i]
    perm = np.zeros((128, 128), np.float32)
    for m in range(128):
        dd = m % 64
        partner = m + 32 if dd < 32 else m - 32
        perm[partner, m] = 1.0
    ident = np.eye(128, dtype=np.float32)
    return dict(ropeC=Ct, ropeS=St, perm=perm, ident=ident)


def make_in_maps(inputs, ncores=8):
    hc = host_consts()
    maps = []
    for c in range(ncores):
        b = c % 4
        m = dict(hc)
        m["xc"] = np.ascontiguousarray(np.concatenate([inputs["ctx"][b], inputs["x"][b]], 0))
        m["cvec"] = np.ascontiguousarray(np.stack([inputs["c"][b], inputs["c_ctx"]], 0))
        for k, v in inputs.items():
            if k in ("x", "c", "ctx", "c_ctx"):
                continue
            m[k] = np.ascontiguousarray(v)
        maps.append(m)
    return maps


def kernel(**inputs):
    inputs = {k: np.asarray(v) for k, v in inputs.items()}
    nc = build()
    maps = make_in_maps(inputs, 8)
    res = run_bass_kernel_spmd(nc, maps, core_ids=list(range(8)))
    out = np.stack([res.results[b]["xres"][NCTX:] for b in range(4)], 0)
    return out.astype(np.float32)
```
